# Optimizing a Trainium2 kernel written in Bass

```python
import math
import jax
import jax.numpy as jnp
from jax import lax
import numpy as np

D_MODEL = 2048
BATCH = 4
SEQ = 4096
DEPTH = 2

GRID_W = 64
CTX_LEN = 256
N_EVEN = (DEPTH + 1) // 2
N_ODD = DEPTH // 2
N_MOD = 6
SSM_WIDTH = D_MODEL // 2
SSM_GROUP = 16
SSM_GROUPS = SSM_WIDTH // SSM_GROUP
SSM_STATE = 64
N_DIR = 2
DT_MIN = 1e-3
DT_MAX = 1e-1
NA_HEAD_DIM = 128
NA_WIDTH = D_MODEL // 2
NA_HEADS = NA_WIDTH // NA_HEAD_DIM
NA_WIN_R = 8
NA_WIN_C = 16
IN_WIDTH = SSM_WIDTH + 3 * NA_WIDTH
MIX_OUT = SSM_WIDTH + NA_WIDTH
CONV_WIDTH = D_MODEL
CONV_K = 31
FFN_DIM = 5632
FFN_CONV_K = 3
EPS = 1e-6

kernel_name = 'hybrid_s5_natten_conformer_dit'


def rms_norm(x, g):
    xf = x.astype(jnp.float32)
    y = xf * lax.rsqrt(jnp.mean(jnp.square(xf), axis=-1, keepdims=True) + EPS)
    return (y * g.astype(jnp.float32)).astype(x.dtype)


def layer_norm(x, g, b):
    xf = x.astype(jnp.float32)
    mu = jnp.mean(xf, axis=-1, keepdims=True)
    var = jnp.mean(jnp.square(xf - mu), axis=-1, keepdims=True)
    y = (xf - mu) * lax.rsqrt(var + EPS)
    return (y * g.astype(jnp.float32) + b.astype(jnp.float32)).astype(x.dtype)


def modulate(h, shift, scale):
    return h * (1.0 + scale) + shift


def dwconv1d(x, w, b):
    k = w.shape[0]
    pad = (k - 1) // 2
    y = lax.conv_general_dilated(
        x, w[:, None, :].astype(x.dtype), window_strides=(1,), padding=[(pad, pad)],
        dimension_numbers=('NWC', 'WIO', 'NWC'), feature_group_count=x.shape[-1])
    return y + b.astype(x.dtype)


def s5_discretize(lam_re, lam_im, log_dt, b_re, b_im):
    lam_re = lam_re.astype(jnp.float32)
    lam_im = lam_im.astype(jnp.float32)
    b_re = b_re.astype(jnp.float32)
    b_im = b_im.astype(jnp.float32)
    dt = jnp.exp(log_dt.astype(jnp.float32))[:, None]
    mag = jnp.exp(lam_re * dt)
    a_re = mag * jnp.cos(lam_im * dt)
    a_im = mag * jnp.sin(lam_im * dt)
    den = jnp.square(lam_re) + jnp.square(lam_im)
    f_re = ((a_re - 1.0) * lam_re + a_im * lam_im) / den
    f_im = (a_im * lam_re - (a_re - 1.0) * lam_im) / den
    bb_re = f_re[..., None] * b_re - f_im[..., None] * b_im
    bb_im = f_re[..., None] * b_im + f_im[..., None] * b_re
    return a_re, a_im, bb_re, bb_im


def _complex_scan_op(left, right):
    a1r, a1i, h1r, h1i = left
    a2r, a2i, h2r, h2i = right
    return (a2r * a1r - a2i * a1i, a2r * a1i + a2i * a1r,
            a2r * h1r - a2i * h1i + h2r, a2r * h1i + a2i * h1r + h2i)


def s5_states(u, a_re, a_im, bb_re, bb_im, h0=None):
    x_re = jnp.einsum('blgc,gpc->blgp', u, bb_re)
    x_im = jnp.einsum('blgc,gpc->blgp', u, bb_im)
    if h0 is not None:
        h0_re, h0_im = h0
        x_re = x_re.at[:, 0].add(a_re * h0_re - a_im * h0_im)
        x_im = x_im.at[:, 0].add(a_re * h0_im + a_im * h0_re)
    seq_len = u.shape[1]
    ar = jnp.broadcast_to(a_re, (1, seq_len) + a_re.shape)
    ai = jnp.broadcast_to(a_im, (1, seq_len) + a_im.shape)
    _, _, h_re, h_im = lax.associative_scan(_complex_scan_op, (ar, ai, x_re, x_im), axis=1)
    return h_re, h_im


def s5_readout(h_re, h_im, c_re, c_im):
    return jnp.einsum('blgp,gcp->blgc', h_re, c_re) - jnp.einsum('blgp,gcp->blgc', h_im, c_im)


def _orient(t, d):
    return jnp.flip(t, axis=1) if d == 1 else t


def s5_mixer(u_lat, u_ctx, lam_re, lam_im, log_dt, b_re, b_im, c_re, c_im, d_skip, w_glu, ctx_out):
    dtype = u_lat.dtype
    bsz, seq_len, _ = u_lat.shape
    ctx_len = u_ctx.shape[1]
    ul = u_lat.astype(jnp.float32)
    uc = u_ctx.astype(jnp.float32)
    ul_g = ul.reshape(bsz, seq_len, SSM_GROUPS, SSM_GROUP)
    uc_g = uc.reshape(bsz, ctx_len, SSM_GROUPS, SSM_GROUP)
    d32 = d_skip.astype(jnp.float32)
    y_lat = d32 * ul
    y_ctx = d32 * uc if ctx_out else None
    for d in range(N_DIR):
        a_re, a_im, bb_re, bb_im = s5_discretize(lam_re[d], lam_im[d], log_dt[d], b_re[d], b_im[d])
        cr = c_re[d].astype(jnp.float32)
        ci = c_im[d].astype(jnp.float32)
        hc_re, hc_im = s5_states(_orient(uc_g, d), a_re, a_im, bb_re, bb_im)
        hl_re, hl_im = s5_states(_orient(ul_g, d), a_re, a_im, bb_re, bb_im,
                                 h0=(hc_re[:, -1], hc_im[:, -1]))
        y_lat = y_lat + _orient(s5_readout(hl_re, hl_im, cr, ci), d).reshape(bsz, seq_len, SSM_WIDTH)
        if ctx_out:
            y_ctx = y_ctx + _orient(s5_readout(hc_re, hc_im, cr, ci), d).reshape(bsz, ctx_len, SSM_WIDTH)
    w32 = w_glu.astype(jnp.float32)

    def glu(y):
        z = jax.nn.gelu(y)
        return (z * jax.nn.sigmoid(z @ w32)).astype(dtype)

    return glu(y_lat), (glu(y_ctx) if ctx_out else None)


def na_latent(q, k, v, k_ctx, v_ctx, rpb):
    bsz, seq_len, n_heads, head_dim = q.shape
    rows = seq_len // GRID_W
    wr = min(NA_WIN_R, rows)
    r = jnp.arange(rows)
    key_rows = jnp.clip(r - wr // 2, 0, rows - wr)[:, None] + jnp.arange(wr)[None, :]
    col = jnp.arange(GRID_W)
    col_start = jnp.clip(col - NA_WIN_C // 2, 0, GRID_W - NA_WIN_C)
    col_ok = (col[None, :] >= col_start[:, None]) & (col[None, :] < col_start[:, None] + NA_WIN_C)
    row_idx = (key_rows - r[:, None]) + (NA_WIN_R - 1)
    col_idx = jnp.clip(col[None, :] - col[:, None] + (NA_WIN_C - 1), 0, 2 * NA_WIN_C - 2)
    bias = rpb.astype(jnp.float32)[:, row_idx[:, None, :, None], col_idx[None, :, None, :]]

    qg = q.reshape(bsz, rows, GRID_W, n_heads, head_dim)
    kg = k.reshape(bsz, rows, GRID_W, n_heads, head_dim)[:, key_rows]
    vg = v.reshape(bsz, rows, GRID_W, n_heads, head_dim)[:, key_rows]
    scale = head_dim ** -0.5
    s_loc = jnp.einsum('brqhd,brikhd->bhrqik', qg, kg, preferred_element_type=jnp.float32) * scale + bias[None]
    s_loc = jnp.where(col_ok[:, None, :], s_loc, jnp.finfo(jnp.float32).min)
    n_loc = wr * GRID_W
    s_loc = s_loc.reshape(bsz, n_heads, rows, GRID_W, n_loc)
    s_ctx = jnp.einsum('brqhd,bchd->bhrqc', qg, k_ctx, preferred_element_type=jnp.float32) * scale
    p = jax.nn.softmax(jnp.concatenate([s_loc, s_ctx], axis=-1), axis=-1).astype(v.dtype)
    p_loc = p[..., :n_loc].reshape(bsz, n_heads, rows, GRID_W, wr, GRID_W)
    p_ctx = p[..., n_loc:]
    o = (jnp.einsum('bhrqik,brikhd->brqhd', p_loc, vg)
         + jnp.einsum('bhrqc,bchd->brqhd', p_ctx, v_ctx))
    return o.reshape(bsz, seq_len, n_heads * head_dim)


def ctx_attention(q, k, v):
    bsz, ctx_len, n_heads, head_dim = q.shape
    s = jnp.einsum('bqhd,bkhd->bhqk', q, k, preferred_element_type=jnp.float32) * head_dim ** -0.5
    p = jax.nn.softmax(s, axis=-1).astype(v.dtype)
    return jnp.einsum('bhqk,bkhd->bqhd', p, v).reshape(bsz, ctx_len, n_heads * head_dim)


def _heads(t):
    return t.reshape(t.shape[0], t.shape[1], NA_HEADS, NA_HEAD_DIM)


def hybrid_mixer(h, hc, w_in, lam_re, lam_im, log_dt, b_re, b_im, c_re, c_im, d_skip, w_glu, rpb, w_out, ctx_out):
    splits = [SSM_WIDTH, SSM_WIDTH + NA_WIDTH, SSM_WIDTH + 2 * NA_WIDTH]
    u, q, k, v = jnp.split(h @ w_in, splits, axis=-1)
    if ctx_out:
        uc, qc, kc, vc = jnp.split(hc @ w_in, splits, axis=-1)
    else:
        uc = hc @ w_in[:, :SSM_WIDTH]
        kc, vc = jnp.split(hc @ w_in[:, SSM_WIDTH + NA_WIDTH:], 2, axis=-1)
    y_s5, y_s5_c = s5_mixer(u, uc, lam_re, lam_im, log_dt, b_re, b_im, c_re, c_im, d_skip, w_glu, ctx_out)
    y_na = na_latent(_heads(q), _heads(k), _heads(v), _heads(kc), _heads(vc), rpb)
    y = jnp.concatenate([y_s5, y_na], axis=-1) @ w_out
    y_c = None
    if ctx_out:
        y_na_c = ctx_attention(_heads(qc), _heads(kc), _heads(vc))
        y_c = jnp.concatenate([y_s5_c, y_na_c], axis=-1) @ w_out
    return y, y_c


def conformer_conv(h, w_pw1, dw_w, dw_b, ln_g, ln_b, w_pw2):
    a, g = jnp.split(h @ w_pw1, 2, axis=-1)
    z = dwconv1d(a * jax.nn.sigmoid(g), dw_w, dw_b)
    z = jax.nn.silu(layer_norm(z, ln_g, ln_b))
    return z @ w_pw2


def conv_ffn(h, w_up, conv_w, conv_b, w_down):
    u, g = jnp.split(dwconv1d(h @ w_up, conv_w, conv_b), 2, axis=-1)
    return (jax.nn.silu(g) * u) @ w_down


def setup_inputs(seed: int = 0) -> dict:
    key = jax.random.key(seed)
    ks = iter(jax.random.split(key, 48))

    def nrm(shape, std):
        return jax.random.normal(next(ks), shape, jnp.float32) * std

    D = D_MODEL
    G, P, Cg = SSM_GROUPS, SSM_STATE, SSM_GROUP
    n = jnp.arange(P, dtype=jnp.float32)
    return {
        'x': nrm((BATCH, SEQ, D), 1.0),
        'c': nrm((BATCH, D), 1.0),
        'ctx': nrm((BATCH, CTX_LEN, D), 1.0),
        'c_ctx': nrm((D,), 1.0),
        'w_mod': nrm((DEPTH, D, N_MOD * D), 0.5 * D ** -0.5),
        'b_mod': nrm((DEPTH, N_MOD * D), 0.02),
        'g_mix': 1.0 + nrm((DEPTH, D), 0.02),
        'g_ffn': 1.0 + nrm((DEPTH, D), 0.02),
        'w_in': nrm((N_EVEN, D, IN_WIDTH), D ** -0.5),
        'ssm_lam_re': -0.5 + nrm((N_EVEN, N_DIR, G, P), 0.01),
        'ssm_lam_im': math.pi * n + nrm((N_EVEN, N_DIR, G, P), 0.01),
        'ssm_log_dt': jax.random.uniform(next(ks), (N_EVEN, N_DIR, G), jnp.float32,
                                         math.log(DT_MIN), math.log(DT_MAX)),
        'ssm_b_re': nrm((N_EVEN, N_DIR, G, P, Cg), (2 * Cg) ** -0.5),
        'ssm_b_im': nrm((N_EVEN, N_DIR, G, P, Cg), (2 * Cg) ** -0.5),
        'ssm_c_re': nrm((N_EVEN, N_DIR, G, Cg, P), 0.5),
        'ssm_c_im': nrm((N_EVEN, N_DIR, G, Cg, P), 0.5),
        'ssm_d': nrm((N_EVEN, SSM_WIDTH), 1.0),
        'ssm_w_glu': nrm((N_EVEN, SSM_WIDTH, SSM_WIDTH), SSM_WIDTH ** -0.5),
        'na_rpb': nrm((N_EVEN, NA_HEADS, 2 * NA_WIN_R - 1, 2 * NA_WIN_C - 1), 0.1),
        'w_out': nrm((N_EVEN, MIX_OUT, D), MIX_OUT ** -0.5),
        'cv_w_pw1': nrm((N_ODD, D, 2 * CONV_WIDTH), D ** -0.5),
        'cv_dw_w': nrm((N_ODD, CONV_K, CONV_WIDTH), CONV_K ** -0.5),
        'cv_dw_b': nrm((N_ODD, CONV_WIDTH), 0.02),
        'cv_ln_g': 1.0 + nrm((N_ODD, CONV_WIDTH), 0.02),
        'cv_ln_b': nrm((N_ODD, CONV_WIDTH), 0.02),
        'cv_w_pw2': nrm((N_ODD, CONV_WIDTH, D), CONV_WIDTH ** -0.5),
        'ffn_w_up': nrm((DEPTH, D, 2 * FFN_DIM), D ** -0.5),
        'ffn_conv_w': nrm((DEPTH, FFN_CONV_K, 2 * FFN_DIM), FFN_CONV_K ** -0.5),
        'ffn_conv_b': nrm((DEPTH, 2 * FFN_DIM), 0.02),
        'ffn_w_down': nrm((DEPTH, FFN_DIM, D), FFN_DIM ** -0.5),
        'g_out': 1.0 + nrm((D,), 0.02),
    }


def reference(x, c, ctx, c_ctx, w_mod, b_mod, g_mix, g_ffn, w_in, ssm_lam_re, ssm_lam_im, ssm_log_dt,
              ssm_b_re, ssm_b_im, ssm_c_re, ssm_c_im, ssm_d, ssm_w_glu, na_rpb, w_out,
              cv_w_pw1, cv_dw_w, cv_dw_b, cv_ln_g, cv_ln_b, cv_w_pw2,
              ffn_w_up, ffn_conv_w, ffn_conv_b, ffn_w_down, g_out):
    x_ctx = ctx
    s_lat = jax.nn.silu(c)[:, None, :]
    s_ctx = jax.nn.silu(c_ctx)[None, None, :]
    for i in range(DEPTH):
        reads_ctx = (i % 2 == 0)
        ctx_next = any(j % 2 == 0 for j in range(i + 1, DEPTH))
        mod = jnp.split(s_lat @ w_mod[i] + b_mod[i], N_MOD, axis=-1)
        h = modulate(rms_norm(x, g_mix[i]), mod[0], mod[1])
        if reads_ctx or ctx_next:
            mod_c = jnp.split(s_ctx @ w_mod[i] + b_mod[i], N_MOD, axis=-1)
            hc = modulate(rms_norm(x_ctx, g_mix[i]), mod_c[0], mod_c[1])
        if reads_ctx:
            e = i // 2
            y, yc = hybrid_mixer(h, hc, w_in[e], ssm_lam_re[e], ssm_lam_im[e], ssm_log_dt[e],
                                 ssm_b_re[e], ssm_b_im[e], ssm_c_re[e], ssm_c_im[e], ssm_d[e],
                                 ssm_w_glu[e], na_rpb[e], w_out[e], ctx_next)
        else:
            o = i // 2
            y = conformer_conv(h, cv_w_pw1[o], cv_dw_w[o], cv_dw_b[o], cv_ln_g[o], cv_ln_b[o], cv_w_pw2[o])
            yc = (conformer_conv(hc, cv_w_pw1[o], cv_dw_w[o], cv_dw_b[o], cv_ln_g[o], cv_ln_b[o], cv_w_pw2[o])
                  if ctx_next else None)
        x = x + mod[2] * y
        h = modulate(rms_norm(x, g_ffn[i]), mod[3], mod[4])
        x = x + mod[5] * conv_ffn(h, ffn_w_up[i], ffn_conv_w[i], ffn_conv_b[i], ffn_w_down[i])
        if ctx_next:
            x_ctx = x_ctx + mod_c[2] * yc
            hc = modulate(rms_norm(x_ctx, g_ffn[i]), mod_c[3], mod_c[4])
            x_ctx = x_ctx + mod_c[5] * conv_ffn(hc, ffn_w_up[i], ffn_conv_w[i], ffn_conv_b[i], ffn_w_down[i])
    return rms_norm(x, g_out)
```

```python
import contextlib
import numpy as np
import concourse.bass as bass
import concourse.mybir as mybir
from concourse.bass_utils import run_bass_kernel_spmd

F32 = mybir.dt.float32
BF16 = mybir.dt.bfloat16
AF = mybir.ActivationFunctionType
ALU = mybir.AluOpType
AX = mybir.AxisListType

ENGS = ("pe", "act", "dve", "pool", "sp")
SAME_ENGINE_INORDER = ("pe",)
NDMA = 32

D = 2048
NPOS = 4096
NCTX = 256
E = 2176
KVE = 2432
TB = 544
NBLK = 4
FF = 5632
EPS = 1e-6
NEG = -30000.0
TWO_PI = 6.283185307179586


class Sched:
    def __init__(self, nc, st):
        self.nc = nc
        self.ops = {e: [] for e in ENGS}
        self.cnt = {e: 0 for e in ENGS}
        self.seen = {e: {} for e in ENGS}
        self.lastw = {}
        self.readers = {}
        self.dma_cnt = [0] * NDMA
        self.dma_rr = {"sp": 0, "pool": 0, "act": 0}
        self.sems = {}
        for e in ENGS:
            self.sems["e_" + e] = st.enter_context(nc.semaphore("sem_" + e))
        for i in range(NDMA):
            self.sems["d_%d" % i] = st.enter_context(nc.semaphore("semd_%d" % i))

    def _deps(self, eng, r, w, nosw=False):
        toks = []
        for b in r:
            t = self.lastw.get(b)
            if t is not None:
                toks.append(t)
        for b in w:
            t = self.lastw.get(b)
            if t is not None:
                toks.append(t)
            toks.extend(self.readers.get(b, ()))
        waits = {}
        for (k, v, e) in toks:
            if e == eng and (eng in SAME_ENGINE_INORDER or nosw):
                continue
            if self.seen[eng].get(k, 0) >= v:
                continue
            if waits.get(k, 0) < v:
                waits[k] = v
        for k, v in waits.items():
            self.seen[eng][k] = v
        return list(waits.items())

    def _commit(self, tok, r, w):
        for b in w:
            self.lastw[b] = tok
            self.readers[b] = []
        for b in r:
            if b not in w:
                self.readers.setdefault(b, []).append(tok)

    def op(self, eng, fn, r=(), w=(), nosw=False):
        waits = self._deps(eng, r, w, nosw)
        self.cnt[eng] += 1
        tok = ("e_" + eng, self.cnt[eng], eng)
        self.ops[eng].append((waits, fn, ("e_" + eng, 1)))
        self._commit(tok, r, w)
        return tok

    def dma(self, eng, fn, r=(), w=()):
        base, n = {"sp": (0, 20), "pool": (20, 12), "act": (0, 20)}[eng]
        i = base + self.dma_rr[eng]
        self.dma_rr[eng] = (self.dma_rr[eng] + 1) % n
        waits = self._deps(eng, r, w)
        k = "d_%d" % i
        prev = 16 * self.dma_cnt[i]
        if prev > 0 and self.seen[eng].get(k, 0) < prev:
            waits.append((k, prev))
            self.seen[eng][k] = prev
        self.dma_cnt[i] += 1
        tok = (k, 16 * self.dma_cnt[i], None)
        self.ops[eng].append((waits, fn, (k, 16)))
        self._commit(tok, r, w)
        return tok

    def barrier(self):
        for eng in ENGS:
            waits = []
            for e2 in ENGS:
                k, v = "e_" + e2, self.cnt[e2]
                if e2 != eng and v > 0 and self.seen[eng].get(k, 0) < v:
                    waits.append((k, v))
                    self.seen[eng][k] = v
            for i in range(NDMA):
                k, v = "d_%d" % i, 16 * self.dma_cnt[i]
                if v > 0 and self.seen[eng].get(k, 0) < v:
                    waits.append((k, v))
                    self.seen[eng][k] = v
            self.ops[eng].append((waits, None, None))
        self.lastw = {}
        self.readers = {}

    def emit(self):
        nc = self.nc
        sems = self.sems
        ops = self.ops
        self.ops = {e: [] for e in ENGS}
        with nc.Block() as block:
            def run(engname, engobj):
                for (waits, fn, inc) in ops[engname]:
                    for (k, v) in waits:
                        engobj.wait_ge(sems[k], v)
                    if fn is not None:
                        ins = fn(engobj)
                        ins.then_inc(sems[inc[0]], inc[1])

            @block.tensor
            def _(e):
                run("pe", e)

            @block.scalar
            def _(e):
                run("act", e)

            @block.vector
            def _(e):
                run("dve", e)

            @block.gpsimd
            def _(e):
                run("pool", e)

            @block.sync
            def _(e):
                run("sp", e)


_UID = [0]


class Ring:
    def __init__(self, st, nc, name, n, shape, dt, psum=False):
        self.tiles = []
        self.ids = []
        _UID[0] += 1
        for i in range(n):
            nm = "%s_%d_%d" % (name, _UID[0], i)
            if psum:
                t = st.enter_context(nc.psum_tensor(nm, shape, dt))
            else:
                t = st.enter_context(nc.sbuf_tensor(nm, shape, dt))
            self.tiles.append(t)
            self.ids.append(nm)
        self.i = 0

    def next(self):
        t, i = self.tiles[self.i], self.ids[self.i]
        self.i = (self.i + 1) % len(self.tiles)
        return t, i


def build_program(stage=99, dbg=False):
    nc = bass.Bass("TRN2", target_bir_lowering=False)

    def din(name, shape, dt=F32):
        return nc.dram_tensor(name, list(shape), dt, kind="ExternalInput").ap()

    def dscr(name, shape, dt):
        return nc.dram_tensor(name, list(shape), dt, kind="Internal").ap()

    x_in = din("xc", [NPOS, D])
    ctx_in = din("ctxc", [NCTX, D])
    cvec = din("cvec", [2, D])
    w_mod = din("w_mod", [2, D, 6 * D])
    b_mod = din("b_mod", [2, 6 * D])
    g_mix = din("g_mix", [2, D])
    g_ffn = din("g_ffn", [2, D])
    g_out = din("g_out", [D])
    w_in = din("w_in", [D, 4096])
    lam_re = din("lam_re", [2, 4096])
    lam_im = din("lam_im", [2, 4096])
    log_dt = din("log_dt", [2, 4096])
    b_re = din("b_re", [2, 4096, 16])
    b_im = din("b_im", [2, 4096, 16])
    c_re = din("c_re", [2, 1024, 64])
    c_im = din("c_im", [2, 1024, 64])
    ssm_d = din("ssm_d", [1024])
    w_glu = din("w_glu", [1024, 1024])
    na_bias = din("na_bias", [3, 8, 128, 640])
    w_out = din("w_out", [D, D])
    pw1 = din("pw1", [D, 4096])
    dw_w = din("dw_w", [31, D])
    dw_b = din("dw_b", [D])
    ln_g = din("ln_g", [D])
    ln_b = din("ln_b", [D])
    pw2 = din("pw2", [D, D])
    w_up = din("w_up", [2, D, 2 * FF])
    fcw = din("fcw", [2, 3, 2 * FF])
    fcb = din("fcb", [2, 2 * FF])
    w_dn = din("w_dn", [2, FF, D])
    out = nc.dram_tensor("out", [2048, D], F32, kind="ExternalOutput").ap()
    dbg_out = None
    if dbg:
        dbg_out = [nc.dram_tensor("dbg%d" % i, [16, 128, E], F32, kind="ExternalOutput").ap() for i in range(2)]
        dbg_y = nc.dram_tensor("dbgy", [16, 128, E], BF16, kind="ExternalOutput").ap()

    xA = dscr("xA", [16, 128, E], F32)
    xB = dscr("xB", [16, 128, E], F32)
    WUP = [dscr("WUP%d" % l, [22, 128, 16 * 512], BF16) for l in range(2)]
    WDN = [dscr("WDN%d" % l, [16, 128, 44 * 128], BF16) for l in range(2)]
    WIN = dscr("WIN", [8, 128, 16 * 512], BF16)
    WPW1 = dscr("WPW1", [8, 128, 16 * 512], BF16)
    WOUT = dscr("WOUT", [4, 128, 16 * 512], BF16)
    WPW2 = dscr("WPW2", [4, 128, 16 * 512], BF16)
    WGLU = dscr("WGLU", [2, 128, 8 * 512], BF16)
    u_tm = dscr("u_tm", [NPOS + NCTX, 1024], BF16)
    q_fm = dscr("q_fm", [8, 128, E], BF16)
    k_fm = dscr("k_fm", [8, 128, KVE + NCTX], BF16)
    v_tm = dscr("v_tm", [KVE + NCTX, 1024], BF16)
    ymix = dscr("ymix", [16, 128, E], BF16)

    with contextlib.ExitStack() as top:
        S = Sched(nc, top)

        def sbt(st, name, shape, dt):
            _UID[0] += 1
            return st.enter_context(nc.sbuf_tensor("%s_%d" % (name, _UID[0]), shape, dt))

        def pst(st, name, shape, dt):
            _UID[0] += 1
            return st.enter_context(nc.psum_tensor("%s_%d" % (name, _UID[0]), shape, dt))

        rr = [0]

        def cast_eng():
            rr[0] = (rr[0] + 1) % 3
            return ("act", "dve", "pool")[rr[0]]

        def copy_fn(eng, o, i):
            if eng == "act":
                return lambda e: e.activation(out=o, in_=i, func=AF.Copy)
            return lambda e: e.tensor_copy(out=o, in_=i)

        ident_f = sbt(top, "ident_f", [128, 128], F32)
        ident_b = sbt(top, "ident_b", [128, 128], BF16)
        NV = 1800
        V = sbt(top, "V", [128, NV], F32)
        modv = sbt(top, "modv", [128, 2 * 6 * 2 * 16], F32)
        der = sbt(top, "der", [128, 10 * 16], F32)

        def MODV(l, j, v):
            o = ((l * 6 + j) * 2 + v) * 16
            return modv[:, o:o + 16]

        S.op("pool", lambda e: e.memset(ident_f[:], 0.0), w=["ident_f"])
        S.op("pool", lambda e: e.affine_select(out=ident_f[:], in_=ident_f[:], pattern=[[-1, 128]],
                                               compare_op=ALU.not_equal, fill=1.0, base=0,
                                               channel_multiplier=1), r=["ident_f"], w=["ident_f"])
        S.op("dve", lambda e: e.tensor_copy(out=ident_b[:], in_=ident_f[:]), r=["ident_f"], w=["ident_b"])

        vecs = []

        def addvec(name, ap1d, n):
            vecs.append((name, ap1d.rearrange("(c p) -> c p", p=128), n // 128))

        addvec("c", cvec[0], D)
        addvec("cctx", cvec[1], D)
        for l in range(2):
            addvec("g_mix%d" % l, g_mix[l], D)
            addvec("g_ffn%d" % l, g_ffn[l], D)
            for j in range(6):
                addvec("b_mod%d_%d" % (l, j), b_mod[l, j * D:(j + 1) * D], D)
        addvec("g_out", g_out, D)
        for k in range(31):
            addvec("dw_w%d" % k, dw_w[k], D)
        addvec("dw_b", dw_b, D)
        addvec("ln_g", ln_g, D)
        addvec("ln_b", ln_b, D)
        for l in range(2):
            for k in range(3):
                addvec("fcw%d_%d" % (l, k), fcw[l, k], 2 * FF)
            addvec("fcb%d" % l, fcb[l], 2 * FF)
        for d in range(2):
            addvec("lam_re%d" % d, lam_re[d], 4096)
            addvec("lam_im%d" % d, lam_im[d], 4096)
            addvec("log_dt%d" % d, log_dt[d], 4096)
        voff = {}
        o = 0
        for (name, ap2, nch) in vecs:
            voff[name] = o
            o += nch
        assert o <= NV, o
        nrows = o

        def VC(name, c0=0, n=16):
            return V[:, voff[name] + c0: voff[name] + c0 + n]

        with contextlib.ExitStack() as ph:
            rowt = Ring(ph, nc, "rowt", 2, [128, 128], F32)
            pvt = Ring(ph, nc, "pvt", 2, [128, 128], F32, psum=True)
            for t in range((nrows + 127) // 128):
                r0, r1 = t * 128, min(nrows, t * 128 + 128)
                tl, tid = rowt.next()
                for (name, ap2, nch) in vecs:
                    a, b = voff[name], voff[name] + nch
                    lo, hi = max(a, r0), min(b, r1)
                    if lo < hi:
                        S.dma("sp", (lambda e, tl=tl, lo=lo, hi=hi, a=a, ap2=ap2, r0=r0:
                                     e.dma_start(out=tl[lo - r0:hi - r0, :], in_=ap2[lo - a:hi - a, :])),
                              w=[tid])
                pt, pid = pvt.next()
                n = r1 - r0
                S.op("pe", (lambda e, pt=pt, tl=tl, n=n: e.transpose(out=pt[:, :n], in_=tl[:n, :],
                                                                    identity=ident_f[:n, :n])),
                     r=[tid, "ident_f"], w=[pid])
                S.op("dve", (lambda e, pt=pt, n=n, r0=r0: e.tensor_copy(out=V[:, r0:r0 + n], in_=pt[:, :n])),
                     r=[pid], w=["V"])
            S.barrier()
            S.emit()

        with contextlib.ExitStack() as ph:
            s_bf = sbt(ph, "s_bf", [128, 2, 16], BF16)
            S.op("act", lambda e: e.activation(out=s_bf[:, 0, :], in_=VC("c"), func=AF.Silu), r=["V"], w=["s_bf"])
            S.op("act", lambda e: e.activation(out=s_bf[:, 1, :], in_=VC("cctx"), func=AF.Silu), r=["V"], w=["s_bf"])
            wf = Ring(ph, nc, "wmf", 3, [128, 2048], F32)
            wb = Ring(ph, nc, "wmb", 3, [128, 2048], BF16)
            pm = Ring(ph, nc, "pm", 2, [128, 512], F32, psum=True)
            msum = Ring(ph, nc, "msum", 2, [128, 32], F32)
            for l in range(2):
                for j in range(6):
                    pt, pid = pm.next()
                    for kc in range(16):
                        f, fid = wf.next()
                        b, bid = wb.next()
                        S.dma("sp", (lambda e, f=f, l=l, j=j, kc=kc: e.dma_start(
                            out=f[:], in_=w_mod[l, kc * 128:(kc + 1) * 128, j * D:(j + 1) * D])), w=[fid])
                        ce = cast_eng()
                        S.op(ce, copy_fn(ce, b[:], f[:]), r=[fid], w=[bid])
                        for m in range(16):
                            S.op("pe", (lambda e, pt=pt, b=b, m=m, kc=kc: e.matmul(
                                pt[:, kc * 32 + 2 * m:kc * 32 + 2 * m + 2], lhsT=b[:, m * 128:(m + 1) * 128],
                                rhs=s_bf[:, :, kc], start=True, stop=True)), r=[bid, "s_bf"], w=[pid])
                    ms, msid = msum.next()
                    S.op("dve", (lambda e, pt=pt, ms=ms: e.tensor_reduce(
                        out=ms[:], in_=pt[:].rearrange("p (k c) -> p c k", k=16), axis=AX.X, op=ALU.add)),
                         r=[pid], w=[msid])
                    for v in range(2):
                        S.op("dve", (lambda e, ms=ms, l=l, j=j, v=v: e.tensor_tensor(
                            out=MODV(l, j, v), in0=ms[:, v:32:2], in1=VC("b_mod%d_%d" % (l, j)), op=ALU.add)),
                             r=[msid, "V"], w=["modv"])

            def derive(idx, gname, l, jscale, v):
                o = idx * 16
                S.op("dve", lambda e: e.tensor_scalar(out=der[:, o:o + 16], in0=MODV(l, jscale, v), scalar1=1.0,
                                                      scalar2=None, op0=ALU.add), r=["modv"], w=["der"])
                S.op("dve", lambda e: e.tensor_tensor(out=der[:, o:o + 16], in0=der[:, o:o + 16], in1=VC(gname),
                                                      op=ALU.mult), r=["der", "V"], w=["der"])
            derive(0, "g_mix0", 0, 1, 0)
            derive(1, "g_mix0", 0, 1, 1)
            derive(2, "g_ffn0", 0, 4, 0)
            derive(3, "g_mix1", 1, 1, 0)
            derive(4, "g_ffn1", 1, 4, 0)
            S.barrier()
            S.emit()

        def DER(i):
            return der[:, i * 16:(i + 1) * 16]

        with contextlib.ExitStack() as ph:
            cf = Ring(ph, nc, "cvf", 2, [128, 8192], F32)
            cb = Ring(ph, nc, "cvb", 2, [128, 8192], BF16)

            def convert(parts, dst2, F):
                f, fid = cf.next()
                b, bid = cb.next()
                for (vf, src) in parts:
                    S.dma("sp", (lambda e, vf=vf, src=src, f=f: e.dma_start(out=vf(f), in_=src)), w=[fid])
                ce = cast_eng()
                S.op(ce, copy_fn(ce, b[:, :F], f[:, :F]), r=[fid], w=[bid])
                S.dma("pool", lambda e: e.dma_start(out=dst2, in_=b[:, :F]), r=[bid], w=["wscr"])

            for (src, dst, ng) in ((w_in, WIN, 8), (w_out, WOUT, 4)):
                v = src.rearrange("(kc p) (g c) -> g p kc c", p=128, c=512)
                for G in range(ng):
                    convert([((lambda f: f[:, :8192].rearrange("p (kc c) -> p kc c", kc=16)), v[G])], dst[G], 8192)
            v = w_glu.rearrange("(kc p) (g c) -> g p kc c", p=128, c=512)
            for G in range(2):
                convert([((lambda f: f[:, :4096].rearrange("p (kc c) -> p kc c", kc=8)), v[G])], WGLU[G], 4096)
            S.barrier()
            S.emit()

        bg_jobs = []
        for l in range(2):
            vu = w_up[l].rearrange("(kc p) (ug g c) -> ug g p kc c", p=128, ug=2, c=256)
            for G in range(22):
                for kq in range(4):
                    parts = []
                    for ug in range(2):
                        parts.append(((lambda f, ug=ug: f[:, :2048].rearrange("p (kc u c) -> p kc u c", kc=4, u=2)[:, :, ug, :]),
                                      vu[ug, G][:, kq * 4:(kq + 1) * 4, :]))
                    bg_jobs.append((parts, WUP[l][G][:, kq * 2048:(kq + 1) * 2048], 2048))
            vd = w_dn[l].rearrange("(j p) (m c) -> m p j c", p=128, c=128)
            for m in range(16):
                for jq in range(4):
                    bg_jobs.append(([((lambda f: f[:, :1408].rearrange("p (j c) -> p j c", j=11)), vd[m][:, jq * 11:(jq + 1) * 11, :])],
                                    WDN[l][m][:, jq * 1408:(jq + 1) * 1408], 1408))
            if l == 0:
                for (src, dst, ng) in ((pw1, WPW1, 8), (pw2, WPW2, 4)):
                    vv = src.rearrange("(kc p) (g c) -> g p kc c", p=128, c=512)
                    for G in range(ng):
                        for kq in range(4):
                            bg_jobs.append(([((lambda f: f[:, :2048].rearrange("p (kc c) -> p kc c", kc=4)), vv[G][:, kq * 4:(kq + 1) * 4, :])],
                                            dst[G][:, kq * 2048:(kq + 1) * 2048], 2048))
        bg_state = {"i": 0, "rr": 0}

        def bg_emit(n, cf, cb):
            for _ in range(n):
                if bg_state["i"] >= len(bg_jobs):
                    return
                parts, dst2, F = bg_jobs[bg_state["i"]]
                bg_state["i"] += 1
                f, fid = cf.next()
                b, bid = cb.next()
                for (vf, src) in parts:
                    S.dma("sp", (lambda e, vf=vf, src=src, f=f: e.dma_start(out=vf(f), in_=src)), w=[fid])
                bg_state["rr"] += 1
                ce = "act"
                S.op(ce, copy_fn(ce, b[:, :F], f[:, :F]), r=[fid], w=[bid])
                S.dma("pool", (lambda e, dst2=dst2, b=b, F=F: e.dma_start(out=dst2, in_=b[:, :F])), r=[bid], w=["wscr"])

        def l0_proj(do_mixer):
            with contextlib.ExitStack() as ph:
                xt = Ring(ph, nc, "xt", 2, [128, D], F32)
                xn = Ring(ph, nc, "xn", 2, [128, D], BF16)
                junk = sbt(ph, "junk", [128, D], BF16)
                ss = Ring(ph, nc, "ss", 4, [128, 2], F32)
                hfm = Ring(ph, nc, "hfm", 2, [128, 16, 512], BF16)
                ptr = Ring(ph, nc, "ptr", 2, [128, 4, 128], BF16, psum=True)
                pxf = Ring(ph, nc, "pxf", 2, [128, 4, 128], F32, psum=True)
                pmm = Ring(ph, nc, "pmm", 3, [128, 512], F32, psum=True)
                xo = Ring(ph, nc, "xo", 2, [128, 4, 128], F32)
                wg = Ring(ph, nc, "wg", 2, [128, 16, 512], BF16)
                ob = Ring(ph, nc, "ob", 3, [128, 512], BF16)
                l_bcf = Ring(ph, nc, "l_bcf", 4, [128, 2048], F32)
                l_bcb = Ring(ph, nc, "l_bcb", 2, [128, 2048], BF16)
                l_tiles = [0]
                ngroups = 9 if do_mixer else 5
                for g in range(ngroups):
                    is_ctx = (g == 8)
                    ntile = 2 if is_ctx else 4
                    ntok = ntile * 128
                    h, hid = hfm.next()
                    Av, Bv = (DER(1), MODV(0, 0, 1)) if is_ctx else (DER(0), MODV(0, 0, 0))
                    for ti in range(ntile):
                        src = ctx_in if is_ctx else x_in
                        p0 = ti * 128 if is_ctx else g * 512 + ti * 128
                        if do_mixer:
                            l_tiles[0] += 1
                            want = (l_tiles[0] * int(len(bg_jobs) * L0_JOB_FRAC)) // 34
                            bg_emit(want - bg_state["i"], l_bcf, l_bcb)
                        x_, xid = xt.next()
                        S.dma("sp", (lambda e, x_=x_, src=src, p0=p0: e.dma_start(out=x_[:], in_=src[p0:p0 + 128, :])),
                              w=[xid])
                        s_, sid = ss.next()
                        S.op("act", (lambda e, x_=x_, s_=s_: e.activation(out=junk[:], in_=x_[:], func=AF.Square,
                                                                          accum_out=s_[:, 0:1])),
                             r=[xid], w=["junk", sid])
                        S.op("act", (lambda e, s_=s_: e.activation(out=s_[:, 1:2], in_=s_[:, 0:1], func=AF.Sqrt,
                                                                   scale=1.0 / D, bias=EPS)), r=[sid], w=[sid])
                        S.op("dve", (lambda e, s_=s_: e.reciprocal(out=s_[:, 1:2], in_=s_[:, 1:2])), r=[sid], w=[sid])
                        n_, nid = xn.next()
                        S.op("dve", (lambda e, n_=n_, x_=x_, s_=s_: e.tensor_scalar(
                            out=n_[:], in0=x_[:], scalar1=s_[:, 1:2], scalar2=None, op0=ALU.mult)),
                             r=[xid, sid], w=[nid])
                        for q4 in range(4):
                            pt, pid = ptr.next()
                            for i in range(4):
                                kc = q4 * 4 + i
                                S.op("pe", (lambda e, pt=pt, i=i, n_=n_, kc=kc: e.transpose(
                                    out=pt[:, i, :], in_=n_[:, kc * 128:(kc + 1) * 128], identity=ident_b[:])),
                                     r=[nid, "ident_b"], w=[pid])
                            for i in range(4):
                                kc = q4 * 4 + i
                                S.op("act", (lambda e, pt=pt, i=i, h=h, kc=kc, ti=ti, Av=Av, Bv=Bv: e.activation(
                                    out=h[:, kc, ti * 128:(ti + 1) * 128], in_=pt[:, i, :], func=AF.Identity,
                                    scale=Av[:, kc:kc + 1], bias=Bv[:, kc:kc + 1])),
                                     r=[pid, "der", "modv"], w=[hid])
                        if (not is_ctx) and p0 < E:
                            for q4 in range(4):
                                pf, pfid = pxf.next()
                                for i in range(4):
                                    kc = q4 * 4 + i
                                    S.op("pe", (lambda e, pf=pf, i=i, x_=x_, kc=kc: e.transpose(
                                        out=pf[:, i, :], in_=x_[:, kc * 128:(kc + 1) * 128], identity=ident_f[:])),
                                         r=[xid, "ident_f"], w=[pfid])
                                o_, oid = xo.next()
                                S.op("dve", (lambda e, o_=o_, pf=pf: e.tensor_copy(out=o_[:], in_=pf[:])),
                                     r=[pfid], w=[oid])
                                S.dma("pool", (lambda e, o_=o_, q4=q4, p0=p0: e.dma_start(
                                    out=xA[q4 * 4:(q4 + 1) * 4, :, p0:p0 + 128].rearrange("c p t -> p c t"),
                                    in_=o_[:])), r=[oid], w=["xA"])
                    if not do_mixer:
                        continue
                    base = 0 if is_ctx else g * 512
                    nq = 0 if is_ctx else min(512, max(0, E - base))
                    nkv = ntok if is_ctx else min(512, max(0, KVE - base))
                    for G in range(8):
                        kind = ("u", "u", "q", "q", "k", "k", "v", "v")[G]
                        n = {"u": ntok, "q": nq, "k": nkv, "v": nkv}[kind]
                        if n == 0:
                            continue
                        w_, wid = wg.next()
                        S.dma("sp", (lambda e, w_=w_, G=G: e.dma_start(
                            out=w_[:].rearrange("p a b -> p (a b)"), in_=WIN[G])), r=["wscr"], w=[wid])
                        if kind in ("u", "v"):
                            for ti in range(n // 128):
                                pt, pid = pmm.next()
                                for kc in range(16):
                                    S.op("pe", (lambda e, pt=pt, h=h, kc=kc, ti=ti, w_=w_: e.matmul(
                                        pt[:, :], lhsT=h[:, kc, ti * 128:(ti + 1) * 128], rhs=w_[:, kc, :],
                                        start=(kc == 0), stop=(kc == 15))), r=[hid, wid], w=[pid])
                                o_, oid = ob.next()
                                ce = ("act", "dve")[ti % 2]
                                S.op(ce, copy_fn(ce, o_[:, :], pt[:, :]), r=[pid], w=[oid])
                                if kind == "u":
                                    row = (NPOS if is_ctx else base) + ti * 128
                                    dst = u_tm[row:row + 128, (G % 2) * 512:(G % 2) * 512 + 512]
                                else:
                                    row = (KVE if is_ctx else base) + ti * 128
                                    dst = v_tm[row:row + 128, (G % 2) * 512:(G % 2) * 512 + 512]
                                S.dma("pool", (lambda e, o_=o_, dst=dst: e.dma_start(out=dst, in_=o_[:, :])),
                                      r=[oid], w=["uv_scr"])
                        else:
                            for m in range(4):
                                pt, pid = pmm.next()
                                for kc in range(16):
                                    S.op("pe", (lambda e, pt=pt, h=h, kc=kc, m=m, w_=w_, n=n: e.matmul(
                                        pt[:, :n], lhsT=w_[:, kc, m * 128:(m + 1) * 128], rhs=h[:, kc, :n],
                                        start=(kc == 0), stop=(kc == 15))), r=[hid, wid], w=[pid])
                                o_, oid = ob.next()
                                ce = ("act", "dve")[m % 2]
                                S.op(ce, copy_fn(ce, o_[:, :n], pt[:, :n]), r=[pid], w=[oid])
                                hd = (G % 2) * 4 + m
                                if kind == "q":
                                    dst = q_fm[hd, :, base:base + n]
                                else:
                                    c0 = KVE if is_ctx else base
                                    dst = k_fm[hd, :, c0:c0 + n]
                                S.dma("pool", (lambda e, o_=o_, dst=dst, n=n: e.dma_start(out=dst, in_=o_[:, :n])),
                                      r=[oid], w=["qk_scr"])
                S.barrier()
                S.emit()

        ones_f = sbt(top, "ones_f", [128, 128], F32)
        S.op("pool", lambda e: e.memset(ones_f[:], 1.0), w=["ones_f"])

        def load_x_cols(ring, xsrc, kc, c0, n, lo, hi):
            t, tid = ring.next()
            a, b = max(c0, lo), min(c0 + n, hi)
            if a > c0 or b < c0 + n:
                S.op("pool", (lambda e, t=t, n=n: e.memset(t[:, :n], 0.0)), w=[tid])
            S.dma("sp", (lambda e, t=t, a=a, b=b, c0=c0, kc=kc: e.dma_start(out=t[:, a - c0:b - c0], in_=xsrc[kc, :, a:b])),
                  r=["xsrc"], w=[tid])
            return t, tid

        def norm_block(ph, xsrc, c0, n, Av, Bv, h, hid, rings):
            xs, sq, pss, rbc = rings
            H = n // 2
            for sbk in range(2):
                o0 = sbk * H
                pt, pid = pss.next()
                for kc in range(16):
                    t, tid = load_x_cols(xs, xsrc, kc, c0 + o0, H, 0, E)
                    S.op("act", (lambda e, t=t, H=H: e.activation(out=t[:, :H], in_=t[:, :H], func=AF.Square)),
                         r=[tid], w=[tid])
                    S.op("pe", (lambda e, pt=pt, t=t, kc=kc, H=H: e.matmul(
                        pt[:, :H], lhsT=ones_f[:], rhs=t[:, :H], start=(kc == 0), stop=(kc == 15))),
                         r=[tid, "ones_f"], w=[pid])
                r_, rid = rbc.next()
                S.op("act", (lambda e, r_=r_, pt=pt, H=H: e.activation(out=r_[:, :H], in_=pt[:, :H], func=AF.Sqrt,
                                                                      scale=1.0 / D, bias=EPS)), r=[pid], w=[rid])
                S.op("dve", (lambda e, r_=r_, H=H: e.reciprocal(out=r_[:, :H], in_=r_[:, :H])), r=[rid], w=[rid])
                for kc in range(16):
                    t, tid = load_x_cols(xs, xsrc, kc, c0 + o0, H, 0, E)
                    S.op("dve", (lambda e, t=t, r_=r_, H=H: e.tensor_tensor(out=t[:, :H], in0=t[:, :H], in1=r_[:, :H],
                                                                            op=ALU.mult)), r=[tid, rid], w=[tid])
                    S.op("act", (lambda e, t=t, h=h, kc=kc, H=H, o0=o0, Av=Av, Bv=Bv: e.activation(
                        out=h[:, kc, o0:o0 + H], in_=t[:, :H], func=AF.Identity, scale=Av[:, kc:kc + 1], bias=Bv[:, kc:kc + 1])),
                         r=[tid, "der", "modv"], w=[hid])

        def ffn_phase(l, xsrc, xdst, Av, Bv, gate):
            NH = TB + 2
            with contextlib.ExitStack() as ph:
                xs = Ring(ph, nc, "f_xs", 4, [128, NH // 2], F32)
                sq = None
                pss = Ring(ph, nc, "f_pss", 2, [128, 512], F32, psum=True)
                rbc = Ring(ph, nc, "f_rbc", 2, [128, NH // 2], F32)
                hr = Ring(ph, nc, "f_h", 1, [128, 16, NH], BF16)
                act = sbt(ph, "f_act", [128, 44, TB], BF16)
                wu = Ring(ph, nc, "f_wu", 2, [128, 16, 512], BF16)
                wd = Ring(ph, nc, "f_wd", 2, [128, 44, 128], BF16)
                pu = Ring(ph, nc, "f_pu", 2, [128, 2, 512], F32, psum=True)
                cv = Ring(ph, nc, "f_cv", 4, [128, TB], F32)
                sg = Ring(ph, nc, "f_sg", 2, [128, TB], F32)
                pdn = Ring(ph, nc, "f_pd", 1, [128, 2, 512], F32, psum=True)
                xo = Ring(ph, nc, "f_xo", 2, [128, TB], F32)
                fw = "fcw%d_" % l
                H2 = NH // 2
                for blk in range(NBLK):
                    t0 = blk * TB
                    h, hid = hr.next()
                    norm_block(ph, xsrc, t0 - 1, NH, Av, Bv, h, hid, (xs, sq, pss, rbc))
                    for G in range(22):
                        w_, wid = wu.next()
                        S.dma("sp", (lambda e, w_=w_, G=G: e.dma_start(
                            out=w_[:].rearrange("p a b -> p (a b)"), in_=WUP[l][G])), r=["wscr"], w=[wid])
                        for jj in range(2):
                            res = []
                            for ug in range(2):
                                ch = ug * 44 + G * 2 + jj
                                pt, pid = pu.next()
                                for hf in range(2):
                                    for kc in range(16):
                                        S.op("pe", (lambda e, pt=pt, w_=w_, kc=kc, ug=ug, jj=jj, hf=hf, h=h: e.matmul(
                                            pt[:, hf, :H2], lhsT=w_[:, kc, ug * 256 + jj * 128: ug * 256 + jj * 128 + 128],
                                            rhs=h[:, kc, hf * H2:(hf + 1) * H2], start=(kc == 0), stop=(kc == 15))),
                                             r=[hid, wid], w=[pid])
                                if blk == 0:
                                    S.op("dve", (lambda e, pt=pt: e.memset(pt[:, 0, 0:1], 0.0)), r=[pid], w=[pid])
                                c_, cid = cv.next()
                                def seg(off):
                                    a = []
                                    split = H2 - off
                                    a.append((0, off, 0, min(split, TB)))
                                    if split < TB:
                                        a.append((1, 0, split, TB))
                                    return a
                                first = True
                                for k, wname in ((1, fw + "1"), (0, fw + "0"), (2, fw + "2")):
                                    wv = V[:, voff[wname] + ch: voff[wname] + ch + 1]
                                    for (hf, pc, lo, hi) in seg(k):
                                        if first:
                                            bv = V[:, voff["fcb%d" % l] + ch: voff["fcb%d" % l] + ch + 1]
                                            S.op("act", (lambda e, c_=c_, pt=pt, hf=hf, pc=pc, lo=lo, hi=hi, wv=wv, bv=bv:
                                                         e.activation(out=c_[:, lo:hi], in_=pt[:, hf, pc:pc + hi - lo],
                                                                      func=AF.Identity, scale=wv, bias=bv)),
                                                 r=[pid, "V"], w=[cid])
                                        else:
                                            S.op("dve", (lambda e, c_=c_, pt=pt, hf=hf, pc=pc, lo=lo, hi=hi, wv=wv:
                                                         e.scalar_tensor_tensor(out=c_[:, lo:hi], in0=pt[:, hf, pc:pc + hi - lo],
                                                                                scalar=wv, in1=c_[:, lo:hi],
                                                                                op0=ALU.mult, op1=ALU.add)),
                                                 r=[pid, cid, "V"], w=[cid])
                                    first = False
                                res.append((c_, cid))
                            (cu, cuid), (cg, cgid) = res
                            s_, sid = sg.next()
                            S.op("act", (lambda e, s_=s_, cg=cg: e.activation(out=s_[:], in_=cg[:], func=AF.Silu)),
                                 r=[cgid], w=[sid])
                            j = G * 2 + jj
                            S.op("pool", (lambda e, s_=s_, cu=cu, j=j: e.tensor_tensor(out=act[:, j, :], in0=s_[:], in1=cu[:],
                                                                                      op=ALU.mult)),
                                 r=[sid, cuid], w=["f_act"])
                    HB = TB // 2
                    for m in range(16):
                        w_, wid = wd.next()
                        S.dma("sp", (lambda e, w_=w_, m=m: e.dma_start(
                            out=w_[:].rearrange("p a b -> p (a b)"), in_=WDN[l][m])), r=["wscr"], w=[wid])
                        o_, oid = load_x_cols(xo, xsrc, m, t0, TB, 0, E)
                        pt, pid = pdn.next()
                        for hf in range(2):
                            for j in range(44):
                                S.op("pe", (lambda e, pt=pt, w_=w_, j=j, hf=hf: e.matmul(
                                    pt[:, hf, :HB], lhsT=w_[:, j, :], rhs=act[:, j, hf * HB:(hf + 1) * HB],
                                    start=(j == 0), stop=(j == 43))), r=["f_act", wid], w=[pid])
                        for hf in range(2):
                            S.op("dve", (lambda e, o_=o_, pt=pt, hf=hf, m=m: e.scalar_tensor_tensor(
                                out=o_[:, hf * HB:(hf + 1) * HB], in0=pt[:, hf, :HB], scalar=gate[:, m:m + 1],
                                in1=o_[:, hf * HB:(hf + 1) * HB], op0=ALU.mult, op1=ALU.add)),
                                 r=[pid, oid, "modv"], w=[oid])
                        S.dma("pool", (lambda e, o_=o_, m=m, t0=t0: e.dma_start(out=xdst[m, :, t0:t0 + TB], in_=o_[:, :TB])),
                              r=[oid], w=["xdst"])
                S.barrier()
                S.emit()

        def proj_residual(ph, wscr, actt, actid, xsrc, xdst, t0, gate, rings):
            wr, pdn, xo = rings
            HB = TB // 2
            for m in range(16):
                w_, wid = wr.next()
                S.dma("sp", (lambda e, w_=w_, m=m: e.dma_start(
                    out=w_[:], in_=wscr[m // 4].rearrange("p (a b) -> p a b", a=16)[:, :, (m % 4) * 128:(m % 4) * 128 + 128])),
                      r=["wscr"], w=[wid])
                o_, oid = load_x_cols(xo, xsrc, m, t0, TB, 0, E)
                pt, pid = pdn.next()
                for hf in range(2):
                    for kc in range(16):
                        S.op("pe", (lambda e, pt=pt, w_=w_, kc=kc, hf=hf: e.matmul(
                            pt[:, hf, :HB], lhsT=w_[:, kc, :], rhs=actt[:, kc, hf * HB:(hf + 1) * HB],
                            start=(kc == 0), stop=(kc == 15))), r=[actid, wid], w=[pid])
                for hf in range(2):
                    S.op("dve", (lambda e, o_=o_, pt=pt, hf=hf, m=m: e.scalar_tensor_tensor(
                        out=o_[:, hf * HB:(hf + 1) * HB], in0=pt[:, hf, :HB], scalar=gate[:, m:m + 1],
                        in1=o_[:, hf * HB:(hf + 1) * HB], op0=ALU.mult, op1=ALU.add)),
                         r=[pid, oid, "modv"], w=[oid])
                S.dma("pool", (lambda e, o_=o_, m=m, t0=t0: e.dma_start(out=xdst[m, :, t0:t0 + TB], in_=o_[:, :TB])),
                      r=[oid], w=["xdst"])

        def mixout_phase(xsrc, xdst, gate):
            with contextlib.ExitStack() as ph:
                yt = Ring(ph, nc, "m_y", 2, [128, 16, TB], BF16)
                wr = Ring(ph, nc, "m_w", 3, [128, 16, 128], BF16)
                pdn = Ring(ph, nc, "m_pd", 2, [128, 2, 512], F32, psum=True)
                xo = Ring(ph, nc, "m_xo", 3, [128, TB], F32)
                for blk in range(NBLK):
                    t0 = blk * TB
                    y_, yid = yt.next()
                    S.dma("sp", (lambda e, y_=y_, t0=t0: e.dma_start(
                        out=y_[:], in_=ymix[:, :, t0:t0 + TB].rearrange("c p t -> p c t"))), r=["ymix"], w=[yid])
                    proj_residual(ph, WOUT, y_, yid, xsrc, xdst, t0, gate, (wr, pdn, xo))
                S.barrier()
                S.emit()

        def conformer_phase(xsrc, xdst, Av, Bv, gate):
            NH = TB + 30
            H2 = NH // 2
            HB = TB // 2
            with contextlib.ExitStack() as ph:
                xs = Ring(ph, nc, "c_xs", 4, [128, H2], F32)
                pss = Ring(ph, nc, "c_pss", 1, [128, 512], F32, psum=True)
                rbc = Ring(ph, nc, "c_rbc", 2, [128, H2], F32)
                hr = Ring(ph, nc, "c_h", 1, [128, 16, NH], BF16)
                w1 = Ring(ph, nc, "c_w1", 4, [128, 16, 128], BF16)
                pa = Ring(ph, nc, "c_pa", 2, [128, 2, 512], F32, psum=True)
                pzd = Ring(ph, nc, "c_pzd", 1, [128, 2, 512], F32, psum=True)
                sgr = Ring(ph, nc, "c_sg", 2, [128, NH], F32)
                cir = Ring(ph, nc, "c_ci", 2, [128, NH], BF16)
                dgr = Ring(ph, nc, "c_dg", 2, [128, 31, 128], BF16)
                zbuf = sbt(ph, "c_z", [128, 16, TB], F32)
                zs = sbt(ph, "c_zs", [128, 16, TB], BF16)
                mean = sbt(ph, "c_mean", [128, TB], F32)
                rstd = sbt(ph, "c_rstd", [128, TB], F32)
                ones_b = sbt(ph, "c_ones", [128, 128], BF16)
                S.op("dve", lambda e: e.tensor_copy(out=ones_b[:], in_=ones_f[:]), r=["ones_f"], w=["c_ones"])
                wr = Ring(ph, nc, "c_w2", 2, [128, 16, 128], BF16)
                xo = Ring(ph, nc, "c_xo", 2, [128, TB], F32)
                v2 = lambda t: t[:, :].rearrange("p (a b) -> p a b", a=2)
                for blk in range(NBLK):
                    t0 = blk * TB
                    h, hid = hr.next()
                    norm_block(ph, xsrc, t0 - 15, NH, Av, Bv, h, hid, (xs, None, pss, rbc))
                    for c in range(16):
                        ws = []
                        for part in range(2):
                            cc = part * 16 + c
                            w_, wid = w1.next()
                            S.dma("sp", (lambda e, w_=w_, cc=cc: e.dma_start(
                                out=w_[:], in_=WPW1[cc // 4].rearrange("p (a b) -> p a b", a=16)[:, :, (cc % 4) * 128:(cc % 4) * 128 + 128])),
                                  r=["wscr"], w=[wid])
                            ws.append((w_, wid))
                        dg, dgid = dgr.next()
                        for k in range(31):
                            wv = V[:, voff["dw_w%d" % k] + c: voff["dw_w%d" % k] + c + 1]
                            S.op("act", (lambda e, dg=dg, k=k, wv=wv: e.activation(out=dg[:, k, :], in_=ident_b[:], func=AF.Identity,
                                                                                 scale=wv)), r=["ident_b", "V"], w=[dgid])
                        pts = []
                        for part in range(2):
                            w_, wid = ws[part]
                            pt, pid = pa.next()
                            for hf in range(2):
                                for kc in range(16):
                                    S.op("pe", (lambda e, pt=pt, w_=w_, kc=kc, hf=hf, h=h: e.matmul(
                                        pt[:, hf, :H2], lhsT=w_[:, kc, :], rhs=h[:, kc, hf * H2:(hf + 1) * H2],
                                        start=(kc == 0), stop=(kc == 15))), r=[hid, wid], w=[pid])
                            pts.append((pt, pid))
                        (pA, pAid), (pG, pGid) = pts
                        sg_, sgid = sgr.next()
                        ci, ciid = cir.next()
                        S.op("act", (lambda e, sg_=sg_, pG=pG: e.activation(
                            out=v2(sg_), in_=pG[:, :, :H2], func=AF.Sigmoid)), r=[pGid], w=[sgid])
                        S.op("dve", (lambda e, ci=ci, pA=pA, sg_=sg_: e.tensor_tensor(
                            out=v2(ci), in0=pA[:, :, :H2], in1=v2(sg_), op=ALU.mult)), r=[pAid, sgid], w=[ciid])
                        if blk == 0:
                            S.op("dve", (lambda e, ci=ci: e.memset(ci[:, 0:15], 0.0)), r=[ciid], w=[ciid])
                        pz, pzid = pzd.next()
                        for hf in range(2):
                            for k in range(31):
                                S.op("pe", (lambda e, pz=pz, dg=dg, k=k, hf=hf, ci=ci: e.matmul(
                                    pz[:, hf, :HB], lhsT=dg[:, k, :], rhs=ci[:, hf * HB + k:hf * HB + k + HB],
                                    start=(k == 0), stop=(k == 30))), r=[dgid, ciid], w=[pzid])
                        bv = V[:, voff["dw_b"] + c: voff["dw_b"] + c + 1]
                        S.op("act", (lambda e, c=c, pz=pz, bv=bv: e.activation(
                            out=zbuf[:, c, :].rearrange("p (a b) -> p a b", a=2), in_=pz[:, :, :HB], func=AF.Identity, bias=bv)),
                             r=[pzid, "V"], w=["c_z%d" % c])
                        S.op("act", (lambda e, c=c: e.activation(out=zs[:, c, :], in_=zbuf[:, c, :], func=AF.Square)),
                             r=["c_z%d" % c], w=["c_zs"])
                    pS, pSid = pa.next()
                    pQ, pQid = pa.next()
                    for hf in range(2):
                        for c in range(16):
                            S.op("pe", (lambda e, pS=pS, hf=hf, c=c: e.matmul(pS[:, hf, :HB], lhsT=ones_f[:], rhs=zbuf[:, c, hf * HB:(hf + 1) * HB],
                                                                             start=(c == 0), stop=(c == 15))), r=["c_z%d" % c, "ones_f"], w=[pSid])
                    for hf in range(2):
                        for c in range(16):
                            S.op("pe", (lambda e, pQ=pQ, hf=hf, c=c: e.matmul(pQ[:, hf, :HB], lhsT=ones_b[:], rhs=zs[:, c, hf * HB:(hf + 1) * HB],
                                                                             start=(c == 0), stop=(c == 15))), r=["c_zs", "c_ones"], w=[pQid])
                    S.op("act", (lambda e, pS=pS: e.activation(out=v2(mean), in_=pS[:, :, :HB], func=AF.Identity, scale=1.0 / D)),
                         r=[pSid], w=["mean"])
                    S.op("dve", (lambda e: e.tensor_tensor(out=rstd[:], in0=mean[:], in1=mean[:], op=ALU.mult)),
                         r=["mean"], w=["rstd"])
                    S.op("dve", (lambda e, pQ=pQ: e.scalar_tensor_tensor(out=v2(rstd), in0=pQ[:, :, :HB], scalar=1.0 / D,
                                                                        in1=v2(rstd), op0=ALU.mult, op1=ALU.subtract)),
                         r=[pQid, "rstd"], w=["rstd"])
                    S.op("act", (lambda e: e.activation(out=rstd[:], in_=rstd[:], func=AF.Sqrt, bias=EPS, scale=1.0)),
                         r=["rstd"], w=["rstd"])
                    S.op("dve", (lambda e: e.reciprocal(out=rstd[:], in_=rstd[:])), r=["rstd"], w=["rstd"])
                    for c in range(16):
                        S.op("dve", (lambda e, c=c: e.tensor_tensor(out=zbuf[:, c, :], in0=zbuf[:, c, :], in1=mean[:], op=ALU.subtract)),
                             r=["c_z%d" % c, "mean"], w=["c_z%d" % c])
                        S.op("dve", (lambda e, c=c: e.tensor_tensor(out=zbuf[:, c, :], in0=zbuf[:, c, :], in1=rstd[:], op=ALU.mult)),
                             r=["c_z%d" % c, "rstd"], w=["c_z%d" % c])
                        gv = V[:, voff["ln_g"] + c: voff["ln_g"] + c + 1]
                        bv = V[:, voff["ln_b"] + c: voff["ln_b"] + c + 1]
                        S.op("act", (lambda e, c=c, gv=gv, bv=bv: e.activation(out=zs[:, c, :], in_=zbuf[:, c, :], func=AF.Silu,
                                                                                scale=gv, bias=bv)),
                             r=["c_z%d" % c, "V"], w=["c_zs"])
                    proj_residual(ph, WPW2, zs, "c_zs", xsrc, xdst, t0, gate, (wr, pzd, xo))
                S.barrier()
                S.emit()

        def na_phase():
            SCALE = 128.0 ** -0.5
            NKT = (KVE + NCTX) // 128
            with contextlib.ExitStack() as ph:
                qs = sbt(ph, "n_q", [128, 4, E], BF16)
                ks = sbt(ph, "n_k", [128, 4, KVE + NCTX], BF16)
                vs = sbt(ph, "n_v", [128, NKT, 512], BF16)
                bint = sbt(ph, "n_bint", [128, 4, 640], F32)
                bedge = Ring(ph, nc, "n_be", 2, [128, 640], F32)
                yna = sbt(ph, "n_y", [128, 4, E], BF16)
                psS = Ring(ph, nc, "n_ps", 2, [128, 2, 512], F32, psum=True)
                psT = Ring(ph, nc, "n_pt", 2, [128, 8, 128], BF16, psum=True)
                scr = Ring(ph, nc, "n_sc", 2, [128, 896], F32)
                pbr = Ring(ph, nc, "n_pb", 2, [128, 896], BF16)
                pTr = Ring(ph, nc, "n_pT", 2, [128, 7, 128], BF16)
                st4 = Ring(ph, nc, "n_st", 4, [128, 4], F32)
                otm = Ring(ph, nc, "n_o", 2, [128, 128], BF16)
                bcf = Ring(ph, nc, "n_bcf", 5, [128, 2048], F32)
                bcb = Ring(ph, nc, "n_bcb", 2, [128, 2048], BF16)
                it_cnt = [0]
                na_j0 = [bg_state["i"]]
                tot_iter = 2 * (E // 128) * 4
                if SBUF_REPORT:
                    print("NA sbuf remaining:", nc.sbuf_bytes_remaining)
                for hg in range(2):
                    S.dma("sp", (lambda e, hg=hg: e.dma_start(out=qs[:], in_=q_fm[hg * 4:hg * 4 + 4].rearrange("h p t -> p h t"))),
                          r=["qk_scr"], w=["n_q"])
                    S.dma("sp", (lambda e, hg=hg: e.dma_start(out=ks[:], in_=k_fm[hg * 4:hg * 4 + 4].rearrange("h p t -> p h t"))),
                          r=["qk_scr"], w=["n_k"])
                    S.dma("sp", (lambda e, hg=hg: e.dma_start(
                        out=vs[:], in_=v_tm[:, hg * 512:hg * 512 + 512].rearrange("(t p) c -> p t c", p=128))),
                          r=["uv_scr"], w=["n_v"])
                    S.dma("sp", (lambda e, hg=hg: e.dma_start(
                        out=bint[:], in_=na_bias[2, hg * 4:hg * 4 + 4].rearrange("h p k -> p h k"))), w=["n_bint"])
                    def tile_gen(hg, qt, h):
                        r = 2 * qt
                        kr0 = max(r - 4, 0)
                        k0 = kr0 * 64
                        it_cnt[0] += 1
                        want = na_j0[0] + (it_cnt[0] * (len(bg_jobs) - na_j0[0]) + tot_iter - 1) // tot_iter
                        bg_emit(want - bg_state["i"], bcf, bcb)
                        if qt < 2:
                            bt, btid = bedge.next()
                            S.dma("sp", (lambda e, bt=bt, qt=qt, hg=hg, h=h: e.dma_start(out=bt[:], in_=na_bias[qt, hg * 4 + h])),
                                  w=[btid])
                            bias_ap = bt[:, :]
                        else:
                            btid = "n_bint"
                            bias_ap = bint[:, h, :]
                        ps, psid = psS.next()
                        S.op("pe", (lambda e, ps=ps, h=h, qt=qt, k0=k0: e.matmul(
                            ps[:, 0, :], lhsT=qs[:, h, qt * 128:(qt + 1) * 128], rhs=ks[:, h, k0:k0 + 512],
                            start=True, stop=True)), r=["n_q", "n_k"], w=[psid])
                        S.op("pe", (lambda e, ps=ps, h=h, qt=qt, k0=k0: e.matmul(
                            ps[:, 1, 0:128], lhsT=qs[:, h, qt * 128:(qt + 1) * 128], rhs=ks[:, h, k0 + 512:k0 + 640],
                            start=True, stop=True)), r=["n_q", "n_k"], w=[psid])
                        S.op("pe", (lambda e, ps=ps, h=h, qt=qt: e.matmul(
                            ps[:, 1, 128:384], lhsT=qs[:, h, qt * 128:(qt + 1) * 128], rhs=ks[:, h, KVE:KVE + NCTX],
                            start=True, stop=True)), r=["n_q", "n_k"], w=[psid])
                        yield
                        sc, scid = scr.next()
                        S.op("dve", (lambda e, sc=sc, ps=ps, bias_ap=bias_ap: e.scalar_tensor_tensor(
                            out=sc[:, 0:512], in0=ps[:, 0, :], scalar=SCALE, in1=bias_ap[:, 0:512],
                            op0=ALU.mult, op1=ALU.add)), r=[psid, btid], w=[scid])
                        S.op("dve", (lambda e, sc=sc, ps=ps, bias_ap=bias_ap: e.scalar_tensor_tensor(
                            out=sc[:, 512:640], in0=ps[:, 1, 0:128], scalar=SCALE, in1=bias_ap[:, 512:640],
                            op0=ALU.mult, op1=ALU.add)), r=[psid, btid], w=[scid])
                        S.op("act", (lambda e, sc=sc, ps=ps: e.activation(out=sc[:, 640:896], in_=ps[:, 1, 128:384],
                                                                         func=AF.Identity, scale=SCALE)),
                             r=[psid], w=[scid])
                        yield
                        st_, stid = st4.next()
                        S.op("dve", (lambda e, st_=st_, sc=sc: e.tensor_reduce(out=st_[:, 0:1], in_=sc[:, :], axis=AX.X, op=ALU.max)),
                             r=[scid], w=[stid])
                        S.op("dve", (lambda e, st_=st_: e.tensor_scalar(out=st_[:, 1:2], in0=st_[:, 0:1], scalar1=-1.0, scalar2=None,
                                                                       op0=ALU.mult)), r=[stid], w=[stid])
                        yield
                        pb, pbid = pbr.next()
                        S.op("act", (lambda e, pb=pb, sc=sc, st_=st_: e.activation(out=pb[:, :], in_=sc[:, :], func=AF.Exp,
                                                                                  bias=st_[:, 1:2], scale=1.0,
                                                                                  accum_out=st_[:, 2:3])),
                             r=[scid, stid], w=[pbid, stid])
                        S.op("dve", (lambda e, st_=st_: e.reciprocal(out=st_[:, 3:4], in_=st_[:, 2:3])), r=[stid], w=[stid])
                        yield
                        pt, ptid = psT.next()
                        for j in range(7):
                            S.op("pe", (lambda e, pt=pt, pb=pb, j=j: e.transpose(out=pt[:, j, :], in_=pb[:, j * 128:(j + 1) * 128],
                                                                                identity=ident_b[:])),
                                 r=[pbid, "ident_b"], w=[ptid])
                        yield
                        pT, pTid = pTr.next()
                        S.op("act", (lambda e, pT=pT, pt=pt: e.activation(out=pT[:, 0:4, :], in_=pt[:, 0:4, :], func=AF.Copy)),
                             r=[ptid], w=[pTid])
                        S.op("dve", (lambda e, pT=pT, pt=pt: e.tensor_copy(out=pT[:, 4:7, :], in_=pt[:, 4:7, :])),
                             r=[ptid], w=[pTid])
                        yield
                        po, poid = ps[:, 1, 384:512], psid + "o"
                        for j in range(7):
                            vt = (kr0 // 2 + j) if j < 5 else (KVE // 128 + (j - 5))
                            S.op("pe", (lambda e, po=po, pT=pT, j=j, vt=vt, h=h: e.matmul(
                                po, lhsT=pT[:, j, :], rhs=vs[:, vt, h * 128:(h + 1) * 128],
                                start=(j == 0), stop=(j == 6))), r=[pTid, "n_v"], w=[poid])
                        yield
                        o_, oid = otm.next()
                        S.op("act", (lambda e, o_=o_, po=po, st_=st_: e.activation(out=o_[:, :], in_=po, func=AF.Identity,
                                                                                  scale=st_[:, 3:4])),
                             r=[poid, stid], w=[oid])
                        yield
                        pot, potid = pt[:, 7, :], ptid + "o"
                        S.op("pe", (lambda e, pot=pot, o_=o_: e.transpose(out=pot, in_=o_[:, :], identity=ident_b[:])),
                             r=[oid, "ident_b"], w=[potid])
                        yield
                        S.op("dve", (lambda e, pot=pot, h=h, qt=qt: e.tensor_copy(out=yna[:, h, qt * 128:(qt + 1) * 128], in_=pot)),
                             r=[potid], w=["n_y"])

                    for qt in range(E // 128):
                        for hp in range(2):
                            gens = [tile_gen(hg, qt, hp * 2), tile_gen(hg, qt, hp * 2 + 1)]
                            live = list(gens)
                            while live:
                                for g_ in list(live):
                                    try:
                                        next(g_)
                                    except StopIteration:
                                        live.remove(g_)
                    S.dma("sp", (lambda e, hg=hg: e.dma_start(
                        out=ymix[8 + hg * 4:8 + hg * 4 + 4].rearrange("h p t -> p h t"), in_=yna[:])), r=["n_y"], w=["ymix"])
                S.barrier()
                S.emit()

        def s5_phase():
            NCHK = 544
            NA_, NB_ = 304, 544
            PA, PB = 256, 512
            PBH, PZ, NZ = 16, 256, 274
            NOUT = E // 8
            PI = 3.141592653589793
            with contextlib.ExitStack() as ph0:
                z_tm = sbt(ph0, "s_ztm", [128, 3, 8, 1024], BF16)
                psX = Ring(ph0, nc, "s_px", 2, [128, 2, 512], F32, psum=True)
                psT = Ring(ph0, nc, "s_pt", 1, [128, 8, 128], BF16, psum=True)
                psMT = pst(ph0, "s_pmt", [128, 4, 128], F32)

                class _One:
                    def next(self_):
                        return psMT, "psM"
                psM = _One()
                psKT = pst(ph0, "s_pkt", [128, 512], F32)
                psY = Ring(ph0, nc, "s_py", 1, [128, 256], F32, psum=True)
                with contextlib.ExitStack() as ph:
                    NS = 24
                    T = sbt(ph, "s_T", [128, 2, NS, 32], F32)
                    PW = sbt(ph, "s_PW", [128, 2, 2, 32, 9], F32)
                    QW = sbt(ph, "s_QW", [128, 2, 3, 32, 10], F32)
                    Bt = sbt(ph, "s_Bt", [128, 2, 2, 32, 16], F32)
                    bb = sbt(ph, "s_bb", [128, 2, 2, 32, 16], F32)
                    bbb = sbt(ph, "s_bbb", [128, 2, 2, 32, 16], BF16)
                    CT = sbt(ph, "s_CT", [128, 2, 2, 32, 16], F32)
                    Cn = Ring(ph, nc, "s_Cn", 2, [128, 128], F32)
                    dvec = sbt(ph, "s_dvec", [128, 64], F32)
                    Sel = sbt(ph, "s_Sel", [16, 8, 128], BF16)
                    KTa = sbt(ph, "s_KTa", [16, 2, 15, 16], BF16)
                    KTb = sbt(ph, "s_KTb", [16, 2, 15, 16], BF16)
                    P_ = "s5par"

                    def dv(fn, r=(P_,), w=(P_,), eng="dve"):
                        S.op(eng, fn, r=list(r), w=list(w))

                    def tt(o_, a, b, op):
                        dv(lambda e: e.tensor_tensor(out=o_, in0=a, in1=b, op=op))

                    def ts(o_, a, s1, op0, s2=None, op1=None):
                        if op1 is None:
                            dv(lambda e: e.tensor_scalar(out=o_, in0=a, scalar1=s1, scalar2=None, op0=op0))
                        else:
                            dv(lambda e: e.tensor_scalar(out=o_, in0=a, scalar1=s1, scalar2=s2, op0=op0, op1=op1))

                    def stt(o_, a, sc, b, op0, op1):
                        dv(lambda e: e.scalar_tensor_tensor(out=o_, in0=a, scalar=sc, in1=b, op0=op0, op1=op1))

                    def act(o_, a, func, **kw):
                        dv(lambda e: e.activation(out=o_, in_=a, func=func, **kw), eng="act")

                    def cmul(ore, oim, are, aim, bre, bim, t1):
                        tt(ore, are, bre, ALU.mult)
                        tt(t1, aim, bim, ALU.mult)
                        tt(ore, ore, t1, ALU.subtract)
                        tt(oim, are, bim, ALU.mult)
                        tt(t1, aim, bre, ALU.mult)
                        tt(oim, oim, t1, ALU.add)

                    dv(lambda e: e.memset(Sel[:], 0.0), eng="pool")
                    for s_ in range(8):
                        dv((lambda e, s_=s_: e.tensor_copy(out=Sel[0:16, s_, s_ * 16:(s_ + 1) * 16], in_=ident_b[0:16, 0:16])),
                           r=(P_, "ident_b"), eng="pool")
                    dv(lambda e: e.memset(KTa[:], 0.0), eng="pool")
                    dv(lambda e: e.memset(KTb[:], 0.0), eng="pool")
                    for s_ in range(8):
                        S.dma("sp", (lambda e, s_=s_: e.dma_start(out=dvec[s_ * 16:(s_ + 1) * 16, :],
                                                                  in_=ssm_d.rearrange("(g c) -> c g", c=16),
                                                                  allow_slow_non_contiguous=True)), w=[P_])
                    for d in range(2):
                        for ri, src in ((0, b_re), (1, b_im)):
                            S.dma("sp", (lambda e, d=d, ri=ri, src=src: e.dma_start(
                                out=Bt[:, d, ri], in_=src[d].rearrange("(pair q) c -> q pair c", q=128))), w=[P_])
                    for d in range(2):
                        for ri, src in ((0, c_re), (1, c_im)):
                            for t8 in range(8):
                                cn, cnid = Cn.next()
                                for dup in range(2):
                                    S.dma("sp", (lambda e, cn=cn, dup=dup, src=src, d=d, t8=t8: e.dma_start(
                                        out=cn[:, dup * 64:(dup + 1) * 64], in_=src[d, t8 * 128:(t8 + 1) * 128, :])), w=[cnid])
                                pm, pmid = psM.next()
                                S.op("pe", (lambda e, pm=pm, cn=cn: e.transpose(out=pm[:, 0, :], in_=cn[:, :], identity=ident_f[:])),
                                     r=[cnid, "ident_f"], w=[pmid])
                                for g2 in range(2):
                                    S.op("dve", (lambda e, pm=pm, g2=g2, d=d, ri=ri, t8=t8: e.tensor_copy(
                                        out=CT[g2 * 64:(g2 + 1) * 64, d, ri, 4 * t8:4 * t8 + 4, :],
                                        in_=pm[g2 * 64:(g2 + 1) * 64, 0, :].rearrange("p (pp x c) -> p pp x c", pp=4, x=2)[:, :, g2, :])),
                                         r=[pmid, P_], w=[P_])
                    for d in range(2):
                        sl = lambda i, d=d: T[:, d, i, :]
                        DT, LR, LI, MAG, TH, CNT, SN, CS, ARE, AIM, FRE, FIM, DEN, T1, T2, XR = [sl(i) for i in range(16)]
                        lre = VC("lam_re%d" % d, 0, 32)
                        lim = VC("lam_im%d" % d, 0, 32)
                        act(DT, VC("log_dt%d" % d, 0, 32), AF.Exp)
                        S.op("dve", (lambda e, LR=LR, lre=lre, DT=DT: e.tensor_tensor(out=LR, in0=lre, in1=DT, op=ALU.mult)), r=[P_, "V"], w=[P_])
                        S.op("dve", (lambda e, LI=LI, lim=lim, DT=DT: e.tensor_tensor(out=LI, in0=lim, in1=DT, op=ALU.mult)), r=[P_, "V"], w=[P_])
                        act(MAG, LR, AF.Exp)
                        for (dst, shift) in ((SN, 0.0), (CS, PI / 2)):
                            ts(TH, LI, TWO_PI + shift, ALU.add)
                            ts(CNT, TH, TWO_PI, ALU.is_ge)
                            for jj in range(2, 6):
                                stt(CNT, TH, TWO_PI * jj, CNT, ALU.is_ge, ALU.add)
                            stt(TH, CNT, -TWO_PI, TH, ALU.mult, ALU.add)
                            ts(TH, TH, PI, ALU.subtract)
                            act(dst, TH, AF.Sin)
                        stt(ARE, CS, -1.0, MAG, ALU.mult, ALU.mult)
                        stt(AIM, SN, -1.0, MAG, ALU.mult, ALU.mult)
                        S.op("dve", (lambda e, DEN=DEN, lre=lre: e.tensor_tensor(out=DEN, in0=lre, in1=lre, op=ALU.mult)), r=[P_, "V"], w=[P_])
                        S.op("dve", (lambda e, T1=T1, lim=lim: e.tensor_tensor(out=T1, in0=lim, in1=lim, op=ALU.mult)), r=[P_, "V"], w=[P_])
                        tt(DEN, DEN, T1, ALU.add)
                        dv(lambda e, DEN=DEN: e.reciprocal(out=DEN, in_=DEN))
                        ts(XR, ARE, 1.0, ALU.subtract)
                        S.op("dve", (lambda e, FRE=FRE, XR=XR, lre=lre: e.tensor_tensor(out=FRE, in0=XR, in1=lre, op=ALU.mult)), r=[P_, "V"], w=[P_])
                        S.op("dve", (lambda e, T1=T1, AIM=AIM, lim=lim: e.tensor_tensor(out=T1, in0=AIM, in1=lim, op=ALU.mult)), r=[P_, "V"], w=[P_])
                        tt(FRE, FRE, T1, ALU.add)
                        tt(FRE, FRE, DEN, ALU.mult)
                        S.op("dve", (lambda e, FIM=FIM, AIM=AIM, lre=lre: e.tensor_tensor(out=FIM, in0=AIM, in1=lre, op=ALU.mult)), r=[P_, "V"], w=[P_])
                        S.op("dve", (lambda e, T1=T1, XR=XR, lim=lim: e.tensor_tensor(out=T1, in0=XR, in1=lim, op=ALU.mult)), r=[P_, "V"], w=[P_])
                        tt(FIM, FIM, T1, ALU.subtract)
                        tt(FIM, FIM, DEN, ALU.mult)
                        pw = lambda ri, k, d=d: PW[:, d, ri, :, k]
                        dv(lambda e, d=d: e.memset(PW[:, d, 0, :, 0], 1.0))
                        dv(lambda e, d=d: e.memset(PW[:, d, 1, :, 0], 0.0))
                        dv(lambda e, d=d, ARE=ARE: e.tensor_copy(out=PW[:, d, 0, :, 1], in_=ARE))
                        dv(lambda e, d=d, AIM=AIM: e.tensor_copy(out=PW[:, d, 1, :, 1], in_=AIM))
                        for k in range(2, 9):
                            cmul(pw(0, k), pw(1, k), pw(0, k - 1), pw(1, k - 1), ARE, AIM, T1)
                        qw = lambda ri, j, d=d: QW[:, d, ri, :, j]
                        dv(lambda e, d=d: e.tensor_copy(out=QW[:, d, 0, :, 0], in_=PW[:, d, 0, :, 8]))
                        dv(lambda e, d=d: e.tensor_copy(out=QW[:, d, 1, :, 0], in_=PW[:, d, 1, :, 8]))
                        for j in range(1, 10):
                            cmul(qw(0, j), qw(1, j), qw(0, j - 1), qw(1, j - 1), qw(0, j - 1), qw(1, j - 1), T1)
                        dv(lambda e, d=d: e.tensor_scalar(out=QW[:, d, 2], in0=QW[:, d, 1], scalar1=-1.0, scalar2=None, op0=ALU.mult))
                        fre_b = FRE.unsqueeze(2).broadcast_to([128, 32, 16])
                        fim_b = FIM.unsqueeze(2).broadcast_to([128, 32, 16])
                        tt(bb[:, d, 0], Bt[:, d, 0], fre_b, ALU.mult)
                        tt(bb[:, d, 1], Bt[:, d, 1], fre_b, ALU.mult)
                        tt(Bt[:, d, 1], Bt[:, d, 1], fim_b, ALU.mult)
                        tt(Bt[:, d, 0], Bt[:, d, 0], fim_b, ALU.mult)
                        tt(bb[:, d, 0], bb[:, d, 0], Bt[:, d, 1], ALU.subtract)
                        tt(bb[:, d, 1], bb[:, d, 1], Bt[:, d, 0], ALU.add)
                        dv(lambda e, d=d: e.tensor_copy(out=bbb[:, d], in_=bb[:, d]))

                    Gr = Ring(ph, nc, "s_G", 2, [128, 2, 2, 9, 16], F32)
                    Gt = sbt(ph, "s_Gt", [128, 9, 16], F32)
                    W1 = Ring(ph, nc, "s_W1", 2, [128, 2, 8, 16], F32)
                    W1t = sbt(ph, "s_W1t", [128, 8, 16], F32)
                    mxp = [Ring(ph, nc, "s_mxp%d" % d, 2, [128, 4, 128], BF16) for d in range(2)]
                    myp = [Ring(ph, nc, "s_myp%d" % d, 2, [128, 2, 256], BF16) for d in range(2)]
                    kblk = [Ring(ph, nc, "s_kb%d" % d, 2, [128, 2, 256], BF16) for d in range(2)]
                    for d in range(2):
                        for rg in (mxp[d], myp[d], kblk[d]):
                            for (tl, tid) in zip(rg.tiles, rg.ids):
                                S.op("pool", (lambda e, tl=tl: e.memset(tl[:], 0.0)), w=[tid])
                    toep = Ring(ph, nc, "s_toep", 2, [128, 2, 128], BF16)
                    ddg = Ring(ph, nc, "s_ddg", 2, [128, 128], F32)
                    UTp = Ring(ph, nc, "s_UTp", 2, [128, 5, 8, 32], BF16)
                    UT2 = Ring(ph, nc, "s_UT2", 2, [128, 5, 2, 128], BF16)
                    UgT = Ring(ph, nc, "s_UgT", 2, [128, 2, NCHK], BF16)
                    XA = [[sbt(ph, "s_XA%d%d" % (i, ri), [128, PA + NA_], F32) for ri in range(2)] for i in range(2)]
                    XB = [[sbt(ph, "s_XB%d%d" % (i, ri), [128, PBH + 272], F32) for ri in range(2)] for i in range(2)]
                    ZB = [[sbt(ph, "s_ZB%d%d" % (i, ri), [128, PZ + NZ], F32) for ri in range(2)] for i in range(2)]
                    for arrs in (XA, XB, ZB):
                        for i in range(2):
                            for ri in range(2):
                                S.op("pool", (lambda e, a=arrs[i][ri]: e.memset(a[:], 0.0)), w=["scanA", "scanB"])
                    Sbf = Ring(ph, nc, "s_Sbf", 2, [128, 4, NOUT], BF16)
                    ytmp = Ring(ph, nc, "s_yt", 2, [128, 3, 256], F32)

                    if SBUF_REPORT:
                        print("S5 sbuf remaining:", nc.sbuf_bytes_remaining)
                    for pair in range(32):
                        ut, utid = UTp.next()
                        for tl in range(5):
                            nr = 128 if tl < 4 else 32
                            S.dma("sp", (lambda e, ut=ut, tl=tl, nr=nr, pair=pair: e.dma_start(
                                out=ut[:nr, tl], in_=u_tm[tl * 1024:tl * 1024 + nr * 8, pair * 32:(pair + 1) * 32].rearrange(
                                    "(q s) c -> q s c", s=8))), r=["uv_scr"], w=[utid])
                        u2, u2id = UT2.next()
                        for tl in range(5):
                            nr = 128 if tl < 4 else 32
                            for g2 in range(2):
                                S.op("pool", (lambda e, u2=u2, ut=ut, tl=tl, nr=nr, g2=g2: e.tensor_copy(
                                    out=u2[:nr, tl, g2, :].rearrange("p (s c) -> p s c", s=8),
                                    in_=ut[:nr, tl, :, g2 * 16:(g2 + 1) * 16])), r=[utid], w=[u2id])
                        ug, ugid = UgT.next()
                        for g2 in range(2):
                            for tl in range(5):
                                nr = 128 if tl < 4 else 32
                                pt, ptid = psT.next()
                                S.op("pe", (lambda e, pt=pt, u2=u2, tl=tl, nr=nr, g2=g2: e.transpose(
                                    out=pt[:, 0, :nr], in_=u2[:nr, tl, g2, :], identity=ident_b[:nr, :nr])),
                                     r=[u2id, "ident_b"], w=[ptid])
                                ce = ("act", "dve")[tl % 2]
                                S.op(ce, copy_fn(ce, ug[:, g2, tl * 128:tl * 128 + nr], pt[:, 0, :nr]), r=[ptid], w=[ugid])
                        G_, Gid = Gr.next()
                        mx_t, my_t, kb_t = [], [], []
                        for d in range(2):
                            pre = PW[:, d, 0, pair, :].unsqueeze(2).broadcast_to([128, 9, 16])
                            pim = PW[:, d, 1, pair, :].unsqueeze(2).broadcast_to([128, 9, 16])
                            cre = CT[:, d, 0, pair, :].unsqueeze(1).broadcast_to([128, 9, 16])
                            cim = CT[:, d, 1, pair, :].unsqueeze(1).broadcast_to([128, 9, 16])
                            gre, gim = G_[:, d, 0], G_[:, d, 1]
                            for (o_, a1, b1, a2, b2, op) in ((gre, cre, pre, cim, pim, ALU.subtract), (gim, cre, pim, cim, pre, ALU.add)):
                                S.op("dve", (lambda e, o_=o_, a1=a1, b1=b1: e.tensor_tensor(out=o_, in0=a1, in1=b1, op=ALU.mult)), r=[P_], w=[Gid], nosw=True)
                                S.op("dve", (lambda e, a2=a2, b2=b2: e.tensor_tensor(out=Gt[:], in0=a2, in1=b2, op=ALU.mult)), r=[P_], w=["s_Gt"], nosw=True)
                                S.op("dve", (lambda e, o_=o_, op=op: e.tensor_tensor(out=o_, in0=o_, in1=Gt[:], op=op)), r=["s_Gt", Gid], w=[Gid], nosw=True)
                            my, myid = myp[d].next()
                            kb, kbid = kblk[d].next()
                            for g2 in range(2):
                                rows = slice(g2 * 64, (g2 + 1) * 64)
                                if d == 0:
                                    ksel = lambda ri, rows=rows, d=d, G_=G_: G_[rows, d, ri, 1:9, :]
                                else:
                                    ksel = lambda ri, rows=rows, d=d, G_=G_: G_[rows, d, ri, 1:9, :][:, ::-1, :]
                                S.op("dve", (lambda e, my=my, rows=rows, g2=g2, ksel=ksel: e.tensor_copy(
                                    out=my[rows, 0, g2 * 128:(g2 + 1) * 128].rearrange("p (t c) -> p t c", t=8), in_=ksel(0))), r=[Gid], w=[myid])
                                S.op("dve", (lambda e, my=my, rows=rows, g2=g2, ksel=ksel: e.tensor_scalar(
                                    out=my[rows, 1, g2 * 128:(g2 + 1) * 128].rearrange("p (t c) -> p t c", t=8), in0=ksel(1),
                                    scalar1=-1.0, scalar2=None, op0=ALU.mult)), r=[Gid], w=[myid])
                                S.op("pool", (lambda e, kb=kb, rows=rows, g2=g2, d=d, G_=G_: e.tensor_copy(
                                    out=kb[rows, 0, g2 * 128:(g2 + 1) * 128].rearrange("p (t c) -> p t c", t=8), in_=G_[rows, d, 0, 0:8, :])),
                                     r=[Gid], w=[kbid])
                                S.op("pool", (lambda e, kb=kb, rows=rows, g2=g2, d=d, G_=G_: e.tensor_scalar(
                                    out=kb[rows, 1, g2 * 128:(g2 + 1) * 128].rearrange("p (t c) -> p t c", t=8), in0=G_[rows, d, 1, 0:8, :],
                                    scalar1=-1.0, scalar2=None, op0=ALU.mult)), r=[Gid], w=[kbid])
                            w1, w1id = W1.next()
                            if d == 0:
                                psel = lambda ri, d=d: PW[:, d, ri, pair, 0:8][:, ::-1].unsqueeze(2).broadcast_to([128, 8, 16])
                            else:
                                psel = lambda ri, d=d: PW[:, d, ri, pair, 0:8].unsqueeze(2).broadcast_to([128, 8, 16])
                            bre = bb[:, d, 0, pair, :].unsqueeze(1).broadcast_to([128, 8, 16])
                            bim = bb[:, d, 1, pair, :].unsqueeze(1).broadcast_to([128, 8, 16])
                            for (o_, a1, b1, a2, b2, op) in ((w1[:, 0], psel(0), bre, psel(1), bim, ALU.subtract),
                                                            (w1[:, 1], psel(0), bim, psel(1), bre, ALU.add)):
                                S.op("dve", (lambda e, o_=o_, a1=a1, b1=b1: e.tensor_tensor(out=o_, in0=a1, in1=b1, op=ALU.mult)), r=[P_], w=[w1id], nosw=True)
                                S.op("dve", (lambda e, a2=a2, b2=b2: e.tensor_tensor(out=W1t[:], in0=a2, in1=b2, op=ALU.mult)), r=[P_], w=["s_W1t"], nosw=True)
                                S.op("dve", (lambda e, o_=o_, op=op: e.tensor_tensor(out=o_, in0=o_, in1=W1t[:], op=op)), r=["s_W1t", w1id], w=[w1id], nosw=True)
                            pm, pmid = psM.next()
                            for ri in range(2):
                                S.op("pe", (lambda e, pm=pm, w1=w1, ri=ri: e.transpose(
                                    out=pm[:, ri, :], in_=w1[:, ri].rearrange("p s c -> p (s c)"), identity=ident_f[:])),
                                     r=[w1id, "ident_f"], w=[pmid])
                            mx, mxid = mxp[d].next()
                            for ri in range(2):
                                for g2 in range(2):
                                    ce = ("act", "dve")[g2]
                                    S.op(ce, copy_fn(ce, mx[:, ri * 2 + g2, g2 * 64:(g2 + 1) * 64], pm[:, ri, g2 * 64:(g2 + 1) * 64]),
                                         r=[pmid], w=[mxid])
                            mx_t.append((mx, mxid)); my_t.append((my, myid)); kb_t.append((kb, kbid))

                        def emit_K_mm(d, pair=pair):
                            kb, kbid = kb_t[d]
                            S.op("pe", (lambda e, d=d, kb=kb, pair=pair: e.matmul(psKT[0:16, d * 256:d * 256 + 256], lhsT=bbb[:, d, 0, pair, :], rhs=kb[:, 0, :],
                                                                                  start=True, stop=False)), r=[P_, kbid], w=["psK%d" % d])
                            S.op("pe", (lambda e, d=d, kb=kb, pair=pair: e.matmul(psKT[0:16, d * 256:d * 256 + 256], lhsT=bbb[:, d, 1, pair, :], rhs=kb[:, 1, :],
                                                                                  start=False, stop=True)), r=[P_, kbid], w=["psK%d" % d])

                        def emit_KT_copy(d):
                            kview = psKT[0:16, d * 256:d * 256 + 256].rearrange("p (g t c) -> p g t c", g=2, t=8)
                            if d == 0:
                                S.op("dve", (lambda e, kview=kview: e.tensor_copy(out=KTa[:, :, 7:15, :], in_=kview)), r=["psK0"], w=["KT"])
                            else:
                                S.op("dve", (lambda e, kview=kview: e.tensor_copy(out=KTb[:, :, 0:8, :], in_=kview[:, :, ::-1, :])),
                                     r=["psK1"], w=["KT"])

                        tp, tpid = toep.next()

                        def emit_toep_mm():
                            for g2 in range(2):
                                n_mm = 0
                                for KT in (KTa, KTb):
                                    for s_ in range(8):
                                        S.op("pe", (lambda e, g2=g2, KT=KT, s_=s_, first=(n_mm == 0), last=(n_mm == 15): e.matmul(
                                            psMT[:, 2 + g2, :], lhsT=Sel[0:16, s_, :],
                                            rhs=KT[0:16, g2, 7 - s_:15 - s_, :].rearrange("p j c -> p (j c)"),
                                            start=first, stop=last)), r=["KT", P_], w=["psTo"])
                                        n_mm += 1

                        def emit_toep_evac(pair=pair, tp=tp, tpid=tpid):
                            for g2 in range(2):
                                g = 2 * pair + g2
                                dd, ddid = ddg.next()
                                S.op("pool", (lambda e, dd=dd, g=g: e.tensor_scalar(out=dd[:], in0=ident_f[:], scalar1=dvec[:, g:g + 1],
                                                                                   scalar2=None, op0=ALU.mult)), r=[P_, "ident_f"], w=[ddid])
                                S.op("dve", (lambda e, tp=tp, g2=g2, dd=dd: e.tensor_tensor(
                                    out=tp[:, g2, :], in0=psMT[:, 2 + g2, :], in1=dd[:], op=ALU.add)),
                                     r=["psTo", ddid], w=[tpid])

                        sb_, sbid = Sbf.next()

                        def emit_X(d, ug=ug, ugid=ugid):
                            mx, mxid = mx_t[d]
                            arr = XA if d == 0 else XB
                            sid = "scanA" if d == 0 else "scanB"
                            for ri in range(2):
                                px, pxid = psX.next()
                                if d == 0:
                                    segs = [(px[:, 0, 0:32], 512, 544), (px[:, 0, 32:304], 0, 272)]
                                else:
                                    segs = [(px[:, 0, 0:512], 0, 512), (px[:, 1, 0:32], 512, 544)]
                                for (o_, c0, c1) in segs:
                                    for g2 in range(2):
                                        S.op("pe", (lambda e, o_=o_, mx=mx, ri=ri, g2=g2, c0=c0, c1=c1, ug=ug: e.matmul(
                                            o_, lhsT=mx[:, ri * 2 + g2, :], rhs=ug[:, g2, c0:c1], start=(g2 == 0), stop=(g2 == 1))),
                                             r=[mxid, ugid], w=[pxid])
                                dst = arr[0][ri]
                                if d == 0:
                                    S.op("act", (lambda e, dst=dst, px=px: e.activation(out=dst[:, PA:PA + NA_], in_=px[:, 0, 0:NA_], func=AF.Copy)),
                                         r=[pxid], w=[sid])
                                else:
                                    zdst = ZB[0][ri]
                                    S.op("act", (lambda e, zdst=zdst, px=px: e.activation(out=zdst[:, PZ + 1:PZ + 274][:, ::-1], in_=px[:, 0, 0:273],
                                                                                        func=AF.Copy)), r=[pxid], w=[sid])
                                    S.op("act", (lambda e, dst=dst, px=px: e.activation(out=dst[:, PBH + 32:PBH + 271][:, ::-1], in_=px[:, 0, 273:512],
                                                                                       func=AF.Copy)), r=[pxid], w=[sid])
                                    S.op("act", (lambda e, dst=dst, px=px: e.activation(out=dst[:, PBH:PBH + 32][:, ::-1], in_=px[:, 1, 0:32],
                                                                                       func=AF.Copy)), r=[pxid], w=[sid])

                        def emit_HS(d, pair=pair, sb_=sb_, sbid=sbid):
                            arr = XA if d == 0 else ZB
                            sid = "scanA" if d == 0 else "scanB"
                            PAD = PA if d == 0 else PZ
                            n = NA_ if d == 0 else NZ
                            if d == 1:
                                m = 272
                                lo_ = PBH - 1
                                srcb, dstb = XB[0], XB[1]
                                lvl = 0
                                while m > 1:
                                    half = m // 2
                                    qre = QW[:, d, 0, pair, lvl:lvl + 1]
                                    qim = QW[:, d, 1, pair, lvl:lvl + 1]
                                    nqim = QW[:, d, 2, pair, lvl:lvl + 1]
                                    ev = lambda t, lo_=lo_, m=m: t[:, lo_:lo_ + m:2]
                                    od = lambda t, lo_=lo_, m=m: t[:, lo_ + 1:lo_ + m:2]
                                    ou = lambda t, half=half: t[:, PBH:PBH + half]
                                    for (o_, a1, s1, b1) in ((ou(dstb[0]), ev(srcb[0]), qre, od(srcb[0])), (ou(dstb[0]), ev(srcb[1]), nqim, ou(dstb[0])),
                                                            (ou(dstb[1]), ev(srcb[1]), qre, od(srcb[1])), (ou(dstb[1]), ev(srcb[0]), qim, ou(dstb[1]))):
                                        S.op("dve", (lambda e, o_=o_, a1=a1, s1=s1, b1=b1: e.scalar_tensor_tensor(
                                            out=o_, in0=a1, scalar=s1, in1=b1, op0=ALU.mult, op1=ALU.add)), r=[sid, P_], w=[sid], nosw=True)
                                    m = half
                                    lo_ = PBH if m % 2 == 0 else PBH - 1
                                    if m % 2 == 1 and m > 1:
                                        m += 1
                                    srcb, dstb = dstb, srcb
                                    lvl += 1
                                for ri in range(2):
                                    S.op("dve", (lambda e, ri=ri, srcb=srcb: e.tensor_copy(out=ZB[0][ri][:, PZ:PZ + 1], in_=srcb[ri][:, PBH:PBH + 1])),
                                         r=[sid], w=[sid])
                            cur = 0
                            j = 0
                            sh = 1
                            while sh < n:
                                src_, dst_ = arr[cur], arr[1 - cur]
                                qre = QW[:, d, 0, pair, j:j + 1]
                                qim = QW[:, d, 1, pair, j:j + 1]
                                nqim = QW[:, d, 2, pair, j:j + 1]
                                lo, hi = PAD, PAD + n
                                for (o_, a1, s1, b1) in ((dst_[0], src_[0], qre, src_[0]), (dst_[0], src_[1], nqim, dst_[0]),
                                                        (dst_[1], src_[1], qre, src_[1]), (dst_[1], src_[0], qim, dst_[1])):
                                    S.op("dve", (lambda e, o_=o_, a1=a1, s1=s1, b1=b1, lo=lo, hi=hi, sh=sh: e.scalar_tensor_tensor(
                                        out=o_[:, lo:hi], in0=a1[:, lo - sh:hi - sh], scalar=s1, in1=b1[:, lo:hi],
                                        op0=ALU.mult, op1=ALU.add)), r=[sid, P_], w=[sid], nosw=True)
                                cur = 1 - cur
                                sh *= 2
                                j += 1
                            fin = arr[cur]
                            for ri in range(2):
                                if d == 0:
                                    S.op("act", (lambda e, sb_=sb_, fin=fin, ri=ri: e.activation(
                                        out=sb_[:, ri, :], in_=fin[ri][:, PA + 31:PA + 31 + NOUT], func=AF.Copy)), r=[sid], w=[sbid])
                                else:
                                    S.op("act", (lambda e, sb_=sb_, fin=fin, ri=ri: e.activation(
                                        out=sb_[:, 2 + ri, :][:, ::-1], in_=fin[ri][:, PZ + 1:PZ + 1 + NOUT], func=AF.Copy)),
                                         r=[sid], w=[sbid])

                        emit_X(0)
                        emit_K_mm(0)
                        emit_K_mm(1)
                        emit_X(1)
                        emit_HS(0)
                        emit_KT_copy(0)
                        emit_KT_copy(1)
                        emit_toep_mm()
                        emit_HS(1)
                        emit_toep_evac()
                        for ct in range(3):
                            nch = 128 if ct < 2 else NOUT - 256
                            py, pyid = psY.next()
                            k = 0
                            for d in range(2):
                                my, myid = my_t[d]
                                for ri in range(2):
                                    S.op("pe", (lambda e, py=py, sb_=sb_, d=d, ri=ri, ct=ct, nch=nch, my=my, first=(k == 0): e.matmul(
                                        py[:nch, :], lhsT=sb_[:, d * 2 + ri, ct * 128:ct * 128 + nch], rhs=my[:, ri, :],
                                        start=first, stop=False)), r=[sbid, myid], w=[pyid])
                                    k += 1
                            for g2 in range(2):
                                S.op("pe", (lambda e, py=py, ug=ug, g2=g2, ct=ct, nch=nch, tp=tp: e.matmul(
                                    py[:nch, g2 * 128:(g2 + 1) * 128], lhsT=ug[:, g2, ct * 128:ct * 128 + nch], rhs=tp[:, g2, :],
                                    start=False, stop=(g2 == 1))), r=[ugid, tpid], w=[pyid])
                            yt, ytid = ytmp.next()
                            S.op("act", (lambda e, yt=yt, py=py, nch=nch: e.activation(out=yt[:nch, 0, :], in_=py[:nch, :], func=AF.Copy)),
                                 r=[pyid], w=[ytid])
                            S.op("pool", (lambda e, yt=yt, nch=nch: e.tensor_tensor(out=yt[:nch, 1, :], in0=yt[:nch, 0, :], in1=yt[:nch, 0, :],
                                                                                   op=ALU.mult)), r=[ytid], w=[ytid])
                            S.op("pool", (lambda e, yt=yt, nch=nch: e.tensor_scalar(out=yt[:nch, 1, :], in0=yt[:nch, 1, :], scalar1=0.044715,
                                                                                   scalar2=1.0, op0=ALU.mult, op1=ALU.add)), r=[ytid], w=[ytid])
                            S.op("pool", (lambda e, yt=yt, nch=nch: e.tensor_tensor(out=yt[:nch, 1, :], in0=yt[:nch, 1, :], in1=yt[:nch, 0, :],
                                                                                   op=ALU.mult)), r=[ytid], w=[ytid])
                            S.op("act", (lambda e, yt=yt, nch=nch: e.activation(out=yt[:nch, 2, :], in_=yt[:nch, 1, :], func=AF.Sigmoid,
                                                                               scale=1.5957691216057308)), r=[ytid], w=[ytid])
                            for g2 in range(2):
                                zo = z_tm[:nch, ct, :, pair * 32 + g2 * 16:pair * 32 + (g2 + 1) * 16]
                                i0 = yt[:nch, 0, g2 * 128:(g2 + 1) * 128].rearrange("p (t c) -> p t c", t=8)
                                i1 = yt[:nch, 2, g2 * 128:(g2 + 1) * 128].rearrange("p (t c) -> p t c", t=8)
                                if S5DBG:
                                    S.op("pool", (lambda e, zo=zo, i0=i0: e.tensor_copy(out=zo, in_=i0)), r=[ytid], w=["s_ztm"])
                                else:
                                    S.op("pool", (lambda e, zo=zo, i0=i0, i1=i1: e.tensor_tensor(out=zo, in0=i0, in1=i1, op=ALU.mult)),
                                         r=[ytid], w=["s_ztm"])
                        if S5BAR:
                            S.barrier()
                    S.barrier()
                    S.emit()
                with contextlib.ExitStack() as ph:
                    z_fm = sbt(ph, "s_zfm", [128, 8, E], BF16)
                    wgl = Ring(ph, nc, "s_wgl", 2, [128, 8, 128], BF16)
                    sgt = Ring(ph, nc, "s_sgt", 2, [128, TB], F32)
                    yo = Ring(ph, nc, "s_yo", 2, [128, TB], BF16)
                    for ct in range(3):
                        nch = 128 if ct < 2 else NOUT - 256
                        for c8 in range(8):
                            pt, ptid = psT.next()
                            for t in range(8):
                                S.op("pe", (lambda e, pt=pt, ct=ct, t=t, c8=c8, nch=nch: e.transpose(
                                    out=pt[:, t, :nch], in_=z_tm[:nch, ct, t, c8 * 128:(c8 + 1) * 128], identity=ident_b[:nch, :nch])),
                                     r=["s_ztm", "ident_b"], w=[ptid])
                            ce = ("act", "dve")[c8 % 2]
                            S.op(ce, copy_fn(ce, z_fm[:, c8, ct * 1024:ct * 1024 + nch * 8].rearrange("p (q t) -> p t q", t=8),
                                             pt[:, :, :nch]), r=[ptid], w=["s_zfm"])
                    HB = TB // 2
                    for blk in range(NBLK):
                        t0 = blk * TB
                        for m in range(8):
                            w_, wid = wgl.next()
                            S.dma("sp", (lambda e, w_=w_, m=m: e.dma_start(
                                out=w_[:], in_=WGLU[m // 4].rearrange("p (a b) -> p a b", a=8)[:, :, (m % 4) * 128:(m % 4) * 128 + 128])),
                                  r=["wscr"], w=[wid])
                            px, pxid = psX.next()
                            for hf in range(2):
                                for kc in range(8):
                                    S.op("pe", (lambda e, px=px, w_=w_, kc=kc, hf=hf, t0=t0: e.matmul(
                                        px[:, hf, :HB], lhsT=w_[:, kc, :], rhs=z_fm[:, kc, t0 + hf * HB:t0 + (hf + 1) * HB],
                                        start=(kc == 0), stop=(kc == 7))), r=["s_zfm", wid], w=[pxid])
                            sg_, sgid = sgt.next()
                            S.op("act", (lambda e, sg_=sg_, px=px: e.activation(out=sg_[:, :].rearrange("p (a b) -> p a b", a=2),
                                                                              in_=px[:, :, :HB], func=AF.Sigmoid)), r=[pxid], w=[sgid])
                            y_, yid = yo.next()
                            if S5DBG:
                                S.op("dve", (lambda e, y_=y_, m=m, t0=t0: e.tensor_copy(out=y_[:], in_=z_fm[:, m, t0:t0 + TB])),
                                     r=[sgid, "s_zfm"], w=[yid])
                            else:
                                S.op("dve", (lambda e, y_=y_, sg_=sg_, m=m, t0=t0: e.tensor_tensor(out=y_[:], in0=sg_[:], in1=z_fm[:, m, t0:t0 + TB],
                                                                                              op=ALU.mult)), r=[sgid, "s_zfm"], w=[yid])
                            S.dma("pool", (lambda e, y_=y_, m=m, t0=t0: e.dma_start(out=ymix[m, :, t0:t0 + TB], in_=y_[:])),
                                  r=[yid], w=["ymix"])
                    S.barrier()
                    S.emit()

        def final_phase(xsrc):
            with contextlib.ExitStack() as ph:
                xs = Ring(ph, nc, "o_xs", 3, [128, 512], F32)
                sq = Ring(ph, nc, "o_sq", 2, [128, 512], F32)
                pss = Ring(ph, nc, "o_pss", 1, [128, 512], F32, psum=True)
                rbc = Ring(ph, nc, "o_rbc", 1, [128, 512], F32)
                pt_ = Ring(ph, nc, "o_pt", 2, [128, 4, 128], F32, psum=True)
                ot = Ring(ph, nc, "o_ot", 2, [128, D], F32)
                hf32 = sbt(ph, "o_h", [128, 16, 512], F32)
                for blk in range(4):
                    c0 = blk * 512
                    n = 512
                    pt, pid = pss.next()
                    for kc in range(16):
                        t, tid = load_x_cols(xs, xsrc, kc, c0, n, 0, E)
                        q, qid = sq.next()
                        S.op("act", (lambda e, q=q, t=t: e.activation(out=q[:], in_=t[:], func=AF.Square)), r=[tid], w=[qid])
                        S.op("pe", (lambda e, pt=pt, q=q, kc=kc: e.matmul(pt[:, :], lhsT=ones_f[:], rhs=q[:, :],
                                                                         start=(kc == 0), stop=(kc == 15))),
                             r=[qid, "ones_f"], w=[pid])
                    r_, rid = rbc.next()
                    S.op("act", (lambda e, r_=r_, pt=pt: e.activation(out=r_[:], in_=pt[:], func=AF.Sqrt, scale=1.0 / D,
                                                                      bias=EPS)), r=[pid], w=[rid])
                    S.op("dve", (lambda e, r_=r_: e.reciprocal(out=r_[:], in_=r_[:])), r=[rid], w=[rid])
                    for kc in range(16):
                        t, tid = load_x_cols(xs, xsrc, kc, c0, n, 0, E)
                        S.op("dve", (lambda e, t=t, r_=r_: e.tensor_tensor(out=t[:], in0=t[:], in1=r_[:], op=ALU.mult)),
                             r=[tid, rid], w=[tid])
                        gv = V[:, voff["g_out"] + kc: voff["g_out"] + kc + 1]
                        S.op("act", (lambda e, t=t, kc=kc, gv=gv: e.activation(out=hf32[:, kc, :], in_=t[:],
                                                                              func=AF.Identity, scale=gv)),
                             r=[tid, "V"], w=["o_h"])
                    for ti in range(4):
                        o_, oid = ot.next()
                        for q4 in range(4):
                            p_, pid2 = pt_.next()
                            for i in range(4):
                                kc = q4 * 4 + i
                                S.op("pe", (lambda e, p_=p_, i=i, kc=kc, ti=ti: e.transpose(
                                    out=p_[:, i, :], in_=hf32[:, kc, ti * 128:(ti + 1) * 128], identity=ident_f[:])),
                                     r=["o_h", "ident_f"], w=[pid2])
                            ce = ("act", "dve")[q4 % 2]
                            S.op(ce, copy_fn(ce, o_[:, q4 * 512:(q4 + 1) * 512],
                                             p_[:].rearrange("p a b -> p (a b)")), r=[pid2], w=[oid])
                        row = c0 + ti * 128
                        S.dma("sp", (lambda e, o_=o_, row=row: e.dma_start(out=out[row:row + 128, :], in_=o_[:])),
                              r=[oid], w=["out"])
                S.barrier()
                S.emit()

        def dump(src, i):
            for c in range(16):
                S.dma("sp", (lambda e, c=c: e.dma_start(out=dbg_out[i][c], in_=src[c])), r=["xsrc"], w=["dbg"])
            S.barrier()

        mix = stage >= 3
        l0_proj(do_mixer=mix)
        if mix:
            s5_phase()
            na_phase()
            if dbg:
                for c in range(16):
                    S.dma("sp", (lambda e, c=c: e.dma_start(out=dbg_y[c], in_=ymix[c])), r=["ymix"], w=["dbgy"])
                S.barrier()
                S.emit()
                return nc
            mixout_phase(xA, xB, MODV(0, 2, 0))
            ffn_phase(0, xB, xA, DER(2), MODV(0, 3, 0), MODV(0, 5, 0))
            conformer_phase(xA, xB, DER(3), MODV(1, 0, 0), MODV(1, 2, 0))
            ffn_phase(1, xB, xA, DER(4), MODV(1, 3, 0), MODV(1, 5, 0))
            final_phase(xA)
        else:
            ffn_phase(0, xA, xB, DER(2), MODV(0, 3, 0), MODV(0, 5, 0))
            conformer_phase(xB, xA, DER(3), MODV(1, 0, 0), MODV(1, 2, 0))
            ffn_phase(1, xA, xB, DER(4), MODV(1, 3, 0), MODV(1, 5, 0))
            final_phase(xB)
    return nc


def _host_prep(inputs):
    f = lambda a: np.ascontiguousarray(np.asarray(a, dtype=np.float32))
    x = inputs["x"]; ctx = inputs["ctx"]
    common = {
        "w_mod": f(inputs["w_mod"]), "b_mod": f(inputs["b_mod"]), "g_mix": f(inputs["g_mix"]),
        "g_ffn": f(inputs["g_ffn"]), "g_out": f(inputs["g_out"]), "w_in": f(inputs["w_in"][0]),
        "ssm_d": f(inputs["ssm_d"][0]), "w_glu": f(inputs["ssm_w_glu"][0]), "w_out": f(inputs["w_out"][0]),
        "pw1": f(inputs["cv_w_pw1"][0]), "dw_b": f(inputs["cv_dw_b"][0]), "ln_g": f(inputs["cv_ln_g"][0]),
        "ln_b": f(inputs["cv_ln_b"][0]), "pw2": f(inputs["cv_w_pw2"][0]), "w_up": f(inputs["ffn_w_up"]),
        "fcb": f(inputs["ffn_conv_b"]), "w_dn": f(inputs["ffn_w_down"]),
    }
    rpb = np.asarray(inputs["na_rpb"][0], np.float32)
    maps = []
    for core in range(8):
        b, half = core // 2, core % 2
        m = dict(common)
        if half == 0:
            m["xc"] = f(x[b]); m["ctxc"] = f(ctx[b]); dirs = (0, 1)
            m["dw_w"] = f(inputs["cv_dw_w"][0]); m["fcw"] = f(inputs["ffn_conv_w"])
        else:
            m["xc"] = f(x[b][::-1]); m["ctxc"] = f(ctx[b][::-1]); dirs = (1, 0)
            m["dw_w"] = f(inputs["cv_dw_w"][0][::-1]); m["fcw"] = f(inputs["ffn_conv_w"][:, ::-1])
        m["cvec"] = f(np.stack([inputs["c"][b], inputs["c_ctx"]]))
        dd = list(dirs)
        m["lam_re"] = f(inputs["ssm_lam_re"][0][dd].reshape(2, 4096))
        m["lam_im"] = f(inputs["ssm_lam_im"][0][dd].reshape(2, 4096))
        m["log_dt"] = f(np.repeat(inputs["ssm_log_dt"][0][dd][:, :, None], 64, axis=2).reshape(2, 4096))
        m["b_re"] = f(inputs["ssm_b_re"][0][dd].reshape(2, 4096, 16))
        m["b_im"] = f(inputs["ssm_b_im"][0][dd].reshape(2, 4096, 16))
        m["c_re"] = f(inputs["ssm_c_re"][0][dd].reshape(2, 1024, 64))
        m["c_im"] = f(inputs["ssm_c_im"][0][dd].reshape(2, 1024, 64))
        m["na_bias"] = _na_bias_table(rpb, half)
        maps.append(m)
    return maps


def _na_bias_table(rpb, half):
    tab = np.full((3, 8, 128, 640), NEG, np.float32)
    for typ in range(3):
        r_even = 2 * typ if typ < 2 else 4
        kr0 = max(r_even - 4, 0)
        for qi in range(128):
            rl = r_even + qi // 64
            cl = qi % 64
            ro, co = (rl, cl) if half == 0 else (63 - rl, 63 - cl)
            rs = min(max(ro - 4, 0), 56)
            cs = min(max(co - 8, 0), 48)
            for kro in range(rs, rs + 8):
                krl = kro if half == 0 else 63 - kro
                jr = krl - kr0
                if jr < 0 or jr >= 10:
                    raise RuntimeError("key row outside block")
                ridx = kro - ro + 7
                for kco in range(cs, cs + 16):
                    kcl = kco if half == 0 else 63 - kco
                    cidx = kco - co + 15
                    tab[typ, :, qi, jr * 64 + kcl] = rpb[:, ridx, cidx]
    return tab


_NC_CACHE = {}


STAGE = 3
NCORES = 8
S5DBG = False
S5BAR = False
SBUF_REPORT = False
L0_JOB_FRAC = 0.45


def kernel(**inputs):
    maps = _host_prep(inputs)
    if "nc" not in _NC_CACHE:
        _NC_CACHE["nc"] = build_program(stage=STAGE)
    nc = _NC_CACHE["nc"]
    res = run_bass_kernel_spmd(nc, maps[:NCORES], core_ids=list(range(NCORES)))
    outp = np.zeros((4, NPOS, D), np.float32)
    for core in range(NCORES):
        b, half = core // 2, core % 2
        y = res.results[core]["out"]
        if half == 0:
            outp[b, :2048] = y
        else:
            outp[b, 2048:] = y[::-1]
    return outp
```

```python
import contextlib
import numpy as np
import concourse.bass as bass
import concourse.mybir as mybir
from concourse.bass_utils import run_bass_kernel_spmd

F32 = mybir.dt.float32
BF16 = mybir.dt.bfloat16
AF = mybir.ActivationFunctionType
ALU = mybir.AluOpType
AX = mybir.AxisListType

ENGS = ("pe", "act", "dve", "pool", "sp")
SAME_ENGINE_INORDER = ("pe",)
NDMA = 32

D = 2048
NPOS = 4096
NCTX = 256
E = 2176
KVE = 2432
TB = 544
NBLK = 4
FF = 5632
EPS = 1e-6
NEG = -30000.0
TWO_PI = 6.283185307179586


class Sched:
    def __init__(self, nc, st):
        self.nc = nc
        self.ops = {e: [] for e in ENGS}
        self.cnt = {e: 0 for e in ENGS}
        self.seen = {e: {} for e in ENGS}
        self.lastw = {}
        self.readers = {}
        self.dma_cnt = [0] * NDMA
        self.dma_rr = {"sp": 0, "pool": 0, "act": 0}
        self.sems = {}
        for e in ENGS:
            self.sems["e_" + e] = st.enter_context(nc.semaphore("sem_" + e))
        for i in range(NDMA):
            self.sems["d_%d" % i] = st.enter_context(nc.semaphore("semd_%d" % i))

    def _deps(self, eng, r, w, nosw=False):
        toks = []
        for b in r:
            t = self.lastw.get(b)
            if t is not None:
                toks.append(t)
        for b in w:
            t = self.lastw.get(b)
            if t is not None:
                toks.append(t)
            toks.extend(self.readers.get(b, ()))
        waits = {}
        for (k, v, e) in toks:
            if e == eng and (eng in SAME_ENGINE_INORDER or nosw):
                continue
            if self.seen[eng].get(k, 0) >= v:
                continue
            if waits.get(k, 0) < v:
                waits[k] = v
        for k, v in waits.items():
            self.seen[eng][k] = v
        return list(waits.items())

    def _commit(self, tok, r, w):
        for b in w:
            self.lastw[b] = tok
            self.readers[b] = []
        for b in r:
            if b not in w:
                self.readers.setdefault(b, []).append(tok)

    def op(self, eng, fn, r=(), w=(), nosw=False):
        waits = self._deps(eng, r, w, nosw)
        self.cnt[eng] += 1
        tok = ("e_" + eng, self.cnt[eng], eng)
        self.ops[eng].append((waits, fn, ("e_" + eng, 1)))
        self._commit(tok, r, w)
        return tok

    def dma(self, eng, fn, r=(), w=()):
        base, n = {"sp": (0, 20), "pool": (20, 12), "act": (0, 20)}[eng]
        i = base + self.dma_rr[eng]
        self.dma_rr[eng] = (self.dma_rr[eng] + 1) % n
        waits = self._deps(eng, r, w)
        k = "d_%d" % i
        prev = 16 * self.dma_cnt[i]
        if prev > 0 and self.seen[eng].get(k, 0) < prev:
            waits.append((k, prev))
            self.seen[eng][k] = prev
        self.dma_cnt[i] += 1
        tok = (k, 16 * self.dma_cnt[i], None)
        self.ops[eng].append((waits, fn, (k, 16)))
        self._commit(tok, r, w)
        return tok

    def barrier(self):
        for eng in ENGS:
            waits = []
            for e2 in ENGS:
                k, v = "e_" + e2, self.cnt[e2]
                if e2 != eng and v > 0 and self.seen[eng].get(k, 0) < v:
                    waits.append((k, v))
                    self.seen[eng][k] = v
            for i in range(NDMA):
                k, v = "d_%d" % i, 16 * self.dma_cnt[i]
                if v > 0 and self.seen[eng].get(k, 0) < v:
                    waits.append((k, v))
                    self.seen[eng][k] = v
            self.ops[eng].append((waits, None, None))
        self.lastw = {}
        self.readers = {}

    def emit(self):
        nc = self.nc
        sems = self.sems
        ops = self.ops
        self.ops = {e: [] for e in ENGS}
        with nc.Block() as block:
            def run(engname, engobj):
                for (waits, fn, inc) in ops[engname]:
                    for (k, v) in waits:
                        engobj.wait_ge(sems[k], v)
                    if fn is not None:
                        ins = fn(engobj)
                        ins.then_inc(sems[inc[0]], inc[1])

            @block.tensor
            def _(e):
                run("pe", e)

            @block.scalar
            def _(e):
                run("act", e)

            @block.vector
            def _(e):
                run("dve", e)

            @block.gpsimd
            def _(e):
                run("pool", e)

            @block.sync
            def _(e):
                run("sp", e)


_UID = [0]


class Ring:
    def __init__(self, st, nc, name, n, shape, dt, psum=False):
        self.tiles = []
        self.ids = []
        _UID[0] += 1
        for i in range(n):
            nm = "%s_%d_%d" % (name, _UID[0], i)
            if psum:
                t = st.enter_context(nc.psum_tensor(nm, shape, dt))
            else:
                t = st.enter_context(nc.sbuf_tensor(nm, shape, dt))
            self.tiles.append(t)
            self.ids.append(nm)
        self.i = 0

    def next(self):
        t, i = self.tiles[self.i], self.ids[self.i]
        self.i = (self.i + 1) % len(self.tiles)
        return t, i


def build_program(stage=99, dbg=False):
    nc = bass.Bass("TRN2", target_bir_lowering=False)

    def din(name, shape, dt=F32):
        return nc.dram_tensor(name, list(shape), dt, kind="ExternalInput").ap()

    def dscr(name, shape, dt):
        return nc.dram_tensor(name, list(shape), dt, kind="Internal").ap()

    x_in = din("xc", [NPOS, D])
    ctx_in = din("ctxc", [NCTX, D])
    cvec = din("cvec", [2, D])
    w_mod = din("w_mod", [2, D, 6 * D])
    b_mod = din("b_mod", [2, 6 * D])
    g_mix = din("g_mix", [2, D])
    g_ffn = din("g_ffn", [2, D])
    g_out = din("g_out", [D])
    w_in = din("w_in", [D, 4096])
    lam_re = din("lam_re", [2, 4096])
    lam_im = din("lam_im", [2, 4096])
    log_dt = din("log_dt", [2, 4096])
    b_re = din("b_re", [2, 4096, 16])
    b_im = din("b_im", [2, 4096, 16])
    c_re = din("c_re", [2, 1024, 64])
    c_im = din("c_im", [2, 1024, 64])
    ssm_d = din("ssm_d", [1024])
    w_glu = din("w_glu", [1024, 1024])
    na_bias = din("na_bias", [3, 8, 128, 640])
    w_out = din("w_out", [D, D])
    pw1 = din("pw1", [D, 4096])
    dw_w = din("dw_w", [31, D])
    dw_b = din("dw_b", [D])
    ln_g = din("ln_g", [D])
    ln_b = din("ln_b", [D])
    pw2 = din("pw2", [D, D])
    w_up = din("w_up", [2, D, 2 * FF])
    fcw = din("fcw", [2, 3, 2 * FF])
    fcb = din("fcb", [2, 2 * FF])
    w_dn = din("w_dn", [2, FF, D])
    out = nc.dram_tensor("out", [2048, D], F32, kind="ExternalOutput").ap()
    dbg_out = None
    if dbg:
        dbg_out = [nc.dram_tensor("dbg%d" % i, [16, 128, E], F32, kind="ExternalOutput").ap() for i in range(2)]
        dbg_y = nc.dram_tensor("dbgy", [16, 128, E], BF16, kind="ExternalOutput").ap()

    xA = dscr("xA", [16, 128, E], F32)
    xB = dscr("xB", [16, 128, E], F32)
    WUP = [dscr("WUP%d" % l, [22, 128, 16 * 512], BF16) for l in range(2)]
    WDN = [dscr("WDN%d" % l, [16, 128, 44 * 128], BF16) for l in range(2)]
    WIN = dscr("WIN", [8, 128, 16 * 512], BF16)
    WPW1 = dscr("WPW1", [8, 128, 16 * 512], BF16)
    WOUT = dscr("WOUT", [4, 128, 16 * 512], BF16)
    WPW2 = dscr("WPW2", [4, 128, 16 * 512], BF16)
    WGLU = dscr("WGLU", [2, 128, 8 * 512], BF16)
    u_tm = dscr("u_tm", [NPOS + NCTX, 1024], BF16)
    q_fm = dscr("q_fm", [8, 128, E], BF16)
    k_fm = dscr("k_fm", [8, 128, KVE + NCTX], BF16)
    v_tm = dscr("v_tm", [KVE + NCTX, 1024], BF16)
    ymix = dscr("ymix", [16, 128, E], BF16)

    with contextlib.ExitStack() as top:
        S = Sched(nc, top)

        def sbt(st, name, shape, dt):
            _UID[0] += 1
            return st.enter_context(nc.sbuf_tensor("%s_%d" % (name, _UID[0]), shape, dt))

        def pst(st, name, shape, dt):
            _UID[0] += 1
            return st.enter_context(nc.psum_tensor("%s_%d" % (name, _UID[0]), shape, dt))

        rr = [0]

        def cast_eng():
            rr[0] = (rr[0] + 1) % 3
            return ("act", "dve", "pool")[rr[0]]

        def copy_fn(eng, o, i):
            if eng == "act":
                return lambda e: e.activation(out=o, in_=i, func=AF.Copy)
            return lambda e: e.tensor_copy(out=o, in_=i)

        ident_f = sbt(top, "ident_f", [128, 128], F32)
        ident_b = sbt(top, "ident_b", [128, 128], BF16)
        NV = 1800
        V = sbt(top, "V", [128, NV], F32)
        modv = sbt(top, "modv", [128, 2 * 6 * 2 * 16], F32)
        der = sbt(top, "der", [128, 10 * 16], F32)

        def MODV(l, j, v):
            o = ((l * 6 + j) * 2 + v) * 16
            return modv[:, o:o + 16]

        S.op("pool", lambda e: e.memset(ident_f[:], 0.0), w=["ident_f"])
        S.op("pool", lambda e: e.affine_select(out=ident_f[:], in_=ident_f[:], pattern=[[-1, 128]],
                                               compare_op=ALU.not_equal, fill=1.0, base=0,
                                               channel_multiplier=1), r=["ident_f"], w=["ident_f"])
        S.op("dve", lambda e: e.tensor_copy(out=ident_b[:], in_=ident_f[:]), r=["ident_f"], w=["ident_b"])

        vecs = []

        def addvec(name, ap1d, n):
            vecs.append((name, ap1d.rearrange("(c p) -> c p", p=128), n // 128))

        addvec("c", cvec[0], D)
        addvec("cctx", cvec[1], D)
        for l in range(2):
            addvec("g_mix%d" % l, g_mix[l], D)
            addvec("g_ffn%d" % l, g_ffn[l], D)
            for j in range(6):
                addvec("b_mod%d_%d" % (l, j), b_mod[l, j * D:(j + 1) * D], D)
        addvec("g_out", g_out, D)
        for k in range(31):
            addvec("dw_w%d" % k, dw_w[k], D)
        addvec("dw_b", dw_b, D)
        addvec("ln_g", ln_g, D)
        addvec("ln_b", ln_b, D)
        for l in range(2):
            for k in range(3):
                addvec("fcw%d_%d" % (l, k), fcw[l, k], 2 * FF)
            addvec("fcb%d" % l, fcb[l], 2 * FF)
        for d in range(2):
            addvec("lam_re%d" % d, lam_re[d], 4096)
            addvec("lam_im%d" % d, lam_im[d], 4096)
            addvec("log_dt%d" % d, log_dt[d], 4096)
        voff = {}
        o = 0
        for (name, ap2, nch) in vecs:
            voff[name] = o
            o += nch
        assert o <= NV, o
        nrows = o

        def VC(name, c0=0, n=16):
            return V[:, voff[name] + c0: voff[name] + c0 + n]

        with contextlib.ExitStack() as ph:
            rowt = Ring(ph, nc, "rowt", 2, [128, 128], F32)
            pvt = Ring(ph, nc, "pvt", 2, [128, 128], F32, psum=True)
            for t in range((nrows + 127) // 128):
                r0, r1 = t * 128, min(nrows, t * 128 + 128)
                tl, tid = rowt.next()
                for (name, ap2, nch) in vecs:
                    a, b = voff[name], voff[name] + nch
                    lo, hi = max(a, r0), min(b, r1)
                    if lo < hi:
                        S.dma("sp", (lambda e, tl=tl, lo=lo, hi=hi, a=a, ap2=ap2, r0=r0:
                                     e.dma_start(out=tl[lo - r0:hi - r0, :], in_=ap2[lo - a:hi - a, :])),
                              w=[tid])
                pt, pid = pvt.next()
                n = r1 - r0
                S.op("pe", (lambda e, pt=pt, tl=tl, n=n: e.transpose(out=pt[:, :n], in_=tl[:n, :],
                                                                    identity=ident_f[:n, :n])),
                     r=[tid, "ident_f"], w=[pid])
                S.op("dve", (lambda e, pt=pt, n=n, r0=r0: e.tensor_copy(out=V[:, r0:r0 + n], in_=pt[:, :n])),
                     r=[pid], w=["V"])
            S.barrier()
            S.emit()

        with contextlib.ExitStack() as ph:
            s_bf = sbt(ph, "s_bf", [128, 2, 16], BF16)
            S.op("act", lambda e: e.activation(out=s_bf[:, 0, :], in_=VC("c"), func=AF.Silu), r=["V"], w=["s_bf"])
            S.op("act", lambda e: e.activation(out=s_bf[:, 1, :], in_=VC("cctx"), func=AF.Silu), r=["V"], w=["s_bf"])
            wf = Ring(ph, nc, "wmf", 3, [128, 2048], F32)
            wb = Ring(ph, nc, "wmb", 3, [128, 2048], BF16)
            pm = Ring(ph, nc, "pm", 2, [128, 512], F32, psum=True)
            msum = Ring(ph, nc, "msum", 2, [128, 32], F32)
            for l in range(2):
                for j in range(6):
                    pt, pid = pm.next()
                    for kc in range(16):
                        f, fid = wf.next()
                        b, bid = wb.next()
                        S.dma("sp", (lambda e, f=f, l=l, j=j, kc=kc: e.dma_start(
                            out=f[:], in_=w_mod[l, kc * 128:(kc + 1) * 128, j * D:(j + 1) * D])), w=[fid])
                        ce = cast_eng()
                        S.op(ce, copy_fn(ce, b[:], f[:]), r=[fid], w=[bid])
                        for m in range(16):
                            S.op("pe", (lambda e, pt=pt, b=b, m=m, kc=kc: e.matmul(
                                pt[:, kc * 32 + 2 * m:kc * 32 + 2 * m + 2], lhsT=b[:, m * 128:(m + 1) * 128],
                                rhs=s_bf[:, :, kc], start=True, stop=True)), r=[bid, "s_bf"], w=[pid])
                    ms, msid = msum.next()
                    S.op("dve", (lambda e, pt=pt, ms=ms: e.tensor_reduce(
                        out=ms[:], in_=pt[:].rearrange("p (k c) -> p c k", k=16), axis=AX.X, op=ALU.add)),
                         r=[pid], w=[msid])
                    for v in range(2):
                        S.op("dve", (lambda e, ms=ms, l=l, j=j, v=v: e.tensor_tensor(
                            out=MODV(l, j, v), in0=ms[:, v:32:2], in1=VC("b_mod%d_%d" % (l, j)), op=ALU.add)),
                             r=[msid, "V"], w=["modv"])

            def derive(idx, gname, l, jscale, v):
                o = idx * 16
                S.op("dve", lambda e: e.tensor_scalar(out=der[:, o:o + 16], in0=MODV(l, jscale, v), scalar1=1.0,
                                                      scalar2=None, op0=ALU.add), r=["modv"], w=["der"])
                S.op("dve", lambda e: e.tensor_tensor(out=der[:, o:o + 16], in0=der[:, o:o + 16], in1=VC(gname),
                                                      op=ALU.mult), r=["der", "V"], w=["der"])
            derive(0, "g_mix0", 0, 1, 0)
            derive(1, "g_mix0", 0, 1, 1)
            derive(2, "g_ffn0", 0, 4, 0)
            derive(3, "g_mix1", 1, 1, 0)
            derive(4, "g_ffn1", 1, 4, 0)
            S.barrier()
            S.emit()

        def DER(i):
            return der[:, i * 16:(i + 1) * 16]

        with contextlib.ExitStack() as ph:
            cf = Ring(ph, nc, "cvf", 2, [128, 8192], F32)
            cb = Ring(ph, nc, "cvb", 2, [128, 8192], BF16)

            def convert(parts, dst2, F):
                f, fid = cf.next()
                b, bid = cb.next()
                for (vf, src) in parts:
                    S.dma("sp", (lambda e, vf=vf, src=src, f=f: e.dma_start(out=vf(f), in_=src)), w=[fid])
                ce = cast_eng()
                S.op(ce, copy_fn(ce, b[:, :F], f[:, :F]), r=[fid], w=[bid])
                S.dma("pool", lambda e: e.dma_start(out=dst2, in_=b[:, :F]), r=[bid], w=["wscr"])

            for (src, dst, ng) in ((w_in, WIN, 8), (w_out, WOUT, 4)):
                v = src.rearrange("(kc p) (g c) -> g p kc c", p=128, c=512)
                for G in range(ng):
                    convert([((lambda f: f[:, :8192].rearrange("p (kc c) -> p kc c", kc=16)), v[G])], dst[G], 8192)
            v = w_glu.rearrange("(kc p) (g c) -> g p kc c", p=128, c=512)
            for G in range(2):
                convert([((lambda f: f[:, :4096].rearrange("p (kc c) -> p kc c", kc=8)), v[G])], WGLU[G], 4096)
            S.barrier()
            S.emit()

        bg_jobs = []
        for l in range(2):
            vu = w_up[l].rearrange("(kc p) (ug g c) -> ug g p kc c", p=128, ug=2, c=256)
            for G in range(22):
                for kq in range(4):
                    parts = []
                    for ug in range(2):
                        parts.append(((lambda f, ug=ug: f[:, :2048].rearrange("p (kc u c) -> p kc u c", kc=4, u=2)[:, :, ug, :]),
                                      vu[ug, G][:, kq * 4:(kq + 1) * 4, :]))
                    bg_jobs.append((parts, WUP[l][G][:, kq * 2048:(kq + 1) * 2048], 2048))
            vd = w_dn[l].rearrange("(j p) (m c) -> m p j c", p=128, c=128)
            for m in range(16):
                for jq in range(4):
                    bg_jobs.append(([((lambda f: f[:, :1408].rearrange("p (j c) -> p j c", j=11)), vd[m][:, jq * 11:(jq + 1) * 11, :])],
                                    WDN[l][m][:, jq * 1408:(jq + 1) * 1408], 1408))
            if l == 0:
                for (src, dst, ng) in ((pw1, WPW1, 8), (pw2, WPW2, 4)):
                    vv = src.rearrange("(kc p) (g c) -> g p kc c", p=128, c=512)
                    for G in range(ng):
                        for kq in range(4):
                            bg_jobs.append(([((lambda f: f[:, :2048].rearrange("p (kc c) -> p kc c", kc=4)), vv[G][:, kq * 4:(kq + 1) * 4, :])],
                                            dst[G][:, kq * 2048:(kq + 1) * 2048], 2048))
        bg_state = {"i": 0, "rr": 0}

        def bg_emit(n, cf, cb):
            for _ in range(n):
                if bg_state["i"] >= len(bg_jobs):
                    return
                parts, dst2, F = bg_jobs[bg_state["i"]]
                bg_state["i"] += 1
                f, fid = cf.next()
                b, bid = cb.next()
                for (vf, src) in parts:
                    S.dma("sp", (lambda e, vf=vf, src=src, f=f: e.dma_start(out=vf(f), in_=src)), w=[fid])
                bg_state["rr"] += 1
                ce = "act"
                S.op(ce, copy_fn(ce, b[:, :F], f[:, :F]), r=[fid], w=[bid])
                S.dma("pool", (lambda e, dst2=dst2, b=b, F=F: e.dma_start(out=dst2, in_=b[:, :F])), r=[bid], w=["wscr"])

        def l0_proj(do_mixer):
            with contextlib.ExitStack() as ph:
                xt = Ring(ph, nc, "xt", 2, [128, D], F32)
                xn = Ring(ph, nc, "xn", 2, [128, D], BF16)
                junk = sbt(ph, "junk", [128, D], BF16)
                ss = Ring(ph, nc, "ss", 4, [128, 2], F32)
                hfm = Ring(ph, nc, "hfm", 2, [128, 16, 512], BF16)
                ptr = Ring(ph, nc, "ptr", 2, [128, 4, 128], BF16, psum=True)
                pxf = Ring(ph, nc, "pxf", 2, [128, 4, 128], F32, psum=True)
                pmm = Ring(ph, nc, "pmm", 3, [128, 512], F32, psum=True)
                xo = Ring(ph, nc, "xo", 2, [128, 4, 128], F32)
                wg = Ring(ph, nc, "wg", 2, [128, 16, 512], BF16)
                ob = Ring(ph, nc, "ob", 3, [128, 512], BF16)
                l_bcf = Ring(ph, nc, "l_bcf", 4, [128, 2048], F32)
                l_bcb = Ring(ph, nc, "l_bcb", 2, [128, 2048], BF16)
                l_tiles = [0]
                ngroups = 9 if do_mixer else 5
                for g in range(ngroups):
                    is_ctx = (g == 8)
                    ntile = 2 if is_ctx else 4
                    ntok = ntile * 128
                    h, hid = hfm.next()
                    Av, Bv = (DER(1), MODV(0, 0, 1)) if is_ctx else (DER(0), MODV(0, 0, 0))
                    for ti in range(ntile):
                        src = ctx_in if is_ctx else x_in
                        p0 = ti * 128 if is_ctx else g * 512 + ti * 128
                        if do_mixer:
                            l_tiles[0] += 1
                            want = (l_tiles[0] * int(len(bg_jobs) * L0_JOB_FRAC)) // 34
                            bg_emit(want - bg_state["i"], l_bcf, l_bcb)
                        x_, xid = xt.next()
                        S.dma("sp", (lambda e, x_=x_, src=src, p0=p0: e.dma_start(out=x_[:], in_=src[p0:p0 + 128, :])),
                              w=[xid])
                        s_, sid = ss.next()
                        S.op("act", (lambda e, x_=x_, s_=s_: e.activation(out=junk[:], in_=x_[:], func=AF.Square,
                                                                          accum_out=s_[:, 0:1])),
                             r=[xid], w=["junk", sid])
                        S.op("act", (lambda e, s_=s_: e.activation(out=s_[:, 1:2], in_=s_[:, 0:1], func=AF.Sqrt,
                                                                   scale=1.0 / D, bias=EPS)), r=[sid], w=[sid])
                        S.op("dve", (lambda e, s_=s_: e.reciprocal(out=s_[:, 1:2], in_=s_[:, 1:2])), r=[sid], w=[sid])
                        n_, nid = xn.next()
                        S.op("dve", (lambda e, n_=n_, x_=x_, s_=s_: e.tensor_scalar(
                            out=n_[:], in0=x_[:], scalar1=s_[:, 1:2], scalar2=None, op0=ALU.mult)),
                             r=[xid, sid], w=[nid])
                        for q4 in range(4):
                            pt, pid = ptr.next()
                            for i in range(4):
                                kc = q4 * 4 + i
                                S.op("pe", (lambda e, pt=pt, i=i, n_=n_, kc=kc: e.transpose(
                                    out=pt[:, i, :], in_=n_[:, kc * 128:(kc + 1) * 128], identity=ident_b[:])),
                                     r=[nid, "ident_b"], w=[pid])
                            for i in range(4):
                                kc = q4 * 4 + i
                                S.op("act", (lambda e, pt=pt, i=i, h=h, kc=kc, ti=ti, Av=Av, Bv=Bv: e.activation(
                                    out=h[:, kc, ti * 128:(ti + 1) * 128], in_=pt[:, i, :], func=AF.Identity,
                                    scale=Av[:, kc:kc + 1], bias=Bv[:, kc:kc + 1])),
                                     r=[pid, "der", "modv"], w=[hid])
                        if (not is_ctx) and p0 < E:
                            for q4 in range(4):
                                pf, pfid = pxf.next()
                                for i in range(4):
                                    kc = q4 * 4 + i
                                    S.op("pe", (lambda e, pf=pf, i=i, x_=x_, kc=kc: e.transpose(
                                        out=pf[:, i, :], in_=x_[:, kc * 128:(kc + 1) * 128], identity=ident_f[:])),
                                         r=[xid, "ident_f"], w=[pfid])
                                o_, oid = xo.next()
                                S.op("dve", (lambda e, o_=o_, pf=pf: e.tensor_copy(out=o_[:], in_=pf[:])),
                                     r=[pfid], w=[oid])
                                S.dma("pool", (lambda e, o_=o_, q4=q4, p0=p0: e.dma_start(
                                    out=xA[q4 * 4:(q4 + 1) * 4, :, p0:p0 + 128].rearrange("c p t -> p c t"),
                                    in_=o_[:])), r=[oid], w=["xA"])
                    if not do_mixer:
                        continue
                    base = 0 if is_ctx else g * 512
                    nq = 0 if is_ctx else min(512, max(0, E - base))
                    nkv = ntok if is_ctx else min(512, max(0, KVE - base))
                    for G in range(8):
                        kind = ("u", "u", "q", "q", "k", "k", "v", "v")[G]
                        n = {"u": ntok, "q": nq, "k": nkv, "v": nkv}[kind]
                        if n == 0:
                            continue
                        w_, wid = wg.next()
                        S.dma("sp", (lambda e, w_=w_, G=G: e.dma_start(
                            out=w_[:].rearrange("p a b -> p (a b)"), in_=WIN[G])), r=["wscr"], w=[wid])
                        if kind in ("u", "v"):
                            for ti in range(n // 128):
                                pt, pid = pmm.next()
                                for kc in range(16):
                                    S.op("pe", (lambda e, pt=pt, h=h, kc=kc, ti=ti, w_=w_: e.matmul(
                                        pt[:, :], lhsT=h[:, kc, ti * 128:(ti + 1) * 128], rhs=w_[:, kc, :],
                                        start=(kc == 0), stop=(kc == 15))), r=[hid, wid], w=[pid])
                                o_, oid = ob.next()
                                ce = ("act", "dve")[ti % 2]
                                S.op(ce, copy_fn(ce, o_[:, :], pt[:, :]), r=[pid], w=[oid])
                                if kind == "u":
                                    row = (NPOS if is_ctx else base) + ti * 128
                                    dst = u_tm[row:row + 128, (G % 2) * 512:(G % 2) * 512 + 512]
                                else:
                                    row = (KVE if is_ctx else base) + ti * 128
                                    dst = v_tm[row:row + 128, (G % 2) * 512:(G % 2) * 512 + 512]
                                S.dma("pool", (lambda e, o_=o_, dst=dst: e.dma_start(out=dst, in_=o_[:, :])),
                                      r=[oid], w=["uv_scr"])
                        else:
                            for m in range(4):
                                pt, pid = pmm.next()
                                for kc in range(16):
                                    S.op("pe", (lambda e, pt=pt, h=h, kc=kc, m=m, w_=w_, n=n: e.matmul(
                                        pt[:, :n], lhsT=w_[:, kc, m * 128:(m + 1) * 128], rhs=h[:, kc, :n],
                                        start=(kc == 0), stop=(kc == 15))), r=[hid, wid], w=[pid])
                                o_, oid = ob.next()
                                ce = ("act", "dve")[m % 2]
                                S.op(ce, copy_fn(ce, o_[:, :n], pt[:, :n]), r=[pid], w=[oid])
                                hd = (G % 2) * 4 + m
                                if kind == "q":
                                    dst = q_fm[hd, :, base:base + n]
                                else:
                                    c0 = KVE if is_ctx else base
                                    dst = k_fm[hd, :, c0:c0 + n]
                                S.dma("pool", (lambda e, o_=o_, dst=dst, n=n: e.dma_start(out=dst, in_=o_[:, :n])),
                                      r=[oid], w=["qk_scr"])
                S.barrier()
                S.emit()

        ones_f = sbt(top, "ones_f", [128, 128], F32)
        S.op("pool", lambda e: e.memset(ones_f[:], 1.0), w=["ones_f"])

        def load_x_cols(ring, xsrc, kc, c0, n, lo, hi):
            t, tid = ring.next()
            a, b = max(c0, lo), min(c0 + n, hi)
            if a > c0 or b < c0 + n:
                S.op("pool", (lambda e, t=t, n=n: e.memset(t[:, :n], 0.0)), w=[tid])
            S.dma("sp", (lambda e, t=t, a=a, b=b, c0=c0, kc=kc: e.dma_start(out=t[:, a - c0:b - c0], in_=xsrc[kc, :, a:b])),
                  r=["xsrc"], w=[tid])
            return t, tid

        def norm_block(ph, xsrc, c0, n, Av, Bv, h, hid, rings):
            xs, sq, pss, rbc = rings
            H = n // 2
            for sbk in range(2):
                o0 = sbk * H
                pt, pid = pss.next()
                for kc in range(16):
                    t, tid = load_x_cols(xs, xsrc, kc, c0 + o0, H, 0, E)
                    S.op("act", (lambda e, t=t, H=H: e.activation(out=t[:, :H], in_=t[:, :H], func=AF.Square)),
                         r=[tid], w=[tid])
                    S.op("pe", (lambda e, pt=pt, t=t, kc=kc, H=H: e.matmul(
                        pt[:, :H], lhsT=ones_f[:], rhs=t[:, :H], start=(kc == 0), stop=(kc == 15))),
                         r=[tid, "ones_f"], w=[pid])
                r_, rid = rbc.next()
                S.op("act", (lambda e, r_=r_, pt=pt, H=H: e.activation(out=r_[:, :H], in_=pt[:, :H], func=AF.Sqrt,
                                                                      scale=1.0 / D, bias=EPS)), r=[pid], w=[rid])
                S.op("dve", (lambda e, r_=r_, H=H: e.reciprocal(out=r_[:, :H], in_=r_[:, :H])), r=[rid], w=[rid])
                for kc in range(16):
                    t, tid = load_x_cols(xs, xsrc, kc, c0 + o0, H, 0, E)
                    S.op("dve", (lambda e, t=t, r_=r_, H=H: e.tensor_tensor(out=t[:, :H], in0=t[:, :H], in1=r_[:, :H],
                                                                            op=ALU.mult)), r=[tid, rid], w=[tid])
                    S.op("act", (lambda e, t=t, h=h, kc=kc, H=H, o0=o0, Av=Av, Bv=Bv: e.activation(
                        out=h[:, kc, o0:o0 + H], in_=t[:, :H], func=AF.Identity, scale=Av[:, kc:kc + 1], bias=Bv[:, kc:kc + 1])),
                         r=[tid, "der", "modv"], w=[hid])

        def ffn_phase(l, xsrc, xdst, Av, Bv, gate):
            NH = TB + 2
            with contextlib.ExitStack() as ph:
                xs = Ring(ph, nc, "f_xs", 4, [128, NH // 2], F32)
                sq = None
                pss = Ring(ph, nc, "f_pss", 2, [128, 512], F32, psum=True)
                rbc = Ring(ph, nc, "f_rbc", 2, [128, NH // 2], F32)
                hr = Ring(ph, nc, "f_h", 1, [128, 16, NH], BF16)
                act = sbt(ph, "f_act", [128, 44, TB], BF16)
                wu = Ring(ph, nc, "f_wu", 2, [128, 16, 512], BF16)
                wd = Ring(ph, nc, "f_wd", 2, [128, 44, 128], BF16)
                pu = Ring(ph, nc, "f_pu", 2, [128, 2, 512], F32, psum=True)
                cv = Ring(ph, nc, "f_cv", 4, [128, TB], F32)
                sg = Ring(ph, nc, "f_sg", 2, [128, TB], F32)
                pdn = Ring(ph, nc, "f_pd", 1, [128, 2, 512], F32, psum=True)
                xo = Ring(ph, nc, "f_xo", 2, [128, TB], F32)
                fw = "fcw%d_" % l
                H2 = NH // 2
                for blk in range(NBLK):
                    t0 = blk * TB
                    h, hid = hr.next()
                    norm_block(ph, xsrc, t0 - 1, NH, Av, Bv, h, hid, (xs, sq, pss, rbc))
                    for G in range(22):
                        w_, wid = wu.next()
                        S.dma("sp", (lambda e, w_=w_, G=G: e.dma_start(
                            out=w_[:].rearrange("p a b -> p (a b)"), in_=WUP[l][G])), r=["wscr"], w=[wid])
                        for jj in range(2):
                            res = []
                            for ug in range(2):
                                ch = ug * 44 + G * 2 + jj
                                pt, pid = pu.next()
                                for hf in range(2):
                                    for kc in range(16):
                                        S.op("pe", (lambda e, pt=pt, w_=w_, kc=kc, ug=ug, jj=jj, hf=hf, h=h: e.matmul(
                                            pt[:, hf, :H2], lhsT=w_[:, kc, ug * 256 + jj * 128: ug * 256 + jj * 128 + 128],
                                            rhs=h[:, kc, hf * H2:(hf + 1) * H2], start=(kc == 0), stop=(kc == 15))),
                                             r=[hid, wid], w=[pid])
                                if blk == 0:
                                    S.op("dve", (lambda e, pt=pt: e.memset(pt[:, 0, 0:1], 0.0)), r=[pid], w=[pid])
                                c_, cid = cv.next()
                                def seg(off):
                                    a = []
                                    split = H2 - off
                                    a.append((0, off, 0, min(split, TB)))
                                    if split < TB:
                                        a.append((1, 0, split, TB))
                                    return a
                                first = True
                                for k, wname in ((1, fw + "1"), (0, fw + "0"), (2, fw + "2")):
                                    wv = V[:, voff[wname] + ch: voff[wname] + ch + 1]
                                    for (hf, pc, lo, hi) in seg(k):
                                        if first:
                                            bv = V[:, voff["fcb%d" % l] + ch: voff["fcb%d" % l] + ch + 1]
                                            S.op("act", (lambda e, c_=c_, pt=pt, hf=hf, pc=pc, lo=lo, hi=hi, wv=wv, bv=bv:
                                                         e.activation(out=c_[:, lo:hi], in_=pt[:, hf, pc:pc + hi - lo],
                                                                      func=AF.Identity, scale=wv, bias=bv)),
                                                 r=[pid, "V"], w=[cid])
                                        else:
                                            S.op("dve", (lambda e, c_=c_, pt=pt, hf=hf, pc=pc, lo=lo, hi=hi, wv=wv:
                                                         e.scalar_tensor_tensor(out=c_[:, lo:hi], in0=pt[:, hf, pc:pc + hi - lo],
                                                                                scalar=wv, in1=c_[:, lo:hi],
                                                                                op0=ALU.mult, op1=ALU.add)),
                                                 r=[pid, cid, "V"], w=[cid], nosw=True)
                                    first = False
                                res.append((c_, cid))
                            (cu, cuid), (cg, cgid) = res
                            s_, sid = sg.next()
                            S.op("act", (lambda e, s_=s_, cg=cg: e.activation(out=s_[:], in_=cg[:], func=AF.Silu)),
                                 r=[cgid], w=[sid])
                            j = G * 2 + jj
                            S.op("pool", (lambda e, s_=s_, cu=cu, j=j: e.tensor_tensor(out=act[:, j, :], in0=s_[:], in1=cu[:],
                                                                                      op=ALU.mult)),
                                 r=[sid, cuid], w=["f_act"])
                    HB = TB // 2
                    for m in range(16):
                        w_, wid = wd.next()
                        S.dma("sp", (lambda e, w_=w_, m=m: e.dma_start(
                            out=w_[:].rearrange("p a b -> p (a b)"), in_=WDN[l][m])), r=["wscr"], w=[wid])
                        o_, oid = load_x_cols(xo, xsrc, m, t0, TB, 0, E)
                        pt, pid = pdn.next()
                        for hf in range(2):
                            for j in range(44):
                                S.op("pe", (lambda e, pt=pt, w_=w_, j=j, hf=hf: e.matmul(
                                    pt[:, hf, :HB], lhsT=w_[:, j, :], rhs=act[:, j, hf * HB:(hf + 1) * HB],
                                    start=(j == 0), stop=(j == 43))), r=["f_act", wid], w=[pid])
                        for hf in range(2):
                            S.op("dve", (lambda e, o_=o_, pt=pt, hf=hf, m=m: e.scalar_tensor_tensor(
                                out=o_[:, hf * HB:(hf + 1) * HB], in0=pt[:, hf, :HB], scalar=gate[:, m:m + 1],
                                in1=o_[:, hf * HB:(hf + 1) * HB], op0=ALU.mult, op1=ALU.add)),
                                 r=[pid, oid, "modv"], w=[oid])
                        S.dma("pool", (lambda e, o_=o_, m=m, t0=t0: e.dma_start(out=xdst[m, :, t0:t0 + TB], in_=o_[:, :TB])),
                              r=[oid], w=["xdst"])
                S.barrier()
                S.emit()

        def proj_residual(ph, wscr, actt, actid, xsrc, xdst, t0, gate, rings):
            wr, pdn, xo = rings
            HB = TB // 2
            for m in range(16):
                w_, wid = wr.next()
                S.dma("sp", (lambda e, w_=w_, m=m: e.dma_start(
                    out=w_[:], in_=wscr[m // 4].rearrange("p (a b) -> p a b", a=16)[:, :, (m % 4) * 128:(m % 4) * 128 + 128])),
                      r=["wscr"], w=[wid])
                o_, oid = load_x_cols(xo, xsrc, m, t0, TB, 0, E)
                pt, pid = pdn.next()
                for hf in range(2):
                    for kc in range(16):
                        S.op("pe", (lambda e, pt=pt, w_=w_, kc=kc, hf=hf: e.matmul(
                            pt[:, hf, :HB], lhsT=w_[:, kc, :], rhs=actt[:, kc, hf * HB:(hf + 1) * HB],
                            start=(kc == 0), stop=(kc == 15))), r=[actid, wid], w=[pid])
                for hf in range(2):
                    S.op("dve", (lambda e, o_=o_, pt=pt, hf=hf, m=m: e.scalar_tensor_tensor(
                        out=o_[:, hf * HB:(hf + 1) * HB], in0=pt[:, hf, :HB], scalar=gate[:, m:m + 1],
                        in1=o_[:, hf * HB:(hf + 1) * HB], op0=ALU.mult, op1=ALU.add)),
                         r=[pid, oid, "modv"], w=[oid])
                S.dma("pool", (lambda e, o_=o_, m=m, t0=t0: e.dma_start(out=xdst[m, :, t0:t0 + TB], in_=o_[:, :TB])),
                      r=[oid], w=["xdst"])

        def mixout_phase(xsrc, xdst, gate):
            with contextlib.ExitStack() as ph:
                yt = Ring(ph, nc, "m_y", 2, [128, 16, TB], BF16)
                wr = Ring(ph, nc, "m_w", 3, [128, 16, 128], BF16)
                pdn = Ring(ph, nc, "m_pd", 2, [128, 2, 512], F32, psum=True)
                xo = Ring(ph, nc, "m_xo", 3, [128, TB], F32)
                for blk in range(NBLK):
                    t0 = blk * TB
                    y_, yid = yt.next()
                    S.dma("sp", (lambda e, y_=y_, t0=t0: e.dma_start(
                        out=y_[:], in_=ymix[:, :, t0:t0 + TB].rearrange("c p t -> p c t"))), r=["ymix"], w=[yid])
                    proj_residual(ph, WOUT, y_, yid, xsrc, xdst, t0, gate, (wr, pdn, xo))
                S.barrier()
                S.emit()

        def conformer_phase(xsrc, xdst, Av, Bv, gate):
            NH = TB + 30
            H2 = NH // 2
            HB = TB // 2
            with contextlib.ExitStack() as ph:
                xs = Ring(ph, nc, "c_xs", 4, [128, H2], F32)
                pss = Ring(ph, nc, "c_pss", 1, [128, 512], F32, psum=True)
                rbc = Ring(ph, nc, "c_rbc", 2, [128, H2], F32)
                hr = Ring(ph, nc, "c_h", 1, [128, 16, NH], BF16)
                w1 = Ring(ph, nc, "c_w1", 4, [128, 16, 128], BF16)
                pa = Ring(ph, nc, "c_pa", 2, [128, 2, 512], F32, psum=True)
                pzd = Ring(ph, nc, "c_pzd", 1, [128, 2, 512], F32, psum=True)
                sgr = Ring(ph, nc, "c_sg", 2, [128, NH], F32)
                cir = Ring(ph, nc, "c_ci", 2, [128, NH], BF16)
                dgr = Ring(ph, nc, "c_dg", 2, [128, 31, 128], BF16)
                zbuf = sbt(ph, "c_z", [128, 16, TB], F32)
                zs = sbt(ph, "c_zs", [128, 16, TB], BF16)
                mean = sbt(ph, "c_mean", [128, TB], F32)
                rstd = sbt(ph, "c_rstd", [128, TB], F32)
                ones_b = sbt(ph, "c_ones", [128, 128], BF16)
                S.op("dve", lambda e: e.tensor_copy(out=ones_b[:], in_=ones_f[:]), r=["ones_f"], w=["c_ones"])
                wr = Ring(ph, nc, "c_w2", 2, [128, 16, 128], BF16)
                xo = Ring(ph, nc, "c_xo", 2, [128, TB], F32)
                v2 = lambda t: t[:, :].rearrange("p (a b) -> p a b", a=2)
                for blk in range(NBLK):
                    t0 = blk * TB
                    h, hid = hr.next()
                    norm_block(ph, xsrc, t0 - 15, NH, Av, Bv, h, hid, (xs, None, pss, rbc))
                    for c in range(16):
                        ws = []
                        for part in range(2):
                            cc = part * 16 + c
                            w_, wid = w1.next()
                            S.dma("sp", (lambda e, w_=w_, cc=cc: e.dma_start(
                                out=w_[:], in_=WPW1[cc // 4].rearrange("p (a b) -> p a b", a=16)[:, :, (cc % 4) * 128:(cc % 4) * 128 + 128])),
                                  r=["wscr"], w=[wid])
                            ws.append((w_, wid))
                        dg, dgid = dgr.next()
                        for k in range(31):
                            wv = V[:, voff["dw_w%d" % k] + c: voff["dw_w%d" % k] + c + 1]
                            S.op("act", (lambda e, dg=dg, k=k, wv=wv: e.activation(out=dg[:, k, :], in_=ident_b[:], func=AF.Identity,
                                                                                 scale=wv)), r=["ident_b", "V"], w=[dgid])
                        pts = []
                        for part in range(2):
                            w_, wid = ws[part]
                            pt, pid = pa.next()
                            for hf in range(2):
                                for kc in range(16):
                                    S.op("pe", (lambda e, pt=pt, w_=w_, kc=kc, hf=hf, h=h: e.matmul(
                                        pt[:, hf, :H2], lhsT=w_[:, kc, :], rhs=h[:, kc, hf * H2:(hf + 1) * H2],
                                        start=(kc == 0), stop=(kc == 15))), r=[hid, wid], w=[pid])
                            pts.append((pt, pid))
                        (pA, pAid), (pG, pGid) = pts
                        sg_, sgid = sgr.next()
                        ci, ciid = cir.next()
                        S.op("act", (lambda e, sg_=sg_, pG=pG: e.activation(
                            out=v2(sg_), in_=pG[:, :, :H2], func=AF.Sigmoid)), r=[pGid], w=[sgid])
                        S.op("dve", (lambda e, ci=ci, pA=pA, sg_=sg_: e.tensor_tensor(
                            out=v2(ci), in0=pA[:, :, :H2], in1=v2(sg_), op=ALU.mult)), r=[pAid, sgid], w=[ciid])
                        if blk == 0:
                            S.op("dve", (lambda e, ci=ci: e.memset(ci[:, 0:15], 0.0)), r=[ciid], w=[ciid])
                        pz, pzid = pzd.next()
                        for hf in range(2):
                            for k in range(31):
                                S.op("pe", (lambda e, pz=pz, dg=dg, k=k, hf=hf, ci=ci: e.matmul(
                                    pz[:, hf, :HB], lhsT=dg[:, k, :], rhs=ci[:, hf * HB + k:hf * HB + k + HB],
                                    start=(k == 0), stop=(k == 30))), r=[dgid, ciid], w=[pzid])
                        bv = V[:, voff["dw_b"] + c: voff["dw_b"] + c + 1]
                        S.op("act", (lambda e, c=c, pz=pz, bv=bv: e.activation(
                            out=zbuf[:, c, :].rearrange("p (a b) -> p a b", a=2), in_=pz[:, :, :HB], func=AF.Identity, bias=bv)),
                             r=[pzid, "V"], w=["c_z%d" % c])
                        S.op("act", (lambda e, c=c: e.activation(out=zs[:, c, :], in_=zbuf[:, c, :], func=AF.Square)),
                             r=["c_z%d" % c], w=["c_zs"])
                    pS, pSid = pa.next()
                    pQ, pQid = pa.next()
                    for hf in range(2):
                        for c in range(16):
                            S.op("pe", (lambda e, pS=pS, hf=hf, c=c: e.matmul(pS[:, hf, :HB], lhsT=ones_f[:], rhs=zbuf[:, c, hf * HB:(hf + 1) * HB],
                                                                             start=(c == 0), stop=(c == 15))), r=["c_z%d" % c, "ones_f"], w=[pSid])
                    for hf in range(2):
                        for c in range(16):
                            S.op("pe", (lambda e, pQ=pQ, hf=hf, c=c: e.matmul(pQ[:, hf, :HB], lhsT=ones_b[:], rhs=zs[:, c, hf * HB:(hf + 1) * HB],
                                                                             start=(c == 0), stop=(c == 15))), r=["c_zs", "c_ones"], w=[pQid])
                    S.op("act", (lambda e, pS=pS: e.activation(out=v2(mean), in_=pS[:, :, :HB], func=AF.Identity, scale=1.0 / D)),
                         r=[pSid], w=["mean"])
                    S.op("dve", (lambda e: e.tensor_tensor(out=rstd[:], in0=mean[:], in1=mean[:], op=ALU.mult)),
                         r=["mean"], w=["rstd"])
                    S.op("dve", (lambda e, pQ=pQ: e.scalar_tensor_tensor(out=v2(rstd), in0=pQ[:, :, :HB], scalar=1.0 / D,
                                                                        in1=v2(rstd), op0=ALU.mult, op1=ALU.subtract)),
                         r=[pQid, "rstd"], w=["rstd"])
                    S.op("act", (lambda e: e.activation(out=rstd[:], in_=rstd[:], func=AF.Sqrt, bias=EPS, scale=1.0)),
                         r=["rstd"], w=["rstd"])
                    S.op("dve", (lambda e: e.reciprocal(out=rstd[:], in_=rstd[:])), r=["rstd"], w=["rstd"])
                    for c in range(16):
                        S.op("dve", (lambda e, c=c: e.tensor_tensor(out=zbuf[:, c, :], in0=zbuf[:, c, :], in1=mean[:], op=ALU.subtract)),
                             r=["c_z%d" % c, "mean"], w=["c_z%d" % c])
                        S.op("dve", (lambda e, c=c: e.tensor_tensor(out=zbuf[:, c, :], in0=zbuf[:, c, :], in1=rstd[:], op=ALU.mult)),
                             r=["c_z%d" % c, "rstd"], w=["c_z%d" % c], nosw=True)
                        gv = V[:, voff["ln_g"] + c: voff["ln_g"] + c + 1]
                        bv = V[:, voff["ln_b"] + c: voff["ln_b"] + c + 1]
                        S.op("act", (lambda e, c=c, gv=gv, bv=bv: e.activation(out=zs[:, c, :], in_=zbuf[:, c, :], func=AF.Silu,
                                                                                scale=gv, bias=bv)),
                             r=["c_z%d" % c, "V"], w=["c_zs"])
                    proj_residual(ph, WPW2, zs, "c_zs", xsrc, xdst, t0, gate, (wr, pzd, xo))
                S.barrier()
                S.emit()

        def na_phase():
            SCALE = 128.0 ** -0.5
            NKT = (KVE + NCTX) // 128
            with contextlib.ExitStack() as ph:
                qs = sbt(ph, "n_q", [128, 4, E], BF16)
                ks = sbt(ph, "n_k", [128, 4, KVE + NCTX], BF16)
                vs = sbt(ph, "n_v", [128, NKT, 512], BF16)
                bint = sbt(ph, "n_bint", [128, 4, 640], F32)
                bedge = Ring(ph, nc, "n_be", 2, [128, 640], F32)
                yna = sbt(ph, "n_y", [128, 4, E], BF16)
                psS = Ring(ph, nc, "n_ps", 2, [128, 2, 512], F32, psum=True)
                psT = Ring(ph, nc, "n_pt", 2, [128, 8, 128], BF16, psum=True)
                scr = Ring(ph, nc, "n_sc", 2, [128, 896], F32)
                pbr = Ring(ph, nc, "n_pb", 2, [128, 896], BF16)
                pTr = Ring(ph, nc, "n_pT", 2, [128, 7, 128], BF16)
                st4 = Ring(ph, nc, "n_st", 4, [128, 4], F32)
                otm = Ring(ph, nc, "n_o", 2, [128, 128], BF16)
                bcf = Ring(ph, nc, "n_bcf", 5, [128, 2048], F32)
                bcb = Ring(ph, nc, "n_bcb", 2, [128, 2048], BF16)
                it_cnt = [0]
                na_j0 = [bg_state["i"]]
                tot_iter = 2 * (E // 128) * 4
                if SBUF_REPORT:
                    print("NA sbuf remaining:", nc.sbuf_bytes_remaining)
                for hg in range(2):
                    S.dma("sp", (lambda e, hg=hg: e.dma_start(out=qs[:], in_=q_fm[hg * 4:hg * 4 + 4].rearrange("h p t -> p h t"))),
                          r=["qk_scr"], w=["n_q"])
                    S.dma("sp", (lambda e, hg=hg: e.dma_start(out=ks[:], in_=k_fm[hg * 4:hg * 4 + 4].rearrange("h p t -> p h t"))),
                          r=["qk_scr"], w=["n_k"])
                    S.dma("sp", (lambda e, hg=hg: e.dma_start(
                        out=vs[:], in_=v_tm[:, hg * 512:hg * 512 + 512].rearrange("(t p) c -> p t c", p=128))),
                          r=["uv_scr"], w=["n_v"])
                    S.dma("sp", (lambda e, hg=hg: e.dma_start(
                        out=bint[:], in_=na_bias[2, hg * 4:hg * 4 + 4].rearrange("h p k -> p h k"))), w=["n_bint"])
                    def tile_gen(hg, qt, h):
                        r = 2 * qt
                        kr0 = max(r - 4, 0)
                        k0 = kr0 * 64
                        it_cnt[0] += 1
                        want = na_j0[0] + (it_cnt[0] * (len(bg_jobs) - na_j0[0]) + tot_iter - 1) // tot_iter
                        bg_emit(want - bg_state["i"], bcf, bcb)
                        if qt < 2:
                            bt, btid = bedge.next()
                            S.dma("sp", (lambda e, bt=bt, qt=qt, hg=hg, h=h: e.dma_start(out=bt[:], in_=na_bias[qt, hg * 4 + h])),
                                  w=[btid])
                            bias_ap = bt[:, :]
                        else:
                            btid = "n_bint"
                            bias_ap = bint[:, h, :]
                        ps, psid = psS.next()
                        S.op("pe", (lambda e, ps=ps, h=h, qt=qt, k0=k0: e.matmul(
                            ps[:, 0, :], lhsT=qs[:, h, qt * 128:(qt + 1) * 128], rhs=ks[:, h, k0:k0 + 512],
                            start=True, stop=True)), r=["n_q", "n_k"], w=[psid])
                        S.op("pe", (lambda e, ps=ps, h=h, qt=qt, k0=k0: e.matmul(
                            ps[:, 1, 0:128], lhsT=qs[:, h, qt * 128:(qt + 1) * 128], rhs=ks[:, h, k0 + 512:k0 + 640],
                            start=True, stop=True)), r=["n_q", "n_k"], w=[psid])
                        S.op("pe", (lambda e, ps=ps, h=h, qt=qt: e.matmul(
                            ps[:, 1, 128:384], lhsT=qs[:, h, qt * 128:(qt + 1) * 128], rhs=ks[:, h, KVE:KVE + NCTX],
                            start=True, stop=True)), r=["n_q", "n_k"], w=[psid])
                        yield
                        sc, scid = scr.next()
                        S.op("dve", (lambda e, sc=sc, ps=ps, bias_ap=bias_ap: e.scalar_tensor_tensor(
                            out=sc[:, 0:512], in0=ps[:, 0, :], scalar=SCALE, in1=bias_ap[:, 0:512],
                            op0=ALU.mult, op1=ALU.add)), r=[psid, btid], w=[scid])
                        S.op("dve", (lambda e, sc=sc, ps=ps, bias_ap=bias_ap: e.scalar_tensor_tensor(
                            out=sc[:, 512:640], in0=ps[:, 1, 0:128], scalar=SCALE, in1=bias_ap[:, 512:640],
                            op0=ALU.mult, op1=ALU.add)), r=[psid, btid], w=[scid])
                        S.op("act", (lambda e, sc=sc, ps=ps: e.activation(out=sc[:, 640:896], in_=ps[:, 1, 128:384],
                                                                         func=AF.Identity, scale=SCALE)),
                             r=[psid], w=[scid])
                        yield
                        st_, stid = st4.next()
                        S.op("dve", (lambda e, st_=st_, sc=sc: e.tensor_reduce(out=st_[:, 0:1], in_=sc[:, :], axis=AX.X, op=ALU.max)),
                             r=[scid], w=[stid])
                        S.op("dve", (lambda e, st_=st_: e.tensor_scalar(out=st_[:, 1:2], in0=st_[:, 0:1], scalar1=-1.0, scalar2=None,
                                                                       op0=ALU.mult)), r=[stid], w=[stid])
                        yield
                        pb, pbid = pbr.next()
                        S.op("act", (lambda e, pb=pb, sc=sc, st_=st_: e.activation(out=pb[:, :], in_=sc[:, :], func=AF.Exp,
                                                                                  bias=st_[:, 1:2], scale=1.0,
                                                                                  accum_out=st_[:, 2:3])),
                             r=[scid, stid], w=[pbid, stid])
                        S.op("dve", (lambda e, st_=st_: e.reciprocal(out=st_[:, 3:4], in_=st_[:, 2:3])), r=[stid], w=[stid])
                        yield
                        pt, ptid = psT.next()
                        for j in range(7):
                            S.op("pe", (lambda e, pt=pt, pb=pb, j=j: e.transpose(out=pt[:, j, :], in_=pb[:, j * 128:(j + 1) * 128],
                                                                                identity=ident_b[:])),
                                 r=[pbid, "ident_b"], w=[ptid])
                        yield
                        pT, pTid = pTr.next()
                        S.op("act", (lambda e, pT=pT, pt=pt: e.activation(out=pT[:, 0:4, :], in_=pt[:, 0:4, :], func=AF.Copy)),
                             r=[ptid], w=[pTid])
                        S.op("dve", (lambda e, pT=pT, pt=pt: e.tensor_copy(out=pT[:, 4:7, :], in_=pt[:, 4:7, :])),
                             r=[ptid], w=[pTid])
                        yield
                        po, poid = ps[:, 1, 384:512], psid + "o"
                        for j in range(7):
                            vt = (kr0 // 2 + j) if j < 5 else (KVE // 128 + (j - 5))
                            S.op("pe", (lambda e, po=po, pT=pT, j=j, vt=vt, h=h: e.matmul(
                                po, lhsT=pT[:, j, :], rhs=vs[:, vt, h * 128:(h + 1) * 128],
                                start=(j == 0), stop=(j == 6))), r=[pTid, "n_v"], w=[poid])
                        yield
                        o_, oid = otm.next()
                        S.op("act", (lambda e, o_=o_, po=po, st_=st_: e.activation(out=o_[:, :], in_=po, func=AF.Identity,
                                                                                  scale=st_[:, 3:4])),
                             r=[poid, stid], w=[oid])
                        yield
                        pot, potid = pt[:, 7, :], ptid + "o"
                        S.op("pe", (lambda e, pot=pot, o_=o_: e.transpose(out=pot, in_=o_[:, :], identity=ident_b[:])),
                             r=[oid, "ident_b"], w=[potid])
                        yield
                        S.op("dve", (lambda e, pot=pot, h=h, qt=qt: e.tensor_copy(out=yna[:, h, qt * 128:(qt + 1) * 128], in_=pot)),
                             r=[potid], w=["n_y"])

                    for qt in range(E // 128):
                        for hp in range(2):
                            gens = [tile_gen(hg, qt, hp * 2), tile_gen(hg, qt, hp * 2 + 1)]
                            live = list(gens)
                            while live:
                                for g_ in list(live):
                                    try:
                                        next(g_)
                                    except StopIteration:
                                        live.remove(g_)
                    S.dma("sp", (lambda e, hg=hg: e.dma_start(
                        out=ymix[8 + hg * 4:8 + hg * 4 + 4].rearrange("h p t -> p h t"), in_=yna[:])), r=["n_y"], w=["ymix"])
                S.barrier()
                S.emit()

        def s5_phase():
            NCHK = 544
            NA_, NB_ = 304, 544
            PA, PB = 256, 512
            PBH, PZ, NZ = 16, 256, 274
            NOUT = E // 8
            PI = 3.141592653589793
            with contextlib.ExitStack() as ph0:
                z_tm = sbt(ph0, "s_ztm", [128, 3, 8, 1024], BF16)
                psX = Ring(ph0, nc, "s_px", 2, [128, 2, 512], F32, psum=True)
                psT = Ring(ph0, nc, "s_pt", 1, [128, 8, 128], BF16, psum=True)
                psMT = pst(ph0, "s_pmt", [128, 4, 128], F32)

                class _One:
                    def next(self_):
                        return psMT, "psM"
                psM = _One()
                psKT = pst(ph0, "s_pkt", [128, 512], F32)
                psY = Ring(ph0, nc, "s_py", 1, [128, 256], F32, psum=True)
                with contextlib.ExitStack() as ph:
                    NS = 24
                    T = sbt(ph, "s_T", [128, 2, NS, 32], F32)
                    PW = sbt(ph, "s_PW", [128, 2, 2, 32, 9], F32)
                    QW = sbt(ph, "s_QW", [128, 2, 3, 32, 10], F32)
                    Bt = sbt(ph, "s_Bt", [128, 2, 2, 32, 16], F32)
                    bb = sbt(ph, "s_bb", [128, 2, 2, 32, 16], F32)
                    bbb = sbt(ph, "s_bbb", [128, 2, 2, 32, 16], BF16)
                    CT = sbt(ph, "s_CT", [128, 2, 2, 32, 16], F32)
                    Cn = Ring(ph, nc, "s_Cn", 2, [128, 128], F32)
                    dvec = sbt(ph, "s_dvec", [128, 64], F32)
                    Sel = sbt(ph, "s_Sel", [16, 8, 128], BF16)
                    KTa = sbt(ph, "s_KTa", [16, 2, 15, 16], BF16)
                    KTb = sbt(ph, "s_KTb", [16, 2, 15, 16], BF16)
                    P_ = "s5par"

                    def dv(fn, r=(P_,), w=(P_,), eng="dve"):
                        S.op(eng, fn, r=list(r), w=list(w))

                    def tt(o_, a, b, op):
                        dv(lambda e: e.tensor_tensor(out=o_, in0=a, in1=b, op=op))

                    def ts(o_, a, s1, op0, s2=None, op1=None):
                        if op1 is None:
                            dv(lambda e: e.tensor_scalar(out=o_, in0=a, scalar1=s1, scalar2=None, op0=op0))
                        else:
                            dv(lambda e: e.tensor_scalar(out=o_, in0=a, scalar1=s1, scalar2=s2, op0=op0, op1=op1))

                    def stt(o_, a, sc, b, op0, op1):
                        dv(lambda e: e.scalar_tensor_tensor(out=o_, in0=a, scalar=sc, in1=b, op0=op0, op1=op1))

                    def act(o_, a, func, **kw):
                        dv(lambda e: e.activation(out=o_, in_=a, func=func, **kw), eng="act")

                    def cmul(ore, oim, are, aim, bre, bim, t1):
                        tt(ore, are, bre, ALU.mult)
                        tt(t1, aim, bim, ALU.mult)
                        tt(ore, ore, t1, ALU.subtract)
                        tt(oim, are, bim, ALU.mult)
                        tt(t1, aim, bre, ALU.mult)
                        tt(oim, oim, t1, ALU.add)

                    dv(lambda e: e.memset(Sel[:], 0.0), eng="pool")
                    for s_ in range(8):
                        dv((lambda e, s_=s_: e.tensor_copy(out=Sel[0:16, s_, s_ * 16:(s_ + 1) * 16], in_=ident_b[0:16, 0:16])),
                           r=(P_, "ident_b"), eng="pool")
                    dv(lambda e: e.memset(KTa[:], 0.0), eng="pool")
                    dv(lambda e: e.memset(KTb[:], 0.0), eng="pool")
                    for s_ in range(8):
                        S.dma("sp", (lambda e, s_=s_: e.dma_start(out=dvec[s_ * 16:(s_ + 1) * 16, :],
                                                                  in_=ssm_d.rearrange("(g c) -> c g", c=16),
                                                                  allow_slow_non_contiguous=True)), w=[P_])
                    for d in range(2):
                        for ri, src in ((0, b_re), (1, b_im)):
                            S.dma("sp", (lambda e, d=d, ri=ri, src=src: e.dma_start(
                                out=Bt[:, d, ri], in_=src[d].rearrange("(pair q) c -> q pair c", q=128))), w=[P_])
                    for d in range(2):
                        for ri, src in ((0, c_re), (1, c_im)):
                            for t8 in range(8):
                                cn, cnid = Cn.next()
                                for dup in range(2):
                                    S.dma("sp", (lambda e, cn=cn, dup=dup, src=src, d=d, t8=t8: e.dma_start(
                                        out=cn[:, dup * 64:(dup + 1) * 64], in_=src[d, t8 * 128:(t8 + 1) * 128, :])), w=[cnid])
                                pm, pmid = psM.next()
                                S.op("pe", (lambda e, pm=pm, cn=cn: e.transpose(out=pm[:, 0, :], in_=cn[:, :], identity=ident_f[:])),
                                     r=[cnid, "ident_f"], w=[pmid])
                                for g2 in range(2):
                                    S.op("dve", (lambda e, pm=pm, g2=g2, d=d, ri=ri, t8=t8: e.tensor_copy(
                                        out=CT[g2 * 64:(g2 + 1) * 64, d, ri, 4 * t8:4 * t8 + 4, :],
                                        in_=pm[g2 * 64:(g2 + 1) * 64, 0, :].rearrange("p (pp x c) -> p pp x c", pp=4, x=2)[:, :, g2, :])),
                                         r=[pmid, P_], w=[P_])
                    for d in range(2):
                        sl = lambda i, d=d: T[:, d, i, :]
                        DT, LR, LI, MAG, TH, CNT, SN, CS, ARE, AIM, FRE, FIM, DEN, T1, T2, XR = [sl(i) for i in range(16)]
                        lre = VC("lam_re%d" % d, 0, 32)
                        lim = VC("lam_im%d" % d, 0, 32)
                        act(DT, VC("log_dt%d" % d, 0, 32), AF.Exp)
                        S.op("dve", (lambda e, LR=LR, lre=lre, DT=DT: e.tensor_tensor(out=LR, in0=lre, in1=DT, op=ALU.mult)), r=[P_, "V"], w=[P_])
                        S.op("dve", (lambda e, LI=LI, lim=lim, DT=DT: e.tensor_tensor(out=LI, in0=lim, in1=DT, op=ALU.mult)), r=[P_, "V"], w=[P_])
                        act(MAG, LR, AF.Exp)
                        for (dst, shift) in ((SN, 0.0), (CS, PI / 2)):
                            ts(TH, LI, TWO_PI + shift, ALU.add)
                            ts(CNT, TH, TWO_PI, ALU.is_ge)
                            for jj in range(2, 6):
                                stt(CNT, TH, TWO_PI * jj, CNT, ALU.is_ge, ALU.add)
                            stt(TH, CNT, -TWO_PI, TH, ALU.mult, ALU.add)
                            ts(TH, TH, PI, ALU.subtract)
                            act(dst, TH, AF.Sin)
                        stt(ARE, CS, -1.0, MAG, ALU.mult, ALU.mult)
                        stt(AIM, SN, -1.0, MAG, ALU.mult, ALU.mult)
                        S.op("dve", (lambda e, DEN=DEN, lre=lre: e.tensor_tensor(out=DEN, in0=lre, in1=lre, op=ALU.mult)), r=[P_, "V"], w=[P_])
                        S.op("dve", (lambda e, T1=T1, lim=lim: e.tensor_tensor(out=T1, in0=lim, in1=lim, op=ALU.mult)), r=[P_, "V"], w=[P_])
                        tt(DEN, DEN, T1, ALU.add)
                        dv(lambda e, DEN=DEN: e.reciprocal(out=DEN, in_=DEN))
                        ts(XR, ARE, 1.0, ALU.subtract)
                        S.op("dve", (lambda e, FRE=FRE, XR=XR, lre=lre: e.tensor_tensor(out=FRE, in0=XR, in1=lre, op=ALU.mult)), r=[P_, "V"], w=[P_])
                        S.op("dve", (lambda e, T1=T1, AIM=AIM, lim=lim: e.tensor_tensor(out=T1, in0=AIM, in1=lim, op=ALU.mult)), r=[P_, "V"], w=[P_])
                        tt(FRE, FRE, T1, ALU.add)
                        tt(FRE, FRE, DEN, ALU.mult)
                        S.op("dve", (lambda e, FIM=FIM, AIM=AIM, lre=lre: e.tensor_tensor(out=FIM, in0=AIM, in1=lre, op=ALU.mult)), r=[P_, "V"], w=[P_])
                        S.op("dve", (lambda e, T1=T1, XR=XR, lim=lim: e.tensor_tensor(out=T1, in0=XR, in1=lim, op=ALU.mult)), r=[P_, "V"], w=[P_])
                        tt(FIM, FIM, T1, ALU.subtract)
                        tt(FIM, FIM, DEN, ALU.mult)
                        pw = lambda ri, k, d=d: PW[:, d, ri, :, k]
                        dv(lambda e, d=d: e.memset(PW[:, d, 0, :, 0], 1.0))
                        dv(lambda e, d=d: e.memset(PW[:, d, 1, :, 0], 0.0))
                        dv(lambda e, d=d, ARE=ARE: e.tensor_copy(out=PW[:, d, 0, :, 1], in_=ARE))
                        dv(lambda e, d=d, AIM=AIM: e.tensor_copy(out=PW[:, d, 1, :, 1], in_=AIM))
                        for k in range(2, 9):
                            cmul(pw(0, k), pw(1, k), pw(0, k - 1), pw(1, k - 1), ARE, AIM, T1)
                        qw = lambda ri, j, d=d: QW[:, d, ri, :, j]
                        dv(lambda e, d=d: e.tensor_copy(out=QW[:, d, 0, :, 0], in_=PW[:, d, 0, :, 8]))
                        dv(lambda e, d=d: e.tensor_copy(out=QW[:, d, 1, :, 0], in_=PW[:, d, 1, :, 8]))
                        for j in range(1, 10):
                            cmul(qw(0, j), qw(1, j), qw(0, j - 1), qw(1, j - 1), qw(0, j - 1), qw(1, j - 1), T1)
                        dv(lambda e, d=d: e.tensor_scalar(out=QW[:, d, 2], in0=QW[:, d, 1], scalar1=-1.0, scalar2=None, op0=ALU.mult))
                        fre_b = FRE.unsqueeze(2).broadcast_to([128, 32, 16])
                        fim_b = FIM.unsqueeze(2).broadcast_to([128, 32, 16])
                        tt(bb[:, d, 0], Bt[:, d, 0], fre_b, ALU.mult)
                        tt(bb[:, d, 1], Bt[:, d, 1], fre_b, ALU.mult)
                        tt(Bt[:, d, 1], Bt[:, d, 1], fim_b, ALU.mult)
                        tt(Bt[:, d, 0], Bt[:, d, 0], fim_b, ALU.mult)
                        tt(bb[:, d, 0], bb[:, d, 0], Bt[:, d, 1], ALU.subtract)
                        tt(bb[:, d, 1], bb[:, d, 1], Bt[:, d, 0], ALU.add)
                        dv(lambda e, d=d: e.tensor_copy(out=bbb[:, d], in_=bb[:, d]))

                    Gr = Ring(ph, nc, "s_G", 2, [128, 2, 2, 9, 16], F32)
                    Gt = sbt(ph, "s_Gt", [128, 9, 16], F32)
                    W1 = Ring(ph, nc, "s_W1", 2, [128, 2, 8, 16], F32)
                    W1t = sbt(ph, "s_W1t", [128, 8, 16], F32)
                    mxp = [Ring(ph, nc, "s_mxp%d" % d, 2, [128, 4, 128], BF16) for d in range(2)]
                    myp = [Ring(ph, nc, "s_myp%d" % d, 2, [128, 2, 256], BF16) for d in range(2)]
                    kblk = [Ring(ph, nc, "s_kb%d" % d, 2, [128, 2, 256], BF16) for d in range(2)]
                    for d in range(2):
                        for rg in (mxp[d], myp[d], kblk[d]):
                            for (tl, tid) in zip(rg.tiles, rg.ids):
                                S.op("pool", (lambda e, tl=tl: e.memset(tl[:], 0.0)), w=[tid])
                    toep = Ring(ph, nc, "s_toep", 2, [128, 2, 128], BF16)
                    ddg = Ring(ph, nc, "s_ddg", 2, [128, 128], F32)
                    UTp = Ring(ph, nc, "s_UTp", 2, [128, 5, 8, 32], BF16)
                    UT2 = Ring(ph, nc, "s_UT2", 2, [128, 5, 2, 128], BF16)
                    UgT = Ring(ph, nc, "s_UgT", 2, [128, 2, NCHK], BF16)
                    XA = [[sbt(ph, "s_XA%d%d" % (i, ri), [128, PA + NA_], F32) for ri in range(2)] for i in range(2)]
                    XB = [[sbt(ph, "s_XB%d%d" % (i, ri), [128, PBH + 272], F32) for ri in range(2)] for i in range(2)]
                    ZB = [[sbt(ph, "s_ZB%d%d" % (i, ri), [128, PZ + NZ], F32) for ri in range(2)] for i in range(2)]
                    for arrs in (XA, XB, ZB):
                        for i in range(2):
                            for ri in range(2):
                                S.op("pool", (lambda e, a=arrs[i][ri]: e.memset(a[:], 0.0)), w=["scanA", "scanB"])
                    Sbf = Ring(ph, nc, "s_Sbf", 2, [128, 4, NOUT], BF16)
                    ytmp = Ring(ph, nc, "s_yt", 2, [128, 3, 256], F32)

                    if SBUF_REPORT:
                        print("S5 sbuf remaining:", nc.sbuf_bytes_remaining)
                    for pair in range(32):
                        ut, utid = UTp.next()
                        for tl in range(5):
                            nr = 128 if tl < 4 else 32
                            S.dma("sp", (lambda e, ut=ut, tl=tl, nr=nr, pair=pair: e.dma_start(
                                out=ut[:nr, tl], in_=u_tm[tl * 1024:tl * 1024 + nr * 8, pair * 32:(pair + 1) * 32].rearrange(
                                    "(q s) c -> q s c", s=8))), r=["uv_scr"], w=[utid])
                        u2, u2id = UT2.next()
                        for tl in range(5):
                            nr = 128 if tl < 4 else 32
                            for g2 in range(2):
                                S.op("pool", (lambda e, u2=u2, ut=ut, tl=tl, nr=nr, g2=g2: e.tensor_copy(
                                    out=u2[:nr, tl, g2, :].rearrange("p (s c) -> p s c", s=8),
                                    in_=ut[:nr, tl, :, g2 * 16:(g2 + 1) * 16])), r=[utid], w=[u2id])
                        ug, ugid = UgT.next()
                        for g2 in range(2):
                            for tl in range(5):
                                nr = 128 if tl < 4 else 32
                                pt, ptid = psT.next()
                                S.op("pe", (lambda e, pt=pt, u2=u2, tl=tl, nr=nr, g2=g2: e.transpose(
                                    out=pt[:, 0, :nr], in_=u2[:nr, tl, g2, :], identity=ident_b[:nr, :nr])),
                                     r=[u2id, "ident_b"], w=[ptid])
                                ce = ("act", "dve")[tl % 2]
                                S.op(ce, copy_fn(ce, ug[:, g2, tl * 128:tl * 128 + nr], pt[:, 0, :nr]), r=[ptid], w=[ugid])
                        G_, Gid = Gr.next()
                        mx_t, my_t, kb_t = [], [], []
                        for d in range(2):
                            pre = PW[:, d, 0, pair, :].unsqueeze(2).broadcast_to([128, 9, 16])
                            pim = PW[:, d, 1, pair, :].unsqueeze(2).broadcast_to([128, 9, 16])
                            cre = CT[:, d, 0, pair, :].unsqueeze(1).broadcast_to([128, 9, 16])
                            cim = CT[:, d, 1, pair, :].unsqueeze(1).broadcast_to([128, 9, 16])
                            gre, gim = G_[:, d, 0], G_[:, d, 1]
                            for (o_, a1, b1, a2, b2, op) in ((gre, cre, pre, cim, pim, ALU.subtract), (gim, cre, pim, cim, pre, ALU.add)):
                                S.op("dve", (lambda e, o_=o_, a1=a1, b1=b1: e.tensor_tensor(out=o_, in0=a1, in1=b1, op=ALU.mult)), r=[P_], w=[Gid], nosw=True)
                                S.op("dve", (lambda e, a2=a2, b2=b2: e.tensor_tensor(out=Gt[:], in0=a2, in1=b2, op=ALU.mult)), r=[P_], w=["s_Gt"], nosw=True)
                                S.op("dve", (lambda e, o_=o_, op=op: e.tensor_tensor(out=o_, in0=o_, in1=Gt[:], op=op)), r=["s_Gt", Gid], w=[Gid], nosw=True)
                            my, myid = myp[d].next()
                            kb, kbid = kblk[d].next()
                            for g2 in range(2):
                                rows = slice(g2 * 64, (g2 + 1) * 64)
                                if d == 0:
                                    ksel = lambda ri, rows=rows, d=d, G_=G_: G_[rows, d, ri, 1:9, :]
                                else:
                                    ksel = lambda ri, rows=rows, d=d, G_=G_: G_[rows, d, ri, 1:9, :][:, ::-1, :]
                                S.op("dve", (lambda e, my=my, rows=rows, g2=g2, ksel=ksel: e.tensor_copy(
                                    out=my[rows, 0, g2 * 128:(g2 + 1) * 128].rearrange("p (t c) -> p t c", t=8), in_=ksel(0))), r=[Gid], w=[myid])
                                S.op("dve", (lambda e, my=my, rows=rows, g2=g2, ksel=ksel: e.tensor_scalar(
                                    out=my[rows, 1, g2 * 128:(g2 + 1) * 128].rearrange("p (t c) -> p t c", t=8), in0=ksel(1),
                                    scalar1=-1.0, scalar2=None, op0=ALU.mult)), r=[Gid], w=[myid])
                                S.op("pool", (lambda e, kb=kb, rows=rows, g2=g2, d=d, G_=G_: e.tensor_copy(
                                    out=kb[rows, 0, g2 * 128:(g2 + 1) * 128].rearrange("p (t c) -> p t c", t=8), in_=G_[rows, d, 0, 0:8, :])),
                                     r=[Gid], w=[kbid])
                                S.op("pool", (lambda e, kb=kb, rows=rows, g2=g2, d=d, G_=G_: e.tensor_scalar(
                                    out=kb[rows, 1, g2 * 128:(g2 + 1) * 128].rearrange("p (t c) -> p t c", t=8), in0=G_[rows, d, 1, 0:8, :],
                                    scalar1=-1.0, scalar2=None, op0=ALU.mult)), r=[Gid], w=[kbid])
                            w1, w1id = W1.next()
                            if d == 0:
                                psel = lambda ri, d=d: PW[:, d, ri, pair, 0:8][:, ::-1].unsqueeze(2).broadcast_to([128, 8, 16])
                            else:
                                psel = lambda ri, d=d: PW[:, d, ri, pair, 0:8].unsqueeze(2).broadcast_to([128, 8, 16])
                            bre = bb[:, d, 0, pair, :].unsqueeze(1).broadcast_to([128, 8, 16])
                            bim = bb[:, d, 1, pair, :].unsqueeze(1).broadcast_to([128, 8, 16])
                            for (o_, a1, b1, a2, b2, op) in ((w1[:, 0], psel(0), bre, psel(1), bim, ALU.subtract),
                                                            (w1[:, 1], psel(0), bim, psel(1), bre, ALU.add)):
                                S.op("dve", (lambda e, o_=o_, a1=a1, b1=b1: e.tensor_tensor(out=o_, in0=a1, in1=b1, op=ALU.mult)), r=[P_], w=[w1id], nosw=True)
                                S.op("dve", (lambda e, a2=a2, b2=b2: e.tensor_tensor(out=W1t[:], in0=a2, in1=b2, op=ALU.mult)), r=[P_], w=["s_W1t"], nosw=True)
                                S.op("dve", (lambda e, o_=o_, op=op: e.tensor_tensor(out=o_, in0=o_, in1=W1t[:], op=op)), r=["s_W1t", w1id], w=[w1id], nosw=True)
                            pm, pmid = psM.next()
                            for ri in range(2):
                                S.op("pe", (lambda e, pm=pm, w1=w1, ri=ri: e.transpose(
                                    out=pm[:, ri, :], in_=w1[:, ri].rearrange("p s c -> p (s c)"), identity=ident_f[:])),
                                     r=[w1id, "ident_f"], w=[pmid])
                            mx, mxid = mxp[d].next()
                            for ri in range(2):
                                for g2 in range(2):
                                    ce = ("act", "dve")[g2]
                                    S.op(ce, copy_fn(ce, mx[:, ri * 2 + g2, g2 * 64:(g2 + 1) * 64], pm[:, ri, g2 * 64:(g2 + 1) * 64]),
                                         r=[pmid], w=[mxid])
                            mx_t.append((mx, mxid)); my_t.append((my, myid)); kb_t.append((kb, kbid))

                        def emit_K_mm(d, pair=pair):
                            kb, kbid = kb_t[d]
                            S.op("pe", (lambda e, d=d, kb=kb, pair=pair: e.matmul(psKT[0:16, d * 256:d * 256 + 256], lhsT=bbb[:, d, 0, pair, :], rhs=kb[:, 0, :],
                                                                                  start=True, stop=False)), r=[P_, kbid], w=["psK%d" % d])
                            S.op("pe", (lambda e, d=d, kb=kb, pair=pair: e.matmul(psKT[0:16, d * 256:d * 256 + 256], lhsT=bbb[:, d, 1, pair, :], rhs=kb[:, 1, :],
                                                                                  start=False, stop=True)), r=[P_, kbid], w=["psK%d" % d])

                        def emit_KT_copy(d):
                            kview = psKT[0:16, d * 256:d * 256 + 256].rearrange("p (g t c) -> p g t c", g=2, t=8)
                            if d == 0:
                                S.op("dve", (lambda e, kview=kview: e.tensor_copy(out=KTa[:, :, 7:15, :], in_=kview)), r=["psK0"], w=["KT"])
                            else:
                                S.op("dve", (lambda e, kview=kview: e.tensor_copy(out=KTb[:, :, 0:8, :], in_=kview[:, :, ::-1, :])),
                                     r=["psK1"], w=["KT"])

                        tp, tpid = toep.next()

                        def emit_toep_mm():
                            for g2 in range(2):
                                n_mm = 0
                                for KT in (KTa, KTb):
                                    for s_ in range(8):
                                        S.op("pe", (lambda e, g2=g2, KT=KT, s_=s_, first=(n_mm == 0), last=(n_mm == 15): e.matmul(
                                            psMT[:, 2 + g2, :], lhsT=Sel[0:16, s_, :],
                                            rhs=KT[0:16, g2, 7 - s_:15 - s_, :].rearrange("p j c -> p (j c)"),
                                            start=first, stop=last)), r=["KT", P_], w=["psTo"])
                                        n_mm += 1

                        def emit_toep_evac(pair=pair, tp=tp, tpid=tpid):
                            for g2 in range(2):
                                g = 2 * pair + g2
                                dd, ddid = ddg.next()
                                S.op("pool", (lambda e, dd=dd, g=g: e.tensor_scalar(out=dd[:], in0=ident_f[:], scalar1=dvec[:, g:g + 1],
                                                                                   scalar2=None, op0=ALU.mult)), r=[P_, "ident_f"], w=[ddid])
                                S.op("dve", (lambda e, tp=tp, g2=g2, dd=dd: e.tensor_tensor(
                                    out=tp[:, g2, :], in0=psMT[:, 2 + g2, :], in1=dd[:], op=ALU.add)),
                                     r=["psTo", ddid], w=[tpid])

                        sb_, sbid = Sbf.next()

                        def emit_X(d, ug=ug, ugid=ugid):
                            mx, mxid = mx_t[d]
                            arr = XA if d == 0 else XB
                            sid = "scanA" if d == 0 else "scanB"
                            for ri in range(2):
                                px, pxid = psX.next()
                                if d == 0:
                                    segs = [(px[:, 0, 0:32], 512, 544), (px[:, 0, 32:304], 0, 272)]
                                else:
                                    segs = [(px[:, 0, 0:512], 0, 512), (px[:, 1, 0:32], 512, 544)]
                                for (o_, c0, c1) in segs:
                                    for g2 in range(2):
                                        S.op("pe", (lambda e, o_=o_, mx=mx, ri=ri, g2=g2, c0=c0, c1=c1, ug=ug: e.matmul(
                                            o_, lhsT=mx[:, ri * 2 + g2, :], rhs=ug[:, g2, c0:c1], start=(g2 == 0), stop=(g2 == 1))),
                                             r=[mxid, ugid], w=[pxid])
                                dst = arr[0][ri]
                                if d == 0:
                                    S.op("act", (lambda e, dst=dst, px=px: e.activation(out=dst[:, PA:PA + NA_], in_=px[:, 0, 0:NA_], func=AF.Copy)),
                                         r=[pxid], w=[sid])
                                else:
                                    zdst = ZB[0][ri]
                                    S.op("act", (lambda e, zdst=zdst, px=px: e.activation(out=zdst[:, PZ + 1:PZ + 274][:, ::-1], in_=px[:, 0, 0:273],
                                                                                        func=AF.Copy)), r=[pxid], w=[sid])
                                    S.op("act", (lambda e, dst=dst, px=px: e.activation(out=dst[:, PBH + 32:PBH + 271][:, ::-1], in_=px[:, 0, 273:512],
                                                                                       func=AF.Copy)), r=[pxid], w=[sid])
                                    S.op("act", (lambda e, dst=dst, px=px: e.activation(out=dst[:, PBH:PBH + 32][:, ::-1], in_=px[:, 1, 0:32],
                                                                                       func=AF.Copy)), r=[pxid], w=[sid])

                        def emit_HS(d, pair=pair, sb_=sb_, sbid=sbid):
                            arr = XA if d == 0 else ZB
                            sid = "scanA" if d == 0 else "scanB"
                            PAD = PA if d == 0 else PZ
                            n = NA_ if d == 0 else NZ
                            if d == 1:
                                m = 272
                                lo_ = PBH - 1
                                srcb, dstb = XB[0], XB[1]
                                lvl = 0
                                while m > 1:
                                    half = m // 2
                                    qre = QW[:, d, 0, pair, lvl:lvl + 1]
                                    qim = QW[:, d, 1, pair, lvl:lvl + 1]
                                    nqim = QW[:, d, 2, pair, lvl:lvl + 1]
                                    ev = lambda t, lo_=lo_, m=m: t[:, lo_:lo_ + m:2]
                                    od = lambda t, lo_=lo_, m=m: t[:, lo_ + 1:lo_ + m:2]
                                    ou = lambda t, half=half: t[:, PBH:PBH + half]
                                    for (o_, a1, s1, b1) in ((ou(dstb[0]), ev(srcb[0]), qre, od(srcb[0])), (ou(dstb[0]), ev(srcb[1]), nqim, ou(dstb[0])),
                                                            (ou(dstb[1]), ev(srcb[1]), qre, od(srcb[1])), (ou(dstb[1]), ev(srcb[0]), qim, ou(dstb[1]))):
                                        S.op("dve", (lambda e, o_=o_, a1=a1, s1=s1, b1=b1: e.scalar_tensor_tensor(
                                            out=o_, in0=a1, scalar=s1, in1=b1, op0=ALU.mult, op1=ALU.add)), r=[sid, P_], w=[sid], nosw=True)
                                    m = half
                                    lo_ = PBH if m % 2 == 0 else PBH - 1
                                    if m % 2 == 1 and m > 1:
                                        m += 1
                                    srcb, dstb = dstb, srcb
                                    lvl += 1
                                for ri in range(2):
                                    S.op("dve", (lambda e, ri=ri, srcb=srcb: e.tensor_copy(out=ZB[0][ri][:, PZ:PZ + 1], in_=srcb[ri][:, PBH:PBH + 1])),
                                         r=[sid], w=[sid])
                            cur = 0
                            j = 0
                            sh = 1
                            while sh < n:
                                src_, dst_ = arr[cur], arr[1 - cur]
                                qre = QW[:, d, 0, pair, j:j + 1]
                                qim = QW[:, d, 1, pair, j:j + 1]
                                nqim = QW[:, d, 2, pair, j:j + 1]
                                lo, hi = PAD, PAD + n
                                for (o_, a1, s1, b1) in ((dst_[0], src_[0], qre, src_[0]), (dst_[0], src_[1], nqim, dst_[0]),
                                                        (dst_[1], src_[1], qre, src_[1]), (dst_[1], src_[0], qim, dst_[1])):
                                    S.op("dve", (lambda e, o_=o_, a1=a1, s1=s1, b1=b1, lo=lo, hi=hi, sh=sh: e.scalar_tensor_tensor(
                                        out=o_[:, lo:hi], in0=a1[:, lo - sh:hi - sh], scalar=s1, in1=b1[:, lo:hi],
                                        op0=ALU.mult, op1=ALU.add)), r=[sid, P_], w=[sid], nosw=True)
                                cur = 1 - cur
                                sh *= 2
                                j += 1
                            fin = arr[cur]
                            for ri in range(2):
                                if d == 0:
                                    S.op("act", (lambda e, sb_=sb_, fin=fin, ri=ri: e.activation(
                                        out=sb_[:, ri, :], in_=fin[ri][:, PA + 31:PA + 31 + NOUT], func=AF.Copy)), r=[sid], w=[sbid])
                                else:
                                    S.op("act", (lambda e, sb_=sb_, fin=fin, ri=ri: e.activation(
                                        out=sb_[:, 2 + ri, :][:, ::-1], in_=fin[ri][:, PZ + 1:PZ + 1 + NOUT], func=AF.Copy)),
                                         r=[sid], w=[sbid])

                        emit_X(0)
                        emit_K_mm(0)
                        emit_K_mm(1)
                        emit_X(1)
                        emit_HS(0)
                        emit_KT_copy(0)
                        emit_KT_copy(1)
                        emit_toep_mm()
                        emit_HS(1)
                        emit_toep_evac()
                        for ct in range(3):
                            nch = 128 if ct < 2 else NOUT - 256
                            py, pyid = psY.next()
                            k = 0
                            for d in range(2):
                                my, myid = my_t[d]
                                for ri in range(2):
                                    S.op("pe", (lambda e, py=py, sb_=sb_, d=d, ri=ri, ct=ct, nch=nch, my=my, first=(k == 0): e.matmul(
                                        py[:nch, :], lhsT=sb_[:, d * 2 + ri, ct * 128:ct * 128 + nch], rhs=my[:, ri, :],
                                        start=first, stop=False)), r=[sbid, myid], w=[pyid])
                                    k += 1
                            for g2 in range(2):
                                S.op("pe", (lambda e, py=py, ug=ug, g2=g2, ct=ct, nch=nch, tp=tp: e.matmul(
                                    py[:nch, g2 * 128:(g2 + 1) * 128], lhsT=ug[:, g2, ct * 128:ct * 128 + nch], rhs=tp[:, g2, :],
                                    start=False, stop=(g2 == 1))), r=[ugid, tpid], w=[pyid])
                            yt, ytid = ytmp.next()
                            S.op("act", (lambda e, yt=yt, py=py, nch=nch: e.activation(out=yt[:nch, 0, :], in_=py[:nch, :], func=AF.Copy)),
                                 r=[pyid], w=[ytid])
                            S.op("pool", (lambda e, yt=yt, nch=nch: e.tensor_tensor(out=yt[:nch, 1, :], in0=yt[:nch, 0, :], in1=yt[:nch, 0, :],
                                                                                   op=ALU.mult)), r=[ytid], w=[ytid])
                            S.op("pool", (lambda e, yt=yt, nch=nch: e.tensor_scalar(out=yt[:nch, 1, :], in0=yt[:nch, 1, :], scalar1=0.044715,
                                                                                   scalar2=1.0, op0=ALU.mult, op1=ALU.add)), r=[ytid], w=[ytid])
                            S.op("pool", (lambda e, yt=yt, nch=nch: e.tensor_tensor(out=yt[:nch, 1, :], in0=yt[:nch, 1, :], in1=yt[:nch, 0, :],
                                                                                   op=ALU.mult)), r=[ytid], w=[ytid])
                            S.op("act", (lambda e, yt=yt, nch=nch: e.activation(out=yt[:nch, 2, :], in_=yt[:nch, 1, :], func=AF.Sigmoid,
                                                                               scale=1.5957691216057308)), r=[ytid], w=[ytid])
                            for g2 in range(2):
                                zo = z_tm[:nch, ct, :, pair * 32 + g2 * 16:pair * 32 + (g2 + 1) * 16]
                                i0 = yt[:nch, 0, g2 * 128:(g2 + 1) * 128].rearrange("p (t c) -> p t c", t=8)
                                i1 = yt[:nch, 2, g2 * 128:(g2 + 1) * 128].rearrange("p (t c) -> p t c", t=8)
                                if S5DBG:
                                    S.op("pool", (lambda e, zo=zo, i0=i0: e.tensor_copy(out=zo, in_=i0)), r=[ytid], w=["s_ztm"])
                                else:
                                    S.op("pool", (lambda e, zo=zo, i0=i0, i1=i1: e.tensor_tensor(out=zo, in0=i0, in1=i1, op=ALU.mult)),
                                         r=[ytid], w=["s_ztm"])
                        if S5BAR:
                            S.barrier()
                    S.barrier()
                    S.emit()
                with contextlib.ExitStack() as ph:
                    z_fm = sbt(ph, "s_zfm", [128, 8, E], BF16)
                    wgl = Ring(ph, nc, "s_wgl", 2, [128, 8, 128], BF16)
                    sgt = Ring(ph, nc, "s_sgt", 2, [128, TB], F32)
                    yo = Ring(ph, nc, "s_yo", 2, [128, TB], BF16)
                    for ct in range(3):
                        nch = 128 if ct < 2 else NOUT - 256
                        for c8 in range(8):
                            pt, ptid = psT.next()
                            for t in range(8):
                                S.op("pe", (lambda e, pt=pt, ct=ct, t=t, c8=c8, nch=nch: e.transpose(
                                    out=pt[:, t, :nch], in_=z_tm[:nch, ct, t, c8 * 128:(c8 + 1) * 128], identity=ident_b[:nch, :nch])),
                                     r=["s_ztm", "ident_b"], w=[ptid])
                            ce = ("act", "dve")[c8 % 2]
                            S.op(ce, copy_fn(ce, z_fm[:, c8, ct * 1024:ct * 1024 + nch * 8].rearrange("p (q t) -> p t q", t=8),
                                             pt[:, :, :nch]), r=[ptid], w=["s_zfm"])
                    HB = TB // 2
                    for blk in range(NBLK):
                        t0 = blk * TB
                        for m in range(8):
                            w_, wid = wgl.next()
                            S.dma("sp", (lambda e, w_=w_, m=m: e.dma_start(
                                out=w_[:], in_=WGLU[m // 4].rearrange("p (a b) -> p a b", a=8)[:, :, (m % 4) * 128:(m % 4) * 128 + 128])),
                                  r=["wscr"], w=[wid])
                            px, pxid = psX.next()
                            for hf in range(2):
                                for kc in range(8):
                                    S.op("pe", (lambda e, px=px, w_=w_, kc=kc, hf=hf, t0=t0: e.matmul(
                                        px[:, hf, :HB], lhsT=w_[:, kc, :], rhs=z_fm[:, kc, t0 + hf * HB:t0 + (hf + 1) * HB],
                                        start=(kc == 0), stop=(kc == 7))), r=["s_zfm", wid], w=[pxid])
                            sg_, sgid = sgt.next()
                            S.op("act", (lambda e, sg_=sg_, px=px: e.activation(out=sg_[:, :].rearrange("p (a b) -> p a b", a=2),
                                                                              in_=px[:, :, :HB], func=AF.Sigmoid)), r=[pxid], w=[sgid])
                            y_, yid = yo.next()
                            if S5DBG:
                                S.op("dve", (lambda e, y_=y_, m=m, t0=t0: e.tensor_copy(out=y_[:], in_=z_fm[:, m, t0:t0 + TB])),
                                     r=[sgid, "s_zfm"], w=[yid])
                            else:
                                S.op("dve", (lambda e, y_=y_, sg_=sg_, m=m, t0=t0: e.tensor_tensor(out=y_[:], in0=sg_[:], in1=z_fm[:, m, t0:t0 + TB],
                                                                                              op=ALU.mult)), r=[sgid, "s_zfm"], w=[yid])
                            S.dma("pool", (lambda e, y_=y_, m=m, t0=t0: e.dma_start(out=ymix[m, :, t0:t0 + TB], in_=y_[:])),
                                  r=[yid], w=["ymix"])
                    S.barrier()
                    S.emit()

        def final_phase(xsrc):
            with contextlib.ExitStack() as ph:
                xs = Ring(ph, nc, "o_xs", 3, [128, 512], F32)
                sq = Ring(ph, nc, "o_sq", 2, [128, 512], F32)
                pss = Ring(ph, nc, "o_pss", 1, [128, 512], F32, psum=True)
                rbc = Ring(ph, nc, "o_rbc", 1, [128, 512], F32)
                pt_ = Ring(ph, nc, "o_pt", 2, [128, 4, 128], F32, psum=True)
                ot = Ring(ph, nc, "o_ot", 2, [128, D], F32)
                hf32 = sbt(ph, "o_h", [128, 16, 512], F32)
                for blk in range(4):
                    c0 = blk * 512
                    n = 512
                    pt, pid = pss.next()
                    for kc in range(16):
                        t, tid = load_x_cols(xs, xsrc, kc, c0, n, 0, E)
                        q, qid = sq.next()
                        S.op("act", (lambda e, q=q, t=t: e.activation(out=q[:], in_=t[:], func=AF.Square)), r=[tid], w=[qid])
                        S.op("pe", (lambda e, pt=pt, q=q, kc=kc: e.matmul(pt[:, :], lhsT=ones_f[:], rhs=q[:, :],
                                                                         start=(kc == 0), stop=(kc == 15))),
                             r=[qid, "ones_f"], w=[pid])
                    r_, rid = rbc.next()
                    S.op("act", (lambda e, r_=r_, pt=pt: e.activation(out=r_[:], in_=pt[:], func=AF.Sqrt, scale=1.0 / D,
                                                                      bias=EPS)), r=[pid], w=[rid])
                    S.op("dve", (lambda e, r_=r_: e.reciprocal(out=r_[:], in_=r_[:])), r=[rid], w=[rid])
                    for kc in range(16):
                        t, tid = load_x_cols(xs, xsrc, kc, c0, n, 0, E)
                        S.op("dve", (lambda e, t=t, r_=r_: e.tensor_tensor(out=t[:], in0=t[:], in1=r_[:], op=ALU.mult)),
                             r=[tid, rid], w=[tid])
                        gv = V[:, voff["g_out"] + kc: voff["g_out"] + kc + 1]
                        S.op("act", (lambda e, t=t, kc=kc, gv=gv: e.activation(out=hf32[:, kc, :], in_=t[:],
                                                                              func=AF.Identity, scale=gv)),
                             r=[tid, "V"], w=["o_h"])
                    for ti in range(4):
                        o_, oid = ot.next()
                        for q4 in range(4):
                            p_, pid2 = pt_.next()
                            for i in range(4):
                                kc = q4 * 4 + i
                                S.op("pe", (lambda e, p_=p_, i=i, kc=kc, ti=ti: e.transpose(
                                    out=p_[:, i, :], in_=hf32[:, kc, ti * 128:(ti + 1) * 128], identity=ident_f[:])),
                                     r=["o_h", "ident_f"], w=[pid2])
                            ce = ("act", "dve")[q4 % 2]
                            S.op(ce, copy_fn(ce, o_[:, q4 * 512:(q4 + 1) * 512],
                                             p_[:].rearrange("p a b -> p (a b)")), r=[pid2], w=[oid])
                        row = c0 + ti * 128
                        S.dma("sp", (lambda e, o_=o_, row=row: e.dma_start(out=out[row:row + 128, :], in_=o_[:])),
                              r=[oid], w=["out"])
                S.barrier()
                S.emit()

        def dump(src, i):
            for c in range(16):
                S.dma("sp", (lambda e, c=c: e.dma_start(out=dbg_out[i][c], in_=src[c])), r=["xsrc"], w=["dbg"])
            S.barrier()

        mix = stage >= 3
        l0_proj(do_mixer=mix)
        if mix:
            s5_phase()
            na_phase()
            if dbg:
                for c in range(16):
                    S.dma("sp", (lambda e, c=c: e.dma_start(out=dbg_y[c], in_=ymix[c])), r=["ymix"], w=["dbgy"])
                S.barrier()
                S.emit()
                return nc
            mixout_phase(xA, xB, MODV(0, 2, 0))
            ffn_phase(0, xB, xA, DER(2), MODV(0, 3, 0), MODV(0, 5, 0))
            conformer_phase(xA, xB, DER(3), MODV(1, 0, 0), MODV(1, 2, 0))
            ffn_phase(1, xB, xA, DER(4), MODV(1, 3, 0), MODV(1, 5, 0))
            final_phase(xA)
        else:
            ffn_phase(0, xA, xB, DER(2), MODV(0, 3, 0), MODV(0, 5, 0))
            conformer_phase(xB, xA, DER(3), MODV(1, 0, 0), MODV(1, 2, 0))
            ffn_phase(1, xA, xB, DER(4), MODV(1, 3, 0), MODV(1, 5, 0))
            final_phase(xB)
    return nc


def _host_prep(inputs):
    f = lambda a: np.ascontiguousarray(np.asarray(a, dtype=np.float32))
    x = inputs["x"]; ctx = inputs["ctx"]
    common = {
        "w_mod": f(inputs["w_mod"]), "b_mod": f(inputs["b_mod"]), "g_mix": f(inputs["g_mix"]),
        "g_ffn": f(inputs["g_ffn"]), "g_out": f(inputs["g_out"]), "w_in": f(inputs["w_in"][0]),
        "ssm_d": f(inputs["ssm_d"][0]), "w_glu": f(inputs["ssm_w_glu"][0]), "w_out": f(inputs["w_out"][0]),
        "pw1": f(inputs["cv_w_pw1"][0]), "dw_b": f(inputs["cv_dw_b"][0]), "ln_g": f(inputs["cv_ln_g"][0]),
        "ln_b": f(inputs["cv_ln_b"][0]), "pw2": f(inputs["cv_w_pw2"][0]), "w_up": f(inputs["ffn_w_up"]),
        "fcb": f(inputs["ffn_conv_b"]), "w_dn": f(inputs["ffn_w_down"]),
    }
    rpb = np.asarray(inputs["na_rpb"][0], np.float32)
    maps = []
    for core in range(8):
        b, half = core // 2, core % 2
        m = dict(common)
        if half == 0:
            m["xc"] = f(x[b]); m["ctxc"] = f(ctx[b]); dirs = (0, 1)
            m["dw_w"] = f(inputs["cv_dw_w"][0]); m["fcw"] = f(inputs["ffn_conv_w"])
        else:
            m["xc"] = f(x[b][::-1]); m["ctxc"] = f(ctx[b][::-1]); dirs = (1, 0)
            m["dw_w"] = f(inputs["cv_dw_w"][0][::-1]); m["fcw"] = f(inputs["ffn_conv_w"][:, ::-1])
        m["cvec"] = f(np.stack([inputs["c"][b], inputs["c_ctx"]]))
        dd = list(dirs)
        m["lam_re"] = f(inputs["ssm_lam_re"][0][dd].reshape(2, 4096))
        m["lam_im"] = f(inputs["ssm_lam_im"][0][dd].reshape(2, 4096))
        m["log_dt"] = f(np.repeat(inputs["ssm_log_dt"][0][dd][:, :, None], 64, axis=2).reshape(2, 4096))
        m["b_re"] = f(inputs["ssm_b_re"][0][dd].reshape(2, 4096, 16))
        m["b_im"] = f(inputs["ssm_b_im"][0][dd].reshape(2, 4096, 16))
        m["c_re"] = f(inputs["ssm_c_re"][0][dd].reshape(2, 1024, 64))
        m["c_im"] = f(inputs["ssm_c_im"][0][dd].reshape(2, 1024, 64))
        m["na_bias"] = _na_bias_table(rpb, half)
        maps.append(m)
    return maps


def _na_bias_table(rpb, half):
    tab = np.full((3, 8, 128, 640), NEG, np.float32)
    for typ in range(3):
        r_even = 2 * typ if typ < 2 else 4
        kr0 = max(r_even - 4, 0)
        for qi in range(128):
            rl = r_even + qi // 64
            cl = qi % 64
            ro, co = (rl, cl) if half == 0 else (63 - rl, 63 - cl)
            rs = min(max(ro - 4, 0), 56)
            cs = min(max(co - 8, 0), 48)
            for kro in range(rs, rs + 8):
                krl = kro if half == 0 else 63 - kro
                jr = krl - kr0
                if jr < 0 or jr >= 10:
                    raise RuntimeError("key row outside block")
                ridx = kro - ro + 7
                for kco in range(cs, cs + 16):
                    kcl = kco if half == 0 else 63 - kco
                    cidx = kco - co + 15
                    tab[typ, :, qi, jr * 64 + kcl] = rpb[:, ridx, cidx]
    return tab


_NC_CACHE = {}


STAGE = 3
NCORES = 8
S5DBG = False
S5BAR = False
SBUF_REPORT = False
L0_JOB_FRAC = 0.45


def kernel(**inputs):
    maps = _host_prep(inputs)
    if "nc" not in _NC_CACHE:
        _NC_CACHE["nc"] = build_program(stage=STAGE)
    nc = _NC_CACHE["nc"]
    res = run_bass_kernel_spmd(nc, maps[:NCORES], core_ids=list(range(NCORES)))
    outp = np.zeros((4, NPOS, D), np.float32)
    for core in range(NCORES):
        b, half = core // 2, core % 2
        y = res.results[core]["out"]
        if half == 0:
            outp[b, :2048] = y
        else:
            outp[b, 2048:] = y[::-1]
    return outp
```

```python
import contextlib
import numpy as np
import concourse.bass as bass
import concourse.mybir as mybir
from concourse.bass_utils import run_bass_kernel_spmd

F32 = mybir.dt.float32
BF16 = mybir.dt.bfloat16
AF = mybir.ActivationFunctionType
ALU = mybir.AluOpType
AX = mybir.AxisListType

ENGS = ("pe", "act", "dve", "pool", "sp")
SAME_ENGINE_INORDER = ("pe",)
NDMA = 32

D = 2048
NPOS = 4096
NCTX = 256
E = 2176
KVE = 2432
TB = 544
NBLK = 4
FF = 5632
EPS = 1e-6
NEG = -30000.0
TWO_PI = 6.283185307179586


class Sched:
    def __init__(self, nc, st):
        self.nc = nc
        self.ops = {e: [] for e in ENGS}
        self.cnt = {e: 0 for e in ENGS}
        self.seen = {e: {} for e in ENGS}
        self.lastw = {}
        self.readers = {}
        self.dma_cnt = [0] * NDMA
        self.dma_rr = {"sp": 0, "pool": 0, "act": 0}
        self.sems = {}
        for e in ENGS:
            self.sems["e_" + e] = st.enter_context(nc.semaphore("sem_" + e))
        for i in range(NDMA):
            self.sems["d_%d" % i] = st.enter_context(nc.semaphore("semd_%d" % i))

    def _deps(self, eng, r, w, nosw=False):
        toks = []
        for b in r:
            t = self.lastw.get(b)
            if t is not None:
                toks.append(t)
        for b in w:
            t = self.lastw.get(b)
            if t is not None:
                toks.append(t)
            toks.extend(self.readers.get(b, ()))
        waits = {}
        for (k, v, e) in toks:
            if e == eng and (eng in SAME_ENGINE_INORDER or nosw):
                continue
            if self.seen[eng].get(k, 0) >= v:
                continue
            if waits.get(k, 0) < v:
                waits[k] = v
        for k, v in waits.items():
            self.seen[eng][k] = v
        return list(waits.items())

    def _commit(self, tok, r, w):
        for b in w:
            self.lastw[b] = tok
            self.readers[b] = []
        for b in r:
            if b not in w:
                self.readers.setdefault(b, []).append(tok)

    def op(self, eng, fn, r=(), w=(), nosw=False):
        waits = self._deps(eng, r, w, nosw)
        self.cnt[eng] += 1
        tok = ("e_" + eng, self.cnt[eng], eng)
        self.ops[eng].append((waits, fn, ("e_" + eng, 1)))
        self._commit(tok, r, w)
        return tok

    def dma(self, eng, fn, r=(), w=()):
        base, n = {"sp": (0, 20), "pool": (20, 12), "act": (0, 20)}[eng]
        i = base + self.dma_rr[eng]
        self.dma_rr[eng] = (self.dma_rr[eng] + 1) % n
        waits = self._deps(eng, r, w)
        k = "d_%d" % i
        prev = 16 * self.dma_cnt[i]
        if prev > 0 and self.seen[eng].get(k, 0) < prev:
            waits.append((k, prev))
            self.seen[eng][k] = prev
        self.dma_cnt[i] += 1
        tok = (k, 16 * self.dma_cnt[i], None)
        self.ops[eng].append((waits, fn, (k, 16)))
        self._commit(tok, r, w)
        return tok

    def barrier(self):
        for eng in ENGS:
            waits = []
            for e2 in ENGS:
                k, v = "e_" + e2, self.cnt[e2]
                if e2 != eng and v > 0 and self.seen[eng].get(k, 0) < v:
                    waits.append((k, v))
                    self.seen[eng][k] = v
            for i in range(NDMA):
                k, v = "d_%d" % i, 16 * self.dma_cnt[i]
                if v > 0 and self.seen[eng].get(k, 0) < v:
                    waits.append((k, v))
                    self.seen[eng][k] = v
            self.ops[eng].append((waits, None, None))
        self.lastw = {}
        self.readers = {}

    def emit(self):
        nc = self.nc
        sems = self.sems
        ops = self.ops
        self.ops = {e: [] for e in ENGS}
        with nc.Block() as block:
            def run(engname, engobj):
                for (waits, fn, inc) in ops[engname]:
                    for (k, v) in waits:
                        engobj.wait_ge(sems[k], v)
                    if fn is not None:
                        ins = fn(engobj)
                        ins.then_inc(sems[inc[0]], inc[1])

            @block.tensor
            def _(e):
                run("pe", e)

            @block.scalar
            def _(e):
                run("act", e)

            @block.vector
            def _(e):
                run("dve", e)

            @block.gpsimd
            def _(e):
                run("pool", e)

            @block.sync
            def _(e):
                run("sp", e)


_UID = [0]


class Ring:
    def __init__(self, st, nc, name, n, shape, dt, psum=False):
        self.tiles = []
        self.ids = []
        _UID[0] += 1
        for i in range(n):
            nm = "%s_%d_%d" % (name, _UID[0], i)
            if psum:
                t = st.enter_context(nc.psum_tensor(nm, shape, dt))
            else:
                t = st.enter_context(nc.sbuf_tensor(nm, shape, dt))
            self.tiles.append(t)
            self.ids.append(nm)
        self.i = 0

    def next(self):
        t, i = self.tiles[self.i], self.ids[self.i]
        self.i = (self.i + 1) % len(self.tiles)
        return t, i


def build_program(stage=99, dbg=False):
    nc = bass.Bass("TRN2", target_bir_lowering=False)

    def din(name, shape, dt=F32):
        return nc.dram_tensor(name, list(shape), dt, kind="ExternalInput").ap()

    def dscr(name, shape, dt):
        return nc.dram_tensor(name, list(shape), dt, kind="Internal").ap()

    x_in = din("xc", [NPOS, D])
    ctx_in = din("ctxc", [NCTX, D])
    cvec = din("cvec", [2, D])
    w_mod = din("w_mod", [2, D, 6 * D])
    b_mod = din("b_mod", [2, 6 * D])
    g_mix = din("g_mix", [2, D])
    g_ffn = din("g_ffn", [2, D])
    g_out = din("g_out", [D])
    w_in = din("w_in", [D, 4096])
    lam_re = din("lam_re", [2, 4096])
    lam_im = din("lam_im", [2, 4096])
    log_dt = din("log_dt", [2, 4096])
    b_re = din("b_re", [2, 4096, 16])
    b_im = din("b_im", [2, 4096, 16])
    c_re = din("c_re", [2, 1024, 64])
    c_im = din("c_im", [2, 1024, 64])
    ssm_d = din("ssm_d", [1024])
    w_glu = din("w_glu", [1024, 1024])
    na_bias = din("na_bias", [3, 8, 128, 640])
    w_out = din("w_out", [D, D])
    pw1 = din("pw1", [D, 4096])
    dw_w = din("dw_w", [31, D])
    dw_b = din("dw_b", [D])
    ln_g = din("ln_g", [D])
    ln_b = din("ln_b", [D])
    pw2 = din("pw2", [D, D])
    w_up = din("w_up", [2, D, 2 * FF])
    fcw = din("fcw", [2, 3, 2 * FF])
    fcb = din("fcb", [2, 2 * FF])
    w_dn = din("w_dn", [2, FF, D])
    out = nc.dram_tensor("out", [2048, D], F32, kind="ExternalOutput").ap()
    dbg_out = None
    if dbg:
        dbg_out = [nc.dram_tensor("dbg%d" % i, [16, 128, E], F32, kind="ExternalOutput").ap() for i in range(2)]
        dbg_y = nc.dram_tensor("dbgy", [16, 128, E], BF16, kind="ExternalOutput").ap()

    xA = dscr("xA", [16, 128, E], F32)
    xB = dscr("xB", [16, 128, E], F32)
    WUP = [dscr("WUP%d" % l, [22, 128, 16 * 512], BF16) for l in range(2)]
    WDN = [dscr("WDN%d" % l, [16, 128, 44 * 128], BF16) for l in range(2)]
    WIN = dscr("WIN", [8, 128, 16 * 512], BF16)
    WPW1 = dscr("WPW1", [8, 128, 16 * 512], BF16)
    WOUT = dscr("WOUT", [4, 128, 16 * 512], BF16)
    WPW2 = dscr("WPW2", [4, 128, 16 * 512], BF16)
    WGLU = dscr("WGLU", [2, 128, 8 * 512], BF16)
    u_tm = dscr("u_tm", [NPOS + NCTX, 1024], BF16)
    q_fm = dscr("q_fm", [8, 128, E], BF16)
    k_fm = dscr("k_fm", [8, 128, KVE + NCTX], BF16)
    v_tm = dscr("v_tm", [KVE + NCTX, 1024], BF16)
    ymix = dscr("ymix", [16, 128, E], BF16)

    with contextlib.ExitStack() as top:
        S = Sched(nc, top)

        def sbt(st, name, shape, dt):
            _UID[0] += 1
            return st.enter_context(nc.sbuf_tensor("%s_%d" % (name, _UID[0]), shape, dt))

        def pst(st, name, shape, dt):
            _UID[0] += 1
            return st.enter_context(nc.psum_tensor("%s_%d" % (name, _UID[0]), shape, dt))

        rr = [0]

        def cast_eng():
            rr[0] = (rr[0] + 1) % 3
            return ("act", "dve", "pool")[rr[0]]

        def copy_fn(eng, o, i):
            if eng == "act":
                return lambda e: e.activation(out=o, in_=i, func=AF.Copy)
            return lambda e: e.tensor_copy(out=o, in_=i)

        ident_f = sbt(top, "ident_f", [128, 128], F32)
        ident_b = sbt(top, "ident_b", [128, 128], BF16)
        NV = 1800
        V = sbt(top, "V", [128, NV], F32)
        modv = sbt(top, "modv", [128, 2 * 6 * 2 * 16], F32)
        der = sbt(top, "der", [128, 10 * 16], F32)

        def MODV(l, j, v):
            o = ((l * 6 + j) * 2 + v) * 16
            return modv[:, o:o + 16]

        S.op("pool", lambda e: e.memset(ident_f[:], 0.0), w=["ident_f"])
        S.op("pool", lambda e: e.affine_select(out=ident_f[:], in_=ident_f[:], pattern=[[-1, 128]],
                                               compare_op=ALU.not_equal, fill=1.0, base=0,
                                               channel_multiplier=1), r=["ident_f"], w=["ident_f"])
        S.op("dve", lambda e: e.tensor_copy(out=ident_b[:], in_=ident_f[:]), r=["ident_f"], w=["ident_b"])

        vecs = []

        def addvec(name, ap1d, n):
            vecs.append((name, ap1d.rearrange("(c p) -> c p", p=128), n // 128))

        addvec("c", cvec[0], D)
        addvec("cctx", cvec[1], D)
        for l in range(2):
            addvec("g_mix%d" % l, g_mix[l], D)
            addvec("g_ffn%d" % l, g_ffn[l], D)
            for j in range(6):
                addvec("b_mod%d_%d" % (l, j), b_mod[l, j * D:(j + 1) * D], D)
        addvec("g_out", g_out, D)
        for k in range(31):
            addvec("dw_w%d" % k, dw_w[k], D)
        addvec("dw_b", dw_b, D)
        addvec("ln_g", ln_g, D)
        addvec("ln_b", ln_b, D)
        for l in range(2):
            for k in range(3):
                addvec("fcw%d_%d" % (l, k), fcw[l, k], 2 * FF)
            addvec("fcb%d" % l, fcb[l], 2 * FF)
        for d in range(2):
            addvec("lam_re%d" % d, lam_re[d], 4096)
            addvec("lam_im%d" % d, lam_im[d], 4096)
            addvec("log_dt%d" % d, log_dt[d], 4096)
        voff = {}
        o = 0
        for (name, ap2, nch) in vecs:
            voff[name] = o
            o += nch
        assert o <= NV, o
        nrows = o

        def VC(name, c0=0, n=16):
            return V[:, voff[name] + c0: voff[name] + c0 + n]

        with contextlib.ExitStack() as ph:
            rowt = Ring(ph, nc, "rowt", 2, [128, 128], F32)
            pvt = Ring(ph, nc, "pvt", 2, [128, 128], F32, psum=True)
            for t in range((nrows + 127) // 128):
                r0, r1 = t * 128, min(nrows, t * 128 + 128)
                tl, tid = rowt.next()
                for (name, ap2, nch) in vecs:
                    a, b = voff[name], voff[name] + nch
                    lo, hi = max(a, r0), min(b, r1)
                    if lo < hi:
                        S.dma("sp", (lambda e, tl=tl, lo=lo, hi=hi, a=a, ap2=ap2, r0=r0:
                                     e.dma_start(out=tl[lo - r0:hi - r0, :], in_=ap2[lo - a:hi - a, :])),
                              w=[tid])
                pt, pid = pvt.next()
                n = r1 - r0
                S.op("pe", (lambda e, pt=pt, tl=tl, n=n: e.transpose(out=pt[:, :n], in_=tl[:n, :],
                                                                    identity=ident_f[:n, :n])),
                     r=[tid, "ident_f"], w=[pid])
                S.op("dve", (lambda e, pt=pt, n=n, r0=r0: e.tensor_copy(out=V[:, r0:r0 + n], in_=pt[:, :n])),
                     r=[pid], w=["V"])
            S.barrier()
            S.emit()

        with contextlib.ExitStack() as ph:
            s_bf = sbt(ph, "s_bf", [128, 2, 16], BF16)
            S.op("act", lambda e: e.activation(out=s_bf[:, 0, :], in_=VC("c"), func=AF.Silu), r=["V"], w=["s_bf"])
            S.op("act", lambda e: e.activation(out=s_bf[:, 1, :], in_=VC("cctx"), func=AF.Silu), r=["V"], w=["s_bf"])
            wf = Ring(ph, nc, "wmf", 3, [128, 2048], F32)
            wb = Ring(ph, nc, "wmb", 3, [128, 2048], BF16)
            pm = Ring(ph, nc, "pm", 2, [128, 512], F32, psum=True)
            msum = Ring(ph, nc, "msum", 2, [128, 32], F32)
            for l in range(2):
                for j in range(6):
                    pt, pid = pm.next()
                    for kc in range(16):
                        f, fid = wf.next()
                        b, bid = wb.next()
                        S.dma("sp", (lambda e, f=f, l=l, j=j, kc=kc: e.dma_start(
                            out=f[:], in_=w_mod[l, kc * 128:(kc + 1) * 128, j * D:(j + 1) * D])), w=[fid])
                        ce = cast_eng()
                        S.op(ce, copy_fn(ce, b[:], f[:]), r=[fid], w=[bid])
                        for m in range(16):
                            S.op("pe", (lambda e, pt=pt, b=b, m=m, kc=kc: e.matmul(
                                pt[:, kc * 32 + 2 * m:kc * 32 + 2 * m + 2], lhsT=b[:, m * 128:(m + 1) * 128],
                                rhs=s_bf[:, :, kc], start=True, stop=True)), r=[bid, "s_bf"], w=[pid])
                    ms, msid = msum.next()
                    S.op("dve", (lambda e, pt=pt, ms=ms: e.tensor_reduce(
                        out=ms[:], in_=pt[:].rearrange("p (k c) -> p c k", k=16), axis=AX.X, op=ALU.add)),
                         r=[pid], w=[msid])
                    for v in range(2):
                        S.op("dve", (lambda e, ms=ms, l=l, j=j, v=v: e.tensor_tensor(
                            out=MODV(l, j, v), in0=ms[:, v:32:2], in1=VC("b_mod%d_%d" % (l, j)), op=ALU.add)),
                             r=[msid, "V"], w=["modv"])

            def derive(idx, gname, l, jscale, v):
                o = idx * 16
                S.op("dve", lambda e: e.tensor_scalar(out=der[:, o:o + 16], in0=MODV(l, jscale, v), scalar1=1.0,
                                                      scalar2=None, op0=ALU.add), r=["modv"], w=["der"])
                S.op("dve", lambda e: e.tensor_tensor(out=der[:, o:o + 16], in0=der[:, o:o + 16], in1=VC(gname),
                                                      op=ALU.mult), r=["der", "V"], w=["der"])
            derive(0, "g_mix0", 0, 1, 0)
            derive(1, "g_mix0", 0, 1, 1)
            derive(2, "g_ffn0", 0, 4, 0)
            derive(3, "g_mix1", 1, 1, 0)
            derive(4, "g_ffn1", 1, 4, 0)
            S.barrier()
            S.emit()

        def DER(i):
            return der[:, i * 16:(i + 1) * 16]

        with contextlib.ExitStack() as ph:
            cf = Ring(ph, nc, "cvf", 2, [128, 8192], F32)
            cb = Ring(ph, nc, "cvb", 2, [128, 8192], BF16)

            def convert(parts, dst2, F):
                f, fid = cf.next()
                b, bid = cb.next()
                for (vf, src) in parts:
                    S.dma("sp", (lambda e, vf=vf, src=src, f=f: e.dma_start(out=vf(f), in_=src)), w=[fid])
                ce = cast_eng()
                S.op(ce, copy_fn(ce, b[:, :F], f[:, :F]), r=[fid], w=[bid])
                S.dma("pool", lambda e: e.dma_start(out=dst2, in_=b[:, :F]), r=[bid], w=["wscr"])

            for (src, dst, ng) in ((w_in, WIN, 8), (w_out, WOUT, 4)):
                v = src.rearrange("(kc p) (g c) -> g p kc c", p=128, c=512)
                for G in range(ng):
                    convert([((lambda f: f[:, :8192].rearrange("p (kc c) -> p kc c", kc=16)), v[G])], dst[G], 8192)
            v = w_glu.rearrange("(kc p) (g c) -> g p kc c", p=128, c=512)
            for G in range(2):
                convert([((lambda f: f[:, :4096].rearrange("p (kc c) -> p kc c", kc=8)), v[G])], WGLU[G], 4096)
            S.barrier()
            S.emit()

        bg_jobs = []
        for l in range(2):
            vu = w_up[l].rearrange("(kc p) (ug g c) -> ug g p kc c", p=128, ug=2, c=256)
            for G in range(22):
                for kq in range(4):
                    parts = []
                    for ug in range(2):
                        parts.append(((lambda f, ug=ug: f[:, :2048].rearrange("p (kc u c) -> p kc u c", kc=4, u=2)[:, :, ug, :]),
                                      vu[ug, G][:, kq * 4:(kq + 1) * 4, :]))
                    bg_jobs.append((parts, WUP[l][G][:, kq * 2048:(kq + 1) * 2048], 2048))
            vd = w_dn[l].rearrange("(j p) (m c) -> m p j c", p=128, c=128)
            for m in range(16):
                for jq in range(4):
                    bg_jobs.append(([((lambda f: f[:, :1408].rearrange("p (j c) -> p j c", j=11)), vd[m][:, jq * 11:(jq + 1) * 11, :])],
                                    WDN[l][m][:, jq * 1408:(jq + 1) * 1408], 1408))
            if l == 0:
                for (src, dst, ng) in ((pw1, WPW1, 8), (pw2, WPW2, 4)):
                    vv = src.rearrange("(kc p) (g c) -> g p kc c", p=128, c=512)
                    for G in range(ng):
                        for kq in range(4):
                            bg_jobs.append(([((lambda f: f[:, :2048].rearrange("p (kc c) -> p kc c", kc=4)), vv[G][:, kq * 4:(kq + 1) * 4, :])],
                                            dst[G][:, kq * 2048:(kq + 1) * 2048], 2048))
        bg_state = {"i": 0, "rr": 0}

        def bg_emit(n, cf, cb):
            for _ in range(n):
                if bg_state["i"] >= len(bg_jobs):
                    return
                parts, dst2, F = bg_jobs[bg_state["i"]]
                bg_state["i"] += 1
                f, fid = cf.next()
                b, bid = cb.next()
                for (vf, src) in parts:
                    S.dma("sp", (lambda e, vf=vf, src=src, f=f: e.dma_start(out=vf(f), in_=src)), w=[fid])
                bg_state["rr"] += 1
                ce = "act"
                S.op(ce, copy_fn(ce, b[:, :F], f[:, :F]), r=[fid], w=[bid])
                S.dma("pool", (lambda e, dst2=dst2, b=b, F=F: e.dma_start(out=dst2, in_=b[:, :F])), r=[bid], w=["wscr"])

        def l0_proj(do_mixer):
            with contextlib.ExitStack() as ph:
                xt = Ring(ph, nc, "xt", 2, [128, D], F32)
                xn = Ring(ph, nc, "xn", 2, [128, D], BF16)
                junk = sbt(ph, "junk", [128, D], BF16)
                ss = Ring(ph, nc, "ss", 4, [128, 2], F32)
                hfm = Ring(ph, nc, "hfm", 2, [128, 16, 512], BF16)
                ptr = Ring(ph, nc, "ptr", 2, [128, 4, 128], BF16, psum=True)
                pxf = Ring(ph, nc, "pxf", 2, [128, 4, 128], F32, psum=True)
                pmm = Ring(ph, nc, "pmm", 3, [128, 512], F32, psum=True)
                xo = Ring(ph, nc, "xo", 2, [128, 4, 128], F32)
                wg = Ring(ph, nc, "wg", 2, [128, 16, 512], BF16)
                ob = Ring(ph, nc, "ob", 3, [128, 512], BF16)
                l_bcf = Ring(ph, nc, "l_bcf", 4, [128, 2048], F32)
                l_bcb = Ring(ph, nc, "l_bcb", 2, [128, 2048], BF16)
                l_tiles = [0]
                ngroups = 9 if do_mixer else 5
                for g in range(ngroups):
                    is_ctx = (g == 8)
                    ntile = 2 if is_ctx else 4
                    ntok = ntile * 128
                    h, hid = hfm.next()
                    Av, Bv = (DER(1), MODV(0, 0, 1)) if is_ctx else (DER(0), MODV(0, 0, 0))
                    for ti in range(ntile):
                        src = ctx_in if is_ctx else x_in
                        p0 = ti * 128 if is_ctx else g * 512 + ti * 128
                        if do_mixer:
                            l_tiles[0] += 1
                            want = (l_tiles[0] * int(len(bg_jobs) * L0_JOB_FRAC)) // 34
                            bg_emit(want - bg_state["i"], l_bcf, l_bcb)
                        x_, xid = xt.next()
                        S.dma("sp", (lambda e, x_=x_, src=src, p0=p0: e.dma_start(out=x_[:], in_=src[p0:p0 + 128, :])),
                              w=[xid])
                        s_, sid = ss.next()
                        S.op("act", (lambda e, x_=x_, s_=s_: e.activation(out=junk[:], in_=x_[:], func=AF.Square,
                                                                          accum_out=s_[:, 0:1])),
                             r=[xid], w=["junk", sid])
                        S.op("act", (lambda e, s_=s_: e.activation(out=s_[:, 1:2], in_=s_[:, 0:1], func=AF.Sqrt,
                                                                   scale=1.0 / D, bias=EPS)), r=[sid], w=[sid])
                        S.op("dve", (lambda e, s_=s_: e.reciprocal(out=s_[:, 1:2], in_=s_[:, 1:2])), r=[sid], w=[sid])
                        n_, nid = xn.next()
                        S.op("dve", (lambda e, n_=n_, x_=x_, s_=s_: e.tensor_scalar(
                            out=n_[:], in0=x_[:], scalar1=s_[:, 1:2], scalar2=None, op0=ALU.mult)),
                             r=[xid, sid], w=[nid])
                        for q4 in range(4):
                            pt, pid = ptr.next()
                            for i in range(4):
                                kc = q4 * 4 + i
                                S.op("pe", (lambda e, pt=pt, i=i, n_=n_, kc=kc: e.transpose(
                                    out=pt[:, i, :], in_=n_[:, kc * 128:(kc + 1) * 128], identity=ident_b[:])),
                                     r=[nid, "ident_b"], w=[pid])
                            for i in range(4):
                                kc = q4 * 4 + i
                                S.op("act", (lambda e, pt=pt, i=i, h=h, kc=kc, ti=ti, Av=Av, Bv=Bv: e.activation(
                                    out=h[:, kc, ti * 128:(ti + 1) * 128], in_=pt[:, i, :], func=AF.Identity,
                                    scale=Av[:, kc:kc + 1], bias=Bv[:, kc:kc + 1])),
                                     r=[pid, "der", "modv"], w=[hid])
                        if (not is_ctx) and p0 < E:
                            for q4 in range(4):
                                pf, pfid = pxf.next()
                                for i in range(4):
                                    kc = q4 * 4 + i
                                    S.op("pe", (lambda e, pf=pf, i=i, x_=x_, kc=kc: e.transpose(
                                        out=pf[:, i, :], in_=x_[:, kc * 128:(kc + 1) * 128], identity=ident_f[:])),
                                         r=[xid, "ident_f"], w=[pfid])
                                o_, oid = xo.next()
                                S.op("dve", (lambda e, o_=o_, pf=pf: e.tensor_copy(out=o_[:], in_=pf[:])),
                                     r=[pfid], w=[oid])
                                S.dma("pool", (lambda e, o_=o_, q4=q4, p0=p0: e.dma_start(
                                    out=xA[q4 * 4:(q4 + 1) * 4, :, p0:p0 + 128].rearrange("c p t -> p c t"),
                                    in_=o_[:])), r=[oid], w=["xA"])
                    if not do_mixer:
                        continue
                    base = 0 if is_ctx else g * 512
                    nq = 0 if is_ctx else min(512, max(0, E - base))
                    nkv = ntok if is_ctx else min(512, max(0, KVE - base))
                    for G in range(8):
                        kind = ("u", "u", "q", "q", "k", "k", "v", "v")[G]
                        n = {"u": ntok, "q": nq, "k": nkv, "v": nkv}[kind]
                        if n == 0:
                            continue
                        w_, wid = wg.next()
                        S.dma("sp", (lambda e, w_=w_, G=G: e.dma_start(
                            out=w_[:].rearrange("p a b -> p (a b)"), in_=WIN[G])), r=["wscr"], w=[wid])
                        if kind in ("u", "v"):
                            for ti in range(n // 128):
                                pt, pid = pmm.next()
                                for kc in range(16):
                                    S.op("pe", (lambda e, pt=pt, h=h, kc=kc, ti=ti, w_=w_: e.matmul(
                                        pt[:, :], lhsT=h[:, kc, ti * 128:(ti + 1) * 128], rhs=w_[:, kc, :],
                                        start=(kc == 0), stop=(kc == 15))), r=[hid, wid], w=[pid])
                                o_, oid = ob.next()
                                ce = ("act", "dve")[ti % 2]
                                S.op(ce, copy_fn(ce, o_[:, :], pt[:, :]), r=[pid], w=[oid])
                                if kind == "u":
                                    row = (NPOS if is_ctx else base) + ti * 128
                                    dst = u_tm[row:row + 128, (G % 2) * 512:(G % 2) * 512 + 512]
                                else:
                                    row = (KVE if is_ctx else base) + ti * 128
                                    dst = v_tm[row:row + 128, (G % 2) * 512:(G % 2) * 512 + 512]
                                S.dma("pool", (lambda e, o_=o_, dst=dst: e.dma_start(out=dst, in_=o_[:, :])),
                                      r=[oid], w=["uv_scr"])
                        else:
                            for m in range(4):
                                pt, pid = pmm.next()
                                for kc in range(16):
                                    S.op("pe", (lambda e, pt=pt, h=h, kc=kc, m=m, w_=w_, n=n: e.matmul(
                                        pt[:, :n], lhsT=w_[:, kc, m * 128:(m + 1) * 128], rhs=h[:, kc, :n],
                                        start=(kc == 0), stop=(kc == 15))), r=[hid, wid], w=[pid])
                                o_, oid = ob.next()
                                ce = ("act", "dve")[m % 2]
                                S.op(ce, copy_fn(ce, o_[:, :n], pt[:, :n]), r=[pid], w=[oid])
                                hd = (G % 2) * 4 + m
                                if kind == "q":
                                    dst = q_fm[hd, :, base:base + n]
                                else:
                                    c0 = KVE if is_ctx else base
                                    dst = k_fm[hd, :, c0:c0 + n]
                                S.dma("pool", (lambda e, o_=o_, dst=dst, n=n: e.dma_start(out=dst, in_=o_[:, :n])),
                                      r=[oid], w=["qk_scr"])
                S.barrier()
                S.emit()

        ones_f = sbt(top, "ones_f", [128, 128], F32)
        S.op("pool", lambda e: e.memset(ones_f[:], 1.0), w=["ones_f"])

        def load_x_cols(ring, xsrc, kc, c0, n, lo, hi):
            t, tid = ring.next()
            a, b = max(c0, lo), min(c0 + n, hi)
            if a > c0 or b < c0 + n:
                S.op("pool", (lambda e, t=t, n=n: e.memset(t[:, :n], 0.0)), w=[tid])
            S.dma("sp", (lambda e, t=t, a=a, b=b, c0=c0, kc=kc: e.dma_start(out=t[:, a - c0:b - c0], in_=xsrc[kc, :, a:b])),
                  r=["xsrc"], w=[tid])
            return t, tid

        def norm_block(ph, xsrc, c0, n, Av, Bv, h, hid, rings):
            xs, sq, pss, rbc = rings
            H = n // 2
            for sbk in range(2):
                o0 = sbk * H
                pt, pid = pss.next()
                for kc in range(16):
                    t, tid = load_x_cols(xs, xsrc, kc, c0 + o0, H, 0, E)
                    S.op("act", (lambda e, t=t, H=H: e.activation(out=t[:, :H], in_=t[:, :H], func=AF.Square)),
                         r=[tid], w=[tid])
                    S.op("pe", (lambda e, pt=pt, t=t, kc=kc, H=H: e.matmul(
                        pt[:, :H], lhsT=ones_f[:], rhs=t[:, :H], start=(kc == 0), stop=(kc == 15))),
                         r=[tid, "ones_f"], w=[pid])
                r_, rid = rbc.next()
                S.op("act", (lambda e, r_=r_, pt=pt, H=H: e.activation(out=r_[:, :H], in_=pt[:, :H], func=AF.Sqrt,
                                                                      scale=1.0 / D, bias=EPS)), r=[pid], w=[rid])
                S.op("dve", (lambda e, r_=r_, H=H: e.reciprocal(out=r_[:, :H], in_=r_[:, :H])), r=[rid], w=[rid])
                for kc in range(16):
                    t, tid = load_x_cols(xs, xsrc, kc, c0 + o0, H, 0, E)
                    S.op("dve", (lambda e, t=t, r_=r_, H=H: e.tensor_tensor(out=t[:, :H], in0=t[:, :H], in1=r_[:, :H],
                                                                            op=ALU.mult)), r=[tid, rid], w=[tid], nosw=True)
                    S.op("act", (lambda e, t=t, h=h, kc=kc, H=H, o0=o0, Av=Av, Bv=Bv: e.activation(
                        out=h[:, kc, o0:o0 + H], in_=t[:, :H], func=AF.Identity, scale=Av[:, kc:kc + 1], bias=Bv[:, kc:kc + 1])),
                         r=[tid, "der", "modv"], w=[hid])

        def ffn_phase(l, xsrc, xdst, Av, Bv, gate):
            NH = TB + 2
            with contextlib.ExitStack() as ph:
                xs = Ring(ph, nc, "f_xs", 4, [128, NH // 2], F32)
                sq = None
                pss = Ring(ph, nc, "f_pss", 2, [128, 512], F32, psum=True)
                rbc = Ring(ph, nc, "f_rbc", 2, [128, NH // 2], F32)
                hr = Ring(ph, nc, "f_h", 1, [128, 16, NH], BF16)
                act = sbt(ph, "f_act", [128, 44, TB], BF16)
                wu = Ring(ph, nc, "f_wu", 2, [128, 16, 512], BF16)
                wd = Ring(ph, nc, "f_wd", 2, [128, 44, 128], BF16)
                pu = Ring(ph, nc, "f_pu", 2, [128, 2, 512], F32, psum=True)
                cv = Ring(ph, nc, "f_cv", 4, [128, TB], F32)
                sg = Ring(ph, nc, "f_sg", 2, [128, TB], F32)
                pdn = Ring(ph, nc, "f_pd", 1, [128, 2, 512], F32, psum=True)
                xo = Ring(ph, nc, "f_xo", 2, [128, TB], F32)
                fw = "fcw%d_" % l
                H2 = NH // 2
                for blk in range(NBLK):
                    t0 = blk * TB
                    h, hid = hr.next()
                    norm_block(ph, xsrc, t0 - 1, NH, Av, Bv, h, hid, (xs, sq, pss, rbc))
                    for G in range(22):
                        w_, wid = wu.next()
                        S.dma("sp", (lambda e, w_=w_, G=G: e.dma_start(
                            out=w_[:].rearrange("p a b -> p (a b)"), in_=WUP[l][G])), r=["wscr"], w=[wid])
                        for jj in range(2):
                            res = []
                            for ug in range(2):
                                ch = ug * 44 + G * 2 + jj
                                pt, pid = pu.next()
                                for hf in range(2):
                                    for kc in range(16):
                                        S.op("pe", (lambda e, pt=pt, w_=w_, kc=kc, ug=ug, jj=jj, hf=hf, h=h: e.matmul(
                                            pt[:, hf, :H2], lhsT=w_[:, kc, ug * 256 + jj * 128: ug * 256 + jj * 128 + 128],
                                            rhs=h[:, kc, hf * H2:(hf + 1) * H2], start=(kc == 0), stop=(kc == 15))),
                                             r=[hid, wid], w=[pid])
                                if blk == 0:
                                    S.op("dve", (lambda e, pt=pt: e.memset(pt[:, 0, 0:1], 0.0)), r=[pid], w=[pid])
                                c_, cid = cv.next()
                                def seg(off):
                                    a = []
                                    split = H2 - off
                                    a.append((0, off, 0, min(split, TB)))
                                    if split < TB:
                                        a.append((1, 0, split, TB))
                                    return a
                                first = True
                                for k, wname in ((1, fw + "1"), (0, fw + "0"), (2, fw + "2")):
                                    wv = V[:, voff[wname] + ch: voff[wname] + ch + 1]
                                    for (hf, pc, lo, hi) in seg(k):
                                        if first:
                                            bv = V[:, voff["fcb%d" % l] + ch: voff["fcb%d" % l] + ch + 1]
                                            S.op("act", (lambda e, c_=c_, pt=pt, hf=hf, pc=pc, lo=lo, hi=hi, wv=wv, bv=bv:
                                                         e.activation(out=c_[:, lo:hi], in_=pt[:, hf, pc:pc + hi - lo],
                                                                      func=AF.Identity, scale=wv, bias=bv)),
                                                 r=[pid, "V"], w=[cid])
                                        else:
                                            S.op("dve", (lambda e, c_=c_, pt=pt, hf=hf, pc=pc, lo=lo, hi=hi, wv=wv:
                                                         e.scalar_tensor_tensor(out=c_[:, lo:hi], in0=pt[:, hf, pc:pc + hi - lo],
                                                                                scalar=wv, in1=c_[:, lo:hi],
                                                                                op0=ALU.mult, op1=ALU.add)),
                                                 r=[pid, cid, "V"], w=[cid], nosw=True)
                                    first = False
                                res.append((c_, cid))
                            (cu, cuid), (cg, cgid) = res
                            s_, sid = sg.next()
                            S.op("act", (lambda e, s_=s_, cg=cg: e.activation(out=s_[:], in_=cg[:], func=AF.Silu)),
                                 r=[cgid], w=[sid])
                            j = G * 2 + jj
                            S.op("pool", (lambda e, s_=s_, cu=cu, j=j: e.tensor_tensor(out=act[:, j, :], in0=s_[:], in1=cu[:],
                                                                                      op=ALU.mult)),
                                 r=[sid, cuid], w=["f_act"])
                    HB = TB // 2
                    for m in range(16):
                        w_, wid = wd.next()
                        S.dma("sp", (lambda e, w_=w_, m=m: e.dma_start(
                            out=w_[:].rearrange("p a b -> p (a b)"), in_=WDN[l][m])), r=["wscr"], w=[wid])
                        o_, oid = load_x_cols(xo, xsrc, m, t0, TB, 0, E)
                        pt, pid = pdn.next()
                        for hf in range(2):
                            for j in range(44):
                                S.op("pe", (lambda e, pt=pt, w_=w_, j=j, hf=hf: e.matmul(
                                    pt[:, hf, :HB], lhsT=w_[:, j, :], rhs=act[:, j, hf * HB:(hf + 1) * HB],
                                    start=(j == 0), stop=(j == 43))), r=["f_act", wid], w=[pid])
                        for hf in range(2):
                            S.op("dve", (lambda e, o_=o_, pt=pt, hf=hf, m=m: e.scalar_tensor_tensor(
                                out=o_[:, hf * HB:(hf + 1) * HB], in0=pt[:, hf, :HB], scalar=gate[:, m:m + 1],
                                in1=o_[:, hf * HB:(hf + 1) * HB], op0=ALU.mult, op1=ALU.add)),
                                 r=[pid, oid, "modv"], w=[oid])
                        S.dma("pool", (lambda e, o_=o_, m=m, t0=t0: e.dma_start(out=xdst[m, :, t0:t0 + TB], in_=o_[:, :TB])),
                              r=[oid], w=["xdst"])
                S.barrier()
                S.emit()

        def proj_residual(ph, wscr, actt, actid, xsrc, xdst, t0, gate, rings):
            wr, pdn, xo = rings
            HB = TB // 2
            for m in range(16):
                w_, wid = wr.next()
                S.dma("sp", (lambda e, w_=w_, m=m: e.dma_start(
                    out=w_[:], in_=wscr[m // 4].rearrange("p (a b) -> p a b", a=16)[:, :, (m % 4) * 128:(m % 4) * 128 + 128])),
                      r=["wscr"], w=[wid])
                o_, oid = load_x_cols(xo, xsrc, m, t0, TB, 0, E)
                pt, pid = pdn.next()
                for hf in range(2):
                    for kc in range(16):
                        S.op("pe", (lambda e, pt=pt, w_=w_, kc=kc, hf=hf: e.matmul(
                            pt[:, hf, :HB], lhsT=w_[:, kc, :], rhs=actt[:, kc, hf * HB:(hf + 1) * HB],
                            start=(kc == 0), stop=(kc == 15))), r=[actid, wid], w=[pid])
                for hf in range(2):
                    S.op("dve", (lambda e, o_=o_, pt=pt, hf=hf, m=m: e.scalar_tensor_tensor(
                        out=o_[:, hf * HB:(hf + 1) * HB], in0=pt[:, hf, :HB], scalar=gate[:, m:m + 1],
                        in1=o_[:, hf * HB:(hf + 1) * HB], op0=ALU.mult, op1=ALU.add)),
                         r=[pid, oid, "modv"], w=[oid])
                S.dma("pool", (lambda e, o_=o_, m=m, t0=t0: e.dma_start(out=xdst[m, :, t0:t0 + TB], in_=o_[:, :TB])),
                      r=[oid], w=["xdst"])

        def mixout_phase(xsrc, xdst, gate):
            with contextlib.ExitStack() as ph:
                yt = Ring(ph, nc, "m_y", 2, [128, 16, TB], BF16)
                wr = Ring(ph, nc, "m_w", 3, [128, 16, 128], BF16)
                pdn = Ring(ph, nc, "m_pd", 2, [128, 2, 512], F32, psum=True)
                xo = Ring(ph, nc, "m_xo", 3, [128, TB], F32)
                for blk in range(NBLK):
                    t0 = blk * TB
                    y_, yid = yt.next()
                    S.dma("sp", (lambda e, y_=y_, t0=t0: e.dma_start(
                        out=y_[:], in_=ymix[:, :, t0:t0 + TB].rearrange("c p t -> p c t"))), r=["ymix"], w=[yid])
                    proj_residual(ph, WOUT, y_, yid, xsrc, xdst, t0, gate, (wr, pdn, xo))
                S.barrier()
                S.emit()

        def conformer_phase(xsrc, xdst, Av, Bv, gate):
            NH = TB + 30
            H2 = NH // 2
            HB = TB // 2
            with contextlib.ExitStack() as ph:
                xs = Ring(ph, nc, "c_xs", 4, [128, H2], F32)
                pss = Ring(ph, nc, "c_pss", 1, [128, 512], F32, psum=True)
                rbc = Ring(ph, nc, "c_rbc", 2, [128, H2], F32)
                hr = Ring(ph, nc, "c_h", 1, [128, 16, NH], BF16)
                w1 = Ring(ph, nc, "c_w1", 4, [128, 16, 128], BF16)
                pa = Ring(ph, nc, "c_pa", 2, [128, 2, 512], F32, psum=True)
                pzd = Ring(ph, nc, "c_pzd", 1, [128, 2, 512], F32, psum=True)
                sgr = Ring(ph, nc, "c_sg", 2, [128, NH], F32)
                cir = Ring(ph, nc, "c_ci", 2, [128, NH], BF16)
                dgr = Ring(ph, nc, "c_dg", 2, [128, 31, 128], BF16)
                zbuf = sbt(ph, "c_z", [128, 16, TB], F32)
                zs = sbt(ph, "c_zs", [128, 16, TB], BF16)
                mean = sbt(ph, "c_mean", [128, TB], F32)
                rstd = sbt(ph, "c_rstd", [128, TB], F32)
                ones_b = sbt(ph, "c_ones", [128, 128], BF16)
                S.op("dve", lambda e: e.tensor_copy(out=ones_b[:], in_=ones_f[:]), r=["ones_f"], w=["c_ones"])
                wr = Ring(ph, nc, "c_w2", 2, [128, 16, 128], BF16)
                xo = Ring(ph, nc, "c_xo", 2, [128, TB], F32)
                v2 = lambda t: t[:, :].rearrange("p (a b) -> p a b", a=2)
                for blk in range(NBLK):
                    t0 = blk * TB
                    h, hid = hr.next()
                    norm_block(ph, xsrc, t0 - 15, NH, Av, Bv, h, hid, (xs, None, pss, rbc))
                    for c in range(16):
                        ws = []
                        for part in range(2):
                            cc = part * 16 + c
                            w_, wid = w1.next()
                            S.dma("sp", (lambda e, w_=w_, cc=cc: e.dma_start(
                                out=w_[:], in_=WPW1[cc // 4].rearrange("p (a b) -> p a b", a=16)[:, :, (cc % 4) * 128:(cc % 4) * 128 + 128])),
                                  r=["wscr"], w=[wid])
                            ws.append((w_, wid))
                        dg, dgid = dgr.next()
                        for k in range(31):
                            wv = V[:, voff["dw_w%d" % k] + c: voff["dw_w%d" % k] + c + 1]
                            S.op("act", (lambda e, dg=dg, k=k, wv=wv: e.activation(out=dg[:, k, :], in_=ident_b[:], func=AF.Identity,
                                                                                 scale=wv)), r=["ident_b", "V"], w=[dgid])
                        pts = []
                        for part in range(2):
                            w_, wid = ws[part]
                            pt, pid = pa.next()
                            for hf in range(2):
                                for kc in range(16):
                                    S.op("pe", (lambda e, pt=pt, w_=w_, kc=kc, hf=hf, h=h: e.matmul(
                                        pt[:, hf, :H2], lhsT=w_[:, kc, :], rhs=h[:, kc, hf * H2:(hf + 1) * H2],
                                        start=(kc == 0), stop=(kc == 15))), r=[hid, wid], w=[pid])
                            pts.append((pt, pid))
                        (pA, pAid), (pG, pGid) = pts
                        sg_, sgid = sgr.next()
                        ci, ciid = cir.next()
                        S.op("act", (lambda e, sg_=sg_, pG=pG: e.activation(
                            out=v2(sg_), in_=pG[:, :, :H2], func=AF.Sigmoid)), r=[pGid], w=[sgid])
                        S.op("dve", (lambda e, ci=ci, pA=pA, sg_=sg_: e.tensor_tensor(
                            out=v2(ci), in0=pA[:, :, :H2], in1=v2(sg_), op=ALU.mult)), r=[pAid, sgid], w=[ciid])
                        if blk == 0:
                            S.op("dve", (lambda e, ci=ci: e.memset(ci[:, 0:15], 0.0)), r=[ciid], w=[ciid])
                        pz, pzid = pzd.next()
                        for hf in range(2):
                            for k in range(31):
                                S.op("pe", (lambda e, pz=pz, dg=dg, k=k, hf=hf, ci=ci: e.matmul(
                                    pz[:, hf, :HB], lhsT=dg[:, k, :], rhs=ci[:, hf * HB + k:hf * HB + k + HB],
                                    start=(k == 0), stop=(k == 30))), r=[dgid, ciid], w=[pzid])
                        bv = V[:, voff["dw_b"] + c: voff["dw_b"] + c + 1]
                        S.op("act", (lambda e, c=c, pz=pz, bv=bv: e.activation(
                            out=zbuf[:, c, :].rearrange("p (a b) -> p a b", a=2), in_=pz[:, :, :HB], func=AF.Identity, bias=bv)),
                             r=[pzid, "V"], w=["c_z%d" % c])
                        S.op("act", (lambda e, c=c: e.activation(out=zs[:, c, :], in_=zbuf[:, c, :], func=AF.Square)),
                             r=["c_z%d" % c], w=["c_zs"])
                    pS, pSid = pa.next()
                    pQ, pQid = pa.next()
                    for hf in range(2):
                        for c in range(16):
                            S.op("pe", (lambda e, pS=pS, hf=hf, c=c: e.matmul(pS[:, hf, :HB], lhsT=ones_f[:], rhs=zbuf[:, c, hf * HB:(hf + 1) * HB],
                                                                             start=(c == 0), stop=(c == 15))), r=["c_z%d" % c, "ones_f"], w=[pSid])
                    for hf in range(2):
                        for c in range(16):
                            S.op("pe", (lambda e, pQ=pQ, hf=hf, c=c: e.matmul(pQ[:, hf, :HB], lhsT=ones_b[:], rhs=zs[:, c, hf * HB:(hf + 1) * HB],
                                                                             start=(c == 0), stop=(c == 15))), r=["c_zs", "c_ones"], w=[pQid])
                    S.op("act", (lambda e, pS=pS: e.activation(out=v2(mean), in_=pS[:, :, :HB], func=AF.Identity, scale=1.0 / D)),
                         r=[pSid], w=["mean"])
                    S.op("dve", (lambda e: e.tensor_tensor(out=rstd[:], in0=mean[:], in1=mean[:], op=ALU.mult)),
                         r=["mean"], w=["rstd"])
                    S.op("dve", (lambda e, pQ=pQ: e.scalar_tensor_tensor(out=v2(rstd), in0=pQ[:, :, :HB], scalar=1.0 / D,
                                                                        in1=v2(rstd), op0=ALU.mult, op1=ALU.subtract)),
                         r=[pQid, "rstd"], w=["rstd"])
                    S.op("act", (lambda e: e.activation(out=rstd[:], in_=rstd[:], func=AF.Sqrt, bias=EPS, scale=1.0)),
                         r=["rstd"], w=["rstd"])
                    S.op("dve", (lambda e: e.reciprocal(out=rstd[:], in_=rstd[:])), r=["rstd"], w=["rstd"])
                    for c in range(16):
                        S.op("dve", (lambda e, c=c: e.tensor_tensor(out=zbuf[:, c, :], in0=zbuf[:, c, :], in1=mean[:], op=ALU.subtract)),
                             r=["c_z%d" % c, "mean"], w=["c_z%d" % c])
                        S.op("dve", (lambda e, c=c: e.tensor_tensor(out=zbuf[:, c, :], in0=zbuf[:, c, :], in1=rstd[:], op=ALU.mult)),
                             r=["c_z%d" % c, "rstd"], w=["c_z%d" % c], nosw=True)
                        gv = V[:, voff["ln_g"] + c: voff["ln_g"] + c + 1]
                        bv = V[:, voff["ln_b"] + c: voff["ln_b"] + c + 1]
                        S.op("act", (lambda e, c=c, gv=gv, bv=bv: e.activation(out=zs[:, c, :], in_=zbuf[:, c, :], func=AF.Silu,
                                                                                scale=gv, bias=bv)),
                             r=["c_z%d" % c, "V"], w=["c_zs"])
                    proj_residual(ph, WPW2, zs, "c_zs", xsrc, xdst, t0, gate, (wr, pzd, xo))
                S.barrier()
                S.emit()

        def na_phase():
            SCALE = 128.0 ** -0.5
            NKT = (KVE + NCTX) // 128
            with contextlib.ExitStack() as ph:
                qs = sbt(ph, "n_q", [128, 4, E], BF16)
                ks = sbt(ph, "n_k", [128, 4, KVE + NCTX], BF16)
                vs = sbt(ph, "n_v", [128, NKT, 512], BF16)
                bint = sbt(ph, "n_bint", [128, 4, 640], F32)
                bedge = Ring(ph, nc, "n_be", 2, [128, 640], F32)
                yna = sbt(ph, "n_y", [128, 4, E], BF16)
                psS = Ring(ph, nc, "n_ps", 2, [128, 2, 512], F32, psum=True)
                psT = Ring(ph, nc, "n_pt", 2, [128, 8, 128], BF16, psum=True)
                scr = Ring(ph, nc, "n_sc", 2, [128, 896], F32)
                pbr = Ring(ph, nc, "n_pb", 2, [128, 896], BF16)
                pTr = Ring(ph, nc, "n_pT", 2, [128, 7, 128], BF16)
                st4 = Ring(ph, nc, "n_st", 4, [128, 4], F32)
                otm = Ring(ph, nc, "n_o", 2, [128, 128], BF16)
                bcf = Ring(ph, nc, "n_bcf", 5, [128, 2048], F32)
                bcb = Ring(ph, nc, "n_bcb", 2, [128, 2048], BF16)
                it_cnt = [0]
                na_j0 = [bg_state["i"]]
                tot_iter = 2 * (E // 128) * 4
                if SBUF_REPORT:
                    print("NA sbuf remaining:", nc.sbuf_bytes_remaining)
                for hg in range(2):
                    S.dma("sp", (lambda e, hg=hg: e.dma_start(out=qs[:], in_=q_fm[hg * 4:hg * 4 + 4].rearrange("h p t -> p h t"))),
                          r=["qk_scr"], w=["n_q"])
                    S.dma("sp", (lambda e, hg=hg: e.dma_start(out=ks[:], in_=k_fm[hg * 4:hg * 4 + 4].rearrange("h p t -> p h t"))),
                          r=["qk_scr"], w=["n_k"])
                    S.dma("sp", (lambda e, hg=hg: e.dma_start(
                        out=vs[:], in_=v_tm[:, hg * 512:hg * 512 + 512].rearrange("(t p) c -> p t c", p=128))),
                          r=["uv_scr"], w=["n_v"])
                    S.dma("sp", (lambda e, hg=hg: e.dma_start(
                        out=bint[:], in_=na_bias[2, hg * 4:hg * 4 + 4].rearrange("h p k -> p h k"))), w=["n_bint"])
                    def tile_gen(hg, qt, h):
                        r = 2 * qt
                        kr0 = max(r - 4, 0)
                        k0 = kr0 * 64
                        it_cnt[0] += 1
                        want = na_j0[0] + (it_cnt[0] * (len(bg_jobs) - na_j0[0]) + tot_iter - 1) // tot_iter
                        bg_emit(want - bg_state["i"], bcf, bcb)
                        if qt < 2:
                            bt, btid = bedge.next()
                            S.dma("sp", (lambda e, bt=bt, qt=qt, hg=hg, h=h: e.dma_start(out=bt[:], in_=na_bias[qt, hg * 4 + h])),
                                  w=[btid])
                            bias_ap = bt[:, :]
                        else:
                            btid = "n_bint"
                            bias_ap = bint[:, h, :]
                        ps, psid = psS.next()
                        S.op("pe", (lambda e, ps=ps, h=h, qt=qt, k0=k0: e.matmul(
                            ps[:, 0, :], lhsT=qs[:, h, qt * 128:(qt + 1) * 128], rhs=ks[:, h, k0:k0 + 512],
                            start=True, stop=True)), r=["n_q", "n_k"], w=[psid])
                        S.op("pe", (lambda e, ps=ps, h=h, qt=qt, k0=k0: e.matmul(
                            ps[:, 1, 0:128], lhsT=qs[:, h, qt * 128:(qt + 1) * 128], rhs=ks[:, h, k0 + 512:k0 + 640],
                            start=True, stop=True)), r=["n_q", "n_k"], w=[psid])
                        S.op("pe", (lambda e, ps=ps, h=h, qt=qt: e.matmul(
                            ps[:, 1, 128:384], lhsT=qs[:, h, qt * 128:(qt + 1) * 128], rhs=ks[:, h, KVE:KVE + NCTX],
                            start=True, stop=True)), r=["n_q", "n_k"], w=[psid])
                        yield
                        sc, scid = scr.next()
                        S.op("dve", (lambda e, sc=sc, ps=ps, bias_ap=bias_ap: e.scalar_tensor_tensor(
                            out=sc[:, 0:512], in0=ps[:, 0, :], scalar=SCALE, in1=bias_ap[:, 0:512],
                            op0=ALU.mult, op1=ALU.add)), r=[psid, btid], w=[scid])
                        S.op("dve", (lambda e, sc=sc, ps=ps, bias_ap=bias_ap: e.scalar_tensor_tensor(
                            out=sc[:, 512:640], in0=ps[:, 1, 0:128], scalar=SCALE, in1=bias_ap[:, 512:640],
                            op0=ALU.mult, op1=ALU.add)), r=[psid, btid], w=[scid])
                        S.op("act", (lambda e, sc=sc, ps=ps: e.activation(out=sc[:, 640:896], in_=ps[:, 1, 128:384],
                                                                         func=AF.Identity, scale=SCALE)),
                             r=[psid], w=[scid])
                        yield
                        st_, stid = st4.next()
                        S.op("dve", (lambda e, st_=st_, sc=sc: e.tensor_reduce(out=st_[:, 0:1], in_=sc[:, :], axis=AX.X, op=ALU.max)),
                             r=[scid], w=[stid])
                        S.op("dve", (lambda e, st_=st_: e.tensor_scalar(out=st_[:, 1:2], in0=st_[:, 0:1], scalar1=-1.0, scalar2=None,
                                                                       op0=ALU.mult)), r=[stid], w=[stid], nosw=True)
                        yield
                        pb, pbid = pbr.next()
                        S.op("act", (lambda e, pb=pb, sc=sc, st_=st_: e.activation(out=pb[:, :], in_=sc[:, :], func=AF.Exp,
                                                                                  bias=st_[:, 1:2], scale=1.0,
                                                                                  accum_out=st_[:, 2:3])),
                             r=[scid, stid], w=[pbid, stid])
                        S.op("dve", (lambda e, st_=st_: e.reciprocal(out=st_[:, 3:4], in_=st_[:, 2:3])), r=[stid], w=[stid])
                        yield
                        pt, ptid = psT.next()
                        for j in range(7):
                            S.op("pe", (lambda e, pt=pt, pb=pb, j=j: e.transpose(out=pt[:, j, :], in_=pb[:, j * 128:(j + 1) * 128],
                                                                                identity=ident_b[:])),
                                 r=[pbid, "ident_b"], w=[ptid])
                        yield
                        pT, pTid = pTr.next()
                        S.op("act", (lambda e, pT=pT, pt=pt: e.activation(out=pT[:, 0:4, :], in_=pt[:, 0:4, :], func=AF.Copy)),
                             r=[ptid], w=[pTid])
                        S.op("dve", (lambda e, pT=pT, pt=pt: e.tensor_copy(out=pT[:, 4:7, :], in_=pt[:, 4:7, :])),
                             r=[ptid], w=[pTid])
                        yield
                        po, poid = ps[:, 1, 384:512], psid + "o"
                        for j in range(7):
                            vt = (kr0 // 2 + j) if j < 5 else (KVE // 128 + (j - 5))
                            S.op("pe", (lambda e, po=po, pT=pT, j=j, vt=vt, h=h: e.matmul(
                                po, lhsT=pT[:, j, :], rhs=vs[:, vt, h * 128:(h + 1) * 128],
                                start=(j == 0), stop=(j == 6))), r=[pTid, "n_v"], w=[poid])
                        yield
                        o_, oid = otm.next()
                        S.op("act", (lambda e, o_=o_, po=po, st_=st_: e.activation(out=o_[:, :], in_=po, func=AF.Identity,
                                                                                  scale=st_[:, 3:4])),
                             r=[poid, stid], w=[oid])
                        yield
                        pot, potid = pt[:, 7, :], ptid + "o"
                        S.op("pe", (lambda e, pot=pot, o_=o_: e.transpose(out=pot, in_=o_[:, :], identity=ident_b[:])),
                             r=[oid, "ident_b"], w=[potid])
                        yield
                        S.op("dve", (lambda e, pot=pot, h=h, qt=qt: e.tensor_copy(out=yna[:, h, qt * 128:(qt + 1) * 128], in_=pot)),
                             r=[potid], w=["n_y"])

                    for qt in range(E // 128):
                        for hp in range(2):
                            gens = [tile_gen(hg, qt, hp * 2), tile_gen(hg, qt, hp * 2 + 1)]
                            live = list(gens)
                            while live:
                                for g_ in list(live):
                                    try:
                                        next(g_)
                                    except StopIteration:
                                        live.remove(g_)
                    S.dma("sp", (lambda e, hg=hg: e.dma_start(
                        out=ymix[8 + hg * 4:8 + hg * 4 + 4].rearrange("h p t -> p h t"), in_=yna[:])), r=["n_y"], w=["ymix"])
                S.barrier()
                S.emit()

        def s5_phase():
            NCHK = 544
            NA_, NB_ = 304, 544
            PA, PB = 256, 512
            PBH, PZ, NZ = 16, 256, 274
            NOUT = E // 8
            PI = 3.141592653589793
            with contextlib.ExitStack() as ph0:
                z_tm = sbt(ph0, "s_ztm", [128, 3, 8, 1024], BF16)
                psX = Ring(ph0, nc, "s_px", 2, [128, 2, 512], F32, psum=True)
                psT = Ring(ph0, nc, "s_pt", 1, [128, 8, 128], BF16, psum=True)
                psMT = pst(ph0, "s_pmt", [128, 4, 128], F32)

                class _One:
                    def next(self_):
                        return psMT, "psM"
                psM = _One()
                psKT = pst(ph0, "s_pkt", [128, 512], F32)
                psY = Ring(ph0, nc, "s_py", 1, [128, 256], F32, psum=True)
                with contextlib.ExitStack() as ph:
                    NS = 24
                    T = sbt(ph, "s_T", [128, 2, NS, 32], F32)
                    PW = sbt(ph, "s_PW", [128, 2, 2, 32, 9], F32)
                    QW = sbt(ph, "s_QW", [128, 2, 3, 32, 10], F32)
                    Bt = sbt(ph, "s_Bt", [128, 2, 2, 32, 16], F32)
                    bb = sbt(ph, "s_bb", [128, 2, 2, 32, 16], F32)
                    bbb = sbt(ph, "s_bbb", [128, 2, 2, 32, 16], BF16)
                    CT = sbt(ph, "s_CT", [128, 2, 2, 32, 16], F32)
                    Cn = Ring(ph, nc, "s_Cn", 2, [128, 128], F32)
                    dvec = sbt(ph, "s_dvec", [128, 64], F32)
                    Sel = sbt(ph, "s_Sel", [16, 8, 128], BF16)
                    KTa = sbt(ph, "s_KTa", [16, 2, 15, 16], BF16)
                    KTb = sbt(ph, "s_KTb", [16, 2, 15, 16], BF16)
                    P_ = "s5par"

                    def dv(fn, r=(P_,), w=(P_,), eng="dve"):
                        S.op(eng, fn, r=list(r), w=list(w))

                    def tt(o_, a, b, op):
                        dv(lambda e: e.tensor_tensor(out=o_, in0=a, in1=b, op=op))

                    def ts(o_, a, s1, op0, s2=None, op1=None):
                        if op1 is None:
                            dv(lambda e: e.tensor_scalar(out=o_, in0=a, scalar1=s1, scalar2=None, op0=op0))
                        else:
                            dv(lambda e: e.tensor_scalar(out=o_, in0=a, scalar1=s1, scalar2=s2, op0=op0, op1=op1))

                    def stt(o_, a, sc, b, op0, op1):
                        dv(lambda e: e.scalar_tensor_tensor(out=o_, in0=a, scalar=sc, in1=b, op0=op0, op1=op1))

                    def act(o_, a, func, **kw):
                        dv(lambda e: e.activation(out=o_, in_=a, func=func, **kw), eng="act")

                    def cmul(ore, oim, are, aim, bre, bim, t1):
                        tt(ore, are, bre, ALU.mult)
                        tt(t1, aim, bim, ALU.mult)
                        tt(ore, ore, t1, ALU.subtract)
                        tt(oim, are, bim, ALU.mult)
                        tt(t1, aim, bre, ALU.mult)
                        tt(oim, oim, t1, ALU.add)

                    dv(lambda e: e.memset(Sel[:], 0.0), eng="pool")
                    for s_ in range(8):
                        dv((lambda e, s_=s_: e.tensor_copy(out=Sel[0:16, s_, s_ * 16:(s_ + 1) * 16], in_=ident_b[0:16, 0:16])),
                           r=(P_, "ident_b"), eng="pool")
                    dv(lambda e: e.memset(KTa[:], 0.0), eng="pool")
                    dv(lambda e: e.memset(KTb[:], 0.0), eng="pool")
                    for s_ in range(8):
                        S.dma("sp", (lambda e, s_=s_: e.dma_start(out=dvec[s_ * 16:(s_ + 1) * 16, :],
                                                                  in_=ssm_d.rearrange("(g c) -> c g", c=16),
                                                                  allow_slow_non_contiguous=True)), w=[P_])
                    for d in range(2):
                        for ri, src in ((0, b_re), (1, b_im)):
                            S.dma("sp", (lambda e, d=d, ri=ri, src=src: e.dma_start(
                                out=Bt[:, d, ri], in_=src[d].rearrange("(pair q) c -> q pair c", q=128))), w=[P_])
                    for d in range(2):
                        for ri, src in ((0, c_re), (1, c_im)):
                            for t8 in range(8):
                                cn, cnid = Cn.next()
                                for dup in range(2):
                                    S.dma("sp", (lambda e, cn=cn, dup=dup, src=src, d=d, t8=t8: e.dma_start(
                                        out=cn[:, dup * 64:(dup + 1) * 64], in_=src[d, t8 * 128:(t8 + 1) * 128, :])), w=[cnid])
                                pm, pmid = psM.next()
                                S.op("pe", (lambda e, pm=pm, cn=cn: e.transpose(out=pm[:, 0, :], in_=cn[:, :], identity=ident_f[:])),
                                     r=[cnid, "ident_f"], w=[pmid])
                                for g2 in range(2):
                                    S.op("dve", (lambda e, pm=pm, g2=g2, d=d, ri=ri, t8=t8: e.tensor_copy(
                                        out=CT[g2 * 64:(g2 + 1) * 64, d, ri, 4 * t8:4 * t8 + 4, :],
                                        in_=pm[g2 * 64:(g2 + 1) * 64, 0, :].rearrange("p (pp x c) -> p pp x c", pp=4, x=2)[:, :, g2, :])),
                                         r=[pmid, P_], w=[P_])
                    for d in range(2):
                        sl = lambda i, d=d: T[:, d, i, :]
                        DT, LR, LI, MAG, TH, CNT, SN, CS, ARE, AIM, FRE, FIM, DEN, T1, T2, XR = [sl(i) for i in range(16)]
                        lre = VC("lam_re%d" % d, 0, 32)
                        lim = VC("lam_im%d" % d, 0, 32)
                        act(DT, VC("log_dt%d" % d, 0, 32), AF.Exp)
                        S.op("dve", (lambda e, LR=LR, lre=lre, DT=DT: e.tensor_tensor(out=LR, in0=lre, in1=DT, op=ALU.mult)), r=[P_, "V"], w=[P_])
                        S.op("dve", (lambda e, LI=LI, lim=lim, DT=DT: e.tensor_tensor(out=LI, in0=lim, in1=DT, op=ALU.mult)), r=[P_, "V"], w=[P_])
                        act(MAG, LR, AF.Exp)
                        for (dst, shift) in ((SN, 0.0), (CS, PI / 2)):
                            ts(TH, LI, TWO_PI + shift, ALU.add)
                            ts(CNT, TH, TWO_PI, ALU.is_ge)
                            for jj in range(2, 6):
                                stt(CNT, TH, TWO_PI * jj, CNT, ALU.is_ge, ALU.add)
                            stt(TH, CNT, -TWO_PI, TH, ALU.mult, ALU.add)
                            ts(TH, TH, PI, ALU.subtract)
                            act(dst, TH, AF.Sin)
                        stt(ARE, CS, -1.0, MAG, ALU.mult, ALU.mult)
                        stt(AIM, SN, -1.0, MAG, ALU.mult, ALU.mult)
                        S.op("dve", (lambda e, DEN=DEN, lre=lre: e.tensor_tensor(out=DEN, in0=lre, in1=lre, op=ALU.mult)), r=[P_, "V"], w=[P_])
                        S.op("dve", (lambda e, T1=T1, lim=lim: e.tensor_tensor(out=T1, in0=lim, in1=lim, op=ALU.mult)), r=[P_, "V"], w=[P_])
                        tt(DEN, DEN, T1, ALU.add)
                        dv(lambda e, DEN=DEN: e.reciprocal(out=DEN, in_=DEN))
                        ts(XR, ARE, 1.0, ALU.subtract)
                        S.op("dve", (lambda e, FRE=FRE, XR=XR, lre=lre: e.tensor_tensor(out=FRE, in0=XR, in1=lre, op=ALU.mult)), r=[P_, "V"], w=[P_])
                        S.op("dve", (lambda e, T1=T1, AIM=AIM, lim=lim: e.tensor_tensor(out=T1, in0=AIM, in1=lim, op=ALU.mult)), r=[P_, "V"], w=[P_])
                        tt(FRE, FRE, T1, ALU.add)
                        tt(FRE, FRE, DEN, ALU.mult)
                        S.op("dve", (lambda e, FIM=FIM, AIM=AIM, lre=lre: e.tensor_tensor(out=FIM, in0=AIM, in1=lre, op=ALU.mult)), r=[P_, "V"], w=[P_])
                        S.op("dve", (lambda e, T1=T1, XR=XR, lim=lim: e.tensor_tensor(out=T1, in0=XR, in1=lim, op=ALU.mult)), r=[P_, "V"], w=[P_])
                        tt(FIM, FIM, T1, ALU.subtract)
                        tt(FIM, FIM, DEN, ALU.mult)
                        pw = lambda ri, k, d=d: PW[:, d, ri, :, k]
                        dv(lambda e, d=d: e.memset(PW[:, d, 0, :, 0], 1.0))
                        dv(lambda e, d=d: e.memset(PW[:, d, 1, :, 0], 0.0))
                        dv(lambda e, d=d, ARE=ARE: e.tensor_copy(out=PW[:, d, 0, :, 1], in_=ARE))
                        dv(lambda e, d=d, AIM=AIM: e.tensor_copy(out=PW[:, d, 1, :, 1], in_=AIM))
                        for k in range(2, 9):
                            cmul(pw(0, k), pw(1, k), pw(0, k - 1), pw(1, k - 1), ARE, AIM, T1)
                        qw = lambda ri, j, d=d: QW[:, d, ri, :, j]
                        dv(lambda e, d=d: e.tensor_copy(out=QW[:, d, 0, :, 0], in_=PW[:, d, 0, :, 8]))
                        dv(lambda e, d=d: e.tensor_copy(out=QW[:, d, 1, :, 0], in_=PW[:, d, 1, :, 8]))
                        for j in range(1, 10):
                            cmul(qw(0, j), qw(1, j), qw(0, j - 1), qw(1, j - 1), qw(0, j - 1), qw(1, j - 1), T1)
                        dv(lambda e, d=d: e.tensor_scalar(out=QW[:, d, 2], in0=QW[:, d, 1], scalar1=-1.0, scalar2=None, op0=ALU.mult))
                        fre_b = FRE.unsqueeze(2).broadcast_to([128, 32, 16])
                        fim_b = FIM.unsqueeze(2).broadcast_to([128, 32, 16])
                        tt(bb[:, d, 0], Bt[:, d, 0], fre_b, ALU.mult)
                        tt(bb[:, d, 1], Bt[:, d, 1], fre_b, ALU.mult)
                        tt(Bt[:, d, 1], Bt[:, d, 1], fim_b, ALU.mult)
                        tt(Bt[:, d, 0], Bt[:, d, 0], fim_b, ALU.mult)
                        tt(bb[:, d, 0], bb[:, d, 0], Bt[:, d, 1], ALU.subtract)
                        tt(bb[:, d, 1], bb[:, d, 1], Bt[:, d, 0], ALU.add)
                        dv(lambda e, d=d: e.tensor_copy(out=bbb[:, d], in_=bb[:, d]))

                    Gr = Ring(ph, nc, "s_G", 2, [128, 2, 2, 9, 16], F32)
                    Gt = sbt(ph, "s_Gt", [128, 9, 16], F32)
                    W1 = Ring(ph, nc, "s_W1", 2, [128, 2, 8, 16], F32)
                    W1t = sbt(ph, "s_W1t", [128, 8, 16], F32)
                    mxp = [Ring(ph, nc, "s_mxp%d" % d, 2, [128, 4, 128], BF16) for d in range(2)]
                    myp = [Ring(ph, nc, "s_myp%d" % d, 2, [128, 2, 256], BF16) for d in range(2)]
                    kblk = [Ring(ph, nc, "s_kb%d" % d, 2, [128, 2, 256], BF16) for d in range(2)]
                    for d in range(2):
                        for rg in (mxp[d], myp[d], kblk[d]):
                            for (tl, tid) in zip(rg.tiles, rg.ids):
                                S.op("pool", (lambda e, tl=tl: e.memset(tl[:], 0.0)), w=[tid])
                    toep = Ring(ph, nc, "s_toep", 2, [128, 2, 128], BF16)
                    ddg = Ring(ph, nc, "s_ddg", 2, [128, 128], F32)
                    UTp = Ring(ph, nc, "s_UTp", 2, [128, 5, 8, 32], BF16)
                    UT2 = Ring(ph, nc, "s_UT2", 2, [128, 5, 2, 128], BF16)
                    UgT = Ring(ph, nc, "s_UgT", 2, [128, 2, NCHK], BF16)
                    XA = [[sbt(ph, "s_XA%d%d" % (i, ri), [128, PA + NA_], F32) for ri in range(2)] for i in range(2)]
                    XB = [[sbt(ph, "s_XB%d%d" % (i, ri), [128, PBH + 272], F32) for ri in range(2)] for i in range(2)]
                    ZB = [[sbt(ph, "s_ZB%d%d" % (i, ri), [128, PZ + NZ], F32) for ri in range(2)] for i in range(2)]
                    for arrs in (XA, XB, ZB):
                        for i in range(2):
                            for ri in range(2):
                                S.op("pool", (lambda e, a=arrs[i][ri]: e.memset(a[:], 0.0)), w=["scanA", "scanB"])
                    Sbf = Ring(ph, nc, "s_Sbf", 2, [128, 4, NOUT], BF16)
                    ytmp = Ring(ph, nc, "s_yt", 2, [128, 3, 256], F32)

                    if SBUF_REPORT:
                        print("S5 sbuf remaining:", nc.sbuf_bytes_remaining)
                    for pair in range(32):
                        ut, utid = UTp.next()
                        for tl in range(5):
                            nr = 128 if tl < 4 else 32
                            S.dma("sp", (lambda e, ut=ut, tl=tl, nr=nr, pair=pair: e.dma_start(
                                out=ut[:nr, tl], in_=u_tm[tl * 1024:tl * 1024 + nr * 8, pair * 32:(pair + 1) * 32].rearrange(
                                    "(q s) c -> q s c", s=8))), r=["uv_scr"], w=[utid])
                        u2, u2id = UT2.next()
                        for tl in range(5):
                            nr = 128 if tl < 4 else 32
                            for g2 in range(2):
                                S.op("pool", (lambda e, u2=u2, ut=ut, tl=tl, nr=nr, g2=g2: e.tensor_copy(
                                    out=u2[:nr, tl, g2, :].rearrange("p (s c) -> p s c", s=8),
                                    in_=ut[:nr, tl, :, g2 * 16:(g2 + 1) * 16])), r=[utid], w=[u2id])
                        ug, ugid = UgT.next()
                        for g2 in range(2):
                            for tl in range(5):
                                nr = 128 if tl < 4 else 32
                                pt, ptid = psT.next()
                                S.op("pe", (lambda e, pt=pt, u2=u2, tl=tl, nr=nr, g2=g2: e.transpose(
                                    out=pt[:, 0, :nr], in_=u2[:nr, tl, g2, :], identity=ident_b[:nr, :nr])),
                                     r=[u2id, "ident_b"], w=[ptid])
                                ce = ("act", "dve")[tl % 2]
                                S.op(ce, copy_fn(ce, ug[:, g2, tl * 128:tl * 128 + nr], pt[:, 0, :nr]), r=[ptid], w=[ugid])
                        G_, Gid = Gr.next()
                        mx_t, my_t, kb_t = [], [], []
                        for d in range(2):
                            pre = PW[:, d, 0, pair, :].unsqueeze(2).broadcast_to([128, 9, 16])
                            pim = PW[:, d, 1, pair, :].unsqueeze(2).broadcast_to([128, 9, 16])
                            cre = CT[:, d, 0, pair, :].unsqueeze(1).broadcast_to([128, 9, 16])
                            cim = CT[:, d, 1, pair, :].unsqueeze(1).broadcast_to([128, 9, 16])
                            gre, gim = G_[:, d, 0], G_[:, d, 1]
                            for (o_, a1, b1, a2, b2, op) in ((gre, cre, pre, cim, pim, ALU.subtract), (gim, cre, pim, cim, pre, ALU.add)):
                                S.op("dve", (lambda e, o_=o_, a1=a1, b1=b1: e.tensor_tensor(out=o_, in0=a1, in1=b1, op=ALU.mult)), r=[P_], w=[Gid], nosw=True)
                                S.op("dve", (lambda e, a2=a2, b2=b2: e.tensor_tensor(out=Gt[:], in0=a2, in1=b2, op=ALU.mult)), r=[P_], w=["s_Gt"], nosw=True)
                                S.op("dve", (lambda e, o_=o_, op=op: e.tensor_tensor(out=o_, in0=o_, in1=Gt[:], op=op)), r=["s_Gt", Gid], w=[Gid], nosw=True)
                            my, myid = myp[d].next()
                            kb, kbid = kblk[d].next()
                            for g2 in range(2):
                                rows = slice(g2 * 64, (g2 + 1) * 64)
                                if d == 0:
                                    ksel = lambda ri, rows=rows, d=d, G_=G_: G_[rows, d, ri, 1:9, :]
                                else:
                                    ksel = lambda ri, rows=rows, d=d, G_=G_: G_[rows, d, ri, 1:9, :][:, ::-1, :]
                                S.op("dve", (lambda e, my=my, rows=rows, g2=g2, ksel=ksel: e.tensor_copy(
                                    out=my[rows, 0, g2 * 128:(g2 + 1) * 128].rearrange("p (t c) -> p t c", t=8), in_=ksel(0))), r=[Gid], w=[myid], nosw=True)
                                S.op("dve", (lambda e, my=my, rows=rows, g2=g2, ksel=ksel: e.tensor_scalar(
                                    out=my[rows, 1, g2 * 128:(g2 + 1) * 128].rearrange("p (t c) -> p t c", t=8), in0=ksel(1),
                                    scalar1=-1.0, scalar2=None, op0=ALU.mult)), r=[Gid], w=[myid], nosw=True)
                                S.op("pool", (lambda e, kb=kb, rows=rows, g2=g2, d=d, G_=G_: e.tensor_copy(
                                    out=kb[rows, 0, g2 * 128:(g2 + 1) * 128].rearrange("p (t c) -> p t c", t=8), in_=G_[rows, d, 0, 0:8, :])),
                                     r=[Gid], w=[kbid])
                                S.op("pool", (lambda e, kb=kb, rows=rows, g2=g2, d=d, G_=G_: e.tensor_scalar(
                                    out=kb[rows, 1, g2 * 128:(g2 + 1) * 128].rearrange("p (t c) -> p t c", t=8), in0=G_[rows, d, 1, 0:8, :],
                                    scalar1=-1.0, scalar2=None, op0=ALU.mult)), r=[Gid], w=[kbid])
                            w1, w1id = W1.next()
                            if d == 0:
                                psel = lambda ri, d=d: PW[:, d, ri, pair, 0:8][:, ::-1].unsqueeze(2).broadcast_to([128, 8, 16])
                            else:
                                psel = lambda ri, d=d: PW[:, d, ri, pair, 0:8].unsqueeze(2).broadcast_to([128, 8, 16])
                            bre = bb[:, d, 0, pair, :].unsqueeze(1).broadcast_to([128, 8, 16])
                            bim = bb[:, d, 1, pair, :].unsqueeze(1).broadcast_to([128, 8, 16])
                            for (o_, a1, b1, a2, b2, op) in ((w1[:, 0], psel(0), bre, psel(1), bim, ALU.subtract),
                                                            (w1[:, 1], psel(0), bim, psel(1), bre, ALU.add)):
                                S.op("dve", (lambda e, o_=o_, a1=a1, b1=b1: e.tensor_tensor(out=o_, in0=a1, in1=b1, op=ALU.mult)), r=[P_], w=[w1id], nosw=True)
                                S.op("dve", (lambda e, a2=a2, b2=b2: e.tensor_tensor(out=W1t[:], in0=a2, in1=b2, op=ALU.mult)), r=[P_], w=["s_W1t"], nosw=True)
                                S.op("dve", (lambda e, o_=o_, op=op: e.tensor_tensor(out=o_, in0=o_, in1=W1t[:], op=op)), r=["s_W1t", w1id], w=[w1id], nosw=True)
                            pm, pmid = psM.next()
                            for ri in range(2):
                                S.op("pe", (lambda e, pm=pm, w1=w1, ri=ri: e.transpose(
                                    out=pm[:, ri, :], in_=w1[:, ri].rearrange("p s c -> p (s c)"), identity=ident_f[:])),
                                     r=[w1id, "ident_f"], w=[pmid])
                            mx, mxid = mxp[d].next()
                            for ri in range(2):
                                for g2 in range(2):
                                    ce = ("act", "dve")[g2]
                                    S.op(ce, copy_fn(ce, mx[:, ri * 2 + g2, g2 * 64:(g2 + 1) * 64], pm[:, ri, g2 * 64:(g2 + 1) * 64]),
                                         r=[pmid], w=[mxid])
                            mx_t.append((mx, mxid)); my_t.append((my, myid)); kb_t.append((kb, kbid))

                        def emit_K_mm(d, pair=pair):
                            kb, kbid = kb_t[d]
                            S.op("pe", (lambda e, d=d, kb=kb, pair=pair: e.matmul(psKT[0:16, d * 256:d * 256 + 256], lhsT=bbb[:, d, 0, pair, :], rhs=kb[:, 0, :],
                                                                                  start=True, stop=False)), r=[P_, kbid], w=["psK%d" % d])
                            S.op("pe", (lambda e, d=d, kb=kb, pair=pair: e.matmul(psKT[0:16, d * 256:d * 256 + 256], lhsT=bbb[:, d, 1, pair, :], rhs=kb[:, 1, :],
                                                                                  start=False, stop=True)), r=[P_, kbid], w=["psK%d" % d])

                        def emit_KT_copy(d):
                            kview = psKT[0:16, d * 256:d * 256 + 256].rearrange("p (g t c) -> p g t c", g=2, t=8)
                            if d == 0:
                                S.op("dve", (lambda e, kview=kview: e.tensor_copy(out=KTa[:, :, 7:15, :], in_=kview)), r=["psK0"], w=["KT"])
                            else:
                                S.op("dve", (lambda e, kview=kview: e.tensor_copy(out=KTb[:, :, 0:8, :], in_=kview[:, :, ::-1, :])),
                                     r=["psK1"], w=["KT"])

                        tp, tpid = toep.next()

                        def emit_toep_mm():
                            for g2 in range(2):
                                n_mm = 0
                                for KT in (KTa, KTb):
                                    for s_ in range(8):
                                        S.op("pe", (lambda e, g2=g2, KT=KT, s_=s_, first=(n_mm == 0), last=(n_mm == 15): e.matmul(
                                            psMT[:, 2 + g2, :], lhsT=Sel[0:16, s_, :],
                                            rhs=KT[0:16, g2, 7 - s_:15 - s_, :].rearrange("p j c -> p (j c)"),
                                            start=first, stop=last)), r=["KT", P_], w=["psTo"])
                                        n_mm += 1

                        def emit_toep_evac(pair=pair, tp=tp, tpid=tpid):
                            for g2 in range(2):
                                g = 2 * pair + g2
                                dd, ddid = ddg.next()
                                S.op("pool", (lambda e, dd=dd, g=g: e.tensor_scalar(out=dd[:], in0=ident_f[:], scalar1=dvec[:, g:g + 1],
                                                                                   scalar2=None, op0=ALU.mult)), r=[P_, "ident_f"], w=[ddid])
                                S.op("dve", (lambda e, tp=tp, g2=g2, dd=dd: e.tensor_tensor(
                                    out=tp[:, g2, :], in0=psMT[:, 2 + g2, :], in1=dd[:], op=ALU.add)),
                                     r=["psTo", ddid], w=[tpid])

                        sb_, sbid = Sbf.next()

                        def emit_X(d, ug=ug, ugid=ugid):
                            mx, mxid = mx_t[d]
                            arr = XA if d == 0 else XB
                            sid = "scanA" if d == 0 else "scanB"
                            for ri in range(2):
                                px, pxid = psX.next()
                                if d == 0:
                                    segs = [(px[:, 0, 0:32], 512, 544), (px[:, 0, 32:304], 0, 272)]
                                else:
                                    segs = [(px[:, 0, 0:512], 0, 512), (px[:, 1, 0:32], 512, 544)]
                                for (o_, c0, c1) in segs:
                                    for g2 in range(2):
                                        S.op("pe", (lambda e, o_=o_, mx=mx, ri=ri, g2=g2, c0=c0, c1=c1, ug=ug: e.matmul(
                                            o_, lhsT=mx[:, ri * 2 + g2, :], rhs=ug[:, g2, c0:c1], start=(g2 == 0), stop=(g2 == 1))),
                                             r=[mxid, ugid], w=[pxid])
                                dst = arr[0][ri]
                                if d == 0:
                                    S.op("act", (lambda e, dst=dst, px=px: e.activation(out=dst[:, PA:PA + NA_], in_=px[:, 0, 0:NA_], func=AF.Copy)),
                                         r=[pxid], w=[sid])
                                else:
                                    zdst = ZB[0][ri]
                                    S.op("act", (lambda e, zdst=zdst, px=px: e.activation(out=zdst[:, PZ + 1:PZ + 274][:, ::-1], in_=px[:, 0, 0:273],
                                                                                        func=AF.Copy)), r=[pxid], w=[sid])
                                    S.op("act", (lambda e, dst=dst, px=px: e.activation(out=dst[:, PBH + 32:PBH + 271][:, ::-1], in_=px[:, 0, 273:512],
                                                                                       func=AF.Copy)), r=[pxid], w=[sid])
                                    S.op("act", (lambda e, dst=dst, px=px: e.activation(out=dst[:, PBH:PBH + 32][:, ::-1], in_=px[:, 1, 0:32],
                                                                                       func=AF.Copy)), r=[pxid], w=[sid])

                        def emit_HS(d, pair=pair, sb_=sb_, sbid=sbid):
                            arr = XA if d == 0 else ZB
                            sid = "scanA" if d == 0 else "scanB"
                            PAD = PA if d == 0 else PZ
                            n = NA_ if d == 0 else NZ
                            if d == 1:
                                m = 272
                                lo_ = PBH - 1
                                srcb, dstb = XB[0], XB[1]
                                lvl = 0
                                while m > 1:
                                    half = m // 2
                                    qre = QW[:, d, 0, pair, lvl:lvl + 1]
                                    qim = QW[:, d, 1, pair, lvl:lvl + 1]
                                    nqim = QW[:, d, 2, pair, lvl:lvl + 1]
                                    ev = lambda t, lo_=lo_, m=m: t[:, lo_:lo_ + m:2]
                                    od = lambda t, lo_=lo_, m=m: t[:, lo_ + 1:lo_ + m:2]
                                    ou = lambda t, half=half: t[:, PBH:PBH + half]
                                    for (o_, a1, s1, b1) in ((ou(dstb[0]), ev(srcb[0]), qre, od(srcb[0])), (ou(dstb[0]), ev(srcb[1]), nqim, ou(dstb[0])),
                                                            (ou(dstb[1]), ev(srcb[1]), qre, od(srcb[1])), (ou(dstb[1]), ev(srcb[0]), qim, ou(dstb[1]))):
                                        S.op("dve", (lambda e, o_=o_, a1=a1, s1=s1, b1=b1: e.scalar_tensor_tensor(
                                            out=o_, in0=a1, scalar=s1, in1=b1, op0=ALU.mult, op1=ALU.add)), r=[sid, P_], w=[sid], nosw=True)
                                    m = half
                                    lo_ = PBH if m % 2 == 0 else PBH - 1
                                    if m % 2 == 1 and m > 1:
                                        m += 1
                                    srcb, dstb = dstb, srcb
                                    lvl += 1
                                for ri in range(2):
                                    S.op("dve", (lambda e, ri=ri, srcb=srcb: e.tensor_copy(out=ZB[0][ri][:, PZ:PZ + 1], in_=srcb[ri][:, PBH:PBH + 1])),
                                         r=[sid], w=[sid])
                            cur = 0
                            j = 0
                            sh = 1
                            while sh < n:
                                src_, dst_ = arr[cur], arr[1 - cur]
                                qre = QW[:, d, 0, pair, j:j + 1]
                                qim = QW[:, d, 1, pair, j:j + 1]
                                nqim = QW[:, d, 2, pair, j:j + 1]
                                lo, hi = PAD, PAD + n
                                for (o_, a1, s1, b1) in ((dst_[0], src_[0], qre, src_[0]), (dst_[0], src_[1], nqim, dst_[0]),
                                                        (dst_[1], src_[1], qre, src_[1]), (dst_[1], src_[0], qim, dst_[1])):
                                    S.op("dve", (lambda e, o_=o_, a1=a1, s1=s1, b1=b1, lo=lo, hi=hi, sh=sh: e.scalar_tensor_tensor(
                                        out=o_[:, lo:hi], in0=a1[:, lo - sh:hi - sh], scalar=s1, in1=b1[:, lo:hi],
                                        op0=ALU.mult, op1=ALU.add)), r=[sid, P_], w=[sid], nosw=True)
                                cur = 1 - cur
                                sh *= 2
                                j += 1
                            fin = arr[cur]
                            for ri in range(2):
                                if d == 0:
                                    S.op("act", (lambda e, sb_=sb_, fin=fin, ri=ri: e.activation(
                                        out=sb_[:, ri, :], in_=fin[ri][:, PA + 31:PA + 31 + NOUT], func=AF.Copy)), r=[sid], w=[sbid])
                                else:
                                    S.op("act", (lambda e, sb_=sb_, fin=fin, ri=ri: e.activation(
                                        out=sb_[:, 2 + ri, :][:, ::-1], in_=fin[ri][:, PZ + 1:PZ + 1 + NOUT], func=AF.Copy)),
                                         r=[sid], w=[sbid])

                        emit_X(0)
                        emit_K_mm(0)
                        emit_K_mm(1)
                        emit_X(1)
                        emit_HS(0)
                        emit_KT_copy(0)
                        emit_KT_copy(1)
                        emit_toep_mm()
                        emit_HS(1)
                        emit_toep_evac()
                        for ct in range(3):
                            nch = 128 if ct < 2 else NOUT - 256
                            py, pyid = psY.next()
                            k = 0
                            for d in range(2):
                                my, myid = my_t[d]
                                for ri in range(2):
                                    S.op("pe", (lambda e, py=py, sb_=sb_, d=d, ri=ri, ct=ct, nch=nch, my=my, first=(k == 0): e.matmul(
                                        py[:nch, :], lhsT=sb_[:, d * 2 + ri, ct * 128:ct * 128 + nch], rhs=my[:, ri, :],
                                        start=first, stop=False)), r=[sbid, myid], w=[pyid])
                                    k += 1
                            for g2 in range(2):
                                S.op("pe", (lambda e, py=py, ug=ug, g2=g2, ct=ct, nch=nch, tp=tp: e.matmul(
                                    py[:nch, g2 * 128:(g2 + 1) * 128], lhsT=ug[:, g2, ct * 128:ct * 128 + nch], rhs=tp[:, g2, :],
                                    start=False, stop=(g2 == 1))), r=[ugid, tpid], w=[pyid])
                            yt, ytid = ytmp.next()
                            S.op("act", (lambda e, yt=yt, py=py, nch=nch: e.activation(out=yt[:nch, 0, :], in_=py[:nch, :], func=AF.Copy)),
                                 r=[pyid], w=[ytid])
                            S.op("pool", (lambda e, yt=yt, nch=nch: e.tensor_tensor(out=yt[:nch, 1, :], in0=yt[:nch, 0, :], in1=yt[:nch, 0, :],
                                                                                   op=ALU.mult)), r=[ytid], w=[ytid])
                            S.op("pool", (lambda e, yt=yt, nch=nch: e.tensor_scalar(out=yt[:nch, 1, :], in0=yt[:nch, 1, :], scalar1=0.044715,
                                                                                   scalar2=1.0, op0=ALU.mult, op1=ALU.add)), r=[ytid], w=[ytid])
                            S.op("pool", (lambda e, yt=yt, nch=nch: e.tensor_tensor(out=yt[:nch, 1, :], in0=yt[:nch, 1, :], in1=yt[:nch, 0, :],
                                                                                   op=ALU.mult)), r=[ytid], w=[ytid])
                            S.op("act", (lambda e, yt=yt, nch=nch: e.activation(out=yt[:nch, 2, :], in_=yt[:nch, 1, :], func=AF.Sigmoid,
                                                                               scale=1.5957691216057308)), r=[ytid], w=[ytid])
                            for g2 in range(2):
                                zo = z_tm[:nch, ct, :, pair * 32 + g2 * 16:pair * 32 + (g2 + 1) * 16]
                                i0 = yt[:nch, 0, g2 * 128:(g2 + 1) * 128].rearrange("p (t c) -> p t c", t=8)
                                i1 = yt[:nch, 2, g2 * 128:(g2 + 1) * 128].rearrange("p (t c) -> p t c", t=8)
                                if S5DBG:
                                    S.op("pool", (lambda e, zo=zo, i0=i0: e.tensor_copy(out=zo, in_=i0)), r=[ytid], w=["s_ztm"])
                                else:
                                    S.op("pool", (lambda e, zo=zo, i0=i0, i1=i1: e.tensor_tensor(out=zo, in0=i0, in1=i1, op=ALU.mult)),
                                         r=[ytid], w=["s_ztm"])
                        if S5BAR:
                            S.barrier()
                    S.barrier()
                    S.emit()
                with contextlib.ExitStack() as ph:
                    z_fm = sbt(ph, "s_zfm", [128, 8, E], BF16)
                    wgl = Ring(ph, nc, "s_wgl", 2, [128, 8, 128], BF16)
                    sgt = Ring(ph, nc, "s_sgt", 2, [128, TB], F32)
                    yo = Ring(ph, nc, "s_yo", 2, [128, TB], BF16)
                    for ct in range(3):
                        nch = 128 if ct < 2 else NOUT - 256
                        for c8 in range(8):
                            pt, ptid = psT.next()
                            for t in range(8):
                                S.op("pe", (lambda e, pt=pt, ct=ct, t=t, c8=c8, nch=nch: e.transpose(
                                    out=pt[:, t, :nch], in_=z_tm[:nch, ct, t, c8 * 128:(c8 + 1) * 128], identity=ident_b[:nch, :nch])),
                                     r=["s_ztm", "ident_b"], w=[ptid])
                            ce = ("act", "dve")[c8 % 2]
                            S.op(ce, copy_fn(ce, z_fm[:, c8, ct * 1024:ct * 1024 + nch * 8].rearrange("p (q t) -> p t q", t=8),
                                             pt[:, :, :nch]), r=[ptid], w=["s_zfm"])
                    HB = TB // 2
                    for blk in range(NBLK):
                        t0 = blk * TB
                        for m in range(8):
                            w_, wid = wgl.next()
                            S.dma("sp", (lambda e, w_=w_, m=m: e.dma_start(
                                out=w_[:], in_=WGLU[m // 4].rearrange("p (a b) -> p a b", a=8)[:, :, (m % 4) * 128:(m % 4) * 128 + 128])),
                                  r=["wscr"], w=[wid])
                            px, pxid = psX.next()
                            for hf in range(2):
                                for kc in range(8):
                                    S.op("pe", (lambda e, px=px, w_=w_, kc=kc, hf=hf, t0=t0: e.matmul(
                                        px[:, hf, :HB], lhsT=w_[:, kc, :], rhs=z_fm[:, kc, t0 + hf * HB:t0 + (hf + 1) * HB],
                                        start=(kc == 0), stop=(kc == 7))), r=["s_zfm", wid], w=[pxid])
                            sg_, sgid = sgt.next()
                            S.op("act", (lambda e, sg_=sg_, px=px: e.activation(out=sg_[:, :].rearrange("p (a b) -> p a b", a=2),
                                                                              in_=px[:, :, :HB], func=AF.Sigmoid)), r=[pxid], w=[sgid])
                            y_, yid = yo.next()
                            if S5DBG:
                                S.op("dve", (lambda e, y_=y_, m=m, t0=t0: e.tensor_copy(out=y_[:], in_=z_fm[:, m, t0:t0 + TB])),
                                     r=[sgid, "s_zfm"], w=[yid])
                            else:
                                S.op("dve", (lambda e, y_=y_, sg_=sg_, m=m, t0=t0: e.tensor_tensor(out=y_[:], in0=sg_[:], in1=z_fm[:, m, t0:t0 + TB],
                                                                                              op=ALU.mult)), r=[sgid, "s_zfm"], w=[yid])
                            S.dma("pool", (lambda e, y_=y_, m=m, t0=t0: e.dma_start(out=ymix[m, :, t0:t0 + TB], in_=y_[:])),
                                  r=[yid], w=["ymix"])
                    S.barrier()
                    S.emit()

        def final_phase(xsrc):
            with contextlib.ExitStack() as ph:
                xs = Ring(ph, nc, "o_xs", 3, [128, 512], F32)
                sq = Ring(ph, nc, "o_sq", 2, [128, 512], F32)
                pss = Ring(ph, nc, "o_pss", 1, [128, 512], F32, psum=True)
                rbc = Ring(ph, nc, "o_rbc", 1, [128, 512], F32)
                pt_ = Ring(ph, nc, "o_pt", 2, [128, 4, 128], F32, psum=True)
                ot = Ring(ph, nc, "o_ot", 2, [128, D], F32)
                hf32 = sbt(ph, "o_h", [128, 16, 512], F32)
                for blk in range(4):
                    c0 = blk * 512
                    n = 512
                    pt, pid = pss.next()
                    for kc in range(16):
                        t, tid = load_x_cols(xs, xsrc, kc, c0, n, 0, E)
                        q, qid = sq.next()
                        S.op("act", (lambda e, q=q, t=t: e.activation(out=q[:], in_=t[:], func=AF.Square)), r=[tid], w=[qid])
                        S.op("pe", (lambda e, pt=pt, q=q, kc=kc: e.matmul(pt[:, :], lhsT=ones_f[:], rhs=q[:, :],
                                                                         start=(kc == 0), stop=(kc == 15))),
                             r=[qid, "ones_f"], w=[pid])
                    r_, rid = rbc.next()
                    S.op("act", (lambda e, r_=r_, pt=pt: e.activation(out=r_[:], in_=pt[:], func=AF.Sqrt, scale=1.0 / D,
                                                                      bias=EPS)), r=[pid], w=[rid])
                    S.op("dve", (lambda e, r_=r_: e.reciprocal(out=r_[:], in_=r_[:])), r=[rid], w=[rid])
                    for kc in range(16):
                        t, tid = load_x_cols(xs, xsrc, kc, c0, n, 0, E)
                        S.op("dve", (lambda e, t=t, r_=r_: e.tensor_tensor(out=t[:], in0=t[:], in1=r_[:], op=ALU.mult)),
                             r=[tid, rid], w=[tid])
                        gv = V[:, voff["g_out"] + kc: voff["g_out"] + kc + 1]
                        S.op("act", (lambda e, t=t, kc=kc, gv=gv: e.activation(out=hf32[:, kc, :], in_=t[:],
                                                                              func=AF.Identity, scale=gv)),
                             r=[tid, "V"], w=["o_h"])
                    for ti in range(4):
                        o_, oid = ot.next()
                        for q4 in range(4):
                            p_, pid2 = pt_.next()
                            for i in range(4):
                                kc = q4 * 4 + i
                                S.op("pe", (lambda e, p_=p_, i=i, kc=kc, ti=ti: e.transpose(
                                    out=p_[:, i, :], in_=hf32[:, kc, ti * 128:(ti + 1) * 128], identity=ident_f[:])),
                                     r=["o_h", "ident_f"], w=[pid2])
                            ce = ("act", "dve")[q4 % 2]
                            S.op(ce, copy_fn(ce, o_[:, q4 * 512:(q4 + 1) * 512],
                                             p_[:].rearrange("p a b -> p (a b)")), r=[pid2], w=[oid])
                        row = c0 + ti * 128
                        S.dma("sp", (lambda e, o_=o_, row=row: e.dma_start(out=out[row:row + 128, :], in_=o_[:])),
                              r=[oid], w=["out"])
                S.barrier()
                S.emit()

        def dump(src, i):
            for c in range(16):
                S.dma("sp", (lambda e, c=c: e.dma_start(out=dbg_out[i][c], in_=src[c])), r=["xsrc"], w=["dbg"])
            S.barrier()

        mix = stage >= 3
        l0_proj(do_mixer=mix)
        if mix:
            s5_phase()
            na_phase()
            if dbg:
                for c in range(16):
                    S.dma("sp", (lambda e, c=c: e.dma_start(out=dbg_y[c], in_=ymix[c])), r=["ymix"], w=["dbgy"])
                S.barrier()
                S.emit()
                return nc
            mixout_phase(xA, xB, MODV(0, 2, 0))
            ffn_phase(0, xB, xA, DER(2), MODV(0, 3, 0), MODV(0, 5, 0))
            conformer_phase(xA, xB, DER(3), MODV(1, 0, 0), MODV(1, 2, 0))
            ffn_phase(1, xB, xA, DER(4), MODV(1, 3, 0), MODV(1, 5, 0))
            final_phase(xA)
        else:
            ffn_phase(0, xA, xB, DER(2), MODV(0, 3, 0), MODV(0, 5, 0))
            conformer_phase(xB, xA, DER(3), MODV(1, 0, 0), MODV(1, 2, 0))
            ffn_phase(1, xA, xB, DER(4), MODV(1, 3, 0), MODV(1, 5, 0))
            final_phase(xB)
    return nc


def _host_prep(inputs):
    f = lambda a: np.ascontiguousarray(np.asarray(a, dtype=np.float32))
    x = inputs["x"]; ctx = inputs["ctx"]
    common = {
        "w_mod": f(inputs["w_mod"]), "b_mod": f(inputs["b_mod"]), "g_mix": f(inputs["g_mix"]),
        "g_ffn": f(inputs["g_ffn"]), "g_out": f(inputs["g_out"]), "w_in": f(inputs["w_in"][0]),
        "ssm_d": f(inputs["ssm_d"][0]), "w_glu": f(inputs["ssm_w_glu"][0]), "w_out": f(inputs["w_out"][0]),
        "pw1": f(inputs["cv_w_pw1"][0]), "dw_b": f(inputs["cv_dw_b"][0]), "ln_g": f(inputs["cv_ln_g"][0]),
        "ln_b": f(inputs["cv_ln_b"][0]), "pw2": f(inputs["cv_w_pw2"][0]), "w_up": f(inputs["ffn_w_up"]),
        "fcb": f(inputs["ffn_conv_b"]), "w_dn": f(inputs["ffn_w_down"]),
    }
    rpb = np.asarray(inputs["na_rpb"][0], np.float32)
    maps = []
    for core in range(8):
        b, half = core // 2, core % 2
        m = dict(common)
        if half == 0:
            m["xc"] = f(x[b]); m["ctxc"] = f(ctx[b]); dirs = (0, 1)
            m["dw_w"] = f(inputs["cv_dw_w"][0]); m["fcw"] = f(inputs["ffn_conv_w"])
        else:
            m["xc"] = f(x[b][::-1]); m["ctxc"] = f(ctx[b][::-1]); dirs = (1, 0)
            m["dw_w"] = f(inputs["cv_dw_w"][0][::-1]); m["fcw"] = f(inputs["ffn_conv_w"][:, ::-1])
        m["cvec"] = f(np.stack([inputs["c"][b], inputs["c_ctx"]]))
        dd = list(dirs)
        m["lam_re"] = f(inputs["ssm_lam_re"][0][dd].reshape(2, 4096))
        m["lam_im"] = f(inputs["ssm_lam_im"][0][dd].reshape(2, 4096))
        m["log_dt"] = f(np.repeat(inputs["ssm_log_dt"][0][dd][:, :, None], 64, axis=2).reshape(2, 4096))
        m["b_re"] = f(inputs["ssm_b_re"][0][dd].reshape(2, 4096, 16))
        m["b_im"] = f(inputs["ssm_b_im"][0][dd].reshape(2, 4096, 16))
        m["c_re"] = f(inputs["ssm_c_re"][0][dd].reshape(2, 1024, 64))
        m["c_im"] = f(inputs["ssm_c_im"][0][dd].reshape(2, 1024, 64))
        m["na_bias"] = _na_bias_table(rpb, half)
        maps.append(m)
    return maps


def _na_bias_table(rpb, half):
    tab = np.full((3, 8, 128, 640), NEG, np.float32)
    for typ in range(3):
        r_even = 2 * typ if typ < 2 else 4
        kr0 = max(r_even - 4, 0)
        for qi in range(128):
            rl = r_even + qi // 64
            cl = qi % 64
            ro, co = (rl, cl) if half == 0 else (63 - rl, 63 - cl)
            rs = min(max(ro - 4, 0), 56)
            cs = min(max(co - 8, 0), 48)
            for kro in range(rs, rs + 8):
                krl = kro if half == 0 else 63 - kro
                jr = krl - kr0
                if jr < 0 or jr >= 10:
                    raise RuntimeError("key row outside block")
                ridx = kro - ro + 7
                for kco in range(cs, cs + 16):
                    kcl = kco if half == 0 else 63 - kco
                    cidx = kco - co + 15
                    tab[typ, :, qi, jr * 64 + kcl] = rpb[:, ridx, cidx]
    return tab


_NC_CACHE = {}


STAGE = 3
NCORES = 8
S5DBG = False
S5BAR = False
SBUF_REPORT = False
L0_JOB_FRAC = 0.45


def kernel(**inputs):
    maps = _host_prep(inputs)
    if "nc" not in _NC_CACHE:
        _NC_CACHE["nc"] = build_program(stage=STAGE)
    nc = _NC_CACHE["nc"]
    res = run_bass_kernel_spmd(nc, maps[:NCORES], core_ids=list(range(NCORES)))
    outp = np.zeros((4, NPOS, D), np.float32)
    for core in range(NCORES):
        b, half = core // 2, core % 2
        y = res.results[core]["out"]
        if half == 0:
            outp[b, :2048] = y
        else:
            outp[b, 2048:] = y[::-1]
    return outp
```

```python
import contextlib
import numpy as np
import concourse.bass as bass
import concourse.mybir as mybir
from concourse.bass_utils import run_bass_kernel_spmd

F32 = mybir.dt.float32
BF16 = mybir.dt.bfloat16
AF = mybir.ActivationFunctionType
ALU = mybir.AluOpType
AX = mybir.AxisListType

ENGS = ("pe", "act", "dve", "pool", "sp")
SAME_ENGINE_INORDER = ("pe",)
NDMA = 32

D = 2048
NPOS = 4096
NCTX = 256
E = 2176
KVE = 2432
TB = 544
NBLK = 4
FF = 5632
EPS = 1e-6
NEG = -30000.0
TWO_PI = 6.283185307179586


class Sched:
    def __init__(self, nc, st):
        self.nc = nc
        self.ops = {e: [] for e in ENGS}
        self.cnt = {e: 0 for e in ENGS}
        self.seen = {e: {} for e in ENGS}
        self.lastw = {}
        self.readers = {}
        self.dma_cnt = [0] * NDMA
        self.dma_rr = {"sp": 0, "pool": 0, "act": 0}
        self.sems = {}
        for e in ENGS:
            self.sems["e_" + e] = st.enter_context(nc.semaphore("sem_" + e))
        for i in range(NDMA):
            self.sems["d_%d" % i] = st.enter_context(nc.semaphore("semd_%d" % i))

    def _deps(self, eng, r, w, nosw=False):
        toks = []
        for b in r:
            t = self.lastw.get(b)
            if t is not None:
                toks.append(t)
        for b in w:
            t = self.lastw.get(b)
            if t is not None:
                toks.append(t)
            toks.extend(self.readers.get(b, ()))
        waits = {}
        for (k, v, e) in toks:
            if e == eng and (eng in SAME_ENGINE_INORDER or nosw):
                continue
            if self.seen[eng].get(k, 0) >= v:
                continue
            if waits.get(k, 0) < v:
                waits[k] = v
        for k, v in waits.items():
            self.seen[eng][k] = v
        return list(waits.items())

    def _commit(self, tok, r, w):
        for b in w:
            self.lastw[b] = tok
            self.readers[b] = []
        for b in r:
            if b not in w:
                self.readers.setdefault(b, []).append(tok)

    def op(self, eng, fn, r=(), w=(), nosw=False):
        waits = self._deps(eng, r, w, nosw)
        self.cnt[eng] += 1
        tok = ("e_" + eng, self.cnt[eng], eng)
        self.ops[eng].append((waits, fn, ("e_" + eng, 1)))
        self._commit(tok, r, w)
        return tok

    def dma(self, eng, fn, r=(), w=()):
        base, n = {"sp": (0, 20), "pool": (20, 12), "act": (0, 20)}[eng]
        i = base + self.dma_rr[eng]
        self.dma_rr[eng] = (self.dma_rr[eng] + 1) % n
        waits = self._deps(eng, r, w)
        k = "d_%d" % i
        prev = 16 * self.dma_cnt[i]
        if prev > 0 and self.seen[eng].get(k, 0) < prev:
            waits.append((k, prev))
            self.seen[eng][k] = prev
        self.dma_cnt[i] += 1
        tok = (k, 16 * self.dma_cnt[i], None)
        self.ops[eng].append((waits, fn, (k, 16)))
        self._commit(tok, r, w)
        return tok

    def barrier(self):
        for eng in ENGS:
            waits = []
            for e2 in ENGS:
                k, v = "e_" + e2, self.cnt[e2]
                if e2 != eng and v > 0 and self.seen[eng].get(k, 0) < v:
                    waits.append((k, v))
                    self.seen[eng][k] = v
            for i in range(NDMA):
                k, v = "d_%d" % i, 16 * self.dma_cnt[i]
                if v > 0 and self.seen[eng].get(k, 0) < v:
                    waits.append((k, v))
                    self.seen[eng][k] = v
            self.ops[eng].append((waits, None, None))
        self.lastw = {}
        self.readers = {}

    def emit(self):
        nc = self.nc
        sems = self.sems
        ops = self.ops
        self.ops = {e: [] for e in ENGS}
        with nc.Block() as block:
            def run(engname, engobj):
                for (waits, fn, inc) in ops[engname]:
                    for (k, v) in waits:
                        engobj.wait_ge(sems[k], v)
                    if fn is not None:
                        ins = fn(engobj)
                        ins.then_inc(sems[inc[0]], inc[1])

            @block.tensor
            def _(e):
                run("pe", e)

            @block.scalar
            def _(e):
                run("act", e)

            @block.vector
            def _(e):
                run("dve", e)

            @block.gpsimd
            def _(e):
                run("pool", e)

            @block.sync
            def _(e):
                run("sp", e)


_UID = [0]


class Ring:
    def __init__(self, st, nc, name, n, shape, dt, psum=False):
        self.tiles = []
        self.ids = []
        _UID[0] += 1
        for i in range(n):
            nm = "%s_%d_%d" % (name, _UID[0], i)
            if psum:
                t = st.enter_context(nc.psum_tensor(nm, shape, dt))
            else:
                t = st.enter_context(nc.sbuf_tensor(nm, shape, dt))
            self.tiles.append(t)
            self.ids.append(nm)
        self.i = 0

    def next(self):
        t, i = self.tiles[self.i], self.ids[self.i]
        self.i = (self.i + 1) % len(self.tiles)
        return t, i


def build_program(stage=99, dbg=False):
    nc = bass.Bass("TRN2", target_bir_lowering=False)

    def din(name, shape, dt=F32):
        return nc.dram_tensor(name, list(shape), dt, kind="ExternalInput").ap()

    def dscr(name, shape, dt):
        return nc.dram_tensor(name, list(shape), dt, kind="Internal").ap()

    x_in = din("xc", [NPOS, D])
    ctx_in = din("ctxc", [NCTX, D])
    cvec = din("cvec", [2, D])
    w_mod = din("w_mod", [2, D, 6 * D])
    b_mod = din("b_mod", [2, 6 * D])
    g_mix = din("g_mix", [2, D])
    g_ffn = din("g_ffn", [2, D])
    g_out = din("g_out", [D])
    w_in = din("w_in", [D, 4096])
    lam_re = din("lam_re", [2, 4096])
    lam_im = din("lam_im", [2, 4096])
    log_dt = din("log_dt", [2, 4096])
    b_re = din("b_re", [2, 4096, 16])
    b_im = din("b_im", [2, 4096, 16])
    c_re = din("c_re", [2, 1024, 64])
    c_im = din("c_im", [2, 1024, 64])
    ssm_d = din("ssm_d", [1024])
    w_glu = din("w_glu", [1024, 1024])
    na_bias = din("na_bias", [3, 8, 128, 640])
    w_out = din("w_out", [D, D])
    pw1 = din("pw1", [D, 4096])
    dw_w = din("dw_w", [31, D])
    dw_b = din("dw_b", [D])
    ln_g = din("ln_g", [D])
    ln_b = din("ln_b", [D])
    pw2 = din("pw2", [D, D])
    w_up = din("w_up", [2, D, 2 * FF])
    fcw = din("fcw", [2, 3, 2 * FF])
    fcb = din("fcb", [2, 2 * FF])
    w_dn = din("w_dn", [2, FF, D])
    out = nc.dram_tensor("out", [2048, D], F32, kind="ExternalOutput").ap()
    dbg_out = None
    if dbg:
        dbg_out = [nc.dram_tensor("dbg%d" % i, [16, 128, E], F32, kind="ExternalOutput").ap() for i in range(2)]
        dbg_y = nc.dram_tensor("dbgy", [16, 128, E], BF16, kind="ExternalOutput").ap()

    xA = dscr("xA", [16, 128, E], F32)
    xB = dscr("xB", [16, 128, E], F32)
    WUP = [dscr("WUP%d" % l, [22, 128, 16 * 512], BF16) for l in range(2)]
    WDN = [dscr("WDN%d" % l, [16, 128, 44 * 128], BF16) for l in range(2)]
    WIN = dscr("WIN", [8, 128, 16 * 512], BF16)
    WPW1 = dscr("WPW1", [8, 128, 16 * 512], BF16)
    WOUT = dscr("WOUT", [4, 128, 16 * 512], BF16)
    WPW2 = dscr("WPW2", [4, 128, 16 * 512], BF16)
    WGLU = dscr("WGLU", [2, 128, 8 * 512], BF16)
    u_tm = dscr("u_tm", [NPOS + NCTX, 1024], BF16)
    q_fm = dscr("q_fm", [8, 128, E], BF16)
    k_fm = dscr("k_fm", [8, 128, KVE + NCTX], BF16)
    v_tm = dscr("v_tm", [KVE + NCTX, 1024], BF16)
    ymix = dscr("ymix", [16, 128, E], BF16)

    with contextlib.ExitStack() as top:
        S = Sched(nc, top)

        def sbt(st, name, shape, dt):
            _UID[0] += 1
            return st.enter_context(nc.sbuf_tensor("%s_%d" % (name, _UID[0]), shape, dt))

        def pst(st, name, shape, dt):
            _UID[0] += 1
            return st.enter_context(nc.psum_tensor("%s_%d" % (name, _UID[0]), shape, dt))

        rr = [0]

        def cast_eng():
            rr[0] = (rr[0] + 1) % 3
            return ("act", "dve", "pool")[rr[0]]

        def copy_fn(eng, o, i):
            if eng == "act":
                return lambda e: e.activation(out=o, in_=i, func=AF.Copy)
            return lambda e: e.tensor_copy(out=o, in_=i)

        ident_f = sbt(top, "ident_f", [128, 128], F32)
        ident_b = sbt(top, "ident_b", [128, 128], BF16)
        NV = 1800
        V = sbt(top, "V", [128, NV], F32)
        modv = sbt(top, "modv", [128, 2 * 6 * 2 * 16], F32)
        der = sbt(top, "der", [128, 10 * 16], F32)

        def MODV(l, j, v):
            o = ((l * 6 + j) * 2 + v) * 16
            return modv[:, o:o + 16]

        S.op("pool", lambda e: e.memset(ident_f[:], 0.0), w=["ident_f"])
        S.op("pool", lambda e: e.affine_select(out=ident_f[:], in_=ident_f[:], pattern=[[-1, 128]],
                                               compare_op=ALU.not_equal, fill=1.0, base=0,
                                               channel_multiplier=1), r=["ident_f"], w=["ident_f"])
        S.op("dve", lambda e: e.tensor_copy(out=ident_b[:], in_=ident_f[:]), r=["ident_f"], w=["ident_b"])

        vecs = []

        def addvec(name, ap1d, n):
            vecs.append((name, ap1d.rearrange("(c p) -> c p", p=128), n // 128))

        addvec("c", cvec[0], D)
        addvec("cctx", cvec[1], D)
        for l in range(2):
            addvec("g_mix%d" % l, g_mix[l], D)
            addvec("g_ffn%d" % l, g_ffn[l], D)
            for j in range(6):
                addvec("b_mod%d_%d" % (l, j), b_mod[l, j * D:(j + 1) * D], D)
        addvec("g_out", g_out, D)
        for k in range(31):
            addvec("dw_w%d" % k, dw_w[k], D)
        addvec("dw_b", dw_b, D)
        addvec("ln_g", ln_g, D)
        addvec("ln_b", ln_b, D)
        for l in range(2):
            for k in range(3):
                addvec("fcw%d_%d" % (l, k), fcw[l, k], 2 * FF)
            addvec("fcb%d" % l, fcb[l], 2 * FF)
        for d in range(2):
            addvec("lam_re%d" % d, lam_re[d], 4096)
            addvec("lam_im%d" % d, lam_im[d], 4096)
            addvec("log_dt%d" % d, log_dt[d], 4096)
        voff = {}
        o = 0
        for (name, ap2, nch) in vecs:
            voff[name] = o
            o += nch
        assert o <= NV, o
        nrows = o

        def VC(name, c0=0, n=16):
            return V[:, voff[name] + c0: voff[name] + c0 + n]

        with contextlib.ExitStack() as ph:
            rowt = Ring(ph, nc, "rowt", 2, [128, 128], F32)
            pvt = Ring(ph, nc, "pvt", 2, [128, 128], F32, psum=True)
            for t in range((nrows + 127) // 128):
                r0, r1 = t * 128, min(nrows, t * 128 + 128)
                tl, tid = rowt.next()
                guard = tid + "_g"
                subids = []
                for (name, ap2, nch) in vecs:
                    a, b = voff[name], voff[name] + nch
                    lo, hi = max(a, r0), min(b, r1)
                    if lo < hi:
                        sid_ = "%s_%d" % (tid, lo)
                        subids.append(sid_)
                        S.dma("sp", (lambda e, tl=tl, lo=lo, hi=hi, a=a, ap2=ap2, r0=r0:
                                     e.dma_start(out=tl[lo - r0:hi - r0, :], in_=ap2[lo - a:hi - a, :])),
                              r=[guard], w=[sid_])
                pt, pid = pvt.next()
                n = r1 - r0
                S.op("pe", (lambda e, pt=pt, tl=tl, n=n: e.transpose(out=pt[:, :n], in_=tl[:n, :],
                                                                    identity=ident_f[:n, :n])),
                     r=subids + ["ident_f"], w=[pid, guard])
                S.op("dve", (lambda e, pt=pt, n=n, r0=r0: e.tensor_copy(out=V[:, r0:r0 + n], in_=pt[:, :n])),
                     r=[pid], w=["V"])
            S.barrier()
            S.emit()

        with contextlib.ExitStack() as ph:
            s_bf = sbt(ph, "s_bf", [128, 2, 16], BF16)
            S.op("act", lambda e: e.activation(out=s_bf[:, 0, :], in_=VC("c"), func=AF.Silu), r=["V"], w=["s_bf"])
            S.op("act", lambda e: e.activation(out=s_bf[:, 1, :], in_=VC("cctx"), func=AF.Silu), r=["V"], w=["s_bf"])
            wf = Ring(ph, nc, "wmf", 3, [128, 2048], F32)
            wb = Ring(ph, nc, "wmb", 3, [128, 2048], BF16)
            pm = Ring(ph, nc, "pm", 2, [128, 512], F32, psum=True)
            msum = Ring(ph, nc, "msum", 2, [128, 32], F32)
            for l in range(2):
                for j in range(6):
                    pt, pid = pm.next()
                    for kc in range(16):
                        f, fid = wf.next()
                        b, bid = wb.next()
                        S.dma("sp", (lambda e, f=f, l=l, j=j, kc=kc: e.dma_start(
                            out=f[:], in_=w_mod[l, kc * 128:(kc + 1) * 128, j * D:(j + 1) * D])), w=[fid])
                        ce = cast_eng()
                        S.op(ce, copy_fn(ce, b[:], f[:]), r=[fid], w=[bid])
                        for m in range(16):
                            S.op("pe", (lambda e, pt=pt, b=b, m=m, kc=kc: e.matmul(
                                pt[:, kc * 32 + 2 * m:kc * 32 + 2 * m + 2], lhsT=b[:, m * 128:(m + 1) * 128],
                                rhs=s_bf[:, :, kc], start=True, stop=True)), r=[bid, "s_bf"], w=[pid])
                    ms, msid = msum.next()
                    S.op("dve", (lambda e, pt=pt, ms=ms: e.tensor_reduce(
                        out=ms[:], in_=pt[:].rearrange("p (k c) -> p c k", k=16), axis=AX.X, op=ALU.add)),
                         r=[pid], w=[msid])
                    for v in range(2):
                        S.op("dve", (lambda e, ms=ms, l=l, j=j, v=v: e.tensor_tensor(
                            out=MODV(l, j, v), in0=ms[:, v:32:2], in1=VC("b_mod%d_%d" % (l, j)), op=ALU.add)),
                             r=[msid, "V"], w=["modv"])

            def derive(idx, gname, l, jscale, v):
                o = idx * 16
                S.op("dve", lambda e: e.tensor_scalar(out=der[:, o:o + 16], in0=MODV(l, jscale, v), scalar1=1.0,
                                                      scalar2=None, op0=ALU.add), r=["modv"], w=["der"])
                S.op("dve", lambda e: e.tensor_tensor(out=der[:, o:o + 16], in0=der[:, o:o + 16], in1=VC(gname),
                                                      op=ALU.mult), r=["der", "V"], w=["der"])
            derive(0, "g_mix0", 0, 1, 0)
            derive(1, "g_mix0", 0, 1, 1)
            derive(2, "g_ffn0", 0, 4, 0)
            derive(3, "g_mix1", 1, 1, 0)
            derive(4, "g_ffn1", 1, 4, 0)
            S.barrier()
            S.emit()

        def DER(i):
            return der[:, i * 16:(i + 1) * 16]

        with contextlib.ExitStack() as ph:
            cf = Ring(ph, nc, "cvf", 2, [128, 8192], F32)
            cb = Ring(ph, nc, "cvb", 2, [128, 8192], BF16)

            def convert(parts, dst2, F):
                f, fid = cf.next()
                b, bid = cb.next()
                for (vf, src) in parts:
                    S.dma("sp", (lambda e, vf=vf, src=src, f=f: e.dma_start(out=vf(f), in_=src)), w=[fid])
                ce = cast_eng()
                S.op(ce, copy_fn(ce, b[:, :F], f[:, :F]), r=[fid], w=[bid])
                S.dma("pool", lambda e: e.dma_start(out=dst2, in_=b[:, :F]), r=[bid], w=["wscr"])

            for (src, dst, ng) in ((w_in, WIN, 8), (w_out, WOUT, 4)):
                v = src.rearrange("(kc p) (g c) -> g p kc c", p=128, c=512)
                for G in range(ng):
                    convert([((lambda f: f[:, :8192].rearrange("p (kc c) -> p kc c", kc=16)), v[G])], dst[G], 8192)
            v = w_glu.rearrange("(kc p) (g c) -> g p kc c", p=128, c=512)
            for G in range(2):
                convert([((lambda f: f[:, :4096].rearrange("p (kc c) -> p kc c", kc=8)), v[G])], WGLU[G], 4096)
            S.barrier()
            S.emit()

        bg_jobs = []
        for l in range(2):
            vu = w_up[l].rearrange("(kc p) (ug g c) -> ug g p kc c", p=128, ug=2, c=256)
            for G in range(22):
                for kq in range(4):
                    parts = []
                    for ug in range(2):
                        parts.append(((lambda f, ug=ug: f[:, :2048].rearrange("p (kc u c) -> p kc u c", kc=4, u=2)[:, :, ug, :]),
                                      vu[ug, G][:, kq * 4:(kq + 1) * 4, :]))
                    bg_jobs.append((parts, WUP[l][G][:, kq * 2048:(kq + 1) * 2048], 2048))
            vd = w_dn[l].rearrange("(j p) (m c) -> m p j c", p=128, c=128)
            for m in range(16):
                for jq in range(4):
                    bg_jobs.append(([((lambda f: f[:, :1408].rearrange("p (j c) -> p j c", j=11)), vd[m][:, jq * 11:(jq + 1) * 11, :])],
                                    WDN[l][m][:, jq * 1408:(jq + 1) * 1408], 1408))
            if l == 0:
                for (src, dst, ng) in ((pw1, WPW1, 8), (pw2, WPW2, 4)):
                    vv = src.rearrange("(kc p) (g c) -> g p kc c", p=128, c=512)
                    for G in range(ng):
                        for kq in range(4):
                            bg_jobs.append(([((lambda f: f[:, :2048].rearrange("p (kc c) -> p kc c", kc=4)), vv[G][:, kq * 4:(kq + 1) * 4, :])],
                                            dst[G][:, kq * 2048:(kq + 1) * 2048], 2048))
        bg_state = {"i": 0, "rr": 0}

        def bg_emit(n, cf, cb):
            for _ in range(n):
                if bg_state["i"] >= len(bg_jobs):
                    return
                parts, dst2, F = bg_jobs[bg_state["i"]]
                bg_state["i"] += 1
                f, fid = cf.next()
                b, bid = cb.next()
                for (vf, src) in parts:
                    S.dma("sp", (lambda e, vf=vf, src=src, f=f: e.dma_start(out=vf(f), in_=src)), w=[fid])
                bg_state["rr"] += 1
                ce = "act"
                S.op(ce, copy_fn(ce, b[:, :F], f[:, :F]), r=[fid], w=[bid])
                S.dma("pool", (lambda e, dst2=dst2, b=b, F=F: e.dma_start(out=dst2, in_=b[:, :F])), r=[bid], w=["wscr"])

        def l0_proj(do_mixer):
            with contextlib.ExitStack() as ph:
                xt = Ring(ph, nc, "xt", 2, [128, D], F32)
                xn = Ring(ph, nc, "xn", 2, [128, D], BF16)
                junk = sbt(ph, "junk", [128, D], BF16)
                ss = Ring(ph, nc, "ss", 4, [128, 2], F32)
                hfm = Ring(ph, nc, "hfm", 2, [128, 16, 512], BF16)
                ptr = Ring(ph, nc, "ptr", 2, [128, 4, 128], BF16, psum=True)
                pxf = Ring(ph, nc, "pxf", 2, [128, 4, 128], F32, psum=True)
                pmm = Ring(ph, nc, "pmm", 3, [128, 512], F32, psum=True)
                xo = Ring(ph, nc, "xo", 2, [128, 4, 128], F32)
                wg = Ring(ph, nc, "wg", 2, [128, 16, 512], BF16)
                ob = Ring(ph, nc, "ob", 3, [128, 512], BF16)
                l_bcf = Ring(ph, nc, "l_bcf", 4, [128, 2048], F32)
                l_bcb = Ring(ph, nc, "l_bcb", 2, [128, 2048], BF16)
                l_tiles = [0]
                ngroups = 9 if do_mixer else 5
                for g in range(ngroups):
                    is_ctx = (g == 8)
                    ntile = 2 if is_ctx else 4
                    ntok = ntile * 128
                    h, hid = hfm.next()
                    Av, Bv = (DER(1), MODV(0, 0, 1)) if is_ctx else (DER(0), MODV(0, 0, 0))
                    for ti in range(ntile):
                        src = ctx_in if is_ctx else x_in
                        p0 = ti * 128 if is_ctx else g * 512 + ti * 128
                        if do_mixer:
                            l_tiles[0] += 1
                            want = (l_tiles[0] * int(len(bg_jobs) * L0_JOB_FRAC)) // 34
                            bg_emit(want - bg_state["i"], l_bcf, l_bcb)
                        x_, xid = xt.next()
                        S.dma("sp", (lambda e, x_=x_, src=src, p0=p0: e.dma_start(out=x_[:], in_=src[p0:p0 + 128, :])),
                              w=[xid])
                        s_, sid = ss.next()
                        S.op("act", (lambda e, x_=x_, s_=s_: e.activation(out=junk[:], in_=x_[:], func=AF.Square,
                                                                          accum_out=s_[:, 0:1])),
                             r=[xid], w=["junk", sid])
                        S.op("act", (lambda e, s_=s_: e.activation(out=s_[:, 1:2], in_=s_[:, 0:1], func=AF.Sqrt,
                                                                   scale=1.0 / D, bias=EPS)), r=[sid], w=[sid])
                        S.op("dve", (lambda e, s_=s_: e.reciprocal(out=s_[:, 1:2], in_=s_[:, 1:2])), r=[sid], w=[sid])
                        n_, nid = xn.next()
                        S.op("dve", (lambda e, n_=n_, x_=x_, s_=s_: e.tensor_scalar(
                            out=n_[:], in0=x_[:], scalar1=s_[:, 1:2], scalar2=None, op0=ALU.mult)),
                             r=[xid, sid], w=[nid])
                        for q4 in range(4):
                            pt, pid = ptr.next()
                            for i in range(4):
                                kc = q4 * 4 + i
                                S.op("pe", (lambda e, pt=pt, i=i, n_=n_, kc=kc: e.transpose(
                                    out=pt[:, i, :], in_=n_[:, kc * 128:(kc + 1) * 128], identity=ident_b[:])),
                                     r=[nid, "ident_b"], w=[pid])
                            for i in range(4):
                                kc = q4 * 4 + i
                                S.op("act", (lambda e, pt=pt, i=i, h=h, kc=kc, ti=ti, Av=Av, Bv=Bv: e.activation(
                                    out=h[:, kc, ti * 128:(ti + 1) * 128], in_=pt[:, i, :], func=AF.Identity,
                                    scale=Av[:, kc:kc + 1], bias=Bv[:, kc:kc + 1])),
                                     r=[pid, "der", "modv"], w=[hid])
                        if (not is_ctx) and p0 < E:
                            for q4 in range(4):
                                pf, pfid = pxf.next()
                                for i in range(4):
                                    kc = q4 * 4 + i
                                    S.op("pe", (lambda e, pf=pf, i=i, x_=x_, kc=kc: e.transpose(
                                        out=pf[:, i, :], in_=x_[:, kc * 128:(kc + 1) * 128], identity=ident_f[:])),
                                         r=[xid, "ident_f"], w=[pfid])
                                o_, oid = xo.next()
                                S.op("dve", (lambda e, o_=o_, pf=pf: e.tensor_copy(out=o_[:], in_=pf[:])),
                                     r=[pfid], w=[oid])
                                S.dma("pool", (lambda e, o_=o_, q4=q4, p0=p0: e.dma_start(
                                    out=xA[q4 * 4:(q4 + 1) * 4, :, p0:p0 + 128].rearrange("c p t -> p c t"),
                                    in_=o_[:])), r=[oid], w=["xA"])
                    if not do_mixer:
                        continue
                    base = 0 if is_ctx else g * 512
                    nq = 0 if is_ctx else min(512, max(0, E - base))
                    nkv = ntok if is_ctx else min(512, max(0, KVE - base))
                    for G in range(8):
                        kind = ("u", "u", "q", "q", "k", "k", "v", "v")[G]
                        n = {"u": ntok, "q": nq, "k": nkv, "v": nkv}[kind]
                        if n == 0:
                            continue
                        w_, wid = wg.next()
                        S.dma("sp", (lambda e, w_=w_, G=G: e.dma_start(
                            out=w_[:].rearrange("p a b -> p (a b)"), in_=WIN[G])), r=["wscr"], w=[wid])
                        if kind in ("u", "v"):
                            for ti in range(n // 128):
                                pt, pid = pmm.next()
                                for kc in range(16):
                                    S.op("pe", (lambda e, pt=pt, h=h, kc=kc, ti=ti, w_=w_: e.matmul(
                                        pt[:, :], lhsT=h[:, kc, ti * 128:(ti + 1) * 128], rhs=w_[:, kc, :],
                                        start=(kc == 0), stop=(kc == 15))), r=[hid, wid], w=[pid])
                                o_, oid = ob.next()
                                ce = ("act", "dve")[ti % 2]
                                S.op(ce, copy_fn(ce, o_[:, :], pt[:, :]), r=[pid], w=[oid])
                                if kind == "u":
                                    row = (NPOS if is_ctx else base) + ti * 128
                                    dst = u_tm[row:row + 128, (G % 2) * 512:(G % 2) * 512 + 512]
                                else:
                                    row = (KVE if is_ctx else base) + ti * 128
                                    dst = v_tm[row:row + 128, (G % 2) * 512:(G % 2) * 512 + 512]
                                S.dma("pool", (lambda e, o_=o_, dst=dst: e.dma_start(out=dst, in_=o_[:, :])),
                                      r=[oid], w=["uv_scr"])
                        else:
                            for m in range(4):
                                pt, pid = pmm.next()
                                for kc in range(16):
                                    S.op("pe", (lambda e, pt=pt, h=h, kc=kc, m=m, w_=w_, n=n: e.matmul(
                                        pt[:, :n], lhsT=w_[:, kc, m * 128:(m + 1) * 128], rhs=h[:, kc, :n],
                                        start=(kc == 0), stop=(kc == 15))), r=[hid, wid], w=[pid])
                                o_, oid = ob.next()
                                ce = ("act", "dve")[m % 2]
                                S.op(ce, copy_fn(ce, o_[:, :n], pt[:, :n]), r=[pid], w=[oid])
                                hd = (G % 2) * 4 + m
                                if kind == "q":
                                    dst = q_fm[hd, :, base:base + n]
                                else:
                                    c0 = KVE if is_ctx else base
                                    dst = k_fm[hd, :, c0:c0 + n]
                                S.dma("pool", (lambda e, o_=o_, dst=dst, n=n: e.dma_start(out=dst, in_=o_[:, :n])),
                                      r=[oid], w=["qk_scr"])
                S.barrier()
                S.emit()

        ones_f = sbt(top, "ones_f", [128, 128], F32)
        S.op("pool", lambda e: e.memset(ones_f[:], 1.0), w=["ones_f"])

        def load_x_cols(ring, xsrc, kc, c0, n, lo, hi):
            t, tid = ring.next()
            a, b = max(c0, lo), min(c0 + n, hi)
            if a > c0 or b < c0 + n:
                S.op("pool", (lambda e, t=t, n=n: e.memset(t[:, :n], 0.0)), w=[tid])
            S.dma("sp", (lambda e, t=t, a=a, b=b, c0=c0, kc=kc: e.dma_start(out=t[:, a - c0:b - c0], in_=xsrc[kc, :, a:b])),
                  r=["xsrc"], w=[tid])
            return t, tid

        def norm_block(ph, xsrc, c0, n, Av, Bv, h, hid, rings):
            xs, sq, pss, rbc = rings
            H = n // 2
            for sbk in range(2):
                o0 = sbk * H
                pt, pid = pss.next()
                for kc in range(16):
                    t, tid = load_x_cols(xs, xsrc, kc, c0 + o0, H, 0, E)
                    S.op("act", (lambda e, t=t, H=H: e.activation(out=t[:, :H], in_=t[:, :H], func=AF.Square)),
                         r=[tid], w=[tid])
                    S.op("pe", (lambda e, pt=pt, t=t, kc=kc, H=H: e.matmul(
                        pt[:, :H], lhsT=ones_f[:], rhs=t[:, :H], start=(kc == 0), stop=(kc == 15))),
                         r=[tid, "ones_f"], w=[pid])
                r_, rid = rbc.next()
                S.op("act", (lambda e, r_=r_, pt=pt, H=H: e.activation(out=r_[:, :H], in_=pt[:, :H], func=AF.Sqrt,
                                                                      scale=1.0 / D, bias=EPS)), r=[pid], w=[rid])
                S.op("dve", (lambda e, r_=r_, H=H: e.reciprocal(out=r_[:, :H], in_=r_[:, :H])), r=[rid], w=[rid])
                for kc in range(16):
                    t, tid = load_x_cols(xs, xsrc, kc, c0 + o0, H, 0, E)
                    S.op("dve", (lambda e, t=t, r_=r_, H=H: e.tensor_tensor(out=t[:, :H], in0=t[:, :H], in1=r_[:, :H],
                                                                            op=ALU.mult)), r=[tid, rid], w=[tid])
                    S.op("act", (lambda e, t=t, h=h, kc=kc, H=H, o0=o0, Av=Av, Bv=Bv: e.activation(
                        out=h[:, kc, o0:o0 + H], in_=t[:, :H], func=AF.Identity, scale=Av[:, kc:kc + 1], bias=Bv[:, kc:kc + 1])),
                         r=[tid, "der", "modv"], w=[hid])

        def ffn_phase(l, xsrc, xdst, Av, Bv, gate):
            NH = TB + 2
            with contextlib.ExitStack() as ph:
                xs = Ring(ph, nc, "f_xs", 4, [128, NH // 2], F32)
                sq = None
                pss = Ring(ph, nc, "f_pss", 2, [128, 512], F32, psum=True)
                rbc = Ring(ph, nc, "f_rbc", 2, [128, NH // 2], F32)
                hr = Ring(ph, nc, "f_h", 1, [128, 16, NH], BF16)
                act = sbt(ph, "f_act", [128, 44, TB], BF16)
                wu = Ring(ph, nc, "f_wu", 2, [128, 16, 512], BF16)
                wd = Ring(ph, nc, "f_wd", 2, [128, 44, 128], BF16)
                pu = Ring(ph, nc, "f_pu", 2, [128, 2, 512], F32, psum=True)
                cv = Ring(ph, nc, "f_cv", 4, [128, TB], F32)
                sg = Ring(ph, nc, "f_sg", 2, [128, TB], F32)
                pdn = Ring(ph, nc, "f_pd", 1, [128, 2, 512], F32, psum=True)
                xo = Ring(ph, nc, "f_xo", 2, [128, TB], F32)
                fw = "fcw%d_" % l
                H2 = NH // 2
                for blk in range(NBLK):
                    t0 = blk * TB
                    h, hid = hr.next()
                    norm_block(ph, xsrc, t0 - 1, NH, Av, Bv, h, hid, (xs, sq, pss, rbc))
                    for G in range(22):
                        w_, wid = wu.next()
                        S.dma("sp", (lambda e, w_=w_, G=G: e.dma_start(
                            out=w_[:].rearrange("p a b -> p (a b)"), in_=WUP[l][G])), r=["wscr"], w=[wid])
                        for jj in range(2):
                            res = []
                            for ug in range(2):
                                ch = ug * 44 + G * 2 + jj
                                pt, pid = pu.next()
                                for hf in range(2):
                                    for kc in range(16):
                                        S.op("pe", (lambda e, pt=pt, w_=w_, kc=kc, ug=ug, jj=jj, hf=hf, h=h: e.matmul(
                                            pt[:, hf, :H2], lhsT=w_[:, kc, ug * 256 + jj * 128: ug * 256 + jj * 128 + 128],
                                            rhs=h[:, kc, hf * H2:(hf + 1) * H2], start=(kc == 0), stop=(kc == 15))),
                                             r=[hid, wid], w=[pid])
                                if blk == 0:
                                    S.op("dve", (lambda e, pt=pt: e.memset(pt[:, 0, 0:1], 0.0)), r=[pid], w=[pid])
                                c_, cid = cv.next()
                                def seg(off):
                                    a = []
                                    split = H2 - off
                                    a.append((0, off, 0, min(split, TB)))
                                    if split < TB:
                                        a.append((1, 0, split, TB))
                                    return a
                                first = True
                                for k, wname in ((1, fw + "1"), (0, fw + "0"), (2, fw + "2")):
                                    wv = V[:, voff[wname] + ch: voff[wname] + ch + 1]
                                    for (hf, pc, lo, hi) in seg(k):
                                        if first:
                                            bv = V[:, voff["fcb%d" % l] + ch: voff["fcb%d" % l] + ch + 1]
                                            S.op("act", (lambda e, c_=c_, pt=pt, hf=hf, pc=pc, lo=lo, hi=hi, wv=wv, bv=bv:
                                                         e.activation(out=c_[:, lo:hi], in_=pt[:, hf, pc:pc + hi - lo],
                                                                      func=AF.Identity, scale=wv, bias=bv)),
                                                 r=[pid, "V"], w=[cid])
                                        else:
                                            S.op("dve", (lambda e, c_=c_, pt=pt, hf=hf, pc=pc, lo=lo, hi=hi, wv=wv:
                                                         e.scalar_tensor_tensor(out=c_[:, lo:hi], in0=pt[:, hf, pc:pc + hi - lo],
                                                                                scalar=wv, in1=c_[:, lo:hi],
                                                                                op0=ALU.mult, op1=ALU.add)),
                                                 r=[pid, cid, "V"], w=[cid], nosw=True)
                                    first = False
                                res.append((c_, cid))
                            (cu, cuid), (cg, cgid) = res
                            s_, sid = sg.next()
                            S.op("act", (lambda e, s_=s_, cg=cg: e.activation(out=s_[:], in_=cg[:], func=AF.Silu)),
                                 r=[cgid], w=[sid])
                            j = G * 2 + jj
                            S.op("pool", (lambda e, s_=s_, cu=cu, j=j: e.tensor_tensor(out=act[:, j, :], in0=s_[:], in1=cu[:],
                                                                                      op=ALU.mult)),
                                 r=[sid, cuid], w=["f_act"])
                    HB = TB // 2
                    for m in range(16):
                        w_, wid = wd.next()
                        S.dma("sp", (lambda e, w_=w_, m=m: e.dma_start(
                            out=w_[:].rearrange("p a b -> p (a b)"), in_=WDN[l][m])), r=["wscr"], w=[wid])
                        o_, oid = load_x_cols(xo, xsrc, m, t0, TB, 0, E)
                        pt, pid = pdn.next()
                        for hf in range(2):
                            for j in range(44):
                                S.op("pe", (lambda e, pt=pt, w_=w_, j=j, hf=hf: e.matmul(
                                    pt[:, hf, :HB], lhsT=w_[:, j, :], rhs=act[:, j, hf * HB:(hf + 1) * HB],
                                    start=(j == 0), stop=(j == 43))), r=["f_act", wid], w=[pid])
                        for hf in range(2):
                            S.op("dve", (lambda e, o_=o_, pt=pt, hf=hf, m=m: e.scalar_tensor_tensor(
                                out=o_[:, hf * HB:(hf + 1) * HB], in0=pt[:, hf, :HB], scalar=gate[:, m:m + 1],
                                in1=o_[:, hf * HB:(hf + 1) * HB], op0=ALU.mult, op1=ALU.add)),
                                 r=[pid, oid, "modv"], w=[oid])
                        S.dma("pool", (lambda e, o_=o_, m=m, t0=t0: e.dma_start(out=xdst[m, :, t0:t0 + TB], in_=o_[:, :TB])),
                              r=[oid], w=["xdst"])
                S.barrier()
                S.emit()

        def proj_residual(ph, wscr, actt, actid, xsrc, xdst, t0, gate, rings):
            wr, pdn, xo = rings
            HB = TB // 2
            for m in range(16):
                w_, wid = wr.next()
                S.dma("sp", (lambda e, w_=w_, m=m: e.dma_start(
                    out=w_[:], in_=wscr[m // 4].rearrange("p (a b) -> p a b", a=16)[:, :, (m % 4) * 128:(m % 4) * 128 + 128])),
                      r=["wscr"], w=[wid])
                o_, oid = load_x_cols(xo, xsrc, m, t0, TB, 0, E)
                pt, pid = pdn.next()
                for hf in range(2):
                    for kc in range(16):
                        S.op("pe", (lambda e, pt=pt, w_=w_, kc=kc, hf=hf: e.matmul(
                            pt[:, hf, :HB], lhsT=w_[:, kc, :], rhs=actt[:, kc, hf * HB:(hf + 1) * HB],
                            start=(kc == 0), stop=(kc == 15))), r=[actid, wid], w=[pid])
                for hf in range(2):
                    S.op("dve", (lambda e, o_=o_, pt=pt, hf=hf, m=m: e.scalar_tensor_tensor(
                        out=o_[:, hf * HB:(hf + 1) * HB], in0=pt[:, hf, :HB], scalar=gate[:, m:m + 1],
                        in1=o_[:, hf * HB:(hf + 1) * HB], op0=ALU.mult, op1=ALU.add)),
                         r=[pid, oid, "modv"], w=[oid])
                S.dma("pool", (lambda e, o_=o_, m=m, t0=t0: e.dma_start(out=xdst[m, :, t0:t0 + TB], in_=o_[:, :TB])),
                      r=[oid], w=["xdst"])

        def mixout_phase(xsrc, xdst, gate):
            with contextlib.ExitStack() as ph:
                yt = Ring(ph, nc, "m_y", 2, [128, 16, TB], BF16)
                wr = Ring(ph, nc, "m_w", 3, [128, 16, 128], BF16)
                pdn = Ring(ph, nc, "m_pd", 2, [128, 2, 512], F32, psum=True)
                xo = Ring(ph, nc, "m_xo", 3, [128, TB], F32)
                for blk in range(NBLK):
                    t0 = blk * TB
                    y_, yid = yt.next()
                    S.dma("sp", (lambda e, y_=y_, t0=t0: e.dma_start(
                        out=y_[:], in_=ymix[:, :, t0:t0 + TB].rearrange("c p t -> p c t"))), r=["ymix"], w=[yid])
                    proj_residual(ph, WOUT, y_, yid, xsrc, xdst, t0, gate, (wr, pdn, xo))
                S.barrier()
                S.emit()

        def conformer_phase(xsrc, xdst, Av, Bv, gate):
            NH = TB + 30
            H2 = NH // 2
            HB = TB // 2
            with contextlib.ExitStack() as ph:
                xs = Ring(ph, nc, "c_xs", 4, [128, H2], F32)
                pss = Ring(ph, nc, "c_pss", 1, [128, 512], F32, psum=True)
                rbc = Ring(ph, nc, "c_rbc", 2, [128, H2], F32)
                hr = Ring(ph, nc, "c_h", 1, [128, 16, NH], BF16)
                w1 = Ring(ph, nc, "c_w1", 4, [128, 16, 128], BF16)
                pa = Ring(ph, nc, "c_pa", 2, [128, 2, 512], F32, psum=True)
                pzd = Ring(ph, nc, "c_pzd", 1, [128, 2, 512], F32, psum=True)
                sgr = Ring(ph, nc, "c_sg", 2, [128, NH], F32)
                cir = Ring(ph, nc, "c_ci", 2, [128, NH], BF16)
                dgr = Ring(ph, nc, "c_dg", 2, [128, 31, 128], BF16)
                zbuf = sbt(ph, "c_z", [128, 16, TB], F32)
                zs = sbt(ph, "c_zs", [128, 16, TB], BF16)
                mean = sbt(ph, "c_mean", [128, TB], F32)
                rstd = sbt(ph, "c_rstd", [128, TB], F32)
                ones_b = sbt(ph, "c_ones", [128, 128], BF16)
                S.op("dve", lambda e: e.tensor_copy(out=ones_b[:], in_=ones_f[:]), r=["ones_f"], w=["c_ones"])
                wr = Ring(ph, nc, "c_w2", 2, [128, 16, 128], BF16)
                xo = Ring(ph, nc, "c_xo", 2, [128, TB], F32)
                v2 = lambda t: t[:, :].rearrange("p (a b) -> p a b", a=2)
                for blk in range(NBLK):
                    t0 = blk * TB
                    h, hid = hr.next()
                    norm_block(ph, xsrc, t0 - 15, NH, Av, Bv, h, hid, (xs, None, pss, rbc))
                    for c in range(16):
                        ws = []
                        for part in range(2):
                            cc = part * 16 + c
                            w_, wid = w1.next()
                            S.dma("sp", (lambda e, w_=w_, cc=cc: e.dma_start(
                                out=w_[:], in_=WPW1[cc // 4].rearrange("p (a b) -> p a b", a=16)[:, :, (cc % 4) * 128:(cc % 4) * 128 + 128])),
                                  r=["wscr"], w=[wid])
                            ws.append((w_, wid))
                        dg, dgid = dgr.next()
                        for k in range(31):
                            wv = V[:, voff["dw_w%d" % k] + c: voff["dw_w%d" % k] + c + 1]
                            S.op("act", (lambda e, dg=dg, k=k, wv=wv: e.activation(out=dg[:, k, :], in_=ident_b[:], func=AF.Identity,
                                                                                 scale=wv)), r=["ident_b", "V"], w=[dgid])
                        pts = []
                        for part in range(2):
                            w_, wid = ws[part]
                            pt, pid = pa.next()
                            for hf in range(2):
                                for kc in range(16):
                                    S.op("pe", (lambda e, pt=pt, w_=w_, kc=kc, hf=hf, h=h: e.matmul(
                                        pt[:, hf, :H2], lhsT=w_[:, kc, :], rhs=h[:, kc, hf * H2:(hf + 1) * H2],
                                        start=(kc == 0), stop=(kc == 15))), r=[hid, wid], w=[pid])
                            pts.append((pt, pid))
                        (pA, pAid), (pG, pGid) = pts
                        sg_, sgid = sgr.next()
                        ci, ciid = cir.next()
                        S.op("act", (lambda e, sg_=sg_, pG=pG: e.activation(
                            out=v2(sg_), in_=pG[:, :, :H2], func=AF.Sigmoid)), r=[pGid], w=[sgid])
                        S.op("dve", (lambda e, ci=ci, pA=pA, sg_=sg_: e.tensor_tensor(
                            out=v2(ci), in0=pA[:, :, :H2], in1=v2(sg_), op=ALU.mult)), r=[pAid, sgid], w=[ciid])
                        if blk == 0:
                            S.op("dve", (lambda e, ci=ci: e.memset(ci[:, 0:15], 0.0)), r=[ciid], w=[ciid])
                        pz, pzid = pzd.next()
                        for hf in range(2):
                            for k in range(31):
                                S.op("pe", (lambda e, pz=pz, dg=dg, k=k, hf=hf, ci=ci: e.matmul(
                                    pz[:, hf, :HB], lhsT=dg[:, k, :], rhs=ci[:, hf * HB + k:hf * HB + k + HB],
                                    start=(k == 0), stop=(k == 30))), r=[dgid, ciid], w=[pzid])
                        bv = V[:, voff["dw_b"] + c: voff["dw_b"] + c + 1]
                        S.op("act", (lambda e, c=c, pz=pz, bv=bv: e.activation(
                            out=zbuf[:, c, :].rearrange("p (a b) -> p a b", a=2), in_=pz[:, :, :HB], func=AF.Identity, bias=bv)),
                             r=[pzid, "V"], w=["c_z%d" % c])
                        S.op("act", (lambda e, c=c: e.activation(out=zs[:, c, :], in_=zbuf[:, c, :], func=AF.Square)),
                             r=["c_z%d" % c], w=["c_zs"])
                    pS, pSid = pa.next()
                    pQ, pQid = pa.next()
                    for hf in range(2):
                        for c in range(16):
                            S.op("pe", (lambda e, pS=pS, hf=hf, c=c: e.matmul(pS[:, hf, :HB], lhsT=ones_f[:], rhs=zbuf[:, c, hf * HB:(hf + 1) * HB],
                                                                             start=(c == 0), stop=(c == 15))), r=["c_z%d" % c, "ones_f"], w=[pSid])
                    for hf in range(2):
                        for c in range(16):
                            S.op("pe", (lambda e, pQ=pQ, hf=hf, c=c: e.matmul(pQ[:, hf, :HB], lhsT=ones_b[:], rhs=zs[:, c, hf * HB:(hf + 1) * HB],
                                                                             start=(c == 0), stop=(c == 15))), r=["c_zs", "c_ones"], w=[pQid])
                    S.op("act", (lambda e, pS=pS: e.activation(out=v2(mean), in_=pS[:, :, :HB], func=AF.Identity, scale=1.0 / D)),
                         r=[pSid], w=["mean"])
                    S.op("dve", (lambda e: e.tensor_tensor(out=rstd[:], in0=mean[:], in1=mean[:], op=ALU.mult)),
                         r=["mean"], w=["rstd"])
                    S.op("dve", (lambda e, pQ=pQ: e.scalar_tensor_tensor(out=v2(rstd), in0=pQ[:, :, :HB], scalar=1.0 / D,
                                                                        in1=v2(rstd), op0=ALU.mult, op1=ALU.subtract)),
                         r=[pQid, "rstd"], w=["rstd"])
                    S.op("act", (lambda e: e.activation(out=rstd[:], in_=rstd[:], func=AF.Sqrt, bias=EPS, scale=1.0)),
                         r=["rstd"], w=["rstd"])
                    S.op("dve", (lambda e: e.reciprocal(out=rstd[:], in_=rstd[:])), r=["rstd"], w=["rstd"])
                    for c in range(16):
                        S.op("dve", (lambda e, c=c: e.tensor_tensor(out=zbuf[:, c, :], in0=zbuf[:, c, :], in1=mean[:], op=ALU.subtract)),
                             r=["c_z%d" % c, "mean"], w=["c_z%d" % c])
                        S.op("dve", (lambda e, c=c: e.tensor_tensor(out=zbuf[:, c, :], in0=zbuf[:, c, :], in1=rstd[:], op=ALU.mult)),
                             r=["c_z%d" % c, "rstd"], w=["c_z%d" % c], nosw=True)
                        gv = V[:, voff["ln_g"] + c: voff["ln_g"] + c + 1]
                        bv = V[:, voff["ln_b"] + c: voff["ln_b"] + c + 1]
                        S.op("act", (lambda e, c=c, gv=gv, bv=bv: e.activation(out=zs[:, c, :], in_=zbuf[:, c, :], func=AF.Silu,
                                                                                scale=gv, bias=bv)),
                             r=["c_z%d" % c, "V"], w=["c_zs"])
                    proj_residual(ph, WPW2, zs, "c_zs", xsrc, xdst, t0, gate, (wr, pzd, xo))
                S.barrier()
                S.emit()

        def na_phase():
            SCALE = 128.0 ** -0.5
            NKT = (KVE + NCTX) // 128
            with contextlib.ExitStack() as ph:
                qs = sbt(ph, "n_q", [128, 4, E], BF16)
                ks = sbt(ph, "n_k", [128, 4, KVE + NCTX], BF16)
                vs = sbt(ph, "n_v", [128, NKT, 512], BF16)
                bint = sbt(ph, "n_bint", [128, 4, 640], F32)
                bedge = Ring(ph, nc, "n_be", 2, [128, 640], F32)
                yna = sbt(ph, "n_y", [128, 4, E], BF16)
                psS = Ring(ph, nc, "n_ps", 2, [128, 2, 512], F32, psum=True)
                psT = Ring(ph, nc, "n_pt", 2, [128, 8, 128], BF16, psum=True)
                scr = Ring(ph, nc, "n_sc", 2, [128, 896], F32)
                pbr = Ring(ph, nc, "n_pb", 2, [128, 896], BF16)
                pTr = Ring(ph, nc, "n_pT", 2, [128, 7, 128], BF16)
                st4 = Ring(ph, nc, "n_st", 4, [128, 4], F32)
                otm = Ring(ph, nc, "n_o", 2, [128, 128], BF16)
                bcf = Ring(ph, nc, "n_bcf", 5, [128, 2048], F32)
                bcb = Ring(ph, nc, "n_bcb", 2, [128, 2048], BF16)
                it_cnt = [0]
                na_j0 = [bg_state["i"]]
                tot_iter = 2 * (E // 128) * 4
                if SBUF_REPORT:
                    print("NA sbuf remaining:", nc.sbuf_bytes_remaining)
                for hg in range(2):
                    S.dma("sp", (lambda e, hg=hg: e.dma_start(out=qs[:], in_=q_fm[hg * 4:hg * 4 + 4].rearrange("h p t -> p h t"))),
                          r=["qk_scr"], w=["n_q"])
                    S.dma("sp", (lambda e, hg=hg: e.dma_start(out=ks[:], in_=k_fm[hg * 4:hg * 4 + 4].rearrange("h p t -> p h t"))),
                          r=["qk_scr"], w=["n_k"])
                    S.dma("sp", (lambda e, hg=hg: e.dma_start(
                        out=vs[:], in_=v_tm[:, hg * 512:hg * 512 + 512].rearrange("(t p) c -> p t c", p=128))),
                          r=["uv_scr"], w=["n_v"])
                    S.dma("sp", (lambda e, hg=hg: e.dma_start(
                        out=bint[:], in_=na_bias[2, hg * 4:hg * 4 + 4].rearrange("h p k -> p h k"))), w=["n_bint"])
                    def tile_gen(hg, qt, h):
                        r = 2 * qt
                        kr0 = max(r - 4, 0)
                        k0 = kr0 * 64
                        it_cnt[0] += 1
                        want = na_j0[0] + (it_cnt[0] * (len(bg_jobs) - na_j0[0]) + tot_iter - 1) // tot_iter
                        bg_emit(want - bg_state["i"], bcf, bcb)
                        if qt < 2:
                            bt, btid = bedge.next()
                            S.dma("sp", (lambda e, bt=bt, qt=qt, hg=hg, h=h: e.dma_start(out=bt[:], in_=na_bias[qt, hg * 4 + h])),
                                  w=[btid])
                            bias_ap = bt[:, :]
                        else:
                            btid = "n_bint"
                            bias_ap = bint[:, h, :]
                        ps, psid = psS.next()
                        S.op("pe", (lambda e, ps=ps, h=h, qt=qt, k0=k0: e.matmul(
                            ps[:, 0, :], lhsT=qs[:, h, qt * 128:(qt + 1) * 128], rhs=ks[:, h, k0:k0 + 512],
                            start=True, stop=True)), r=["n_q", "n_k"], w=[psid])
                        S.op("pe", (lambda e, ps=ps, h=h, qt=qt, k0=k0: e.matmul(
                            ps[:, 1, 0:128], lhsT=qs[:, h, qt * 128:(qt + 1) * 128], rhs=ks[:, h, k0 + 512:k0 + 640],
                            start=True, stop=True)), r=["n_q", "n_k"], w=[psid])
                        S.op("pe", (lambda e, ps=ps, h=h, qt=qt: e.matmul(
                            ps[:, 1, 128:384], lhsT=qs[:, h, qt * 128:(qt + 1) * 128], rhs=ks[:, h, KVE:KVE + NCTX],
                            start=True, stop=True)), r=["n_q", "n_k"], w=[psid])
                        yield
                        sc, scid = scr.next()
                        S.op("dve", (lambda e, sc=sc, ps=ps, bias_ap=bias_ap: e.scalar_tensor_tensor(
                            out=sc[:, 0:512], in0=ps[:, 0, :], scalar=SCALE, in1=bias_ap[:, 0:512],
                            op0=ALU.mult, op1=ALU.add)), r=[psid, btid], w=[scid])
                        S.op("dve", (lambda e, sc=sc, ps=ps, bias_ap=bias_ap: e.scalar_tensor_tensor(
                            out=sc[:, 512:640], in0=ps[:, 1, 0:128], scalar=SCALE, in1=bias_ap[:, 512:640],
                            op0=ALU.mult, op1=ALU.add)), r=[psid, btid], w=[scid])
                        S.op("act", (lambda e, sc=sc, ps=ps: e.activation(out=sc[:, 640:896], in_=ps[:, 1, 128:384],
                                                                         func=AF.Identity, scale=SCALE)),
                             r=[psid], w=[scid])
                        yield
                        st_, stid = st4.next()
                        S.op("dve", (lambda e, st_=st_, sc=sc: e.tensor_reduce(out=st_[:, 0:1], in_=sc[:, :], axis=AX.X, op=ALU.max)),
                             r=[scid], w=[stid])
                        S.op("dve", (lambda e, st_=st_: e.tensor_scalar(out=st_[:, 1:2], in0=st_[:, 0:1], scalar1=-1.0, scalar2=None,
                                                                       op0=ALU.mult)), r=[stid], w=[stid])
                        yield
                        pb, pbid = pbr.next()
                        S.op("act", (lambda e, pb=pb, sc=sc, st_=st_: e.activation(out=pb[:, :], in_=sc[:, :], func=AF.Exp,
                                                                                  bias=st_[:, 1:2], scale=1.0,
                                                                                  accum_out=st_[:, 2:3])),
                             r=[scid, stid], w=[pbid, stid])
                        S.op("dve", (lambda e, st_=st_: e.reciprocal(out=st_[:, 3:4], in_=st_[:, 2:3])), r=[stid], w=[stid])
                        yield
                        pt, ptid = psT.next()
                        for j in range(7):
                            S.op("pe", (lambda e, pt=pt, pb=pb, j=j: e.transpose(out=pt[:, j, :], in_=pb[:, j * 128:(j + 1) * 128],
                                                                                identity=ident_b[:])),
                                 r=[pbid, "ident_b"], w=[ptid])
                        yield
                        pT, pTid = pTr.next()
                        S.op("act", (lambda e, pT=pT, pt=pt: e.activation(out=pT[:, 0:4, :], in_=pt[:, 0:4, :], func=AF.Copy)),
                             r=[ptid], w=[pTid])
                        S.op("dve", (lambda e, pT=pT, pt=pt: e.tensor_copy(out=pT[:, 4:7, :], in_=pt[:, 4:7, :])),
                             r=[ptid], w=[pTid])
                        yield
                        po, poid = ps[:, 1, 384:512], psid + "o"
                        for j in range(7):
                            vt = (kr0 // 2 + j) if j < 5 else (KVE // 128 + (j - 5))
                            S.op("pe", (lambda e, po=po, pT=pT, j=j, vt=vt, h=h: e.matmul(
                                po, lhsT=pT[:, j, :], rhs=vs[:, vt, h * 128:(h + 1) * 128],
                                start=(j == 0), stop=(j == 6))), r=[pTid, "n_v"], w=[poid])
                        yield
                        o_, oid = otm.next()
                        S.op("act", (lambda e, o_=o_, po=po, st_=st_: e.activation(out=o_[:, :], in_=po, func=AF.Identity,
                                                                                  scale=st_[:, 3:4])),
                             r=[poid, stid], w=[oid])
                        yield
                        pot, potid = pt[:, 7, :], ptid + "o"
                        S.op("pe", (lambda e, pot=pot, o_=o_: e.transpose(out=pot, in_=o_[:, :], identity=ident_b[:])),
                             r=[oid, "ident_b"], w=[potid])
                        yield
                        S.op("dve", (lambda e, pot=pot, h=h, qt=qt: e.tensor_copy(out=yna[:, h, qt * 128:(qt + 1) * 128], in_=pot)),
                             r=[potid], w=["n_y"])

                    for qt in range(E // 128):
                        for hp in range(2):
                            gens = [tile_gen(hg, qt, hp * 2), tile_gen(hg, qt, hp * 2 + 1)]
                            live = list(gens)
                            while live:
                                for g_ in list(live):
                                    try:
                                        next(g_)
                                    except StopIteration:
                                        live.remove(g_)
                    S.dma("sp", (lambda e, hg=hg: e.dma_start(
                        out=ymix[8 + hg * 4:8 + hg * 4 + 4].rearrange("h p t -> p h t"), in_=yna[:])), r=["n_y"], w=["ymix"])
                S.barrier()
                S.emit()

        def s5_phase():
            NCHK = 544
            NA_, NB_ = 304, 544
            PA, PB = 256, 512
            PBH, PZ, NZ = 16, 256, 274
            NOUT = E // 8
            PI = 3.141592653589793
            with contextlib.ExitStack() as ph0:
                z_tm = sbt(ph0, "s_ztm", [128, 3, 8, 1024], BF16)
                psX = Ring(ph0, nc, "s_px", 2, [128, 2, 512], F32, psum=True)
                psT = Ring(ph0, nc, "s_pt", 1, [128, 8, 128], BF16, psum=True)
                psMT = pst(ph0, "s_pmt", [128, 4, 128], F32)

                class _One:
                    def next(self_):
                        return psMT, "psM"
                psM = _One()
                psKT = pst(ph0, "s_pkt", [128, 512], F32)
                psY = Ring(ph0, nc, "s_py", 1, [128, 256], F32, psum=True)
                with contextlib.ExitStack() as ph:
                    NS = 24
                    T = sbt(ph, "s_T", [128, 2, NS, 32], F32)
                    PW = sbt(ph, "s_PW", [128, 2, 2, 32, 9], F32)
                    QW = sbt(ph, "s_QW", [128, 2, 3, 32, 10], F32)
                    Bt = sbt(ph, "s_Bt", [128, 2, 2, 32, 16], F32)
                    bb = sbt(ph, "s_bb", [128, 2, 2, 32, 16], F32)
                    bbb = sbt(ph, "s_bbb", [128, 2, 2, 32, 16], BF16)
                    CT = sbt(ph, "s_CT", [128, 2, 2, 32, 16], F32)
                    Cn = Ring(ph, nc, "s_Cn", 2, [128, 128], F32)
                    dvec = sbt(ph, "s_dvec", [128, 64], F32)
                    Sel = sbt(ph, "s_Sel", [16, 8, 128], BF16)
                    KTa = sbt(ph, "s_KTa", [16, 2, 15, 16], BF16)
                    KTb = sbt(ph, "s_KTb", [16, 2, 15, 16], BF16)
                    P_ = "s5par"

                    def dv(fn, r=(P_,), w=(P_,), eng="dve"):
                        S.op(eng, fn, r=list(r), w=list(w))

                    def tt(o_, a, b, op):
                        dv(lambda e: e.tensor_tensor(out=o_, in0=a, in1=b, op=op))

                    def ts(o_, a, s1, op0, s2=None, op1=None):
                        if op1 is None:
                            dv(lambda e: e.tensor_scalar(out=o_, in0=a, scalar1=s1, scalar2=None, op0=op0))
                        else:
                            dv(lambda e: e.tensor_scalar(out=o_, in0=a, scalar1=s1, scalar2=s2, op0=op0, op1=op1))

                    def stt(o_, a, sc, b, op0, op1):
                        dv(lambda e: e.scalar_tensor_tensor(out=o_, in0=a, scalar=sc, in1=b, op0=op0, op1=op1))

                    def act(o_, a, func, **kw):
                        dv(lambda e: e.activation(out=o_, in_=a, func=func, **kw), eng="act")

                    def cmul(ore, oim, are, aim, bre, bim, t1):
                        tt(ore, are, bre, ALU.mult)
                        tt(t1, aim, bim, ALU.mult)
                        tt(ore, ore, t1, ALU.subtract)
                        tt(oim, are, bim, ALU.mult)
                        tt(t1, aim, bre, ALU.mult)
                        tt(oim, oim, t1, ALU.add)

                    dv(lambda e: e.memset(Sel[:], 0.0), eng="pool")
                    for s_ in range(8):
                        dv((lambda e, s_=s_: e.tensor_copy(out=Sel[0:16, s_, s_ * 16:(s_ + 1) * 16], in_=ident_b[0:16, 0:16])),
                           r=(P_, "ident_b"), eng="pool")
                    dv(lambda e: e.memset(KTa[:], 0.0), eng="pool")
                    dv(lambda e: e.memset(KTb[:], 0.0), eng="pool")
                    for s_ in range(8):
                        S.dma("sp", (lambda e, s_=s_: e.dma_start(out=dvec[s_ * 16:(s_ + 1) * 16, :],
                                                                  in_=ssm_d.rearrange("(g c) -> c g", c=16),
                                                                  allow_slow_non_contiguous=True)), w=[P_])
                    for d in range(2):
                        for ri, src in ((0, b_re), (1, b_im)):
                            S.dma("sp", (lambda e, d=d, ri=ri, src=src: e.dma_start(
                                out=Bt[:, d, ri], in_=src[d].rearrange("(pair q) c -> q pair c", q=128))), w=[P_])
                    for d in range(2):
                        for ri, src in ((0, c_re), (1, c_im)):
                            for t8 in range(8):
                                cn, cnid = Cn.next()
                                for dup in range(2):
                                    S.dma("sp", (lambda e, cn=cn, dup=dup, src=src, d=d, t8=t8: e.dma_start(
                                        out=cn[:, dup * 64:(dup + 1) * 64], in_=src[d, t8 * 128:(t8 + 1) * 128, :])), w=[cnid])
                                pm, pmid = psM.next()
                                S.op("pe", (lambda e, pm=pm, cn=cn: e.transpose(out=pm[:, 0, :], in_=cn[:, :], identity=ident_f[:])),
                                     r=[cnid, "ident_f"], w=[pmid])
                                for g2 in range(2):
                                    S.op("dve", (lambda e, pm=pm, g2=g2, d=d, ri=ri, t8=t8: e.tensor_copy(
                                        out=CT[g2 * 64:(g2 + 1) * 64, d, ri, 4 * t8:4 * t8 + 4, :],
                                        in_=pm[g2 * 64:(g2 + 1) * 64, 0, :].rearrange("p (pp x c) -> p pp x c", pp=4, x=2)[:, :, g2, :])),
                                         r=[pmid, P_], w=[P_])
                    for d in range(2):
                        sl = lambda i, d=d: T[:, d, i, :]
                        DT, LR, LI, MAG, TH, CNT, SN, CS, ARE, AIM, FRE, FIM, DEN, T1, T2, XR = [sl(i) for i in range(16)]
                        lre = VC("lam_re%d" % d, 0, 32)
                        lim = VC("lam_im%d" % d, 0, 32)
                        act(DT, VC("log_dt%d" % d, 0, 32), AF.Exp)
                        S.op("dve", (lambda e, LR=LR, lre=lre, DT=DT: e.tensor_tensor(out=LR, in0=lre, in1=DT, op=ALU.mult)), r=[P_, "V"], w=[P_])
                        S.op("dve", (lambda e, LI=LI, lim=lim, DT=DT: e.tensor_tensor(out=LI, in0=lim, in1=DT, op=ALU.mult)), r=[P_, "V"], w=[P_])
                        act(MAG, LR, AF.Exp)
                        for (dst, shift) in ((SN, 0.0), (CS, PI / 2)):
                            ts(TH, LI, TWO_PI + shift, ALU.add)
                            ts(CNT, TH, TWO_PI, ALU.is_ge)
                            for jj in range(2, 6):
                                stt(CNT, TH, TWO_PI * jj, CNT, ALU.is_ge, ALU.add)
                            stt(TH, CNT, -TWO_PI, TH, ALU.mult, ALU.add)
                            ts(TH, TH, PI, ALU.subtract)
                            act(dst, TH, AF.Sin)
                        stt(ARE, CS, -1.0, MAG, ALU.mult, ALU.mult)
                        stt(AIM, SN, -1.0, MAG, ALU.mult, ALU.mult)
                        S.op("dve", (lambda e, DEN=DEN, lre=lre: e.tensor_tensor(out=DEN, in0=lre, in1=lre, op=ALU.mult)), r=[P_, "V"], w=[P_])
                        S.op("dve", (lambda e, T1=T1, lim=lim: e.tensor_tensor(out=T1, in0=lim, in1=lim, op=ALU.mult)), r=[P_, "V"], w=[P_])
                        tt(DEN, DEN, T1, ALU.add)
                        dv(lambda e, DEN=DEN: e.reciprocal(out=DEN, in_=DEN))
                        ts(XR, ARE, 1.0, ALU.subtract)
                        S.op("dve", (lambda e, FRE=FRE, XR=XR, lre=lre: e.tensor_tensor(out=FRE, in0=XR, in1=lre, op=ALU.mult)), r=[P_, "V"], w=[P_])
                        S.op("dve", (lambda e, T1=T1, AIM=AIM, lim=lim: e.tensor_tensor(out=T1, in0=AIM, in1=lim, op=ALU.mult)), r=[P_, "V"], w=[P_])
                        tt(FRE, FRE, T1, ALU.add)
                        tt(FRE, FRE, DEN, ALU.mult)
                        S.op("dve", (lambda e, FIM=FIM, AIM=AIM, lre=lre: e.tensor_tensor(out=FIM, in0=AIM, in1=lre, op=ALU.mult)), r=[P_, "V"], w=[P_])
                        S.op("dve", (lambda e, T1=T1, XR=XR, lim=lim: e.tensor_tensor(out=T1, in0=XR, in1=lim, op=ALU.mult)), r=[P_, "V"], w=[P_])
                        tt(FIM, FIM, T1, ALU.subtract)
                        tt(FIM, FIM, DEN, ALU.mult)
                        pw = lambda ri, k, d=d: PW[:, d, ri, :, k]
                        dv(lambda e, d=d: e.memset(PW[:, d, 0, :, 0], 1.0))
                        dv(lambda e, d=d: e.memset(PW[:, d, 1, :, 0], 0.0))
                        dv(lambda e, d=d, ARE=ARE: e.tensor_copy(out=PW[:, d, 0, :, 1], in_=ARE))
                        dv(lambda e, d=d, AIM=AIM: e.tensor_copy(out=PW[:, d, 1, :, 1], in_=AIM))
                        for k in range(2, 9):
                            cmul(pw(0, k), pw(1, k), pw(0, k - 1), pw(1, k - 1), ARE, AIM, T1)
                        qw = lambda ri, j, d=d: QW[:, d, ri, :, j]
                        dv(lambda e, d=d: e.tensor_copy(out=QW[:, d, 0, :, 0], in_=PW[:, d, 0, :, 8]))
                        dv(lambda e, d=d: e.tensor_copy(out=QW[:, d, 1, :, 0], in_=PW[:, d, 1, :, 8]))
                        for j in range(1, 10):
                            cmul(qw(0, j), qw(1, j), qw(0, j - 1), qw(1, j - 1), qw(0, j - 1), qw(1, j - 1), T1)
                        dv(lambda e, d=d: e.tensor_scalar(out=QW[:, d, 2], in0=QW[:, d, 1], scalar1=-1.0, scalar2=None, op0=ALU.mult))
                        fre_b = FRE.unsqueeze(2).broadcast_to([128, 32, 16])
                        fim_b = FIM.unsqueeze(2).broadcast_to([128, 32, 16])
                        tt(bb[:, d, 0], Bt[:, d, 0], fre_b, ALU.mult)
                        tt(bb[:, d, 1], Bt[:, d, 1], fre_b, ALU.mult)
                        tt(Bt[:, d, 1], Bt[:, d, 1], fim_b, ALU.mult)
                        tt(Bt[:, d, 0], Bt[:, d, 0], fim_b, ALU.mult)
                        tt(bb[:, d, 0], bb[:, d, 0], Bt[:, d, 1], ALU.subtract)
                        tt(bb[:, d, 1], bb[:, d, 1], Bt[:, d, 0], ALU.add)
                        dv(lambda e, d=d: e.tensor_copy(out=bbb[:, d], in_=bb[:, d]))

                    Gr = Ring(ph, nc, "s_G", 2, [128, 2, 2, 9, 16], F32)
                    Gt = sbt(ph, "s_Gt", [128, 9, 16], F32)
                    W1 = Ring(ph, nc, "s_W1", 2, [128, 2, 8, 16], F32)
                    W1t = sbt(ph, "s_W1t", [128, 8, 16], F32)
                    mxp = [Ring(ph, nc, "s_mxp%d" % d, 2, [128, 4, 128], BF16) for d in range(2)]
                    myp = [Ring(ph, nc, "s_myp%d" % d, 2, [128, 2, 256], BF16) for d in range(2)]
                    kblk = [Ring(ph, nc, "s_kb%d" % d, 2, [128, 2, 256], BF16) for d in range(2)]
                    for d in range(2):
                        for rg in (mxp[d], myp[d], kblk[d]):
                            for (tl, tid) in zip(rg.tiles, rg.ids):
                                S.op("pool", (lambda e, tl=tl: e.memset(tl[:], 0.0)), w=[tid])
                    toep = Ring(ph, nc, "s_toep", 2, [128, 2, 128], BF16)
                    ddg = Ring(ph, nc, "s_ddg", 2, [128, 128], F32)
                    UTp = Ring(ph, nc, "s_UTp", 2, [128, 5, 8, 32], BF16)
                    UT2 = Ring(ph, nc, "s_UT2", 2, [128, 5, 2, 128], BF16)
                    UgT = Ring(ph, nc, "s_UgT", 2, [128, 2, NCHK], BF16)
                    XA = [[sbt(ph, "s_XA%d%d" % (i, ri), [128, PA + NA_], F32) for ri in range(2)] for i in range(2)]
                    XB = [[sbt(ph, "s_XB%d%d" % (i, ri), [128, PBH + 272], F32) for ri in range(2)] for i in range(2)]
                    ZB = [[sbt(ph, "s_ZB%d%d" % (i, ri), [128, PZ + NZ], F32) for ri in range(2)] for i in range(2)]
                    for arrs in (XA, XB, ZB):
                        for i in range(2):
                            for ri in range(2):
                                S.op("pool", (lambda e, a=arrs[i][ri]: e.memset(a[:], 0.0)), w=["scanA", "scanB"])
                    Sbf = Ring(ph, nc, "s_Sbf", 2, [128, 4, NOUT], BF16)
                    ytmp = Ring(ph, nc, "s_yt", 2, [128, 3, 256], F32)

                    if SBUF_REPORT:
                        print("S5 sbuf remaining:", nc.sbuf_bytes_remaining)
                    for pair in range(32):
                        ut, utid = UTp.next()
                        for tl in range(5):
                            nr = 128 if tl < 4 else 32
                            S.dma("sp", (lambda e, ut=ut, tl=tl, nr=nr, pair=pair: e.dma_start(
                                out=ut[:nr, tl], in_=u_tm[tl * 1024:tl * 1024 + nr * 8, pair * 32:(pair + 1) * 32].rearrange(
                                    "(q s) c -> q s c", s=8))), r=["uv_scr"], w=[utid])
                        u2, u2id = UT2.next()
                        for tl in range(5):
                            nr = 128 if tl < 4 else 32
                            for g2 in range(2):
                                S.op("pool", (lambda e, u2=u2, ut=ut, tl=tl, nr=nr, g2=g2: e.tensor_copy(
                                    out=u2[:nr, tl, g2, :].rearrange("p (s c) -> p s c", s=8),
                                    in_=ut[:nr, tl, :, g2 * 16:(g2 + 1) * 16])), r=[utid], w=[u2id])
                        ug, ugid = UgT.next()
                        for g2 in range(2):
                            for tl in range(5):
                                nr = 128 if tl < 4 else 32
                                pt, ptid = psT.next()
                                S.op("pe", (lambda e, pt=pt, u2=u2, tl=tl, nr=nr, g2=g2: e.transpose(
                                    out=pt[:, 0, :nr], in_=u2[:nr, tl, g2, :], identity=ident_b[:nr, :nr])),
                                     r=[u2id, "ident_b"], w=[ptid])
                                ce = ("act", "dve")[tl % 2]
                                S.op(ce, copy_fn(ce, ug[:, g2, tl * 128:tl * 128 + nr], pt[:, 0, :nr]), r=[ptid], w=[ugid])
                        G_, Gid = Gr.next()
                        mx_t, my_t, kb_t = [], [], []
                        for d in range(2):
                            pre = PW[:, d, 0, pair, :].unsqueeze(2).broadcast_to([128, 9, 16])
                            pim = PW[:, d, 1, pair, :].unsqueeze(2).broadcast_to([128, 9, 16])
                            cre = CT[:, d, 0, pair, :].unsqueeze(1).broadcast_to([128, 9, 16])
                            cim = CT[:, d, 1, pair, :].unsqueeze(1).broadcast_to([128, 9, 16])
                            gre, gim = G_[:, d, 0], G_[:, d, 1]
                            for (o_, a1, b1, a2, b2, op) in ((gre, cre, pre, cim, pim, ALU.subtract), (gim, cre, pim, cim, pre, ALU.add)):
                                S.op("dve", (lambda e, o_=o_, a1=a1, b1=b1: e.tensor_tensor(out=o_, in0=a1, in1=b1, op=ALU.mult)), r=[P_], w=[Gid], nosw=True)
                                S.op("dve", (lambda e, a2=a2, b2=b2: e.tensor_tensor(out=Gt[:], in0=a2, in1=b2, op=ALU.mult)), r=[P_], w=["s_Gt"], nosw=True)
                                S.op("dve", (lambda e, o_=o_, op=op: e.tensor_tensor(out=o_, in0=o_, in1=Gt[:], op=op)), r=["s_Gt", Gid], w=[Gid], nosw=True)
                            my, myid = myp[d].next()
                            kb, kbid = kblk[d].next()
                            for g2 in range(2):
                                rows = slice(g2 * 64, (g2 + 1) * 64)
                                if d == 0:
                                    ksel = lambda ri, rows=rows, d=d, G_=G_: G_[rows, d, ri, 1:9, :]
                                else:
                                    ksel = lambda ri, rows=rows, d=d, G_=G_: G_[rows, d, ri, 1:9, :][:, ::-1, :]
                                S.op("dve", (lambda e, my=my, rows=rows, g2=g2, ksel=ksel: e.tensor_copy(
                                    out=my[rows, 0, g2 * 128:(g2 + 1) * 128].rearrange("p (t c) -> p t c", t=8), in_=ksel(0))), r=[Gid], w=[myid])
                                S.op("dve", (lambda e, my=my, rows=rows, g2=g2, ksel=ksel: e.tensor_scalar(
                                    out=my[rows, 1, g2 * 128:(g2 + 1) * 128].rearrange("p (t c) -> p t c", t=8), in0=ksel(1),
                                    scalar1=-1.0, scalar2=None, op0=ALU.mult)), r=[Gid], w=[myid])
                                S.op("pool", (lambda e, kb=kb, rows=rows, g2=g2, d=d, G_=G_: e.tensor_copy(
                                    out=kb[rows, 0, g2 * 128:(g2 + 1) * 128].rearrange("p (t c) -> p t c", t=8), in_=G_[rows, d, 0, 0:8, :])),
                                     r=[Gid], w=[kbid])
                                S.op("pool", (lambda e, kb=kb, rows=rows, g2=g2, d=d, G_=G_: e.tensor_scalar(
                                    out=kb[rows, 1, g2 * 128:(g2 + 1) * 128].rearrange("p (t c) -> p t c", t=8), in0=G_[rows, d, 1, 0:8, :],
                                    scalar1=-1.0, scalar2=None, op0=ALU.mult)), r=[Gid], w=[kbid])
                            w1, w1id = W1.next()
                            if d == 0:
                                psel = lambda ri, d=d: PW[:, d, ri, pair, 0:8][:, ::-1].unsqueeze(2).broadcast_to([128, 8, 16])
                            else:
                                psel = lambda ri, d=d: PW[:, d, ri, pair, 0:8].unsqueeze(2).broadcast_to([128, 8, 16])
                            bre = bb[:, d, 0, pair, :].unsqueeze(1).broadcast_to([128, 8, 16])
                            bim = bb[:, d, 1, pair, :].unsqueeze(1).broadcast_to([128, 8, 16])
                            for (o_, a1, b1, a2, b2, op) in ((w1[:, 0], psel(0), bre, psel(1), bim, ALU.subtract),
                                                            (w1[:, 1], psel(0), bim, psel(1), bre, ALU.add)):
                                S.op("dve", (lambda e, o_=o_, a1=a1, b1=b1: e.tensor_tensor(out=o_, in0=a1, in1=b1, op=ALU.mult)), r=[P_], w=[w1id], nosw=True)
                                S.op("dve", (lambda e, a2=a2, b2=b2: e.tensor_tensor(out=W1t[:], in0=a2, in1=b2, op=ALU.mult)), r=[P_], w=["s_W1t"], nosw=True)
                                S.op("dve", (lambda e, o_=o_, op=op: e.tensor_tensor(out=o_, in0=o_, in1=W1t[:], op=op)), r=["s_W1t", w1id], w=[w1id], nosw=True)
                            pm, pmid = psM.next()
                            for ri in range(2):
                                S.op("pe", (lambda e, pm=pm, w1=w1, ri=ri: e.transpose(
                                    out=pm[:, ri, :], in_=w1[:, ri].rearrange("p s c -> p (s c)"), identity=ident_f[:])),
                                     r=[w1id, "ident_f"], w=[pmid])
                            mx, mxid = mxp[d].next()
                            for ri in range(2):
                                for g2 in range(2):
                                    ce = ("act", "dve")[g2]
                                    S.op(ce, copy_fn(ce, mx[:, ri * 2 + g2, g2 * 64:(g2 + 1) * 64], pm[:, ri, g2 * 64:(g2 + 1) * 64]),
                                         r=[pmid], w=[mxid])
                            mx_t.append((mx, mxid)); my_t.append((my, myid)); kb_t.append((kb, kbid))

                        def emit_K_mm(d, pair=pair):
                            kb, kbid = kb_t[d]
                            S.op("pe", (lambda e, d=d, kb=kb, pair=pair: e.matmul(psKT[0:16, d * 256:d * 256 + 256], lhsT=bbb[:, d, 0, pair, :], rhs=kb[:, 0, :],
                                                                                  start=True, stop=False)), r=[P_, kbid], w=["psK%d" % d])
                            S.op("pe", (lambda e, d=d, kb=kb, pair=pair: e.matmul(psKT[0:16, d * 256:d * 256 + 256], lhsT=bbb[:, d, 1, pair, :], rhs=kb[:, 1, :],
                                                                                  start=False, stop=True)), r=[P_, kbid], w=["psK%d" % d])

                        def emit_KT_copy(d):
                            kview = psKT[0:16, d * 256:d * 256 + 256].rearrange("p (g t c) -> p g t c", g=2, t=8)
                            if d == 0:
                                S.op("dve", (lambda e, kview=kview: e.tensor_copy(out=KTa[:, :, 7:15, :], in_=kview)), r=["psK0"], w=["KT"])
                            else:
                                S.op("dve", (lambda e, kview=kview: e.tensor_copy(out=KTb[:, :, 0:8, :], in_=kview[:, :, ::-1, :])),
                                     r=["psK1"], w=["KT"])

                        tp, tpid = toep.next()

                        def emit_toep_mm():
                            for g2 in range(2):
                                n_mm = 0
                                for KT in (KTa, KTb):
                                    for s_ in range(8):
                                        S.op("pe", (lambda e, g2=g2, KT=KT, s_=s_, first=(n_mm == 0), last=(n_mm == 15): e.matmul(
                                            psMT[:, 2 + g2, :], lhsT=Sel[0:16, s_, :],
                                            rhs=KT[0:16, g2, 7 - s_:15 - s_, :].rearrange("p j c -> p (j c)"),
                                            start=first, stop=last)), r=["KT", P_], w=["psTo"])
                                        n_mm += 1

                        def emit_toep_evac(pair=pair, tp=tp, tpid=tpid):
                            for g2 in range(2):
                                g = 2 * pair + g2
                                dd, ddid = ddg.next()
                                S.op("pool", (lambda e, dd=dd, g=g: e.tensor_scalar(out=dd[:], in0=ident_f[:], scalar1=dvec[:, g:g + 1],
                                                                                   scalar2=None, op0=ALU.mult)), r=[P_, "ident_f"], w=[ddid])
                                S.op("dve", (lambda e, tp=tp, g2=g2, dd=dd: e.tensor_tensor(
                                    out=tp[:, g2, :], in0=psMT[:, 2 + g2, :], in1=dd[:], op=ALU.add)),
                                     r=["psTo", ddid], w=[tpid])

                        sb_, sbid = Sbf.next()

                        def emit_X(d, ug=ug, ugid=ugid):
                            mx, mxid = mx_t[d]
                            arr = XA if d == 0 else XB
                            sid = "scanA" if d == 0 else "scanB"
                            for ri in range(2):
                                px, pxid = psX.next()
                                if d == 0:
                                    segs = [(px[:, 0, 0:32], 512, 544), (px[:, 0, 32:304], 0, 272)]
                                else:
                                    segs = [(px[:, 0, 0:512], 0, 512), (px[:, 1, 0:32], 512, 544)]
                                for (o_, c0, c1) in segs:
                                    for g2 in range(2):
                                        S.op("pe", (lambda e, o_=o_, mx=mx, ri=ri, g2=g2, c0=c0, c1=c1, ug=ug: e.matmul(
                                            o_, lhsT=mx[:, ri * 2 + g2, :], rhs=ug[:, g2, c0:c1], start=(g2 == 0), stop=(g2 == 1))),
                                             r=[mxid, ugid], w=[pxid])
                                dst = arr[0][ri]
                                if d == 0:
                                    S.op("act", (lambda e, dst=dst, px=px: e.activation(out=dst[:, PA:PA + NA_], in_=px[:, 0, 0:NA_], func=AF.Copy)),
                                         r=[pxid], w=[sid])
                                else:
                                    zdst = ZB[0][ri]
                                    S.op("act", (lambda e, zdst=zdst, px=px: e.activation(out=zdst[:, PZ + 1:PZ + 274][:, ::-1], in_=px[:, 0, 0:273],
                                                                                        func=AF.Copy)), r=[pxid], w=[sid])
                                    S.op("act", (lambda e, dst=dst, px=px: e.activation(out=dst[:, PBH + 32:PBH + 271][:, ::-1], in_=px[:, 0, 273:512],
                                                                                       func=AF.Copy)), r=[pxid], w=[sid])
                                    S.op("act", (lambda e, dst=dst, px=px: e.activation(out=dst[:, PBH:PBH + 32][:, ::-1], in_=px[:, 1, 0:32],
                                                                                       func=AF.Copy)), r=[pxid], w=[sid])

                        def emit_HS(d, pair=pair, sb_=sb_, sbid=sbid):
                            arr = XA if d == 0 else ZB
                            sid = "scanA" if d == 0 else "scanB"
                            PAD = PA if d == 0 else PZ
                            n = NA_ if d == 0 else NZ
                            if d == 1:
                                m = 272
                                lo_ = PBH - 1
                                srcb, dstb = XB[0], XB[1]
                                lvl = 0
                                while m > 1:
                                    half = m // 2
                                    qre = QW[:, d, 0, pair, lvl:lvl + 1]
                                    qim = QW[:, d, 1, pair, lvl:lvl + 1]
                                    nqim = QW[:, d, 2, pair, lvl:lvl + 1]
                                    ev = lambda t, lo_=lo_, m=m: t[:, lo_:lo_ + m:2]
                                    od = lambda t, lo_=lo_, m=m: t[:, lo_ + 1:lo_ + m:2]
                                    ou = lambda t, half=half: t[:, PBH:PBH + half]
                                    for (o_, a1, s1, b1) in ((ou(dstb[0]), ev(srcb[0]), qre, od(srcb[0])), (ou(dstb[0]), ev(srcb[1]), nqim, ou(dstb[0])),
                                                            (ou(dstb[1]), ev(srcb[1]), qre, od(srcb[1])), (ou(dstb[1]), ev(srcb[0]), qim, ou(dstb[1]))):
                                        S.op("dve", (lambda e, o_=o_, a1=a1, s1=s1, b1=b1: e.scalar_tensor_tensor(
                                            out=o_, in0=a1, scalar=s1, in1=b1, op0=ALU.mult, op1=ALU.add)), r=[sid, P_], w=[sid], nosw=True)
                                    m = half
                                    lo_ = PBH if m % 2 == 0 else PBH - 1
                                    if m % 2 == 1 and m > 1:
                                        m += 1
                                    srcb, dstb = dstb, srcb
                                    lvl += 1
                                for ri in range(2):
                                    S.op("dve", (lambda e, ri=ri, srcb=srcb: e.tensor_copy(out=ZB[0][ri][:, PZ:PZ + 1], in_=srcb[ri][:, PBH:PBH + 1])),
                                         r=[sid], w=[sid])
                            cur = 0
                            j = 0
                            sh = 1
                            while sh < n:
                                src_, dst_ = arr[cur], arr[1 - cur]
                                qre = QW[:, d, 0, pair, j:j + 1]
                                qim = QW[:, d, 1, pair, j:j + 1]
                                nqim = QW[:, d, 2, pair, j:j + 1]
                                lo, hi = PAD, PAD + n
                                for (o_, a1, s1, b1) in ((dst_[0], src_[0], qre, src_[0]), (dst_[0], src_[1], nqim, dst_[0]),
                                                        (dst_[1], src_[1], qre, src_[1]), (dst_[1], src_[0], qim, dst_[1])):
                                    S.op("dve", (lambda e, o_=o_, a1=a1, s1=s1, b1=b1, lo=lo, hi=hi, sh=sh: e.scalar_tensor_tensor(
                                        out=o_[:, lo:hi], in0=a1[:, lo - sh:hi - sh], scalar=s1, in1=b1[:, lo:hi],
                                        op0=ALU.mult, op1=ALU.add)), r=[sid, P_], w=[sid], nosw=True)
                                cur = 1 - cur
                                sh *= 2
                                j += 1
                            fin = arr[cur]
                            for ri in range(2):
                                if d == 0:
                                    S.op("act", (lambda e, sb_=sb_, fin=fin, ri=ri: e.activation(
                                        out=sb_[:, ri, :], in_=fin[ri][:, PA + 31:PA + 31 + NOUT], func=AF.Copy)), r=[sid], w=[sbid])
                                else:
                                    S.op("act", (lambda e, sb_=sb_, fin=fin, ri=ri: e.activation(
                                        out=sb_[:, 2 + ri, :][:, ::-1], in_=fin[ri][:, PZ + 1:PZ + 1 + NOUT], func=AF.Copy)),
                                         r=[sid], w=[sbid])

                        emit_X(0)
                        emit_K_mm(0)
                        emit_K_mm(1)
                        emit_X(1)
                        emit_HS(0)
                        emit_KT_copy(0)
                        emit_KT_copy(1)
                        emit_toep_mm()
                        emit_HS(1)
                        emit_toep_evac()
                        for ct in range(3):
                            nch = 128 if ct < 2 else NOUT - 256
                            py, pyid = psY.next()
                            k = 0
                            for d in range(2):
                                my, myid = my_t[d]
                                for ri in range(2):
                                    S.op("pe", (lambda e, py=py, sb_=sb_, d=d, ri=ri, ct=ct, nch=nch, my=my, first=(k == 0): e.matmul(
                                        py[:nch, :], lhsT=sb_[:, d * 2 + ri, ct * 128:ct * 128 + nch], rhs=my[:, ri, :],
                                        start=first, stop=False)), r=[sbid, myid], w=[pyid])
                                    k += 1
                            for g2 in range(2):
                                S.op("pe", (lambda e, py=py, ug=ug, g2=g2, ct=ct, nch=nch, tp=tp: e.matmul(
                                    py[:nch, g2 * 128:(g2 + 1) * 128], lhsT=ug[:, g2, ct * 128:ct * 128 + nch], rhs=tp[:, g2, :],
                                    start=False, stop=(g2 == 1))), r=[ugid, tpid], w=[pyid])
                            yt, ytid = ytmp.next()
                            S.op("act", (lambda e, yt=yt, py=py, nch=nch: e.activation(out=yt[:nch, 0, :], in_=py[:nch, :], func=AF.Copy)),
                                 r=[pyid], w=[ytid])
                            S.op("pool", (lambda e, yt=yt, nch=nch: e.tensor_tensor(out=yt[:nch, 1, :], in0=yt[:nch, 0, :], in1=yt[:nch, 0, :],
                                                                                   op=ALU.mult)), r=[ytid], w=[ytid])
                            S.op("pool", (lambda e, yt=yt, nch=nch: e.tensor_scalar(out=yt[:nch, 1, :], in0=yt[:nch, 1, :], scalar1=0.044715,
                                                                                   scalar2=1.0, op0=ALU.mult, op1=ALU.add)), r=[ytid], w=[ytid])
                            S.op("pool", (lambda e, yt=yt, nch=nch: e.tensor_tensor(out=yt[:nch, 1, :], in0=yt[:nch, 1, :], in1=yt[:nch, 0, :],
                                                                                   op=ALU.mult)), r=[ytid], w=[ytid])
                            S.op("act", (lambda e, yt=yt, nch=nch: e.activation(out=yt[:nch, 2, :], in_=yt[:nch, 1, :], func=AF.Sigmoid,
                                                                               scale=1.5957691216057308)), r=[ytid], w=[ytid])
                            for g2 in range(2):
                                zo = z_tm[:nch, ct, :, pair * 32 + g2 * 16:pair * 32 + (g2 + 1) * 16]
                                i0 = yt[:nch, 0, g2 * 128:(g2 + 1) * 128].rearrange("p (t c) -> p t c", t=8)
                                i1 = yt[:nch, 2, g2 * 128:(g2 + 1) * 128].rearrange("p (t c) -> p t c", t=8)
                                if S5DBG:
                                    S.op("pool", (lambda e, zo=zo, i0=i0: e.tensor_copy(out=zo, in_=i0)), r=[ytid], w=["s_ztm"])
                                else:
                                    S.op("pool", (lambda e, zo=zo, i0=i0, i1=i1: e.tensor_tensor(out=zo, in0=i0, in1=i1, op=ALU.mult)),
                                         r=[ytid], w=["s_ztm"])
                        if S5BAR:
                            S.barrier()
                    S.barrier()
                    S.emit()
                with contextlib.ExitStack() as ph:
                    z_fm = sbt(ph, "s_zfm", [128, 8, E], BF16)
                    wgl = Ring(ph, nc, "s_wgl", 2, [128, 8, 128], BF16)
                    sgt = Ring(ph, nc, "s_sgt", 2, [128, TB], F32)
                    yo = Ring(ph, nc, "s_yo", 2, [128, TB], BF16)
                    for ct in range(3):
                        nch = 128 if ct < 2 else NOUT - 256
                        for c8 in range(8):
                            pt, ptid = psT.next()
                            for t in range(8):
                                S.op("pe", (lambda e, pt=pt, ct=ct, t=t, c8=c8, nch=nch: e.transpose(
                                    out=pt[:, t, :nch], in_=z_tm[:nch, ct, t, c8 * 128:(c8 + 1) * 128], identity=ident_b[:nch, :nch])),
                                     r=["s_ztm", "ident_b"], w=[ptid])
                            ce = ("act", "dve")[c8 % 2]
                            S.op(ce, copy_fn(ce, z_fm[:, c8, ct * 1024:ct * 1024 + nch * 8].rearrange("p (q t) -> p t q", t=8),
                                             pt[:, :, :nch]), r=[ptid], w=["s_zfm"])
                    HB = TB // 2
                    for blk in range(NBLK):
                        t0 = blk * TB
                        for m in range(8):
                            w_, wid = wgl.next()
                            S.dma("sp", (lambda e, w_=w_, m=m: e.dma_start(
                                out=w_[:], in_=WGLU[m // 4].rearrange("p (a b) -> p a b", a=8)[:, :, (m % 4) * 128:(m % 4) * 128 + 128])),
                                  r=["wscr"], w=[wid])
                            px, pxid = psX.next()
                            for hf in range(2):
                                for kc in range(8):
                                    S.op("pe", (lambda e, px=px, w_=w_, kc=kc, hf=hf, t0=t0: e.matmul(
                                        px[:, hf, :HB], lhsT=w_[:, kc, :], rhs=z_fm[:, kc, t0 + hf * HB:t0 + (hf + 1) * HB],
                                        start=(kc == 0), stop=(kc == 7))), r=["s_zfm", wid], w=[pxid])
                            sg_, sgid = sgt.next()
                            S.op("act", (lambda e, sg_=sg_, px=px: e.activation(out=sg_[:, :].rearrange("p (a b) -> p a b", a=2),
                                                                              in_=px[:, :, :HB], func=AF.Sigmoid)), r=[pxid], w=[sgid])
                            y_, yid = yo.next()
                            if S5DBG:
                                S.op("dve", (lambda e, y_=y_, m=m, t0=t0: e.tensor_copy(out=y_[:], in_=z_fm[:, m, t0:t0 + TB])),
                                     r=[sgid, "s_zfm"], w=[yid])
                            else:
                                S.op("dve", (lambda e, y_=y_, sg_=sg_, m=m, t0=t0: e.tensor_tensor(out=y_[:], in0=sg_[:], in1=z_fm[:, m, t0:t0 + TB],
                                                                                              op=ALU.mult)), r=[sgid, "s_zfm"], w=[yid])
                            S.dma("pool", (lambda e, y_=y_, m=m, t0=t0: e.dma_start(out=ymix[m, :, t0:t0 + TB], in_=y_[:])),
                                  r=[yid], w=["ymix"])
                    S.barrier()
                    S.emit()

        def final_phase(xsrc):
            with contextlib.ExitStack() as ph:
                xs = Ring(ph, nc, "o_xs", 3, [128, 512], F32)
                sq = Ring(ph, nc, "o_sq", 2, [128, 512], F32)
                pss = Ring(ph, nc, "o_pss", 1, [128, 512], F32, psum=True)
                rbc = Ring(ph, nc, "o_rbc", 1, [128, 512], F32)
                pt_ = Ring(ph, nc, "o_pt", 2, [128, 4, 128], F32, psum=True)
                ot = Ring(ph, nc, "o_ot", 2, [128, D], F32)
                hf32 = sbt(ph, "o_h", [128, 16, 512], F32)
                for blk in range(4):
                    c0 = blk * 512
                    n = 512
                    pt, pid = pss.next()
                    for kc in range(16):
                        t, tid = load_x_cols(xs, xsrc, kc, c0, n, 0, E)
                        q, qid = sq.next()
                        S.op("act", (lambda e, q=q, t=t: e.activation(out=q[:], in_=t[:], func=AF.Square)), r=[tid], w=[qid])
                        S.op("pe", (lambda e, pt=pt, q=q, kc=kc: e.matmul(pt[:, :], lhsT=ones_f[:], rhs=q[:, :],
                                                                         start=(kc == 0), stop=(kc == 15))),
                             r=[qid, "ones_f"], w=[pid])
                    r_, rid = rbc.next()
                    S.op("act", (lambda e, r_=r_, pt=pt: e.activation(out=r_[:], in_=pt[:], func=AF.Sqrt, scale=1.0 / D,
                                                                      bias=EPS)), r=[pid], w=[rid])
                    S.op("dve", (lambda e, r_=r_: e.reciprocal(out=r_[:], in_=r_[:])), r=[rid], w=[rid])
                    for kc in range(16):
                        t, tid = load_x_cols(xs, xsrc, kc, c0, n, 0, E)
                        S.op("dve", (lambda e, t=t, r_=r_: e.tensor_tensor(out=t[:], in0=t[:], in1=r_[:], op=ALU.mult)),
                             r=[tid, rid], w=[tid])
                        gv = V[:, voff["g_out"] + kc: voff["g_out"] + kc + 1]
                        S.op("act", (lambda e, t=t, kc=kc, gv=gv: e.activation(out=hf32[:, kc, :], in_=t[:],
                                                                              func=AF.Identity, scale=gv)),
                             r=[tid, "V"], w=["o_h"])
                    for ti in range(4):
                        o_, oid = ot.next()
                        for q4 in range(4):
                            p_, pid2 = pt_.next()
                            for i in range(4):
                                kc = q4 * 4 + i
                                S.op("pe", (lambda e, p_=p_, i=i, kc=kc, ti=ti: e.transpose(
                                    out=p_[:, i, :], in_=hf32[:, kc, ti * 128:(ti + 1) * 128], identity=ident_f[:])),
                                     r=["o_h", "ident_f"], w=[pid2])
                            ce = ("act", "dve")[q4 % 2]
                            S.op(ce, copy_fn(ce, o_[:, q4 * 512:(q4 + 1) * 512],
                                             p_[:].rearrange("p a b -> p (a b)")), r=[pid2], w=[oid])
                        row = c0 + ti * 128
                        S.dma("sp", (lambda e, o_=o_, row=row: e.dma_start(out=out[row:row + 128, :], in_=o_[:])),
                              r=[oid], w=["out"])
                S.barrier()
                S.emit()

        def dump(src, i):
            for c in range(16):
                S.dma("sp", (lambda e, c=c: e.dma_start(out=dbg_out[i][c], in_=src[c])), r=["xsrc"], w=["dbg"])
            S.barrier()

        mix = stage >= 3
        l0_proj(do_mixer=mix)
        if mix:
            s5_phase()
            na_phase()
            if dbg:
                for c in range(16):
                    S.dma("sp", (lambda e, c=c: e.dma_start(out=dbg_y[c], in_=ymix[c])), r=["ymix"], w=["dbgy"])
                S.barrier()
                S.emit()
                return nc
            mixout_phase(xA, xB, MODV(0, 2, 0))
            ffn_phase(0, xB, xA, DER(2), MODV(0, 3, 0), MODV(0, 5, 0))
            conformer_phase(xA, xB, DER(3), MODV(1, 0, 0), MODV(1, 2, 0))
            ffn_phase(1, xB, xA, DER(4), MODV(1, 3, 0), MODV(1, 5, 0))
            final_phase(xA)
        else:
            ffn_phase(0, xA, xB, DER(2), MODV(0, 3, 0), MODV(0, 5, 0))
            conformer_phase(xB, xA, DER(3), MODV(1, 0, 0), MODV(1, 2, 0))
            ffn_phase(1, xA, xB, DER(4), MODV(1, 3, 0), MODV(1, 5, 0))
            final_phase(xB)
    return nc


def _host_prep(inputs):
    f = lambda a: np.ascontiguousarray(np.asarray(a, dtype=np.float32))
    x = inputs["x"]; ctx = inputs["ctx"]
    common = {
        "w_mod": f(inputs["w_mod"]), "b_mod": f(inputs["b_mod"]), "g_mix": f(inputs["g_mix"]),
        "g_ffn": f(inputs["g_ffn"]), "g_out": f(inputs["g_out"]), "w_in": f(inputs["w_in"][0]),
        "ssm_d": f(inputs["ssm_d"][0]), "w_glu": f(inputs["ssm_w_glu"][0]), "w_out": f(inputs["w_out"][0]),
        "pw1": f(inputs["cv_w_pw1"][0]), "dw_b": f(inputs["cv_dw_b"][0]), "ln_g": f(inputs["cv_ln_g"][0]),
        "ln_b": f(inputs["cv_ln_b"][0]), "pw2": f(inputs["cv_w_pw2"][0]), "w_up": f(inputs["ffn_w_up"]),
        "fcb": f(inputs["ffn_conv_b"]), "w_dn": f(inputs["ffn_w_down"]),
    }
    rpb = np.asarray(inputs["na_rpb"][0], np.float32)
    maps = []
    for core in range(8):
        b, half = core // 2, core % 2
        m = dict(common)
        if half == 0:
            m["xc"] = f(x[b]); m["ctxc"] = f(ctx[b]); dirs = (0, 1)
            m["dw_w"] = f(inputs["cv_dw_w"][0]); m["fcw"] = f(inputs["ffn_conv_w"])
        else:
            m["xc"] = f(x[b][::-1]); m["ctxc"] = f(ctx[b][::-1]); dirs = (1, 0)
            m["dw_w"] = f(inputs["cv_dw_w"][0][::-1]); m["fcw"] = f(inputs["ffn_conv_w"][:, ::-1])
        m["cvec"] = f(np.stack([inputs["c"][b], inputs["c_ctx"]]))
        dd = list(dirs)
        m["lam_re"] = f(inputs["ssm_lam_re"][0][dd].reshape(2, 4096))
        m["lam_im"] = f(inputs["ssm_lam_im"][0][dd].reshape(2, 4096))
        m["log_dt"] = f(np.repeat(inputs["ssm_log_dt"][0][dd][:, :, None], 64, axis=2).reshape(2, 4096))
        m["b_re"] = f(inputs["ssm_b_re"][0][dd].reshape(2, 4096, 16))
        m["b_im"] = f(inputs["ssm_b_im"][0][dd].reshape(2, 4096, 16))
        m["c_re"] = f(inputs["ssm_c_re"][0][dd].reshape(2, 1024, 64))
        m["c_im"] = f(inputs["ssm_c_im"][0][dd].reshape(2, 1024, 64))
        m["na_bias"] = _na_bias_table(rpb, half)
        maps.append(m)
    return maps


def _na_bias_table(rpb, half):
    tab = np.full((3, 8, 128, 640), NEG, np.float32)
    for typ in range(3):
        r_even = 2 * typ if typ < 2 else 4
        kr0 = max(r_even - 4, 0)
        for qi in range(128):
            rl = r_even + qi // 64
            cl = qi % 64
            ro, co = (rl, cl) if half == 0 else (63 - rl, 63 - cl)
            rs = min(max(ro - 4, 0), 56)
            cs = min(max(co - 8, 0), 48)
            for kro in range(rs, rs + 8):
                krl = kro if half == 0 else 63 - kro
                jr = krl - kr0
                if jr < 0 or jr >= 10:
                    raise RuntimeError("key row outside block")
                ridx = kro - ro + 7
                for kco in range(cs, cs + 16):
                    kcl = kco if half == 0 else 63 - kco
                    cidx = kco - co + 15
                    tab[typ, :, qi, jr * 64 + kcl] = rpb[:, ridx, cidx]
    return tab


_NC_CACHE = {}


STAGE = 3
NCORES = 8
S5DBG = False
S5BAR = False
SBUF_REPORT = False
L0_JOB_FRAC = 0.45


def kernel(**inputs):
    maps = _host_prep(inputs)
    if "nc" not in _NC_CACHE:
        _NC_CACHE["nc"] = build_program(stage=STAGE)
    nc = _NC_CACHE["nc"]
    res = run_bass_kernel_spmd(nc, maps[:NCORES], core_ids=list(range(NCORES)))
    outp = np.zeros((4, NPOS, D), np.float32)
    for core in range(NCORES):
        b, half = core // 2, core % 2
        y = res.results[core]["out"]
        if half == 0:
            outp[b, :2048] = y
        else:
            outp[b, 2048:] = y[::-1]
    return outp
```
